# Optimizing a Trainium2 kernel written in Bass

```python
import math
import jax, jax.numpy as jnp
from jax import lax
import numpy as np

D_MODEL = 1024
BATCH = 2
SEQ = 8192
DEPTH = 4
DEC_BATCH = 32
DEC_SEQ = 4
PAST_LEN = 8192
PAGE_SIZE = 128

N_MIXERS = 3
N_A = len(range(0, DEPTH, N_MIXERS))
N_B = len(range(1, DEPTH, N_MIXERS))
N_C = len(range(2, DEPTH, N_MIXERS))
NORM_EPS = 1e-6
LN_EPS = 1e-5
DN_DK = 128
DN_DV = 128
DN_HK = D_MODEL // DN_DK
DN_HV = 2 * DN_HK
DN_CONV = 4
DN_CHUNK = 64
DN_QK = DN_HK * DN_DK
DN_VW = DN_HV * DN_DV
DN_CONV_DIM = 2 * DN_QK + DN_VW
DN_IN = DN_CONV_DIM + DN_VW + 2 * DN_HV
SG_WIDTH = D_MODEL
SG_CHUNK = 128
SG_GROUPS = 8
SG_GW = SG_WIDTH // SG_GROUPS
NSA_DH = 64
NSA_HQ = D_MODEL // NSA_DH
NSA_HKV = 4
NSA_G = NSA_HQ // NSA_HKV
CMP_STRIDE = 16
CMP_BLOCK = 2 * CMP_STRIDE
SLC_BLOCK = 64
N_SEL = 16
WINDOW = 512
Q_BLOCK = 128
FORCE_BONUS = 1e4
NSA_IN = NSA_HQ * NSA_DH + 6 * NSA_HKV * NSA_DH + 3 * NSA_HQ
ROPE_THETA = 10000.0
FFN_HIDDEN = ((8 * D_MODEL // 3 + 255) // 256) * 256

kernel_name = 'hybrid_deltanet_sgmlp_nsa_step'


def rms_norm(x, g, eps=NORM_EPS):
    xf = x.astype(jnp.float32)
    y = xf * lax.rsqrt(jnp.mean(xf * xf, -1, keepdims=True) + eps)
    return (y * g.astype(jnp.float32)).astype(x.dtype)


def layer_norm(x, g, b, eps=LN_EPS):
    xf = x.astype(jnp.float32)
    mu = jnp.mean(xf, -1, keepdims=True)
    xc = xf - mu
    y = xc * lax.rsqrt(jnp.mean(xc * xc, -1, keepdims=True) + eps)
    return (y * g.astype(jnp.float32) + b.astype(jnp.float32)).astype(x.dtype)


def l2_norm(x, eps=1e-6):
    return x * lax.rsqrt(jnp.sum(x * x, -1, keepdims=True) + eps)


def masked_softmax(s, mask):
    s = jnp.where(mask, s, -jnp.inf)
    m = jnp.max(s, -1, keepdims=True)
    m = jnp.where(jnp.isfinite(m), m, 0.0)
    p = jnp.where(mask, jnp.exp(s - m), 0.0)
    d = jnp.sum(p, -1, keepdims=True)
    return p / jnp.where(d > 0, d, 1.0)


def rope(x, pos):
    half = x.shape[-1] // 2
    inv = ROPE_THETA ** (-jnp.arange(half, dtype=jnp.float32) / half)
    ang = pos.astype(jnp.float32)[:, None] * inv[None, :]
    cos, sin = jnp.cos(ang)[None, :, None, :], jnp.sin(ang)[None, :, None, :]
    xf = x.astype(jnp.float32)
    x1, x2 = xf[..., :half], xf[..., half:]
    return jnp.concatenate([x1 * cos - x2 * sin, x2 * cos + x1 * sin], -1).astype(x.dtype)


def swiglu(x, wg, wu, wd):
    return (jax.nn.silu(x @ wg) * (x @ wu)) @ wd


def gated_delta_rule(q, k, v, g, beta, s0):
    B, L, H, _ = q.shape
    C = DN_CHUNK
    n = -(-L // C)
    pad = n * C - L

    def to_chunks(a):
        a = jnp.pad(a, [(0, 0), (0, pad)] + [(0, 0)] * (a.ndim - 2))
        return jnp.moveaxis(a.reshape((B, n, C) + a.shape[2:]), 3, 2)

    qc, kc, vc, gc, bc = [to_chunks(a) for a in (q, k, v, g, beta)]
    gcum = jnp.cumsum(gc, -1)
    tri = jnp.tril(jnp.ones((C, C), bool))
    stri = jnp.tril(jnp.ones((C, C), bool), -1)
    decay = jnp.exp(jnp.where(tri, gcum[..., :, None] - gcum[..., None, :], -jnp.inf))
    kb = kc * bc[..., None]
    a_mat = jnp.where(stri, jnp.einsum('bnhid,bnhjd->bnhij', kb, kc) * decay, 0.0) + jnp.eye(C, dtype=jnp.float32)
    rhs = jnp.concatenate([vc * bc[..., None], kb * jnp.exp(gcum)[..., None]], -1)
    sol = lax.linalg.triangular_solve(a_mat, rhs, left_side=True, lower=True)
    u, w = sol[..., :DN_DV], sol[..., DN_DV:]
    qk = jnp.where(tri, jnp.einsum('bnhid,bnhjd->bnhij', qc, kc) * decay, 0.0)
    qg = qc * jnp.exp(gcum)[..., None]
    kdec = kc * jnp.exp(gcum[..., -1:] - gcum)[..., None]
    glast = jnp.exp(gcum[..., -1])

    def step(S, xs):
        qk_i, qg_i, u_i, w_i, kdec_i, gl_i = xs
        v_new = u_i - jnp.einsum('bhck,bhkv->bhcv', w_i, S)
        o = jnp.einsum('bhck,bhkv->bhcv', qg_i, S) + jnp.einsum('bhij,bhjv->bhiv', qk_i, v_new)
        S = S * gl_i[..., None, None] + jnp.einsum('bhck,bhcv->bhkv', kdec_i, v_new)
        return S, o

    xs = tuple(jnp.moveaxis(a, 1, 0) for a in (qk, qg, u, w, kdec, glast))
    S, o = lax.scan(step, s0, xs)
    o = jnp.moveaxis(jnp.moveaxis(o, 0, 1), 2, 3).reshape(B, n * C, H, DN_DV)[:, :L]
    return o, S


def delta_mixer(x, conv_buf, s0, w_in, conv_w, a_log, dt_bias, norm_g, w_out):
    B, L, _ = x.shape
    f32 = jnp.float32
    proj = x @ w_in
    qkv, z, b_raw, a_raw = jnp.split(proj, [DN_CONV_DIM, DN_CONV_DIM + DN_VW, DN_CONV_DIM + DN_VW + DN_HV], axis=-1)
    full = jnp.concatenate([conv_buf.astype(x.dtype), qkv], axis=1)
    conv = sum(full[:, j:j + L] * conv_w[j] for j in range(DN_CONV))
    conv = jax.nn.silu(conv)
    new_buf = full[:, L:]
    q, k, v = jnp.split(conv, [DN_QK, 2 * DN_QK], axis=-1)
    rep = DN_HV // DN_HK
    q = jnp.repeat(l2_norm(q.reshape(B, L, DN_HK, DN_DK).astype(f32)) * DN_DK ** -0.5, rep, axis=2)
    k = jnp.repeat(l2_norm(k.reshape(B, L, DN_HK, DN_DK).astype(f32)), rep, axis=2)
    v = v.reshape(B, L, DN_HV, DN_DV).astype(f32)
    beta = jax.nn.sigmoid(b_raw.astype(f32))
    g = -jnp.exp(a_log.astype(f32)) * jax.nn.softplus(a_raw.astype(f32) + dt_bias.astype(f32))
    o, S = gated_delta_rule(q, k, v, g, beta, s0.astype(f32))
    o = rms_norm(o, norm_g) * jax.nn.silu(z.reshape(B, L, DN_HV, DN_DV).astype(f32))
    y = o.reshape(B, L, DN_VW).astype(x.dtype) @ w_out
    return y, S.astype(s0.dtype), new_buf


def spatial_gating_mixer(x, w_in, ln_g, ln_b, w_sp, b_sp, w_out):
    B, L, _ = x.shape
    u, v = jnp.split(jax.nn.gelu(x @ w_in), 2, axis=-1)
    v = layer_norm(v, ln_g, ln_b)
    n = -(-L // SG_CHUNK)
    vc = jnp.pad(v, ((0, 0), (0, n * SG_CHUNK - L), (0, 0))).reshape(B, n, SG_CHUNK, SG_GROUPS, SG_GW)
    mixed = jnp.einsum('gts,bnsgc->bntgc', jnp.tril(w_sp), vc) + b_sp.T[None, None, :, :, None]
    mixed = mixed.reshape(B, n * SG_CHUNK, SG_WIDTH)[:, :L]
    return (u * mixed) @ w_out, v


def nsa_project(x, pos, w_in, q_norm, k_norm):
    B, L, _ = x.shape
    qw, hd = NSA_HQ * NSA_DH, NSA_HKV * NSA_DH
    parts = jnp.split(x @ w_in, [qw + i * hd for i in range(7)], axis=-1)
    q, kc, vc, ks, vs, kw, vw, gate = parts
    heads = lambda a, h: a.reshape(B, L, h, NSA_DH)
    q = rms_norm(heads(q, NSA_HQ), q_norm)
    q_rot = rope(q, pos)
    ks = rope(rms_norm(heads(ks, NSA_HKV), k_norm[1]), pos)
    kw = rope(rms_norm(heads(kw, NSA_HKV), k_norm[2]), pos)
    gate = jax.nn.sigmoid(gate.astype(jnp.float32)).reshape(B, L, NSA_HQ, 3)
    return (q, q_rot, gate, heads(kc, NSA_HKV), heads(vc, NSA_HKV), ks, heads(vs, NSA_HKV), kw, heads(vw, NSA_HKV))


def compress_rows(rows, pe, w):
    B, T = rows.shape[:2]
    n_seg = -(-T // CMP_STRIDE)
    rows = jnp.pad(rows, ((0, 0), (0, n_seg * CMP_STRIDE - T), (0, 0), (0, 0)))
    seg = rows.reshape(B, n_seg, CMP_STRIDE, NSA_HKV, NSA_DH)
    w = w.reshape(CMP_BLOCK, NSA_DH, NSA_DH)
    pe_term = jnp.einsum('ld,lde->e', pe, w)
    return (jnp.einsum('bnlhd,lde->bnhe', seg[:, :-1], w[:CMP_STRIDE])
            + jnp.einsum('bnlhd,lde->bnhe', seg[:, 1:], w[CMP_STRIDE:]) + pe_term)


def nsa_attend(q, q_rot, gate, pos0, kc_all, vc_all, ks_all, vs_all, slab_fn, k_norm_cmp, pe_k, pe_v, w_ck, w_cv):
    f32 = jnp.float32
    B, L = q.shape[:2]
    T = kc_all.shape[1]
    kcmp = rms_norm(compress_rows(kc_all, pe_k, w_ck), k_norm_cmp).astype(f32)
    vcmp = compress_rows(vc_all, pe_v, w_cv).astype(f32)
    n_cmp = kcmp.shape[1]
    cmp_start = jnp.arange(n_cmp) * CMP_STRIDE
    cmp_end = cmp_start + CMP_BLOCK - 1
    n_slc = -(-T // SLC_BLOCK)
    slc_start = jnp.arange(n_slc) * SLC_BLOCK
    ov = (jnp.minimum(cmp_start[:, None] + CMP_BLOCK, slc_start[None, :] + SLC_BLOCK)
          - jnp.maximum(cmp_start[:, None], slc_start[None, :]))
    cmp_to_slc = jnp.clip(ov, 0, None).astype(f32) / CMP_BLOCK

    def to_blocks(a):
        a = jnp.pad(a, ((0, 0), (0, n_slc * SLC_BLOCK - T), (0, 0), (0, 0)))
        return a.reshape(B, n_slc, SLC_BLOCK, NSA_HKV, NSA_DH).transpose(0, 3, 1, 2, 4)

    ksb, vsb = to_blocks(ks_all), to_blocks(vs_all)
    k_sel = min(N_SEL, n_slc)
    qb = min(Q_BLOCK, L)
    nblk = L // qb
    gather = jax.vmap(jax.vmap(lambda blk, i: blk[i]))
    scale = NSA_DH ** -0.5
    blk_ids = jnp.arange(n_slc)
    m_sel = k_sel * SLC_BLOCK

    def one_block(args):
        bi, qg, qrg, gt = args
        qpos = pos0 + bi * qb + jnp.arange(qb)
        qg = qg.reshape(B, qb, NSA_HKV, NSA_G, NSA_DH).astype(f32)
        qrg = qrg.reshape(B, qb, NSA_HKV, NSA_G, NSA_DH).astype(f32)
        s = jnp.einsum('bqhgd,bkhd->bhgqk', qg, kcmp) * scale
        p_cmp = masked_softmax(s, cmp_end[None, :] <= qpos[:, None])
        o_cmp = jnp.einsum('bhgqk,bkhd->bqhgd', p_cmp, vcmp)
        imp = jnp.einsum('bhgqk,kj->bhqj', p_cmp, cmp_to_slc)
        cur = (qpos // SLC_BLOCK)[:, None]
        eligible = blk_ids[None, :] * SLC_BLOCK <= qpos[:, None]
        forced = (blk_ids[None, :] == 0) | (blk_ids[None, :] == cur) | (blk_ids[None, :] == cur - 1)
        score = jnp.where(eligible, imp + jnp.where(forced, FORCE_BONUS, 0.0), -jnp.inf)
        _, idx = lax.top_k(score, k_sel)
        k_g = gather(ksb, idx).astype(f32).reshape(B, NSA_HKV, qb, m_sel, NSA_DH)
        v_g = gather(vsb, idx).astype(f32).reshape(B, NSA_HKV, qb, m_sel, NSA_DH)
        kpos = idx[..., None] * SLC_BLOCK + jnp.arange(SLC_BLOCK)
        m_slc = (kpos <= qpos[None, None, :, None, None]).reshape(B, NSA_HKV, 1, qb, m_sel)
        s = jnp.einsum('bqhgd,bhqmd->bhgqm', qrg, k_g) * scale
        o_slc = jnp.einsum('bhgqm,bhqmd->bqhgd', masked_softmax(s, m_slc), v_g)
        kw, vw, kwpos = slab_fn(bi)
        dpos = qpos[:, None] - kwpos[None, :]
        m_win = (dpos >= 0) & (dpos <= WINDOW) & (kwpos[None, :] >= 0)
        s = jnp.einsum('bqhgd,bkhd->bhgqk', qrg, kw.astype(f32)) * scale
        o_swa = jnp.einsum('bhgqk,bkhd->bqhgd', masked_softmax(s, m_win), vw.astype(f32))
        gt = gt.reshape(B, qb, NSA_HKV, NSA_G, 3)
        o = gt[..., 0:1] * o_cmp + gt[..., 1:2] * o_slc + gt[..., 2:3] * o_swa
        return o.reshape(B, qb, NSA_HQ, NSA_DH)

    def split_blocks(a):
        return jnp.moveaxis(a.reshape((B, nblk, qb) + a.shape[2:]), 1, 0)

    o = lax.map(one_block, (jnp.arange(nblk), split_blocks(q), split_blocks(q_rot), split_blocks(gate)))
    return jnp.moveaxis(o, 0, 1).reshape(B, L, NSA_HQ, NSA_DH)


def nsa_mixer(x, pos0, past, w_in, q_norm, k_norm, pe_k, pe_v, w_ck, w_cv, w_out):
    B, L, _ = x.shape
    q, q_rot, gate, kc, vc, ks, vs, kw, vw = nsa_project(x, pos0 + jnp.arange(L), w_in, q_norm, k_norm)
    if past is None:
        kc_all, vc_all, ks_all, vs_all = kc, vc, ks, vs
        qb = min(Q_BLOCK, L)
        padw = ((0, 0), (WINDOW, 0), (0, 0), (0, 0))
        kw_pad, vw_pad = jnp.pad(kw, padw), jnp.pad(vw, padw)

        def slab_fn(bi):
            start = bi * qb
            size = WINDOW + qb
            return (lax.dynamic_slice_in_dim(kw_pad, start, size, 1),
                    lax.dynamic_slice_in_dim(vw_pad, start, size, 1),
                    pos0 + start - WINDOW + jnp.arange(size))
        n_keep = min(WINDOW, L)
        win_k, win_v = kw[:, L - n_keep:], vw[:, L - n_keep:]
    else:
        pc_k, pc_v, ps_k, ps_v, buf_k, buf_v = past
        cat = lambda a, b: jnp.concatenate([a.astype(b.dtype), b], axis=1)
        kc_all, vc_all, ks_all, vs_all = cat(pc_k, kc), cat(pc_v, vc), cat(ps_k, ks), cat(ps_v, vs)
        kw_all, vw_all = cat(buf_k, kw), cat(buf_v, vw)
        wb = buf_k.shape[1]
        kw_pos = pos0 - wb + jnp.arange(wb + L)
        slab_fn = lambda bi: (kw_all, vw_all, kw_pos)
        win_k, win_v = kw_all[:, L:], vw_all[:, L:]
    o = nsa_attend(q, q_rot, gate, pos0, kc_all, vc_all, ks_all, vs_all, slab_fn, k_norm[0], pe_k, pe_v, w_ck, w_cv)
    y = o.reshape(B, L, NSA_HQ * NSA_DH).astype(x.dtype) @ w_out
    return y, (kc, vc, ks, vs, win_k, win_v)


def paged_rows(pool, page_table):
    g = pool[page_table]
    return g.reshape((g.shape[0], g.shape[1] * g.shape[2]) + g.shape[3:])


def setup_inputs(seed: int = 0) -> dict:
    key = jax.random.key(seed)
    keys = iter(jax.random.split(key, 48))
    f32 = jnp.float32
    D = D_MODEL

    def nrm(shape, scale=1.0):
        return jax.random.normal(next(keys), shape, f32) * scale

    def gain(shape):
        return 1.0 + nrm(shape, 0.05)

    n_pages = PAST_LEN // PAGE_SIZE
    n_used = DEC_BATCH * n_pages
    n_pool = n_used + n_used // 4
    wb = min(WINDOW, PAST_LEN)
    pool_shape = (N_C, n_pool, PAGE_SIZE, NSA_HKV, NSA_DH)
    x_prompt = nrm((BATCH, SEQ, D))
    x_sample = nrm((DEC_BATCH, DEC_SEQ, D))
    state_delta = nrm((N_A, DEC_BATCH, DN_HV, DN_DK, DN_DV), 0.1)
    state_conv = nrm((N_A, DEC_BATCH, DN_CONV - 1, DN_CONV_DIM))
    cache_swa_k = nrm((N_C, DEC_BATCH, wb, NSA_HKV, NSA_DH))
    cache_swa_v = nrm((N_C, DEC_BATCH, wb, NSA_HKV, NSA_DH))
    cache_cmp_k = nrm(pool_shape)
    cache_cmp_v = nrm(pool_shape)
    cache_slc_k = nrm(pool_shape)
    cache_slc_v = nrm(pool_shape)
    page_table = jax.random.permutation(next(keys), n_pool)[:n_used].reshape(DEC_BATCH, n_pages).astype(jnp.int32)
    norm_mix = gain((DEPTH, D))
    norm_ffn = gain((DEPTH, D))
    dn_w_in = nrm((N_A, D, DN_IN), D ** -0.5)
    dn_conv_w = nrm((N_A, DN_CONV, DN_CONV_DIM), DN_CONV ** -0.5)
    dn_a_log = jnp.log(jax.random.uniform(next(keys), (N_A, DN_HV), f32, 1.0, 16.0))
    dt = jnp.exp(jax.random.uniform(next(keys), (N_A, DN_HV), f32, math.log(1e-3), math.log(1e-1)))
    dn_dt_bias = dt + jnp.log(-jnp.expm1(-dt))
    dn_norm = gain((N_A, DN_DV))
    dn_w_out = nrm((N_A, DN_VW, D), DN_VW ** -0.5)
    sg_w_in = nrm((N_B, D, 2 * SG_WIDTH), D ** -0.5)
    sg_ln_g = gain((N_B, SG_WIDTH))
    sg_ln_b = nrm((N_B, SG_WIDTH), 0.02)
    sg_w_spatial = nrm((N_B, SG_GROUPS, SG_CHUNK, SG_CHUNK), SG_CHUNK ** -0.5)
    sg_b_spatial = 1.0 + nrm((N_B, SG_GROUPS, SG_CHUNK), 0.1)
    sg_w_out = nrm((N_B, SG_WIDTH, D), SG_WIDTH ** -0.5)
    nsa_w_in = nrm((N_C, D, NSA_IN), D ** -0.5)
    nsa_q_norm = gain((N_C, NSA_DH))
    nsa_k_norm = gain((N_C, 3, NSA_DH))
    nsa_cmp_pe_k = nrm((N_C, CMP_BLOCK, NSA_DH), 0.1)
    nsa_cmp_pe_v = nrm((N_C, CMP_BLOCK, NSA_DH), 0.1)
    nsa_cmp_w_k = nrm((N_C, CMP_BLOCK * NSA_DH, NSA_DH), (CMP_BLOCK * NSA_DH) ** -0.5)
    nsa_cmp_w_v = nrm((N_C, CMP_BLOCK * NSA_DH, NSA_DH), (CMP_BLOCK * NSA_DH) ** -0.5)
    nsa_w_out = nrm((N_C, NSA_HQ * NSA_DH, D), (NSA_HQ * NSA_DH) ** -0.5)
    ffn_w_gate = nrm((DEPTH, D, FFN_HIDDEN), D ** -0.5)
    ffn_w_up = nrm((DEPTH, D, FFN_HIDDEN), D ** -0.5)
    ffn_w_down = nrm((DEPTH, FFN_HIDDEN, D), FFN_HIDDEN ** -0.5)
    return {'x_prompt': x_prompt, 'x_sample': x_sample, 'state_delta': state_delta, 'state_conv': state_conv,
            'cache_swa_k': cache_swa_k, 'cache_swa_v': cache_swa_v, 'cache_cmp_k': cache_cmp_k,
            'cache_cmp_v': cache_cmp_v, 'cache_slc_k': cache_slc_k, 'cache_slc_v': cache_slc_v,
            'page_table': page_table, 'norm_mix': norm_mix, 'norm_ffn': norm_ffn,
            'dn_w_in': dn_w_in, 'dn_conv_w': dn_conv_w, 'dn_a_log': dn_a_log, 'dn_dt_bias': dn_dt_bias,
            'dn_norm': dn_norm, 'dn_w_out': dn_w_out, 'sg_w_in': sg_w_in, 'sg_ln_g': sg_ln_g,
            'sg_ln_b': sg_ln_b, 'sg_w_spatial': sg_w_spatial, 'sg_b_spatial': sg_b_spatial,
            'sg_w_out': sg_w_out, 'nsa_w_in': nsa_w_in, 'nsa_q_norm': nsa_q_norm, 'nsa_k_norm': nsa_k_norm,
            'nsa_cmp_pe_k': nsa_cmp_pe_k, 'nsa_cmp_pe_v': nsa_cmp_pe_v, 'nsa_cmp_w_k': nsa_cmp_w_k,
            'nsa_cmp_w_v': nsa_cmp_w_v, 'nsa_w_out': nsa_w_out, 'ffn_w_gate': ffn_w_gate,
            'ffn_w_up': ffn_w_up, 'ffn_w_down': ffn_w_down}


def reference(x_prompt, x_sample, state_delta, state_conv, cache_swa_k, cache_swa_v, cache_cmp_k, cache_cmp_v,
              cache_slc_k, cache_slc_v, page_table, norm_mix, norm_ffn, dn_w_in, dn_conv_w, dn_a_log, dn_dt_bias,
              dn_norm, dn_w_out, sg_w_in, sg_ln_g, sg_ln_b, sg_w_spatial, sg_b_spatial, sg_w_out, nsa_w_in,
              nsa_q_norm, nsa_k_norm, nsa_cmp_pe_k, nsa_cmp_pe_v, nsa_cmp_w_k, nsa_cmp_w_v, nsa_w_out,
              ffn_w_gate, ffn_w_up, ffn_w_down):
    past_len = page_table.shape[1] * cache_cmp_k.shape[2]
    bp = x_prompt.shape[0]
    hp, hs = x_prompt, x_sample
    dn_S_p, dn_S_s, dn_c_p, dn_c_s = [], [], [], []
    sg_v_s = []
    c_p = ([], [], [], [], [], [])
    c_s = ([], [], [], [], [], [])
    for i in range(DEPTH):
        kind, j = i % N_MIXERS, i // N_MIXERS
        ap, a_s = rms_norm(hp, norm_mix[i]), rms_norm(hs, norm_mix[i])
        if kind == 0:
            prm = (dn_w_in[j], dn_conv_w[j], dn_a_log[j], dn_dt_bias[j], dn_norm[j], dn_w_out[j])
            zero_buf = jnp.zeros((bp, DN_CONV - 1, DN_CONV_DIM), hp.dtype)
            zero_S = jnp.zeros((bp, DN_HV, DN_DK, DN_DV), state_delta.dtype)
            mp, Sp, cp = delta_mixer(ap, zero_buf, zero_S, *prm)
            ms, Ss, cs = delta_mixer(a_s, state_conv[j], state_delta[j], *prm)
            dn_S_p.append(Sp); dn_S_s.append(Ss); dn_c_p.append(cp); dn_c_s.append(cs)
        elif kind == 1:
            prm = (sg_w_in[j], sg_ln_g[j], sg_ln_b[j], sg_w_spatial[j], sg_b_spatial[j], sg_w_out[j])
            mp, _ = spatial_gating_mixer(ap, *prm)
            ms, vs = spatial_gating_mixer(a_s, *prm)
            sg_v_s.append(vs)
        else:
            prm = (nsa_w_in[j], nsa_q_norm[j], nsa_k_norm[j], nsa_cmp_pe_k[j], nsa_cmp_pe_v[j],
                   nsa_cmp_w_k[j], nsa_cmp_w_v[j], nsa_w_out[j])
            past = (paged_rows(cache_cmp_k[j], page_table), paged_rows(cache_cmp_v[j], page_table),
                    paged_rows(cache_slc_k[j], page_table), paged_rows(cache_slc_v[j], page_table),
                    cache_swa_k[j], cache_swa_v[j])
            mp, new_p = nsa_mixer(ap, 0, None, *prm)
            ms, new_s = nsa_mixer(a_s, past_len, past, *prm)
            for lst, a in zip(c_p, new_p):
                lst.append(a)
            for lst, a in zip(c_s, new_s):
                lst.append(a)
        hp = hp + mp.astype(hp.dtype)
        hs = hs + ms.astype(hs.dtype)
        hp = hp + swiglu(rms_norm(hp, norm_ffn[i]), ffn_w_gate[i], ffn_w_up[i], ffn_w_down[i]).astype(hp.dtype)
        hs = hs + swiglu(rms_norm(hs, norm_ffn[i]), ffn_w_gate[i], ffn_w_up[i], ffn_w_down[i]).astype(hs.dtype)
    return (hp, hs,
            jnp.stack(dn_S_p), jnp.stack(dn_S_s), jnp.stack(dn_c_p), jnp.stack(dn_c_s),
            jnp.stack(sg_v_s),
            jnp.stack(c_p[4]), jnp.stack(c_p[5]), jnp.stack(c_s[4]), jnp.stack(c_s[5]),
            jnp.stack(c_p[0]), jnp.stack(c_p[1]), jnp.stack(c_p[2]), jnp.stack(c_p[3]),
            jnp.stack(c_s[0]), jnp.stack(c_s[1]), jnp.stack(c_s[2]), jnp.stack(c_s[3]))
```

```python
import contextlib
import numpy as np
import concourse.bass as bass
import concourse.mybir as mybir
from concourse.bass_utils import run_bass_kernel_spmd

F32 = mybir.dt.float32
BF16 = mybir.dt.bfloat16
I32 = mybir.dt.int32
AF = mybir.ActivationFunctionType
ALU = mybir.AluOpType
AX = mybir.AxisListType

D = 1024
KC = 8
FH = 2816
HC = 22
DEPTH = 4
NCORES = 8
SB = 4
SL = 4
ST = SB * SL
NORM_EPS = 1e-6


class T:
    def __init__(self, h, name):
        self.h = h
        self.name = name
        self.w = None
        self.r = {}

    def __getitem__(self, k):
        return self.h[k]


class Prog:
    def __init__(self, nc):
        self.nc = nc
        self.engs = {}
        self.stack = []
        for nm, h in (("pe", nc.tensor), ("act", nc.scalar), ("dve", nc.vector), ("pool", nc.gpsimd), ("sp", nc.sync)):
            self.engs[nm] = dict(h=h, sem=self._sem("s_" + nm), cnt=0, known={})
        self.dq = {}
        for q in ("sp", "pool"):
            self.dq[q] = dict(sems=[self._sem(f"d_{q}{i}") for i in range(8)], cnt=[0] * 8, nxt=0)
        self.ninst = 0

    def _sem(self, name):
        cm = self.nc.semaphore(name)
        s = cm.__enter__()
        self.stack.append(cm)
        return s

    def sb(self, name, shape, dt):
        self.uid = getattr(self, "uid", 0) + 1
        cm = self.nc.sbuf_tensor(f"sb{self.uid}_{name}", shape, dt)
        h = cm.__enter__()
        self.stack.append(cm)
        return T(h, name)

    def ps(self, name, shape, dt):
        self.uid = getattr(self, "uid", 0) + 1
        cm = self.nc.psum_tensor(f"ps{self.uid}_{name}", shape, dt)
        h = cm.__enter__()
        self.stack.append(cm)
        t = T(h, name)
        t.psum = True
        return t

    def dram(self, name, shape, dt, kind="Internal"):
        h = self.nc.dram_tensor(name, shape, dt, kind=kind)
        return T(h.ap(), name)

    @contextlib.contextmanager
    def scope(self):
        n = len(self.stack)
        self.barrier()
        yield
        self.barrier()
        while len(self.stack) > n:
            self.stack.pop().__exit__(None, None, None)

    def _need(self, e, src, n):
        E = self.engs[e]
        if E["known"].get(src, 0) >= n:
            return
        if src in self.engs:
            sem, val = self.engs[src]["sem"], n
        else:
            q, i = src
            sem, val = self.dq[q]["sems"][i], 16 * n
        E["h"].wait_ge(sem, val)
        E["known"][src] = n
        self.ninst += 1

    def _deps(self, e, reads, writes, same_ok=False):
        for t in reads:
            if t.w is not None and not (same_ok and t.w[0] == e):
                self._need(e, *t.w)
            if getattr(t, "psum", False):
                for src, n in t.r.items():
                    if src != e:
                        self._need(e, src, n)
        for t in writes:
            if t.w is not None and not (same_ok and t.w[0] == e):
                self._need(e, *t.w)
            for src, n in t.r.items():
                if src == e:
                    continue
                self._need(e, src, n)

    def _mark(self, stream, n, reads, writes):
        for t in writes:
            t.w = (stream, n)
            t.r = {}
        for t in reads:
            if t in writes:
                continue
            t.r[stream] = n

    def op(self, e, fn, reads=(), writes=(), same_ok=False):
        E = self.engs[e]
        if e != "pe":
            same_ok = False
        self._deps(e, reads, writes, same_ok)
        ins = fn(E["h"])
        E["cnt"] += 1
        ins.then_inc(E["sem"], 1)
        self._mark(e, E["cnt"], reads, writes)
        self.ninst += 1
        return ins

    def dma(self, q, out, in_, reads=(), writes=(), **kw):
        Q = self.dq[q]
        i = Q["nxt"]
        Q["nxt"] = (i + 1) % len(Q["sems"])
        stream = (q, i)
        if Q["cnt"][i] > 0:
            self._need(q, stream, Q["cnt"][i])
        self._deps(q, reads, writes)
        ins = self.engs[q]["h"].dma_start(out=out, in_=in_, **kw)
        Q["cnt"][i] += 1
        ins.then_inc(Q["sems"][i], 16)
        self._mark(stream, Q["cnt"][i], reads, writes)
        self.ninst += 1
        return ins

    def idma(self, out, in_, idx_ap, reads=(), writes=()):
        Q = self.dq["pool"]
        i = Q["nxt"]
        Q["nxt"] = (i + 1) % len(Q["sems"])
        stream = ("pool", i)
        if Q["cnt"][i] > 0:
            self._need("pool", stream, Q["cnt"][i])
        self._deps("pool", reads, writes)
        ins = self.nc.gpsimd.indirect_dma_start(out=out, out_offset=None, in_=in_,
                                                in_offset=bass.IndirectOffsetOnAxis(ap=idx_ap, axis=0))
        Q["cnt"][i] += 1
        ins.then_inc(Q["sems"][i], 16)
        self._mark(stream, Q["cnt"][i], reads, writes)
        self.ninst += 1
        return ins

    def barrier(self):
        streams = [(e, self.engs[e]["cnt"]) for e in ("pe", "act", "dve", "pool") if self.engs[e]["cnt"]]
        for q in self.dq:
            for i, c in enumerate(self.dq[q]["cnt"]):
                if c:
                    streams.append(((q, i), c))
        for e in ("pe", "act", "dve", "pool", "sp"):
            for s, c in streams:
                if s != e:
                    self._need(e, s, c)

    def close(self):
        self.barrier()
        while self.stack:
            self.stack.pop().__exit__(None, None, None)


class Ctx:
    pass


def tiles_of(S):
    tl = [(t0, 512) for t0 in range(0, S, 512)]
    tl.append((S, ST))
    return tl


def phase_in(p, c):
    with p.scope():
        xin = [p.sb(f"xin{i}", [128, D], F32) for i in range(4)]
        ho = [p.sb(f"ho{i}", [128, KC, 512], F32) for i in range(2)]
        pst = [p.ps(f"pst{i}", [128, 512], F32) for i in range(4)]
        it = 0
        for ti, (t0, tw) in enumerate(tiles_of(c.S)):
            nsub = (tw + 127) // 128
            for j in range(nsub):
                r = min(128, tw - j * 128)
                src = c.xp[t0 + j * 128:t0 + j * 128 + r, :] if t0 < c.S else c.xs[:, :]
                p.dma("sp", xin[j][:r, :], src, writes=[xin[j]])
            hb = ho[ti % 2]
            for kc in range(KC):
                P = pst[it % 4]
                it += 1
                for j in range(nsub):
                    r = min(128, tw - j * 128)
                    p.op("pe", lambda e: e.transpose(P[:, j * 128:j * 128 + r], xin[j][:r, kc * 128:(kc + 1) * 128], c.ident[:r, :r]),
                         reads=[xin[j]], writes=[P], same_ok=True)
                eng = "dve" if kc % 2 == 0 else "act"
                if eng == "dve":
                    p.op("dve", lambda e: e.tensor_copy(out=hb[:, kc, :tw], in_=P[:, :tw]), reads=[P], writes=[hb], same_ok=True)
                else:
                    p.op("act", lambda e: e.copy(out=hb[:, kc, :tw], in_=P[:, :tw]), reads=[P], writes=[hb], same_ok=True)
            p.dma("pool", c.H[:, :, t0:t0 + tw].rearrange("k p t -> p k t"), hb[:, :, :tw], reads=[hb], writes=[c.H])


def phase_out(p, c):
    with p.scope():
        hi = [p.sb(f"hi{i}", [128, KC, 512], F32) for i in range(2)]
        yo = [p.sb(f"yo{i}", [128, D], F32) for i in range(2)]
        pst = [p.ps(f"pso{i}", [128, 512], F32) for i in range(4)]
        it = 0
        for ti, (t0, tw) in enumerate(tiles_of(c.S)):
            hb = hi[ti % 2]
            p.dma("sp", hb[:, :, :tw], c.H[:, :, t0:t0 + tw].rearrange("k p t -> p k t"), reads=[c.H], writes=[hb])
            nsub = (tw + 127) // 128
            for j in range(nsub):
                r = min(128, tw - j * 128)
                yb = yo[it % 2]
                for half in range(2):
                    P = pst[(2 * it + half) % 4]
                    for k4 in range(4):
                        kc = half * 4 + k4
                        p.op("pe", lambda e: e.transpose(P[:r, k4 * 128:(k4 + 1) * 128], hb[:, kc, j * 128:j * 128 + r], c.ident[:, :]),
                             reads=[hb], writes=[P], same_ok=True)
                    if half == 0:
                        p.op("dve", lambda e: e.tensor_copy(out=yb[:r, 0:512], in_=P[:r, :]), reads=[P], writes=[yb], same_ok=True)
                    else:
                        p.op("act", lambda e: e.copy(out=yb[:r, 512:1024], in_=P[:r, :]), reads=[P], writes=[yb], same_ok=True)
                it += 1
                dst = c.yp[t0 + j * 128:t0 + j * 128 + r, :] if t0 < c.S else c.ys[:, :]
                p.dma("pool", dst, yb[:r, :], reads=[yb], writes=[c.yp if t0 < c.S else c.ys])


def rmsnorm_tile(p, c, hT, xn, tw, gcol, ps_stat, tmp, rstd):
    for kc in range(KC):
        sq = tmp[kc % 2]
        p.op("act", lambda e: e.activation(out=sq[:, :tw], in_=hT[:, kc, :tw], func=AF.Square), reads=[hT], writes=[sq])
        p.op("pe", lambda e: e.matmul(ps_stat[:, :tw], lhsT=c.onesD[:, :], rhs=sq[:, :tw], start=(kc == 0), stop=(kc == KC - 1)),
             reads=[sq, c.onesD], writes=[ps_stat], same_ok=True)
    p.op("act", lambda e: e.activation(out=rstd[:, :tw], in_=ps_stat[:, :tw], func=AF.Sqrt, bias=c.eps_norm[:, 0:1], scale=1.0),
         reads=[ps_stat, c.eps_norm], writes=[rstd])
    p.op("dve", lambda e: e.reciprocal(out=rstd[:, :tw], in_=rstd[:, :tw]), reads=[rstd], writes=[rstd])
    for kc in range(KC):
        p.op("dve", lambda e: e.scalar_tensor_tensor(out=xn[:, kc, :tw], in0=hT[:, kc, :tw], scalar=gcol(kc), in1=rstd[:, :tw],
                                                   op0=ALU.mult, op1=ALU.mult), reads=[hT, rstd], writes=[xn], same_ok=True)


def phase_ffn(p, c, li):
    with p.scope():
        wg = p.sb("wg", [128, KC, FH], BF16)
        wu = p.sb("wu", [128, KC, FH], BF16)
        wd = p.sb("wd", [128, HC, D], BF16)
        for kc in range(KC):
            p.dma("pool", wg[:, kc, :], c.w_gate[li, :, kc, :], writes=[wg])
            p.dma("pool", wu[:, kc, :], c.w_up[li, :, kc, :], writes=[wu])
        for hc in range(HC):
            p.dma("pool", wd[:, hc, :], c.w_down[li, :, hc, :], writes=[wd])
        hTs = [p.sb(f"hT{i}", [128, KC, 512], F32) for i in range(2)]
        xn = p.sb("xn", [128, KC, 512], BF16)
        hh = p.sb("hh", [128, HC, 512], BF16)
        tmp = [p.sb(f"sq{i}", [128, 512], F32) for i in range(2)]
        rstd = p.sb("rstd", [128, 512], F32)
        sg = [p.sb(f"sg{i}", [128, 512], F32) for i in range(2)]
        ps_stat = p.ps("ps_stat", [128, 512], F32)
        psg = [p.ps(f"psg{i}", [128, 512], F32) for i in range(2)]
        psu = [p.ps(f"psu{i}", [128, 512], F32) for i in range(2)]
        pso = [p.ps(f"pso{i}", [128, 512], F32) for i in range(2)]
        it = 0
        io = 0
        for ti, (t0, tw) in enumerate(tiles_of(c.S)):
            hT = hTs[ti % 2]
            p.dma("sp", hT[:, :, :tw], c.H[:, :, t0:t0 + tw].rearrange("k p t -> p k t"), reads=[c.H], writes=[hT])
            rmsnorm_tile(p, c, hT, xn, tw, lambda kc: c.nffn[:, li, kc:kc + 1], ps_stat, tmp, rstd)
            for hc in range(HC):
                G, U, S_ = psg[it % 2], psu[it % 2], sg[it % 2]
                it += 1
                for kc in range(KC):
                    p.op("pe", lambda e: e.matmul(G[:, :tw], lhsT=wg[:, kc, hc * 128:(hc + 1) * 128], rhs=xn[:, kc, :tw],
                                                  start=(kc == 0), stop=(kc == KC - 1)), reads=[wg, xn], writes=[G], same_ok=True)
                for kc in range(KC):
                    p.op("pe", lambda e: e.matmul(U[:, :tw], lhsT=wu[:, kc, hc * 128:(hc + 1) * 128], rhs=xn[:, kc, :tw],
                                                  start=(kc == 0), stop=(kc == KC - 1)), reads=[wu, xn], writes=[U], same_ok=True)
                p.op("act", lambda e: e.activation(out=S_[:, :tw], in_=G[:, :tw], func=AF.Silu), reads=[G], writes=[S_])
                p.op("dve", lambda e: e.tensor_tensor(out=hh[:, hc, :tw], in0=S_[:, :tw], in1=U[:, :tw], op=ALU.mult),
                     reads=[S_, U], writes=[hh], same_ok=True)
            for oc in range(KC):
                O = pso[io % 2]
                io += 1
                for hc in range(HC):
                    p.op("pe", lambda e: e.matmul(O[:, :tw], lhsT=wd[:, hc, oc * 128:(oc + 1) * 128], rhs=hh[:, hc, :tw],
                                                  start=(hc == 0), stop=(hc == HC - 1)), reads=[wd, hh], writes=[O], same_ok=True)
                p.op("dve", lambda e: e.tensor_tensor(out=hT[:, oc, :tw], in0=hT[:, oc, :tw], in1=O[:, :tw], op=ALU.add),
                     reads=[O, hT], writes=[hT], same_ok=True)
            p.dma("sp", c.H[:, :, t0:t0 + tw].rearrange("k p t -> p k t"), hT[:, :, :tw], reads=[hT], writes=[c.H])


LN_EPS = 1e-5
import os
SGSTOP = int(os.environ.get('SGSTOP', '9'))
SGSUB = int(os.environ.get('SGSUB', '9'))
SGEV = os.environ.get('SGEV', 'dve')
DNSTOP = int(os.environ.get('DNSTOP', '9'))
NSSTOP = int(os.environ.get('NSSTOP', '9'))
DN2STOP = int(os.environ.get('DN2STOP', '9'))
MIXERS = (0, 1, 2)
GELU_C = 1.5957691216057308


def phase_sg(p, c, li, j):
    S = c.S
    with p.scope():
        win = p.sb("sgwin", [128, KC, 2048], BF16)
        wout = p.sb("sgwout", [128, KC, D], BF16)
        for kc in range(KC):
            p.dma("pool", win[:, kc, :], c.sg_w_in[j, :, kc, :], writes=[win])
            p.dma("pool", wout[:, kc, :], c.sg_w_out[j, :, kc, :], writes=[wout])
        lng = p.sb("lng", [128, KC], F32)
        lnb = p.sb("lnb", [128, KC], F32)
        p.dma("sp", lng[:, :], c.sg_ln_g[j], writes=[lng])
        p.dma("sp", lnb[:, :], c.sg_ln_b[j], writes=[lnb])
        wspf = p.sb("wspf", [128, KC, 128], F32)
        wsp = p.sb("wsp", [128, KC, 128], BF16)
        p.dma("sp", wspf[:, :, :], c.sg_wspT[j], writes=[wspf])
        for g in range(KC):
            p.op("pool", lambda e: e.affine_select(out=wspf[:, g, :], in_=wspf[:, g, :], pattern=[[1, 128]], compare_op=ALU.is_ge,
                                                   fill=0.0, base=0, channel_multiplier=-1), reads=[wspf], writes=[wspf], same_ok=True)
        p.op("dve", lambda e: e.tensor_copy(out=wsp[:, :, :], in_=wspf[:, :, :]), reads=[wspf], writes=[wsp])
        bsp = p.sb("bsp", [128, KC, 512], F32)
        p.dma("sp", bsp[:, :, :], c.sg_bsp4[j].partition_broadcast(128), writes=[bsp])
        wsm_f = p.sb("wsmf", [ST, KC, ST], F32)
        wsm = p.sb("wsm", [ST, KC, ST], BF16)
        p.op("dve", lambda e: e.memset(wsm_f[:, :, :], 0.0), writes=[wsm_f])
        for b in range(SB):
            p.dma("sp", wsm_f[b * SL:(b + 1) * SL, :, b * SL:(b + 1) * SL], wspf[0:SL, :, 0:SL], reads=[wspf], writes=[wsm_f])
        p.op("dve", lambda e: e.tensor_copy(out=wsm[:, :, :], in_=wsm_f[:, :, :]), reads=[wsm_f], writes=[wsm])
        bsm = p.sb("bsm", [128, KC, ST], F32)
        for b in range(SB):
            p.op("pool", lambda e: e.tensor_copy(out=bsm[:, :, b * SL:(b + 1) * SL], in_=bsp[:, :, 0:SL]), reads=[bsp], writes=[bsm], same_ok=True)
        eps_ln = p.sb("eps_ln", [128, 1], F32)
        p.op("dve", lambda e: e.memset(eps_ln[:, :], LN_EPS), writes=[eps_ln])

        hTs = [p.sb(f"hT{i}", [128, KC, 512], F32) for i in range(2)]
        xn = p.sb("xn", [128, KC, 512], BF16)
        uT = p.sb("uT", [128, KC, 512], BF16)
        vT = p.sb("vT", [128, KC, 512], F32)
        um = p.sb("um", [128, KC, 512], BF16)
        vtok = [p.sb(f"vtok{i}", [128, D], BF16) for i in range(4)]
        vtokf = p.sb("vtokf", [ST, D], F32)
        tmp = [p.sb(f"sq{i}", [128, 512], F32) for i in range(2)]
        t2 = [p.sb(f"t2{i}", [128, 512], F32) for i in range(2)]
        rstd = p.sb("rstd", [128, 512], F32)
        mean = p.sb("mean", [128, 512], F32)
        ps_stat = p.ps("ps_stat", [128, 512], F32)
        ps_s2 = p.ps("ps_s2", [128, 512], F32)
        psA = [p.ps(f"psA{i}", [128, 512], F32) for i in range(2)]
        psT = [p.ps(f"psT{i}", [128, 512], F32) for i in range(2)]
        psM = [p.ps(f"psM{i}", [128, 512], F32) for i in range(2)]
        ia = 0
        itr = 0
        im = 0
        for ti, (t0, tw) in enumerate(tiles_of(S)):
            smp = t0 >= S
            hT = hTs[ti % 2]
            p.dma("sp", hT[:, :, :tw], c.H[:, :, t0:t0 + tw].rearrange("k p t -> p k t"), reads=[c.H], writes=[hT])
            rmsnorm_tile(p, c, hT, xn, tw, lambda kc: c.nmix[:, li, kc:kc + 1], ps_stat, tmp, rstd)
            for fc in range(16):
                A = psA[ia % 2]
                x2, tt = tmp[ia % 2], t2[ia % 2]
                ia += 1
                for kc in range(KC):
                    p.op("pe", lambda e: e.matmul(A[:, :tw], lhsT=win[:, kc, fc * 128:(fc + 1) * 128], rhs=xn[:, kc, :tw],
                                                  start=(kc == 0), stop=(kc == KC - 1)), reads=[win, xn], writes=[A], same_ok=True)
                p.op("act", lambda e: e.activation(out=x2[:, :tw], in_=A[:, :tw], func=AF.Square), reads=[A], writes=[x2])
                p.op("dve", lambda e: e.tensor_scalar(out=x2[:, :tw], in0=x2[:, :tw], scalar1=0.044715, scalar2=1.0, op0=ALU.mult, op1=ALU.add),
                     reads=[x2], writes=[x2])
                p.op("dve", lambda e: e.tensor_tensor(out=x2[:, :tw], in0=x2[:, :tw], in1=A[:, :tw], op=ALU.mult), reads=[x2, A], writes=[x2])
                p.op("act", lambda e: e.activation(out=tt[:, :tw], in_=x2[:, :tw], func=AF.Sigmoid, scale=GELU_C), reads=[x2], writes=[tt])
                dst, dk = (uT, fc) if fc < 8 else (vT, fc - 8)
                p.op("dve", lambda e: e.tensor_tensor(out=dst[:, dk, :tw], in0=tt[:, :tw], in1=A[:, :tw], op=ALU.mult),
                     reads=[tt, A], writes=[dst], same_ok=True)
            for kc in range(KC if SGSTOP >= 2 else 0):
                sq = tmp[kc % 2]
                p.op("act", lambda e: e.activation(out=sq[:, :tw], in_=vT[:, kc, :tw], func=AF.Square), reads=[vT], writes=[sq])
                p.op("pe", lambda e: e.matmul(ps_s2[:, :tw], lhsT=c.onesD[:, :], rhs=sq[:, :tw], start=(kc == 0), stop=(kc == KC - 1)),
                     reads=[sq, c.onesD], writes=[ps_s2], same_ok=True)
            if SGSTOP < 2:
                p.dma("sp", c.H[:, :, t0:t0 + tw].rearrange("k p t -> p k t"), hT[:, :, :tw], reads=[hT], writes=[c.H])
                continue
            for kc in range(KC):
                p.op("pe", lambda e: e.matmul(ps_stat[:, :tw], lhsT=c.onesD[:, :], rhs=vT[:, kc, :tw], start=(kc == 0), stop=(kc == KC - 1)),
                     reads=[vT, c.onesD], writes=[ps_stat], same_ok=True)
            p.op("act", lambda e: e.copy(out=mean[:, :tw], in_=ps_stat[:, :tw]), reads=[ps_stat], writes=[mean])
            if SGSUB < 2:
                p.dma("sp", c.H[:, :, t0:t0 + tw].rearrange("k p t -> p k t"), hT[:, :, :tw], reads=[hT], writes=[c.H])
                continue
            m2 = tmp[0]
            p.op("dve", lambda e: e.tensor_tensor(out=m2[:, :tw], in0=mean[:, :tw], in1=mean[:, :tw], op=ALU.mult), reads=[mean], writes=[m2])
            p.op("dve", lambda e: e.tensor_tensor(out=m2[:, :tw], in0=ps_s2[:, :tw], in1=m2[:, :tw], op=ALU.subtract), reads=[ps_s2, m2], writes=[m2])
            if SGSUB < 3:
                p.dma("sp", c.H[:, :, t0:t0 + tw].rearrange("k p t -> p k t"), hT[:, :, :tw], reads=[hT], writes=[c.H])
                continue
            p.op("act", lambda e: e.activation(out=rstd[:, :tw], in_=m2[:, :tw], func=AF.Sqrt, bias=eps_ln[:, 0:1], scale=1.0),
                 reads=[m2, eps_ln], writes=[rstd])
            p.op("dve", lambda e: e.reciprocal(out=rstd[:, :tw], in_=rstd[:, :tw]), reads=[rstd], writes=[rstd])
            if SGSUB < 4:
                p.dma("sp", c.H[:, :, t0:t0 + tw].rearrange("k p t -> p k t"), hT[:, :, :tw], reads=[hT], writes=[c.H])
                continue
            for kc in range(KC):
                p.op("dve", lambda e: e.tensor_tensor(out=vT[:, kc, :tw], in0=vT[:, kc, :tw], in1=mean[:, :tw], op=ALU.subtract),
                     reads=[vT, mean], writes=[vT], same_ok=True)
                p.op("dve", lambda e: e.tensor_tensor(out=vT[:, kc, :tw], in0=vT[:, kc, :tw], in1=rstd[:, :tw], op=ALU.mult),
                     reads=[vT, rstd], writes=[vT], same_ok=True)
                p.op("dve", lambda e: e.tensor_scalar(out=vT[:, kc, :tw], in0=vT[:, kc, :tw], scalar1=lng[:, kc:kc + 1], scalar2=lnb[:, kc:kc + 1],
                                                      op0=ALU.mult, op1=ALU.add), reads=[vT, lng, lnb], writes=[vT], same_ok=True)
            if SGSTOP < 3:
                p.dma("sp", c.H[:, :, t0:t0 + tw].rearrange("k p t -> p k t"), hT[:, :, :tw], reads=[hT], writes=[c.H])
                continue
            nsub = (tw + 127) // 128
            for sj in range(nsub):
                r = min(128, tw - sj * 128)
                for half in range(2):
                    P = psT[itr % 2]
                    itr += 1
                    for g4 in range(4):
                        g = half * 4 + g4
                        p.op("pe", lambda e: e.transpose(P[:r, g4 * 128:(g4 + 1) * 128], vT[:, g, sj * 128:sj * 128 + r], c.ident[:, :]),
                             reads=[vT], writes=[P], same_ok=True)
                    eng = "act" if half else "dve"
                    if SGEV:
                        eng = SGEV
                    if SGSUB == 5:
                        continue
                    if smp:
                        p.op("dve", lambda e: e.tensor_copy(out=vtokf[:r, half * 512:(half + 1) * 512], in_=P[:r, :]),
                             reads=[P], writes=[vtokf], same_ok=True)
                    if SGSUB == 6:
                        continue
                    if eng == "dve":
                        p.op("dve", lambda e: e.tensor_copy(out=vtok[sj][:r, half * 512:(half + 1) * 512], in_=P[:r, :]),
                             reads=[P], writes=[vtok[sj]], same_ok=True)
                    else:
                        p.op("act", lambda e: e.copy(out=vtok[sj][:r, half * 512:(half + 1) * 512], in_=P[:r, :]),
                             reads=[P], writes=[vtok[sj]], same_ok=True)
            if smp and SGSUB > 7:
                p.dma("pool", c.o_sgv[j], vtokf[:, :], reads=[vtokf], writes=[c.o_sgv])
            if SGSTOP < 4:
                p.dma("sp", c.H[:, :, t0:t0 + tw].rearrange("k p t -> p k t"), hT[:, :, :tw], reads=[hT], writes=[c.H])
                continue
            for g in range(KC):
                M = psM[im % 2]
                tb = t2[im % 2]
                im += 1
                for sj in range(nsub):
                    r = min(128, tw - sj * 128)
                    rhs = wsm[:r, g, :r] if smp else wsp[:, g, :]
                    p.op("pe", lambda e: e.matmul(M[:, sj * 128:sj * 128 + r], lhsT=vtok[sj][:r, g * 128:(g + 1) * 128], rhs=rhs, start=True, stop=True),
                         reads=[vtok[sj], wsm if smp else wsp], writes=[M], same_ok=True)
                bias = bsm[:, g, :tw] if smp else bsp[:, g, :tw]
                p.op("dve", lambda e: e.tensor_tensor(out=tb[:, :tw], in0=M[:, :tw], in1=bias, op=ALU.add), reads=[M, bsm, bsp], writes=[tb])
                p.op("pool", lambda e: e.tensor_tensor(out=um[:, g, :tw], in0=tb[:, :tw], in1=uT[:, g, :tw], op=ALU.mult),
                     reads=[tb, uT], writes=[um])
            if SGSTOP < 5:
                p.dma("sp", c.H[:, :, t0:t0 + tw].rearrange("k p t -> p k t"), hT[:, :, :tw], reads=[hT], writes=[c.H])
                continue
            for oc in range(KC):
                O = psA[ia % 2]
                ia += 1
                for g in range(KC):
                    p.op("pe", lambda e: e.matmul(O[:, :tw], lhsT=wout[:, g, oc * 128:(oc + 1) * 128], rhs=um[:, g, :tw],
                                                  start=(g == 0), stop=(g == KC - 1)), reads=[wout, um], writes=[O], same_ok=True)
                p.op("dve", lambda e: e.tensor_tensor(out=hT[:, oc, :tw], in0=hT[:, oc, :tw], in1=O[:, :tw], op=ALU.add),
                     reads=[O, hT], writes=[hT], same_ok=True)
            p.dma("sp", c.H[:, :, t0:t0 + tw].rearrange("k p t -> p k t"), hT[:, :, :tw], reads=[hT], writes=[c.H])


DN_CC = 32
DN_W = 6208


def phase_dn1(p, c, li, j):
    S = c.S
    with p.scope():
        win = p.sb("dnwin", [128, KC, DN_W], BF16)
        for kc in range(KC):
            p.dma("pool", win[:, kc, :], c.dn_w_in[j, :, kc, :], writes=[win])
        cw = p.sb("cw", [128, DN_CC, 4], F32)
        p.dma("sp", cw[:, :, :], c.dn_cw[j], writes=[cw])
        ab = p.sb("ab", [64, 2], F32)
        p.dma("sp", ab[:, :], c.dn_ab[j], writes=[ab])
        nalog = p.sb("nalog", [64, 1], F32)
        p.op("act", lambda e: e.activation(out=nalog[:, :], in_=ab[:, 0:1], func=AF.Exp), reads=[ab], writes=[nalog])
        p.op("dve", lambda e: e.tensor_scalar(out=nalog[:, :], in0=nalog[:, :], scalar1=-1.0, scalar2=None, op0=ALU.mult),
             reads=[nalog], writes=[nalog])
        one_c = p.sb("one_c", [128, 1], F32)
        p.op("dve", lambda e: e.memset(one_c[:, :], 1.0), writes=[one_c])
        eps_c = p.sb("eps_c", [128, 1], F32)
        p.op("dve", lambda e: e.memset(eps_c[:, :], 1e-6), writes=[eps_c])
        carry = p.sb("carry", [128, DN_CC, 3], F32)
        p.op("dve", lambda e: e.memset(carry[:, :, :], 0.0), writes=[carry])
        cs_tok = p.sb("cs_tok", [SB * 3, 4096], F32)
        p.dma("sp", cs_tok[:, :], c.dn_cs_in[j], writes=[cs_tok])
        cs_fm = p.sb("cs_fm", [128, DN_CC, SB * 3], F32)
        newc = p.sb("newc", [128, DN_CC, SB * 3], F32)
        ptr = [p.ps(f"ptr{i}", [128, 512], F32) for i in range(2)]
        for cc in range(DN_CC):
            P = ptr[cc % 2]
            p.op("pe", lambda e: e.transpose(P[:, 0:SB * 3], cs_tok[:, cc * 128:(cc + 1) * 128], c.ident[:SB * 3, :SB * 3]),
                 reads=[cs_tok], writes=[P])
            p.op("dve", lambda e: e.tensor_copy(out=cs_fm[:, cc, :], in_=P[:, 0:SB * 3]), reads=[P], writes=[cs_fm])

        hTs = [p.sb(f"hT{i}", [128, KC, 512], F32) for i in range(2)]
        xn = p.sb("xn", [128, KC, 512], BF16)
        tmp = [p.sb(f"sq{i}", [128, 512], F32) for i in range(2)]
        rstd = p.sb("rstd", [128, 512], F32)
        full = [p.sb(f"full{i}", [128, 520], F32) for i in range(2)]
        fulls = [p.sb(f"fulls{i}", [128, SB, 7], F32) for i in range(2)]
        acc = [p.sb(f"acc{i}", [128, 512], F32) for i in range(2)]
        cs = [p.sb(f"cs{i}", [128, 512], F32) for i in range(2)]
        rn = [p.sb(f"rn{i}", [128, 512], F32) for i in range(2)]
        zo = [p.sb(f"zo{i}", [128, 512], BF16) for i in range(2)]
        gt = p.sb("gt", [64, 512], F32)
        ps_stat = p.ps("ps_stat", [128, 512], F32)
        psA = [p.ps(f"psA{i}", [128, 512], F32) for i in range(2)]
        psN = [p.ps(f"psN{i}", [128, 512], F32) for i in range(2)]
        p.op("dve", lambda e: e.memset(gt[:, :], 0.0), writes=[gt])
        ia = 0
        for ti, (t0, tw) in enumerate(tiles_of(S)):
            smp = t0 >= S
            hT = hTs[ti % 2]
            p.dma("sp", hT[:, :, :tw], c.H[:, :, t0:t0 + tw].rearrange("k p t -> p k t"), reads=[c.H], writes=[hT])
            rmsnorm_tile(p, c, hT, xn, tw, lambda kc: c.nmix[:, li, kc:kc + 1], ps_stat, tmp, rstd)
            for cc in range(DN_CC):
                A = psA[ia % 2]
                fu, ac, co, rr, NP = full[ia % 2], acc[ia % 2], cs[ia % 2], rn[ia % 2], psN[ia % 2]
                fs = fulls[ia % 2]
                ia += 1
                for kc in range(KC):
                    p.op("pe", lambda e: e.matmul(A[:, :tw], lhsT=win[:, kc, cc * 128:(cc + 1) * 128], rhs=xn[:, kc, :tw],
                                                  start=(kc == 0), stop=(kc == KC - 1)), reads=[win, xn], writes=[A], same_ok=True)
                if not smp:
                    p.op("act", lambda e: e.activation(out=fu[:, 3:3 + tw], in_=A[:, :tw], func=AF.Copy), reads=[A], writes=[fu])
                    p.op("dve", lambda e: e.tensor_copy(out=fu[:, 0:3], in_=carry[:, cc, :]), reads=[carry], writes=[fu])
                    p.op("dve", lambda e: e.tensor_copy(out=carry[:, cc, :], in_=fu[:, tw:tw + 3]), reads=[fu], writes=[carry])
                    taps = [fu[:, jj:jj + tw] for jj in range(4)]
                    accv, cov, rrv = ac[:, :tw], co[:, :tw], rr[:, :tw]
                    srcs = [fu]
                else:
                    p.op("dve", lambda e: e.tensor_copy(out=fs[:, :, 3:7], in_=A[:, :tw].rearrange("p (b t) -> p b t", b=SB)),
                         reads=[A], writes=[fs])
                    p.op("dve", lambda e: e.tensor_copy(out=fs[:, :, 0:3], in_=cs_fm[:, cc, :].rearrange("p (b t) -> p b t", b=SB)),
                         reads=[cs_fm], writes=[fs])
                    p.op("dve", lambda e: e.tensor_copy(out=newc[:, cc, :].rearrange("p (b t) -> p b t", b=SB), in_=fs[:, :, 4:7]),
                         reads=[fs], writes=[newc])
                    taps = [fs[:, :, jj:jj + SL] for jj in range(4)]
                    accv = ac[:, :tw].rearrange("p (b t) -> p b t", b=SB)
                    cov, rrv = co[:, :tw], rr[:, :tw]
                    srcs = [fs]
                p.op("dve", lambda e: e.tensor_scalar(out=accv, in0=taps[0], scalar1=cw[:, cc, 0:1], scalar2=None, op0=ALU.mult),
                     reads=srcs + [cw], writes=[ac])
                for jj in range(1, 4):
                    p.op("dve", lambda e: e.scalar_tensor_tensor(out=accv, in0=taps[jj], scalar=cw[:, cc, jj:jj + 1], in1=accv,
                                                                 op0=ALU.mult, op1=ALU.add), reads=srcs + [cw, ac], writes=[ac])
                p.op("act", lambda e: e.activation(out=cov, in_=ac[:, :tw], func=AF.Silu), reads=[ac], writes=[co])
                if cc < 16:
                    p.op("act", lambda e: e.activation(out=rrv, in_=cov, func=AF.Square), reads=[co], writes=[rr])
                    p.op("pe", lambda e: e.matmul(NP[:, :tw], lhsT=c.ones[:, :], rhs=rrv, start=True, stop=True),
                         reads=[rr, c.ones], writes=[NP])
                    p.op("act", lambda e: e.activation(out=rrv, in_=NP[:, :tw], func=AF.Sqrt, bias=eps_c[:, 0:1], scale=1.0),
                         reads=[NP, eps_c], writes=[rr])
                    p.op("dve", lambda e: e.reciprocal(out=rrv, in_=rrv), reads=[rr], writes=[rr])
                    sc = 128 ** -0.5 if cc < 8 else 1.0
                    p.op("dve", lambda e: e.scalar_tensor_tensor(out=cov, in0=cov, scalar=sc, in1=rrv, op0=ALU.mult, op1=ALU.mult),
                         reads=[co, rr], writes=[co])
                p.dma("sp", c.QKVT[cc, :, t0:t0 + tw], cov, reads=[co], writes=[c.QKVT])
            for zc in range(16):
                A = psA[ia % 2]
                Z = zo[ia % 2]
                ia += 1
                for kc in range(KC):
                    p.op("pe", lambda e: e.matmul(A[:, :tw], lhsT=win[:, kc, 4096 + zc * 128:4096 + (zc + 1) * 128], rhs=xn[:, kc, :tw],
                                                  start=(kc == 0), stop=(kc == KC - 1)), reads=[win, xn], writes=[A], same_ok=True)
                p.op("act", lambda e: e.activation(out=Z[:, :tw], in_=A[:, :tw], func=AF.Silu), reads=[A], writes=[Z])
                p.dma("sp", c.ZT[zc, :, t0:t0 + tw], Z[:, :tw], reads=[Z], writes=[c.ZT])
            A = psA[ia % 2]
            ia += 1
            for kc in range(KC):
                p.op("pe", lambda e: e.matmul(A[:64, :tw], lhsT=win[:, kc, 6144:6208], rhs=xn[:, kc, :tw],
                                              start=(kc == 0), stop=(kc == KC - 1)), reads=[win, xn], writes=[A], same_ok=True)
            p.op("act", lambda e: e.activation(out=gt[0:16, :tw], in_=A[0:16, :tw], func=AF.Sigmoid), reads=[A], writes=[gt])
            p.op("act", lambda e: e.activation(out=gt[32:48, :tw], in_=A[32:48, :tw], func=AF.Exp, bias=ab[32:48, 1:2], scale=1.0),
                 reads=[A, ab, gt], writes=[gt])
            p.op("act", lambda e: e.activation(out=gt[32:48, :tw], in_=gt[32:48, :tw], func=AF.Ln, bias=one_c[32:48, 0:1], scale=1.0),
                 reads=[gt, one_c], writes=[gt])
            p.op("dve", lambda e: e.tensor_scalar(out=gt[32:48, :tw], in0=gt[32:48, :tw], scalar1=nalog[32:48, 0:1], scalar2=None, op0=ALU.mult),
                 reads=[gt, nalog], writes=[gt])
            p.dma("sp", c.GT[:, t0:t0 + tw], gt[:, :tw], reads=[gt], writes=[c.GT])
        ctok = p.sb("ctok", [SB * 3, 4096], F32)
        for which in range(2):
            n = 3 if which == 0 else SB * 3
            for cc in range(DN_CC):
                P = ptr[cc % 2]
                src = carry[:, cc, :] if which == 0 else newc[:, cc, :]
                p.op("pe", lambda e: e.transpose(P[:n, 0:128], src, c.ident[:, :]), reads=[carry, newc], writes=[P])
                p.op("dve", lambda e: e.tensor_copy(out=ctok[:n, cc * 128:(cc + 1) * 128], in_=P[:n, 0:128]), reads=[P], writes=[ctok])
            if which == 0:
                p.dma("sp", c.o_dc_p[j], ctok[:3, :], reads=[ctok], writes=[c.o_dc_p])
            else:
                p.dma("sp", c.o_dc_s[j].rearrange("b t c -> (b t) c"), ctok[:, :], reads=[ctok], writes=[c.o_dc_s])


DN_BIG = 30000.0
GRP = 4


def dn_consts(C, nseq):
    sl = C // nseq
    seq = np.arange(C) // sl
    same = seq[:, None] == seq[None, :]
    i = np.arange(C)
    f = np.float32
    triU = (same & (i[:, None] <= i[None, :])).astype(f)
    blk = same.astype(f)
    okA = same & (i[None, :] <= i[:, None])
    maskA = np.where(okA, 0.0, DN_BIG).astype(f)
    okB = same & (i[None, :] >= i[:, None])
    maskB = np.where(okB, 0.0, -DN_BIG).astype(f)
    strict = (same & (i[None, :] < i[:, None])).astype(f)
    seqones = np.zeros((C, nseq, 128), f)
    seqmask = np.zeros((C, nseq), f)
    colmask = np.zeros((128, nseq, C), f)
    for b in range(nseq):
        seqones[seq == b, b, :] = 1.0
        seqmask[seq == b, b] = 1.0
        colmask[:, b, seq == b] = 1.0
    pack = np.concatenate([triU, blk, maskA, maskB, strict, seqmask, seqones.reshape(C, nseq * 128)], 1)
    return np.ascontiguousarray(pack), np.ascontiguousarray(colmask)


def dn_pack_layout(C, nseq):
    off = {}
    o = 0
    for nm, w in (("triU", C), ("blk", C), ("maskA", C), ("maskB", C), ("strict", C), ("seqmask", nseq), ("seqones", nseq * 128)):
        off[nm] = (o, o + w)
        o += w
    return off, o


def phase_dn2(p, c, li, j):
    S = c.S
    with p.scope():
        wout = p.sb("dnwout", [128, 16, D], BF16)
        for h in range(16):
            p.dma("pool", wout[:, h, :], c.dn_w_out[j, :, h, :], writes=[wout])
        ng = p.sb("ng", [128, 1], F32)
        p.dma("sp", ng[:, :], c.dn_ng[j], writes=[ng])
        sel = p.sb("sel16", [16, 16, 128], F32)
        p.dma("sp", sel[:, :, :], c.sel16_d[:, :, :], writes=[sel])
        eps_c = p.sb("eps_c", [128, 1], F32)
        p.op("dve", lambda e: e.memset(eps_c[:, :], 1e-6), writes=[eps_c])
        onesDV = p.sb("onesDV", [128, 128], F32)
        p.op("dve", lambda e: e.memset(onesDV[:, :], 1.0 / 128), writes=[onesDV])
        geo = {}
        for key, C, nseq, pk_d, cm_d in (("p", 128, 1, c.dnc_p, c.dncm_p), ("s", ST, SB, c.dnc_s, c.dncm_s)):
            off, tot = dn_pack_layout(C, nseq)
            pk = p.sb(f"pk_{key}", [C, tot], F32)
            p.dma("sp", pk[:, :], pk_d[:, :], writes=[pk])
            cm = p.sb(f"cm_{key}", [128, nseq, C], F32)
            p.dma("sp", cm[:, :, :], cm_d[:, :, :], writes=[cm])
            geo[key] = (C, nseq, pk, off, cm)
        Sp = p.sb("Sp", [128, 16, 128], F32)
        Spb = p.sb("Spb", [128, 16, 128], BF16)
        Ss = p.sb("Ss", [128, SB * 16, 128], F32)
        Ssb = p.sb("Ssb", [128, SB * 16, 128], BF16)
        p.op("dve", lambda e: e.memset(Sp[:, :, :], 0.0), writes=[Sp])
        p.op("pool", lambda e: e.memset(Spb[:, :, :], 0.0), writes=[Spb])
        p.dma("sp", Ss[:, :, :], c.dn_S_in[j].rearrange("b h k v -> k (b h) v"), writes=[Ss])
        p.op("dve", lambda e: e.tensor_copy(out=Ssb[:, :, :], in_=Ss[:, :, :]), reads=[Ss], writes=[Ssb])

        gtc = p.sb("gtc", [64, 128], F32)
        bg = p.sb("bg", [128, 64], F32)
        gc = p.sb("gc", [128, 16], F32)
        ngc = p.sb("ngc", [128, 16], F32)
        gcT = p.sb("gcT", [16, 128], F32)
        glt = p.sb("glt", [128, 16], F32)
        egc = p.sb("egc", [128, 16], F32)
        bek = p.sb("bek", [128, 16], F32)
        kdc = p.sb("kdc", [128, 16], F32)
        glb = p.sb("glb", [128, SB, 16], F32)
        kT = [p.sb(f"kT{i}", [128, 128], F32) for i in range(2)]
        qT = [p.sb(f"qT{i}", [128, 128], F32) for i in range(2)]
        kT2 = [p.sb(f"kT2{i}", [128, 128], F32) for i in range(2)]
        KKs = [p.sb(f"KK{i}", [128, 128], F32) for i in range(2)]
        KQs = [p.sb(f"KQ{i}", [128, 128], F32) for i in range(2)]
        ktok = [p.sb(f"ktok{i}", [128, 128], F32) for i in range(2)]
        vT = [p.sb(f"vT{i}", [128, 128], F32) for i in range(GRP)]
        Dm = [p.sb(f"Dm{i}", [128, 128], F32) for i in range(GRP)]
        DTm = [p.sb(f"DTm{i}", [128, 128], F32) for i in range(GRP)]
        Eb = [p.sb(f"Eb{i}", [128, 128], F32) for i in range(GRP)]
        NM = [[p.sb(f"NM{i}_{k}", [128, 256], F32) for k in range(2)] for i in range(GRP)]
        X = [[p.sb(f"X{i}_{k}", [128, 256], F32) for k in range(2)] for i in range(GRP)]
        qgT = [p.sb(f"qgT{i}", [128, 128], BF16) for i in range(GRP)]
        qkT = [p.sb(f"qkT{i}", [128, 128], BF16) for i in range(GRP)]
        wT = [p.sb(f"wT{i}", [128, 128], BF16) for i in range(GRP)]
        kde = [p.sb(f"kde{i}", [128, 128], BF16) for i in range(GRP)]
        vnew = [p.sb(f"vnew{i}", [128, 128], BF16) for i in range(GRP)]
        wTb = [p.sb(f"wTb{i}", [128, SB, ST], BF16) for i in range(GRP)]
        qgTb = [p.sb(f"qgTb{i}", [128, SB, ST], BF16) for i in range(GRP)]
        kdeb = [p.sb(f"kdeb{i}", [ST, SB, 128], BF16) for i in range(GRP)]
        sqo = [p.sb(f"sqo{i}", [128, 128], F32) for i in range(GRP)]
        rso = [p.sb(f"rso{i}", [128, 128], F32) for i in range(GRP)]
        ont = p.sb("ont", [128, 16, 128], BF16)
        zT = p.sb("zT", [128, 16, 128], BF16)
        hch = p.sb("hch", [128, KC, 128], F32)
        PH = [p.ps(f"PH{i}", [128, 512], F32) for i in range(GRP)]
        PR = [p.ps(f"PR{i}", [128, 512], F32) for i in range(4)]

        chunks = [("p", t0) for t0 in range(0, S, 128)] + [("s", S)]
        for key, t0 in chunks:
            C, nseq, pk, off, cm = geo[key]
            smp = key == "s"
            St, Sb = (Ss, Ssb) if smp else (Sp, Spb)
            nst = 2 if smp else 7
            def pkv(nm, r0=0, r1=None):
                a, b = off[nm]
                return pk[:C, a:b]
            p.dma("sp", gtc[:, :C], c.GT[:, t0:t0 + C], reads=[c.GT], writes=[gtc])
            p.dma("sp", zT[:, :, :C], c.ZT[:, :, t0:t0 + C].rearrange("h p t -> p h t"), reads=[c.ZT], writes=[zT])
            p.dma("sp", hch[:, :, :C], c.H[:, :, t0:t0 + C].rearrange("k p t -> p k t"), reads=[c.H], writes=[hch])
            R0 = PR[0]
            p.op("pe", lambda e: e.transpose(R0[:C, 0:64], gtc[:, :C], c.ident[:64, :64]), reads=[gtc], writes=[R0])
            p.op("dve", lambda e: e.tensor_copy(out=bg[:C, :], in_=R0[:C, 0:64]), reads=[R0], writes=[bg])
            R1 = PR[1]
            p.op("pe", lambda e: e.matmul(R1[:C, 0:16], lhsT=pkv("triU"), rhs=bg[:C, 32:48], start=True, stop=True), reads=[pk, bg], writes=[R1])
            p.op("pe", lambda e: e.matmul(R1[:C, 16:32], lhsT=pkv("blk"), rhs=bg[:C, 32:48], start=True, stop=True), reads=[pk, bg], writes=[R1], same_ok=True)
            p.op("pe", lambda e: e.matmul(R1[:16, 64:64 + C], lhsT=bg[:C, 32:48], rhs=pkv("triU"), start=True, stop=True), reads=[pk, bg], writes=[R1], same_ok=True)
            a0, _ = off["seqones"]
            for b in range(nseq):
                p.op("pe", lambda e: e.matmul(R1[:, 256 + b * 16:256 + (b + 1) * 16], lhsT=pk[:C, a0 + b * 128:a0 + (b + 1) * 128], rhs=bg[:C, 32:48],
                                              start=True, stop=True), reads=[pk, bg], writes=[R1], same_ok=True)
            p.op("dve", lambda e: e.tensor_copy(out=gc[:C, :], in_=R1[:C, 0:16]), reads=[R1], writes=[gc])
            p.op("dve", lambda e: e.tensor_scalar(out=ngc[:C, :], in0=R1[:C, 0:16], scalar1=-1.0, scalar2=None, op0=ALU.mult), reads=[R1], writes=[ngc])
            p.op("dve", lambda e: e.tensor_tensor(out=glt[:C, :], in0=R1[:C, 16:32], in1=gc[:C, :], op=ALU.subtract), reads=[R1, gc], writes=[glt])
            p.op("dve", lambda e: e.tensor_copy(out=gcT[:, :C], in_=R1[:16, 64:64 + C]), reads=[R1], writes=[gcT])
            p.op("act", lambda e: e.activation(out=glb[:, :nseq, :], in_=R1[:, 256:256 + nseq * 16].rearrange("p (b h) -> p b h", b=nseq), func=AF.Exp),
                 reads=[R1], writes=[glb])
            p.op("act", lambda e: e.activation(out=egc[:C, :], in_=gc[:C, :], func=AF.Exp), reads=[gc], writes=[egc])
            p.op("act", lambda e: e.activation(out=kdc[:C, :], in_=glt[:C, :], func=AF.Exp), reads=[glt], writes=[kdc])
            p.op("dve", lambda e: e.tensor_tensor(out=bek[:C, :], in0=egc[:C, :], in1=bg[:C, 0:16], op=ALU.mult), reads=[egc, bg], writes=[bek])

            if DN2STOP < 2:
                continue
            for g0 in range(0, 16, GRP):
                heads = list(range(g0, g0 + GRP))
                for qi in range(2):
                    hk = g0 // 2 + qi
                    p.dma("sp", kT[qi][:, :C], c.QKVT[8 + hk, :, t0:t0 + C], reads=[c.QKVT], writes=[kT[qi]])
                    p.dma("sp", qT[qi][:, :C], c.QKVT[hk, :, t0:t0 + C], reads=[c.QKVT], writes=[qT[qi]])
                    p.dma("sp", kT2[qi][:, :C], c.QKVT[8 + hk, :, t0:t0 + C], reads=[c.QKVT], writes=[kT2[qi]])
                for gi, h in enumerate(heads):
                    p.dma("sp", vT[gi][:, :C], c.QKVT[16 + h, :, t0:t0 + C], reads=[c.QKVT], writes=[vT[gi]])
                for qi in range(2):
                    R = PR[2 + qi]
                    p.op("pe", lambda e: e.matmul(R[:C, 0:C], lhsT=kT[qi][:, :C], rhs=kT2[qi][:, :C], start=True, stop=True), reads=[kT[qi], kT2[qi]], writes=[R])
                    p.op("pe", lambda e: e.matmul(R[:C, 128:128 + C], lhsT=kT[qi][:, :C], rhs=qT[qi][:, :C], start=True, stop=True),
                         reads=[kT[qi], qT[qi]], writes=[R], same_ok=True)
                    p.op("pe", lambda e: e.transpose(R[:C, 256:384], kT[qi][:, :C], c.ident[:, :]), reads=[kT[qi]], writes=[R], same_ok=True)
                    p.op("dve", lambda e: e.tensor_copy(out=KKs[qi][:C, :C], in_=R[:C, 0:C]), reads=[R], writes=[KKs[qi]])
                    p.op("dve", lambda e: e.tensor_copy(out=KQs[qi][:C, :C], in_=R[:C, 128:128 + C]), reads=[R], writes=[KQs[qi]])
                    p.op("dve", lambda e: e.tensor_copy(out=ktok[qi][:C, :], in_=R[:C, 256:384]), reads=[R], writes=[ktok[qi]])
                if DN2STOP < 3:
                    continue
                for gi, h in enumerate(heads):
                    qi = gi // 2
                    P = PH[gi]
                    p.op("pe", lambda e: e.transpose(P[:C, 0:128], vT[gi][:, :C], c.ident[:, :]), reads=[vT[gi]], writes=[P])
                    p.op("pe", lambda e: e.matmul(P[:, 128:128 + C], lhsT=sel[:, h, :], rhs=gcT[:, :C], start=True, stop=True),
                         reads=[sel, gcT], writes=[P], same_ok=True)
                    p.op("pe", lambda e: e.matmul(P[:C, 256:256 + C], lhsT=sel[:, h, :C], rhs=gcT[:, :C], start=True, stop=False),
                         reads=[sel, gcT], writes=[P], same_ok=True)
                    p.op("pe", lambda e: e.matmul(P[:C, 256:256 + C], lhsT=c.ident[:C, :C], rhs=pkv("maskA"), start=False, stop=True),
                         reads=[pk], writes=[P], same_ok=True)
                    p.op("pe", lambda e: e.matmul(P[:C, 384:384 + C], lhsT=sel[:, h, :C], rhs=gcT[:, :C], start=True, stop=False),
                         reads=[sel, gcT], writes=[P], same_ok=True)
                    p.op("pe", lambda e: e.matmul(P[:C, 384:384 + C], lhsT=c.ident[:C, :C], rhs=pkv("maskB"), start=False, stop=True),
                         reads=[pk], writes=[P], same_ok=True)
                    X0 = X[gi][0]
                    p.op("dve", lambda e: e.tensor_scalar(out=X0[:C, 0:128], in0=P[:C, 0:128], scalar1=bg[:C, h:h + 1], scalar2=None, op0=ALU.mult),
                         reads=[P, bg], writes=[X0])
                    p.op("pool", lambda e: e.tensor_scalar(out=X0[:C, 128:256], in0=ktok[qi][:C, :], scalar1=bek[:C, h:h + 1], scalar2=None, op0=ALU.mult),
                         reads=[ktok[qi], bek], writes=[X0])
                    p.op("act", lambda e: e.activation(out=Eb[gi][:, :C], in_=P[:, 128:128 + C], func=AF.Exp), reads=[P], writes=[Eb[gi]])
                    p.op("act", lambda e: e.activation(out=Dm[gi][:C, :C], in_=P[:C, 256:256 + C], func=AF.Exp, bias=gc[:C, h:h + 1], scale=-1.0),
                         reads=[P, gc], writes=[Dm[gi]])
                    p.op("act", lambda e: e.activation(out=DTm[gi][:C, :C], in_=P[:C, 384:384 + C], func=AF.Exp, bias=ngc[:C, h:h + 1], scale=1.0),
                         reads=[P, ngc], writes=[DTm[gi]])
                    NM0 = NM[gi][0]
                    p.op("dve", lambda e: e.scalar_tensor_tensor(out=NM0[:C, 0:C], in0=KKs[qi][:C, :C], scalar=bg[:C, h:h + 1], in1=Dm[gi][:C, :C],
                                                                 op0=ALU.mult, op1=ALU.mult), reads=[KKs[qi], bg, Dm[gi]], writes=[NM0])
                    p.op("pool", lambda e: e.tensor_tensor(out=NM0[:C, 0:C], in0=NM0[:C, 0:C], in1=pkv("strict"), op=ALU.mult), reads=[NM0, pk], writes=[NM0])
                    p.op("pool", lambda e: e.tensor_tensor(out=qkT[gi][:C, :C], in0=KQs[qi][:C, :C], in1=DTm[gi][:C, :C], op=ALU.mult),
                         reads=[KQs[qi], DTm[gi]], writes=[qkT[gi]])
                    p.op("dve", lambda e: e.tensor_tensor(out=qgT[gi][:, :C], in0=qT[qi][:, :C], in1=Eb[gi][:, :C], op=ALU.mult),
                         reads=[qT[qi], Eb[gi]], writes=[qgT[gi]])
                    p.op("pool", lambda e: e.tensor_scalar(out=kde[gi][:C, :], in0=ktok[qi][:C, :], scalar1=kdc[:C, h:h + 1], scalar2=None, op0=ALU.mult),
                         reads=[ktok[qi], kdc], writes=[kde[gi]])
                if DN2STOP < 4:
                    continue
                for gi, h in enumerate(heads):
                    P = PH[gi]
                    p.op("pe", lambda e: e.transpose(P[:C, 0:C], NM[gi][0][:C, 0:C], c.ident[:C, :C]), reads=[NM[gi][0]], writes=[P])
                for gi, h in enumerate(heads):
                    P = PH[gi]
                    p.op("act", lambda e: e.activation(out=NM[gi][0][:C, 128:128 + C], in_=P[:C, 0:C], func=AF.Copy), reads=[P], writes=[NM[gi][0]])
                if DN2STOP < 5:
                    continue
                for k in range(nst):
                    cur, nxt = k % 2, (k + 1) % 2
                    last = k == nst - 1
                    for gi, h in enumerate(heads):
                        P = PH[gi]
                        NMk, Xk = NM[gi][cur], X[gi][cur]
                        p.op("pe", lambda e: e.matmul(P[:C, 256:512], lhsT=NMk[:C, 128:128 + C], rhs=Xk[:C, :], start=True, stop=True),
                             reads=[NMk, Xk], writes=[P])
                        if not last:
                            p.op("pe", lambda e: e.matmul(P[:C, 0:C], lhsT=NMk[:C, 128:128 + C], rhs=NMk[:C, 0:C], start=True, stop=True),
                                 reads=[NMk], writes=[P], same_ok=True)
                            p.op("pe", lambda e: e.matmul(P[:C, 128:128 + C], lhsT=NMk[:C, 0:C], rhs=NMk[:C, 128:128 + C], start=True, stop=True),
                                 reads=[NMk], writes=[P], same_ok=True)
                    for gi, h in enumerate(heads):
                        P = PH[gi]
                        Xk, Xn = X[gi][cur], X[gi][nxt]
                        p.op("dve", lambda e: e.tensor_tensor(out=Xn[:C, :], in0=Xk[:C, :], in1=P[:C, 256:512], op=(ALU.subtract if k == 0 else ALU.add)),
                             reads=[Xk, P], writes=[Xn])
                        if not last:
                            NMn = NM[gi][nxt]
                            if C == 128:
                                p.op("act", lambda e: e.activation(out=NMn[:C, :], in_=P[:C, 0:256], func=AF.Copy), reads=[P], writes=[NMn])
                            else:
                                p.op("act", lambda e: e.activation(out=NMn[:C, 0:C], in_=P[:C, 0:C], func=AF.Copy), reads=[P], writes=[NMn])
                                p.op("act", lambda e: e.activation(out=NMn[:C, 128:128 + C], in_=P[:C, 128:128 + C], func=AF.Copy), reads=[P], writes=[NMn])
                if DN2STOP < 6:
                    continue
                fin = nst % 2
                for gi, h in enumerate(heads):
                    P = PH[gi]
                    Xf = X[gi][fin]
                    p.op("pe", lambda e: e.transpose(P[:, 0:C], Xf[:C, 128:256], c.ident[:C, :C]), reads=[Xf], writes=[P])
                    p.op("act", lambda e: e.activation(out=wT[gi][:, :C], in_=P[:, 0:C], func=AF.Copy), reads=[P], writes=[wT[gi]])
                    if smp:
                        for b in range(nseq):
                            p.op("pool", lambda e: e.tensor_tensor(out=wTb[gi][:, b, :], in0=wT[gi][:, :C], in1=cm[:, b, :], op=ALU.mult),
                                 reads=[wT[gi], cm], writes=[wTb[gi]])
                            p.op("pool", lambda e: e.tensor_tensor(out=qgTb[gi][:, b, :], in0=qgT[gi][:, :C], in1=cm[:, b, :], op=ALU.mult),
                                 reads=[qgT[gi], cm], writes=[qgTb[gi]])
                            a1, _ = off["seqmask"]
                            p.op("pool", lambda e: e.tensor_scalar(out=kdeb[gi][:C, b, :], in0=kde[gi][:C, :], scalar1=pk[:C, a1 + b:a1 + b + 1], scalar2=None,
                                                                   op0=ALU.mult), reads=[kde[gi], pk], writes=[kdeb[gi]])
                for gi, h in enumerate(heads):
                    P = PH[gi]
                    for b in range(nseq):
                        lw = wTb[gi][:, b, :] if smp else wT[gi][:, :C]
                        p.op("pe", lambda e: e.matmul(P[:C, 128:256], lhsT=lw, rhs=Sb[:, b * 16 + h, :], start=(b == 0), stop=(b == nseq - 1)),
                             reads=[wTb[gi] if smp else wT[gi], Sb], writes=[P], same_ok=True)
                for gi, h in enumerate(heads):
                    P = PH[gi]
                    Xf = X[gi][fin]
                    p.op("dve", lambda e: e.tensor_tensor(out=vnew[gi][:C, :], in0=Xf[:C, 0:128], in1=P[:C, 128:256], op=ALU.subtract),
                         reads=[Xf, P], writes=[vnew[gi]])
                for gi, h in enumerate(heads):
                    P = PH[gi]
                    for b in range(nseq):
                        rq = qgTb[gi][:, b, :] if smp else qgT[gi][:, :C]
                        p.op("pe", lambda e: e.matmul(P[:, 256:256 + C], lhsT=Sb[:, b * 16 + h, :], rhs=rq, start=(b == 0), stop=False),
                             reads=[qgTb[gi] if smp else qgT[gi], Sb], writes=[P], same_ok=True)
                    p.op("pe", lambda e: e.matmul(P[:, 256:256 + C], lhsT=vnew[gi][:C, :], rhs=qkT[gi][:C, :C], start=False, stop=True),
                         reads=[vnew[gi], qkT[gi]], writes=[P], same_ok=True)
                    R = PR[gi % 2]
                    for b in range(nseq):
                        lk = kdeb[gi][:C, b, :] if smp else kde[gi][:C, :]
                        p.op("pe", lambda e: e.matmul(R[:, b * 128:(b + 1) * 128], lhsT=lk, rhs=vnew[gi][:C, :], start=True, stop=True),
                             reads=[kdeb[gi] if smp else kde[gi], vnew[gi]], writes=[R], same_ok=True)
                    for b in range(nseq):
                        si = b * 16 + h
                        p.op("dve", lambda e: e.scalar_tensor_tensor(out=St[:, si, :], in0=St[:, si, :], scalar=glb[:, b, h:h + 1], in1=R[:, b * 128:(b + 1) * 128],
                                                                     op0=ALU.mult, op1=ALU.add), reads=[St, glb, R], writes=[St])
                        p.op("pool", lambda e: e.tensor_copy(out=Sb[:, si, :], in_=St[:, si, :]), reads=[St], writes=[Sb])
                if DN2STOP < 7:
                    continue
                for gi, h in enumerate(heads):
                    P = PH[gi]
                    p.op("act", lambda e: e.activation(out=sqo[gi][:, :C], in_=P[:, 256:256 + C], func=AF.Square), reads=[P], writes=[sqo[gi]])
                    p.op("pe", lambda e: e.matmul(P[:, 0:C], lhsT=onesDV[:, :], rhs=sqo[gi][:, :C], start=True, stop=True), reads=[onesDV, sqo[gi]], writes=[P], same_ok=True)
                for gi, h in enumerate(heads):
                    P = PH[gi]
                    p.op("act", lambda e: e.activation(out=rso[gi][:, :C], in_=P[:, 0:C], func=AF.Sqrt, bias=eps_c[:, 0:1], scale=1.0),
                         reads=[P, eps_c], writes=[rso[gi]])
                    p.op("dve", lambda e: e.reciprocal(out=rso[gi][:, :C], in_=rso[gi][:, :C]), reads=[rso[gi]], writes=[rso[gi]])
                    p.op("dve", lambda e: e.tensor_tensor(out=rso[gi][:, :C], in0=rso[gi][:, :C], in1=P[:, 256:256 + C], op=ALU.mult),
                         reads=[rso[gi], P], writes=[rso[gi]])
                    p.op("dve", lambda e: e.scalar_tensor_tensor(out=ont[:, h, :C], in0=rso[gi][:, :C], scalar=ng[:, 0:1], in1=zT[:, h, :C],
                                                                 op0=ALU.mult, op1=ALU.mult), reads=[rso[gi], ng, zT], writes=[ont])
            if DN2STOP < 8:
                continue
            for oc in range(KC):
                R = PR[oc % 4]
                for h in range(16):
                    p.op("pe", lambda e: e.matmul(R[:, 0:C], lhsT=wout[:, h, oc * 128:(oc + 1) * 128], rhs=ont[:, h, :C], start=(h == 0), stop=(h == 15)),
                         reads=[wout, ont], writes=[R], same_ok=True)
                p.op("dve", lambda e: e.tensor_tensor(out=hch[:, oc, :C], in0=hch[:, oc, :C], in1=R[:, 0:C], op=ALU.add), reads=[hch, R], writes=[hch])
            p.dma("sp", c.H[:, :, t0:t0 + C].rearrange("k p t -> p k t"), hch[:, :, :C], reads=[hch], writes=[c.H])
        p.dma("sp", c.o_dS_p[j].rearrange("h k v -> k h v"), Sp[:, :, :], reads=[Sp], writes=[c.o_dS_p])
        p.dma("sp", c.o_dS_s[j].rearrange("b h k v -> k (b h) v"), Ss[:, :, :], reads=[Ss], writes=[c.o_dS_s])


NSA_W = 2608
PAST = 8192


def phase_nsa1(p, c, li, j):
    S = c.S
    with p.scope():
        win = p.sb("nswin", [128, KC, NSA_W], BF16)
        for kc in range(KC):
            p.dma("pool", win[:, kc, :], c.ns_w_in[j, :, kc, :], writes=[win])
        gains = p.sb("nsg", [128, 4], F32)
        p.dma("sp", gains[:, :], c.ns_gains[j], writes=[gains])
        rotT = p.sb("rotT", [128, 128], F32)
        p.dma("sp", rotT[:, :], c.ns_rotT[:, :], writes=[rotT])
        blk64 = p.sb("blk64", [128, 128], F32)
        p.op("dve", lambda e: e.memset(blk64[:, :], 0.0), writes=[blk64])
        p.op("dve", lambda e: e.memset(blk64[0:64, 0:64], 1.0 / 64), writes=[blk64])
        p.op("dve", lambda e: e.memset(blk64[64:128, 64:128], 1.0 / 64), writes=[blk64])
        eps_c = p.sb("eps_c", [128, 1], F32)
        p.op("dve", lambda e: e.memset(eps_c[:, :], NORM_EPS), writes=[eps_c])

        hTs = [p.sb(f"hT{i}", [128, KC, 512], F32) for i in range(2)]
        xn = p.sb("xn", [128, KC, 512], BF16)
        tmp = [p.sb(f"sq{i}", [128, 512], F32) for i in range(2)]
        rstd = p.sb("rstd", [128, 512], F32)
        cosT = p.sb("cosT", [128, 512], F32)
        sinT = p.sb("sinT", [128, 512], F32)
        xq = [p.sb(f"xq{i}", [128, 512], F32) for i in range(2)]
        xr = [p.sb(f"xr{i}", [128, 512], F32) for i in range(2)]
        rs = [p.sb(f"rs{i}", [128, 512], F32) for i in range(2)]
        ob = [p.sb(f"ob{i}", [128, 512], BF16) for i in range(2)]
        ob2 = [p.sb(f"ob2{i}", [128, 512], BF16) for i in range(2)]
        gt = p.sb("gt", [48, 512], F32)
        tokf = [p.sb(f"tokf{i}", [128, 1536], F32) for i in range(2)]
        tokb = [p.sb(f"tokb{i}", [128, 512], BF16) for i in range(2)]
        kfm = p.sb("kfm", [128, 4, 512], F32)
        ps_stat = p.ps("ps_stat", [128, 512], F32)
        psA = [p.ps(f"psA{i}", [128, 512], F32) for i in range(2)]
        psB = [p.ps(f"psB{i}", [128, 512], F32) for i in range(2)]
        psT = [p.ps(f"psT{i}", [128, 512], F32) for i in range(3)]
        ia = 0
        it = 0
        W = min(512, S)
        for ti, (t0, tw) in enumerate(tiles_of(S)):
            smp = t0 >= S
            hT = hTs[ti % 2]
            p.dma("sp", hT[:, :, :tw], c.H[:, :, t0:t0 + tw].rearrange("k p t -> p k t"), reads=[c.H], writes=[hT])
            p.dma("sp", cosT[:, :tw], c.ns_cos[:, t0:t0 + tw], writes=[cosT])
            p.dma("sp", sinT[:, :tw], c.ns_sin[:, t0:t0 + tw], writes=[sinT])
            rmsnorm_tile(p, c, hT, xn, tw, lambda kc: c.nmix[:, li, kc:kc + 1], ps_stat, tmp, rstd)
            for ch in list(range(0, 14)) + [16, 17]:
                A, B = psA[ia % 2], psB[ia % 2]
                x_, xr_, r_, o_, o2_ = xq[ia % 2], xr[ia % 2], rs[ia % 2], ob[ia % 2], ob2[ia % 2]
                ia += 1
                for kc in range(KC):
                    p.op("pe", lambda e: e.matmul(A[:, :tw], lhsT=win[:, kc, ch * 128:(ch + 1) * 128], rhs=xn[:, kc, :tw],
                                                  start=(kc == 0), stop=(kc == KC - 1)), reads=[win, xn], writes=[A], same_ok=True)
                if ch in (8, 9, 10, 11):
                    p.op("dve", lambda e: e.tensor_copy(out=o_[:, :tw], in_=A[:, :tw]), reads=[A], writes=[o_])
                    dst = c.KCT if ch < 10 else c.VCT
                    p.dma("sp", dst[ch % 2, :, t0:t0 + tw], o_[:, :tw], reads=[o_], writes=[dst])
                    continue
                gcol = 0 if ch < 8 else (2 if ch < 14 else 3)
                p.op("act", lambda e: e.activation(out=r_[:, :tw], in_=A[:, :tw], func=AF.Square), reads=[A], writes=[r_])
                p.op("pe", lambda e: e.matmul(B[:, :tw], lhsT=blk64[:, :], rhs=r_[:, :tw], start=True, stop=True), reads=[blk64, r_], writes=[B])
                p.op("act", lambda e: e.activation(out=r_[:, :tw], in_=B[:, :tw], func=AF.Sqrt, bias=eps_c[:, 0:1], scale=1.0),
                     reads=[B, eps_c], writes=[r_])
                p.op("dve", lambda e: e.reciprocal(out=r_[:, :tw], in_=r_[:, :tw]), reads=[r_], writes=[r_])
                p.op("dve", lambda e: e.scalar_tensor_tensor(out=x_[:, :tw], in0=A[:, :tw], scalar=gains[:, gcol:gcol + 1], in1=r_[:, :tw],
                                                             op0=ALU.mult, op1=ALU.mult), reads=[A, gains, r_], writes=[x_])
                if ch < 8:
                    p.op("pool", lambda e: e.tensor_copy(out=o_[:, :tw], in_=x_[:, :tw]), reads=[x_], writes=[o_])
                    p.dma("sp", c.QT[ch, :, t0:t0 + tw], o_[:, :tw], reads=[o_], writes=[c.QT])
                p.op("pe", lambda e: e.matmul(B[:, :tw], lhsT=rotT[:, :], rhs=x_[:, :tw], start=True, stop=True), reads=[rotT, x_], writes=[B])
                p.op("dve", lambda e: e.tensor_tensor(out=xr_[:, :tw], in0=B[:, :tw], in1=sinT[:, :tw], op=ALU.mult), reads=[B, sinT], writes=[xr_])
                p.op("pool", lambda e: e.tensor_tensor(out=x_[:, :tw], in0=x_[:, :tw], in1=cosT[:, :tw], op=ALU.mult), reads=[x_, cosT], writes=[x_])
                if ch < 8:
                    p.op("dve", lambda e: e.tensor_tensor(out=o2_[:, :tw], in0=x_[:, :tw], in1=xr_[:, :tw], op=ALU.add), reads=[x_, xr_], writes=[o2_])
                    p.dma("sp", c.QRT[ch, :, t0:t0 + tw], o2_[:, :tw], reads=[o2_], writes=[c.QRT])
                else:
                    ki = (ch - 12) if ch < 14 else (ch - 14)
                    p.op("dve", lambda e: e.tensor_tensor(out=kfm[:, ki, :tw], in0=x_[:, :tw], in1=xr_[:, :tw], op=ALU.add), reads=[x_, xr_], writes=[kfm])
                    p.op("pool", lambda e: e.tensor_copy(out=o2_[:, :tw], in_=kfm[:, ki, :tw]), reads=[kfm], writes=[o2_])
                    dst = c.KST if ch < 14 else c.KWT
                    p.dma("sp", dst[ch % 2, :, t0:t0 + tw], o2_[:, :tw], reads=[o2_], writes=[dst])
            A = psA[ia % 2]
            ia += 1
            for kc in range(KC):
                p.op("pe", lambda e: e.matmul(A[:48, :tw], lhsT=win[:, kc, 2560:2608], rhs=xn[:, kc, :tw],
                                              start=(kc == 0), stop=(kc == KC - 1)), reads=[win, xn], writes=[A], same_ok=True)
            p.op("act", lambda e: e.activation(out=gt[:, :tw], in_=A[:48, :tw], func=AF.Sigmoid), reads=[A], writes=[gt])
            p.dma("sp", c.GTn[:, t0:t0 + tw], gt[:, :tw], reads=[gt], writes=[c.GTn])
            nsub = (tw + 127) // 128
            for sj in range(nsub):
                r = min(128, tw - sj * 128)
                tf = tokf[sj % 2]
                for cb in range(3):
                    Pt = psT[cb]
                    for kc in range(KC):
                        p.op("pe", lambda e: e.matmul(Pt[:r, :], lhsT=xn[:, kc, sj * 128:sj * 128 + r], rhs=win[:, kc, 1024 + cb * 512:1536 + cb * 512],
                                                      start=(kc == 0), stop=(kc == KC - 1)), reads=[win, xn], writes=[Pt], same_ok=True)
                    p.op("dve", lambda e: e.tensor_copy(out=tf[:r, cb * 512:(cb + 1) * 512], in_=Pt[:r, :]), reads=[Pt], writes=[tf])
                for ki in range(4):
                    Pt = psT[ki % 3]
                    p.op("pe", lambda e: e.transpose(Pt[:r, 0:128], kfm[:, ki, sj * 128:sj * 128 + r], c.ident[:, :]), reads=[kfm], writes=[Pt])
                    col = (512 if ki < 2 else 1024) + (ki % 2) * 128
                    p.op("dve", lambda e: e.tensor_copy(out=tf[:r, col:col + 128], in_=Pt[:r, 0:128]), reads=[Pt], writes=[tf])
                tb = tokb[sj % 2]
                p.op("pool", lambda e: e.tensor_copy(out=tb[:r, 0:256], in_=tf[:r, 768:1024]), reads=[tf], writes=[tb])
                p.op("pool", lambda e: e.tensor_copy(out=tb[:r, 256:512], in_=tf[:r, 1280:1536]), reads=[tf], writes=[tb])
                r0 = t0 + sj * 128
                p.dma("sp", c.VS[r0:r0 + r, :], tb[:r, 0:256], reads=[tb], writes=[c.VS])
                p.dma("sp", c.VW[r0:r0 + r, :], tb[:r, 256:512], reads=[tb], writes=[c.VW])
                if not smp:
                    for oi in range(4):
                        p.dma("sp", c.o_new_p[oi, r0:r0 + r, :], tf[:r, oi * 256:(oi + 1) * 256], reads=[tf], writes=[c.o_new_p])
                    if r0 >= S - W:
                        for oi in range(2):
                            p.dma("sp", c.o_swa_p[oi, r0 - (S - W):r0 - (S - W) + r, :], tf[:r, 1024 + oi * 256:1280 + oi * 256],
                                  reads=[tf], writes=[c.o_swa_p])
                else:
                    for oi in range(4):
                        p.dma("sp", c.o_new_s[oi, :, :], tf[:r, oi * 256:(oi + 1) * 256], reads=[tf], writes=[c.o_new_s])
                    for b in range(SB):
                        for oi in range(2):
                            p.dma("sp", c.o_swa_s[oi, b, 512 - SL:512, :], tf[b * SL:(b + 1) * SL, 1024 + oi * 256:1280 + oi * 256],
                                  reads=[tf], writes=[c.o_swa_s])
        for oi, src in enumerate((c.ns_swa_k, c.ns_swa_v)):
            for b in range(SB):
                p.dma("sp", c.o_swa_s[oi, b, 0:512 - SL, :], src[j, b, SL:512, :], writes=[c.o_swa_s])


NEG = -1.0e9
BONUS = 1.0e4


def nsa_consts():
    f = np.float32
    ql = np.arange(128)[:, None]
    x = np.arange(256)[None, :]
    jp = x - 128
    W = np.zeros((128, 256), f)
    W = np.where(jp > 1, NEG, W)
    W = np.where((jp == 1) & (ql >= 64), BONUS, W)
    W = np.where((jp == 1) & (ql < 64), NEG, W)
    W = np.where(jp == 0, BONUS, W)
    W = np.where((jp == -1) & (ql < 64), BONUS, W)
    n = np.arange(512)[:, None]
    jj = np.arange(128)[None, :]
    ov = np.minimum(16 * n + 32, 64 * jj + 64) - np.maximum(16 * n, 64 * jj)
    c2s = (np.clip(ov, 0, None) / 32.0).astype(f)
    c2s[511] = 0.0
    c2s = np.ascontiguousarray(c2s.reshape(4, 128, 128).transpose(1, 0, 2))
    sel48 = np.zeros((48, 48, 64), f)
    for r in range(48):
        sel48[r, r, :] = 1.0
    return {"ns_wtab": W.astype(f), "ns_c2s": c2s, "ns_sel48": sel48}


def nsa_seq(p, c, j, env, sq):
    T = sq["T"]
    NTL = T // 128
    ncmp = sq["ncmp"]
    scale = 0.125
    (wck, wcv, pekT, pevT, g0, ones64, onesb, identb, wtab, c2s, sel48, tiny) = env["consts"]
    (kcs, vcs, ksT, kwT, vs, vw, kcmpT, vcmp, Qg, QRg, Gg, Pc, Pn, E, Pm, sc, sc2, m8, selb, selx, rden, gb, oacc, ob, sqk, rsk, pet, pvr) = env["tiles"]
    (PS_S, PS_T, PS_DEN, PS_O, PS_IMP, PS_X) = env["psum"]
    for hk in range(4):
        ch, p0 = hk // 2, (hk % 2) * 64
        p.dma("sp", kcs[:, :T], sq["KCT"][ch, p0:p0 + 64, 0:T], reads=[sq["KCT_t"]], writes=[kcs])
        p.dma("sp", vcs[:, :T], sq["VCT"][ch, p0:p0 + 64, 0:T], reads=[sq["VCT_t"]], writes=[vcs])
        p.dma("sp", ksT[:, :T], sq["KST"][ch, p0:p0 + 64, 0:T], reads=[sq["KST_t"]], writes=[ksT])
        kw0 = sq.get("kw0", 0)
        p.dma("sp", kwT[:, kw0:T], sq["KWT"][ch, p0:p0 + 64, kw0:T], reads=[sq["KWT_t"]], writes=[kwT])
        p.dma("sp", vs[:, :NTL, :], sq["VS"][0:T, hk * 64:(hk + 1) * 64].rearrange("(n p) d -> p n d", p=128), reads=[sq["VS_t"]], writes=[vs])
        p.dma("sp", vw[:, kw0 // 128:NTL, :], sq["VW"][kw0:T, hk * 64:(hk + 1) * 64].rearrange("(n p) d -> p n d", p=128), reads=[sq["VW_t"]], writes=[vw])
        X0 = PS_X[0]
        kv = kcs[:, 0:16 * (ncmp + 1)].rearrange("p (n s) -> p n s", s=16)
        for l in range(32):
            rhs = kv[:, 0:ncmp, l] if l < 16 else kv[:, 1:ncmp + 1, l - 16]
            p.op("pe", lambda e: e.matmul(X0[:64, :ncmp], lhsT=wck[0:64, l, :], rhs=rhs, start=(l == 0), stop=(l == 31)),
                 reads=[wck, kcs], writes=[X0], same_ok=True)
        p.op("dve", lambda e: e.tensor_scalar(out=sqk[:, :ncmp], in0=X0[:64, :ncmp], scalar1=pet[:, 0:1], scalar2=None, op0=ALU.add),
             reads=[X0, pet], writes=[sqk])
        p.op("act", lambda e: e.activation(out=rsk[:, :ncmp], in_=sqk[:, :ncmp], func=AF.Square), reads=[sqk], writes=[rsk])
        X1 = PS_X[1]
        p.op("pe", lambda e: e.matmul(X1[:64, :ncmp], lhsT=ones64[:, :], rhs=rsk[:, :ncmp], start=True, stop=True), reads=[ones64, rsk], writes=[X1])
        p.op("act", lambda e: e.activation(out=rsk[:, :ncmp], in_=X1[:64, :ncmp], func=AF.Sqrt, bias=tiny[:64, 1:2], scale=1.0),
             reads=[X1, tiny], writes=[rsk])
        p.op("dve", lambda e: e.reciprocal(out=rsk[:, :ncmp], in_=rsk[:, :ncmp]), reads=[rsk], writes=[rsk])
        p.op("dve", lambda e: e.memset(kcmpT[:, :], 0.0), writes=[kcmpT])
        p.op("dve", lambda e: e.scalar_tensor_tensor(out=kcmpT[:, :ncmp], in0=sqk[:, :ncmp], scalar=g0[:, 0:1], in1=rsk[:, :ncmp], op0=ALU.mult, op1=ALU.mult),
             reads=[sqk, g0, rsk], writes=[kcmpT])
        vv = vcs[:, 0:16 * (ncmp + 1)].rearrange("p (n s) -> p n s", s=16)
        nct = (ncmp + 127) // 128
        p.op("pool", lambda e: e.memset(vcmp[:, :, :], 0.0), writes=[vcmp])
        for nt in range(nct):
            n0 = nt * 128
            nn = min(128, ncmp - n0)
            Xv = PS_X[nt % 2]
            for l in range(32):
                lhsT = vv[:, n0:n0 + nn, l] if l < 16 else vv[:, n0 + 1:n0 + nn + 1, l - 16]
                p.op("pe", lambda e: e.matmul(Xv[:nn, 0:64], lhsT=lhsT, rhs=wcv[0:64, l, :], start=(l == 0), stop=False),
                     reads=[wcv, vcs], writes=[Xv], same_ok=True)
            p.op("pe", lambda e: e.matmul(Xv[:nn, 0:64], lhsT=onesb[0:1, :nn], rhs=pvr[0:1, :], start=False, stop=True),
                 reads=[onesb, pvr], writes=[Xv], same_ok=True)
            p.op("dve", lambda e: e.tensor_copy(out=vcmp[:nn, nt, :], in_=Xv[:nn, 0:64]), reads=[Xv], writes=[vcmp])
        for (bi, qc0, o0) in sq["blocks"]:
            for g in range(4):
                qh = 4 * hk + g
                p.dma("sp", Qg[:, g, :], sq["QT"][qh // 2, (qh % 2) * 64:(qh % 2) * 64 + 64, qc0:qc0 + 128], reads=[sq["QT_t"]], writes=[Qg])
                p.dma("sp", QRg[:, g, :], sq["QRT"][qh // 2, (qh % 2) * 64:(qh % 2) * 64 + 64, qc0:qc0 + 128], reads=[sq["QRT_t"]], writes=[QRg])
            p.dma("sp", Gg[:, :], sq["GT"][:, qc0:qc0 + 128], reads=[sq["GT_t"]], writes=[Gg])
            Qf = Qg[:, :, :].rearrange("p g q -> p (g q)")
            QRf = QRg[:, :, :].rearrange("p g q -> p (g q)")
            ncl = min(nct, (8 * bi + 6) // 128 + 1)
            for nt in range(ncl):
                Sp = PS_S[nt % 2]
                p.op("pe", lambda e: e.matmul(Sp[:, :], lhsT=kcmpT[:, nt * 128:(nt + 1) * 128], rhs=Qf, start=True, stop=True),
                     reads=[kcmpT, Qg], writes=[Sp])
                p.op("act", lambda e: e.activation(out=Pc[:, nt, :], in_=Sp[:, :], func=AF.Exp, scale=scale), reads=[Sp], writes=[Pc])
                p.op("pool", lambda e: e.affine_select(out=Pc[:, nt, :].rearrange("p (g q) -> p g q", g=4), in_=Pc[:, nt, :].rearrange("p (g q) -> p g q", g=4),
                                                       pattern=[[0, 4], [1, 128]], compare_op=ALU.is_ge, fill=0.0,
                                                       base=128 * bi - 31 - 2048 * nt, channel_multiplier=-16), reads=[Pc], writes=[Pc])
                p.op("pe", lambda e: e.matmul(PS_DEN[:, :], lhsT=onesb[:, :], rhs=Pc[:, nt, :], start=(nt == 0), stop=(nt == ncl - 1)),
                     reads=[onesb, Pc], writes=[PS_DEN], same_ok=True)
            p.op("dve", lambda e: e.tensor_scalar(out=rden[:, :], in0=PS_DEN[:, :], scalar1=1e-30, scalar2=None, op0=ALU.max), reads=[PS_DEN], writes=[rden])
            p.op("dve", lambda e: e.reciprocal(out=rden[:, :], in_=rden[:, :]), reads=[rden], writes=[rden])
            for nt in range(ncl):
                p.op("dve", lambda e: e.tensor_tensor(out=Pn[:, nt, :], in0=Pc[:, nt, :], in1=rden[:, :], op=ALU.mult), reads=[Pc, rden], writes=[Pn])
            for nt in range(ncl):
                p.op("pe", lambda e: e.matmul(PS_O[:64, :], lhsT=vcmp[:, nt, :], rhs=Pn[:, nt, :], start=(nt == 0), stop=(nt == ncl - 1)),
                     reads=[vcmp, Pn], writes=[PS_O], same_ok=True)
            k = 0
            for nt in range(ncl):
                for g in range(4):
                    p.op("pe", lambda e: e.matmul(PS_IMP[:, 0:128], lhsT=Pn[:, nt, g * 128:(g + 1) * 128], rhs=c2s[:, nt, :], start=(k == 0), stop=(k == 4 * ncl - 1)),
                         reads=[Pn, c2s], writes=[PS_IMP], same_ok=True)
                    k += 1
            for br in range(3):
                Xg = PS_X[br % 2]
                for g in range(4):
                    r = (4 * hk + g) * 3 + br
                    p.op("pe", lambda e: e.matmul(Xg[:64, g * 128:(g + 1) * 128], lhsT=sel48[:, r, :], rhs=Gg[:, :], start=True, stop=True),
                         reads=[sel48, Gg], writes=[Xg], same_ok=True)
                p.op("dve", lambda e: e.tensor_copy(out=gb[:, br, :], in_=Xg[:64, 0:512]), reads=[Xg], writes=[gb])
            p.op("dve", lambda e: e.tensor_tensor(out=oacc[:, :], in0=PS_O[:64, :], in1=gb[:, 0, :], op=ALU.mult), reads=[PS_O, gb], writes=[oacc])
            p.op("dve", lambda e: e.memset(sc[:, 128:136], NEG), writes=[sc])
            p.op("dve", lambda e: e.tensor_tensor(out=sc[:, 0:128], in0=PS_IMP[:, 0:128], in1=wtab[:, 128 - 2 * bi:256 - 2 * bi], op=ALU.add),
                 reads=[PS_IMP, wtab], writes=[sc])
            if bi == 64:
                p.op("dve", lambda e: e.tensor_copy(out=sc[:, 128:130], in_=wtab[:, 128:130]), reads=[wtab], writes=[sc])
            if bi >= 1:
                p.op("dve", lambda e: e.tensor_scalar(out=sc[:, 0:1], in0=sc[:, 0:1], scalar1=BONUS, scalar2=None, op0=ALU.add), reads=[sc], writes=[sc])
            p.op("dve", lambda e: e.max(out=m8[:, :], in_=sc[:, :]), reads=[sc], writes=[m8])
            p.op("dve", lambda e: e.match_replace(out=sc2[:, :], in_to_replace=m8[:, :], in_values=sc[:, :], imm_value=-3.0e9), reads=[sc, m8], writes=[sc2])
            p.op("dve", lambda e: e.max(out=m8[:, :], in_=sc2[:, :]), reads=[sc2], writes=[m8])
            p.op("dve", lambda e: e.match_replace(out=sc2[:, :], in_to_replace=m8[:, :], in_values=sc2[:, :], imm_value=-3.0e9), reads=[sc2, m8], writes=[sc2])
            p.op("dve", lambda e: e.tensor_tensor(out=sc[:, :], in0=sc[:, :], in1=sc2[:, :], op=ALU.subtract), reads=[sc, sc2], writes=[sc])
            p.op("dve", lambda e: e.tensor_scalar(out=selb[:, :], in0=sc[:, :], scalar1=1.0, scalar2=None, op0=ALU.min), reads=[sc], writes=[selb])
            nb = 2 * (bi + 1)
            p.op("dve", lambda e: e.tensor_copy(out=selx[:, 0:nb * 64].rearrange("p (j k) -> p j k", k=64),
                                                in_=selb[:, 0:nb].unsqueeze(2).to_broadcast([128, nb, 64])), reads=[selb], writes=[selx])
            for kt in range(bi + 1):
                Sp = PS_S[kt % 2]
                Ek, Pk = E[kt % 2], Pm[kt % 2]
                p.op("pe", lambda e: e.matmul(Sp[:, :], lhsT=ksT[:, kt * 128:(kt + 1) * 128], rhs=QRf, start=True, stop=True), reads=[ksT, QRg], writes=[Sp])
                p.op("pe", lambda e: e.transpose(PS_T[:, 0:128], selx[:, kt * 128:(kt + 1) * 128], identb[:, :]), reads=[selx, identb], writes=[PS_T])
                p.op("act", lambda e: e.activation(out=Ek[:, :], in_=Sp[:, :], func=AF.Exp, scale=scale), reads=[Sp], writes=[Ek])
                p.op("dve", lambda e: e.tensor_tensor(out=Pk[:, :].rearrange("p (g q) -> p g q", g=4), in0=Ek[:, :].rearrange("p (g q) -> p g q", g=4),
                                                      in1=PS_T[:, 0:128].unsqueeze(1).to_broadcast([128, 4, 128]), op=ALU.mult), reads=[Ek, PS_T], writes=[Pk])
                if kt == bi:
                    p.op("pool", lambda e: e.affine_select(out=Pk[:, :].rearrange("p (g q) -> p g q", g=4), in_=Pk[:, :].rearrange("p (g q) -> p g q", g=4),
                                                           pattern=[[0, 4], [1, 128]], compare_op=ALU.is_ge, fill=0.0, base=0, channel_multiplier=-1),
                         reads=[Pk], writes=[Pk])
                p.op("pe", lambda e: e.matmul(PS_DEN[:64, :], lhsT=onesb[:, 0:64], rhs=Pk[:, :], start=(kt == 0), stop=(kt == bi)),
                     reads=[onesb, Pk], writes=[PS_DEN], same_ok=True)
                p.op("pe", lambda e: e.matmul(PS_O[:64, :], lhsT=vs[:, kt, :], rhs=Pk[:, :], start=(kt == 0), stop=(kt == bi)),
                     reads=[vs, Pk], writes=[PS_O], same_ok=True)
            p.op("dve", lambda e: e.tensor_scalar(out=rden[:64, :], in0=PS_DEN[:64, :], scalar1=1e-30, scalar2=None, op0=ALU.max), reads=[PS_DEN], writes=[rden])
            p.op("dve", lambda e: e.reciprocal(out=rden[:64, :], in_=rden[:64, :]), reads=[rden], writes=[rden])
            p.op("pool", lambda e: e.tensor_tensor(out=rden[:64, :], in0=rden[:64, :], in1=gb[:, 1, :], op=ALU.mult), reads=[rden, gb], writes=[rden])
            p.op("dve", lambda e: e.tensor_tensor(out=rden[:64, :], in0=rden[:64, :], in1=PS_O[:64, :], op=ALU.mult), reads=[rden, PS_O], writes=[rden])
            p.op("pool", lambda e: e.tensor_tensor(out=oacc[:, :], in0=oacc[:, :], in1=rden[:64, :], op=ALU.add), reads=[oacc, rden], writes=[oacc])
            kts = list(range(max(0, bi - 4), bi + 1))
            for ii, kt in enumerate(kts):
                Sp = PS_S[kt % 2]
                Ek = E[kt % 2]
                p.op("pe", lambda e: e.matmul(Sp[:, :], lhsT=kwT[:, kt * 128:(kt + 1) * 128], rhs=QRf, start=True, stop=True), reads=[kwT, QRg], writes=[Sp])
                p.op("act", lambda e: e.activation(out=Ek[:, :], in_=Sp[:, :], func=AF.Exp, scale=scale), reads=[Sp], writes=[Ek])
                if kt == bi:
                    p.op("pool", lambda e: e.affine_select(out=Ek[:, :].rearrange("p (g q) -> p g q", g=4), in_=Ek[:, :].rearrange("p (g q) -> p g q", g=4),
                                                           pattern=[[0, 4], [1, 128]], compare_op=ALU.is_ge, fill=0.0, base=0, channel_multiplier=-1),
                         reads=[Ek], writes=[Ek])
                if kt == bi - 4:
                    p.op("pool", lambda e: e.affine_select(out=Ek[:, :].rearrange("p (g q) -> p g q", g=4), in_=Ek[:, :].rearrange("p (g q) -> p g q", g=4),
                                                           pattern=[[0, 4], [-1, 128]], compare_op=ALU.is_ge, fill=0.0, base=0, channel_multiplier=1),
                         reads=[Ek], writes=[Ek])
                p.op("pe", lambda e: e.matmul(PS_DEN[:64, :], lhsT=onesb[:, 0:64], rhs=Ek[:, :], start=(ii == 0), stop=(ii == len(kts) - 1)),
                     reads=[onesb, Ek], writes=[PS_DEN], same_ok=True)
                p.op("pe", lambda e: e.matmul(PS_O[:64, :], lhsT=vw[:, kt, :], rhs=Ek[:, :], start=(ii == 0), stop=(ii == len(kts) - 1)),
                     reads=[vw, Ek], writes=[PS_O], same_ok=True)
            p.op("dve", lambda e: e.tensor_scalar(out=rden[:64, :], in0=PS_DEN[:64, :], scalar1=1e-30, scalar2=None, op0=ALU.max), reads=[PS_DEN], writes=[rden])
            p.op("dve", lambda e: e.reciprocal(out=rden[:64, :], in_=rden[:64, :]), reads=[rden], writes=[rden])
            p.op("pool", lambda e: e.tensor_tensor(out=rden[:64, :], in0=rden[:64, :], in1=gb[:, 2, :], op=ALU.mult), reads=[rden, gb], writes=[rden])
            p.op("dve", lambda e: e.tensor_tensor(out=rden[:64, :], in0=rden[:64, :], in1=PS_O[:64, :], op=ALU.mult), reads=[rden, PS_O], writes=[rden])
            p.op("dve", lambda e: e.tensor_tensor(out=ob[:, :], in0=oacc[:, :], in1=rden[:64, :], op=ALU.add), reads=[oacc, rden], writes=[ob])
            nq = sq["nq"]
            for g in range(4):
                p.dma("sp", c.OT[4 * hk + g, :, o0:o0 + nq], ob[:, g * 128:g * 128 + nq], reads=[ob], writes=[c.OT])


def phase_nsa2(p, c, li, j):
    S = c.S
    with p.scope():
        f32c = lambda nm, shape, src: (lambda t: (p.dma("sp", t[tuple(slice(None) for _ in shape)], src, writes=[t]), t)[1])(p.sb(nm, shape, F32))
        wck = p.sb("wck", [128, 32, 64], BF16)
        wcv = p.sb("wcv", [128, 32, 64], BF16)
        p.dma("pool", wck[:, :, :], c.ns_wck[j], writes=[wck])
        p.dma("pool", wcv[:, :, :], c.ns_wcv[j], writes=[wcv])
        pekT = p.sb("pekT", [64, 32], BF16)
        pevT = p.sb("pevT", [64, 32], BF16)
        p.dma("pool", pekT[:, :], c.ns_pekT[j], writes=[pekT])
        p.dma("pool", pevT[:, :], c.ns_pevT[j], writes=[pevT])
        g0 = p.sb("g0", [64, 1], F32)
        p.dma("sp", g0[:, :], c.ns_gains[j, 0:64, 1:2], writes=[g0], allow_slow_non_contiguous=True)
        ones64 = p.sb("ones64", [64, 64], F32)
        p.op("dve", lambda e: e.memset(ones64[:, :], 1.0 / 64), writes=[ones64])
        onesb = p.sb("onesb", [128, 128], BF16)
        p.op("dve", lambda e: e.memset(onesb[:, :], 1.0), writes=[onesb])
        identb = p.sb("identb", [128, 128], BF16)
        p.op("dve", lambda e: e.tensor_copy(out=identb[:, :], in_=c.ident[:, :]), reads=[c.ident], writes=[identb])
        wtab = p.sb("wtab", [128, 256], F32)
        p.dma("sp", wtab[:, :], c.ns_wtab[:, :], writes=[wtab])
        c2s = p.sb("c2s", [128, 4, 128], BF16)
        p.dma("pool", c2s[:, :, :], c.ns_c2s[:, :, :], writes=[c2s])
        sel48 = p.sb("sel48", [48, 48, 64], F32)
        p.dma("sp", sel48[:, :, :], c.ns_sel48[:, :, :], writes=[sel48])
        tiny = p.sb("tiny", [128, 2], F32)
        p.op("dve", lambda e: e.memset(tiny[:, 0:1], 1e-30), writes=[tiny])
        p.op("dve", lambda e: e.memset(tiny[:, 1:2], NORM_EPS), writes=[tiny])
        TMAX = max(S, PAST + 128)
        kcs = p.sb("kcs", [64, TMAX], BF16)
        vcs = p.sb("vcs", [64, TMAX], BF16)
        ksT = p.sb("ksT", [64, TMAX], BF16)
        kwT = p.sb("kwT", [64, TMAX], BF16)
        vs = p.sb("vs", [128, TMAX // 128, 64], BF16)
        vw = p.sb("vw", [128, TMAX // 128, 64], BF16)
        kcmpT = p.sb("kcmpT", [64, 512], BF16)
        vcmp = p.sb("vcmp", [128, 4, 64], BF16)
        Qg = p.sb("Qg", [64, 4, 128], BF16)
        QRg = p.sb("QRg", [64, 4, 128], BF16)
        Gg = p.sb("Gg", [48, 128], F32)
        Pc = p.sb("Pc", [128, 4, 512], BF16)
        Pn = p.sb("Pn", [128, 4, 512], BF16)
        E = [p.sb(f"E{i}", [128, 512], BF16) for i in range(2)]
        Pm = [p.sb(f"Pm{i}", [128, 512], BF16) for i in range(2)]
        sc = p.sb("sc", [128, 136], F32)
        sc2 = p.sb("sc2", [128, 136], F32)
        m8 = p.sb("m8", [128, 8], F32)
        selb = p.sb("selb", [128, 136], BF16)
        selx = p.sb("selx", [128, TMAX], BF16)
        rden = p.sb("rden", [128, 512], F32)
        gb = p.sb("gb", [64, 3, 512], F32)
        oacc = p.sb("oacc", [64, 512], F32)
        ob = p.sb("ob", [64, 512], BF16)
        sqk = p.sb("sqk", [64, 512], F32)
        rsk = p.sb("rsk", [64, 512], F32)
        pet = p.sb("pet", [64, 1], F32)
        pvr = p.sb("pvr", [1, 64], BF16)
        PS_S = [p.ps(f"PS_S{i}", [128, 512], F32) for i in range(2)]
        PS_T = p.ps("PS_T", [128, 1024], BF16)
        PS_DEN = p.ps("PS_DEN", [128, 512], F32)
        PS_O = p.ps("PS_O", [128, 512], F32)
        PS_IMP = p.ps("PS_IMP", [128, 512], F32)
        PS_X = [p.ps(f"PS_X{i}", [128, 512], F32) for i in range(2)]
        for l in range(32):
            p.op("pe", lambda e: e.matmul(PS_X[0][:64, 0:1], lhsT=wck[0:64, l, :], rhs=pekT[:, l:l + 1], start=(l == 0), stop=(l == 31)),
                 reads=[wck, pekT], writes=[PS_X[0]], same_ok=True)
        p.op("dve", lambda e: e.tensor_copy(out=pet[:, :], in_=PS_X[0][:64, 0:1]), reads=[PS_X[0]], writes=[pet])
        for l in range(32):
            p.op("pe", lambda e: e.matmul(PS_X[1][0:1, 0:64], lhsT=pevT[:, l:l + 1], rhs=wcv[0:64, l, :], start=(l == 0), stop=(l == 31)),
                 reads=[wcv, pevT], writes=[PS_X[1]], same_ok=True)
        p.op("dve", lambda e: e.tensor_copy(out=pvr[:, :], in_=PS_X[1][0:1, 0:64]), reads=[PS_X[1]], writes=[pvr])
        env = dict(consts=(wck, wcv, pekT, pevT, g0, ones64, onesb, identb, wtab, c2s, sel48, tiny),
                   tiles=(kcs, vcs, ksT, kwT, vs, vw, kcmpT, vcmp, Qg, QRg, Gg, Pc, Pn, E, Pm, sc, sc2, m8, selb, selx, rden, gb, oacc, ob, sqk, rsk, pet, pvr),
                   psum=(PS_S, PS_T, PS_DEN, PS_O, PS_IMP, PS_X))
        sqp = dict(T=S, ncmp=S // 16 - 1, nq=128, KCT=c.KCT, VCT=c.VCT, KST=c.KST, KWT=c.KWT, VS=c.VS, VW=c.VW, QT=c.QT, QRT=c.QRT, GT=c.GTn,
                   KCT_t=c.KCT, VCT_t=c.VCT, KST_t=c.KST, KWT_t=c.KWT, VS_t=c.VS, VW_t=c.VW, QT_t=c.QT, QRT_t=c.QRT, GT_t=c.GTn,
                   blocks=[(bi, bi * 128, bi * 128) for bi in range(S // 128)])
        nsa_seq(p, c, j, env, sqp)
        TV = PAST + 128
        iof = p.sb("iof", [128, 1], F32)
        p.dma("sp", iof[:, :], c.ns_iota[:, :], writes=[iof])
        ptb = p.sb("ptb", [128, 64], I32)
        ptf = p.sb("ptf", [128, 64], F32)
        idx = p.sb("idx", [128, 64], I32)
        pgs = [p.sb(f"pgs{i}", [128, 256], BF16) for i in range(4)]
        stg = [p.sb(f"stg{i}", [128, 1024], BF16) for i in range(2)]
        zz = p.sb("zz", [128, 512], BF16)
        zf = p.sb("zf", [48, 128], F32)
        p.op("dve", lambda e: e.memset(zz[:, :], 0.0), writes=[zz])
        p.op("dve", lambda e: e.memset(zf[:, :], 0.0), writes=[zf])
        for tns in (c.KCTv, c.VCTv, c.KSTv, c.KWTv):
            for chn in range(2):
                p.dma("sp", tns[chn, :, PAST:TV], zz[:, 0:128], reads=[zz], writes=[tns])
        for tns in (c.VSv, c.VWv):
            p.dma("sp", tns[PAST:TV, :], zz[:, 0:256], reads=[zz], writes=[tns])
        for tns in (c.QTv, c.QRTv):
            for chn in range(8):
                p.dma("sp", tns[chn, :, :], zz[:, 0:128], reads=[zz], writes=[tns])
        p.dma("sp", c.GTv[:, :], zf[:, :], reads=[zf], writes=[c.GTv])
        for b in range(SB):
            p.dma("sp", ptb[:, :], c.ns_pt[b].partition_broadcast(128), writes=[ptb])
            p.op("dve", lambda e: e.tensor_copy(out=ptf[:, :], in_=ptb[:, :]), reads=[ptb], writes=[ptf])
            p.op("dve", lambda e: e.tensor_scalar(out=ptf[:, :], in0=ptf[:, :], scalar1=128.0, scalar2=iof[:, 0:1], op0=ALU.mult, op1=ALU.add),
                 reads=[ptf, iof], writes=[ptf])
            p.op("dve", lambda e: e.tensor_copy(out=idx[:, :], in_=ptf[:, :]), reads=[ptf], writes=[idx])
            igrp = 0
            for pool_d, dst in ((c.ns_cmp_k, c.KCTv), (c.ns_cmp_v, c.VCTv), (c.ns_slc_k, c.KSTv)):
                for pg4 in range(16):
                    sg_ = stg[igrp % 2]
                    igrp += 1
                    for q in range(4):
                        k = pg4 * 4 + q
                        p.idma(pgs[q][:, :], pool_d[:, :], idx[:, k:k + 1], reads=[idx, pool_d], writes=[pgs[q]])
                    for q in range(4):
                        for chn in range(2):
                            p.op("pe", lambda e: e.transpose(PS_T[:, (chn * 4 + q) * 128:(chn * 4 + q + 1) * 128], pgs[q][:, chn * 128:(chn + 1) * 128], identb[:, :]),
                                 reads=[pgs[q], identb], writes=[PS_T], same_ok=True)
                    p.op("dve", lambda e: e.tensor_copy(out=sg_[:, :], in_=PS_T[:, :]), reads=[PS_T], writes=[sg_])
                    for chn in range(2):
                        p.dma("sp", dst[chn, :, pg4 * 512:(pg4 + 1) * 512], sg_[:, chn * 512:(chn + 1) * 512], reads=[sg_], writes=[dst])
            for k in range(64):
                pq = pgs[k % 4]
                p.idma(pq[:, :], c.ns_slc_v[:, :], idx[:, k:k + 1], reads=[idx, c.ns_slc_v], writes=[pq])
                p.dma("sp", c.VSv[k * 128:(k + 1) * 128, :], pq[:, :], reads=[pq], writes=[c.VSv])
            sg_ = stg[igrp % 2]
            igrp += 1
            for q in range(4):
                p.dma("pool", pgs[q][:, :], c.ns_swa_k[j, b, q * 128:(q + 1) * 128, :], writes=[pgs[q]])
            for q in range(4):
                for chn in range(2):
                    p.op("pe", lambda e: e.transpose(PS_T[:, (chn * 4 + q) * 128:(chn * 4 + q + 1) * 128], pgs[q][:, chn * 128:(chn + 1) * 128], identb[:, :]),
                         reads=[pgs[q], identb], writes=[PS_T], same_ok=True)
            p.op("dve", lambda e: e.tensor_copy(out=sg_[:, :], in_=PS_T[:, :]), reads=[PS_T], writes=[sg_])
            for chn in range(2):
                p.dma("sp", c.KWTv[chn, :, PAST - 512:PAST], sg_[:, chn * 512:(chn + 1) * 512], reads=[sg_], writes=[c.KWTv])
            for q in range(4):
                p.dma("pool", pgs[q][:, :], c.ns_swa_v[j, b, q * 128:(q + 1) * 128, :], writes=[pgs[q]])
                p.dma("sp", c.VWv[PAST - 512 + q * 128:PAST - 512 + (q + 1) * 128, :], pgs[q][:, :], reads=[pgs[q]], writes=[c.VWv])
            s0 = S + b * SL
            for src, dst in ((c.KCT, c.KCTv), (c.VCT, c.VCTv), (c.KST, c.KSTv), (c.KWT, c.KWTv)):
                p.dma("sp", dst[:, :, PAST:PAST + SL], src[:, :, s0:s0 + SL], reads=[src], writes=[dst])
            for src, dst in ((c.VS, c.VSv), (c.VW, c.VWv)):
                p.dma("sp", dst[PAST:PAST + SL, :], src[s0:s0 + SL, :], reads=[src], writes=[dst])
            for src, dst in ((c.QT, c.QTv), (c.QRT, c.QRTv)):
                p.dma("sp", dst[:, :, 0:SL], src[:, :, s0:s0 + SL], reads=[src], writes=[dst])
            p.dma("sp", c.GTv[:, 0:SL], c.GTn[:, s0:s0 + SL], reads=[c.GTn], writes=[c.GTv])
            sqv = dict(T=TV, kw0=PAST - 512, ncmp=PAST // 16 - 1, nq=SL, KCT=c.KCTv, VCT=c.VCTv, KST=c.KSTv, KWT=c.KWTv, VS=c.VSv, VW=c.VWv, QT=c.QTv, QRT=c.QRTv, GT=c.GTv,
                       KCT_t=c.KCTv, VCT_t=c.VCTv, KST_t=c.KSTv, KWT_t=c.KWTv, VS_t=c.VSv, VW_t=c.VWv, QT_t=c.QTv, QRT_t=c.QRTv, GT_t=c.GTv,
                       blocks=[(PAST // 128, 0, s0)])
            nsa_seq(p, c, j, env, sqv)
    with p.scope():
        wout = p.sb("nswout", [64, 16, D], BF16)
        p.dma("pool", wout[:, :, :], c.ns_w_out[j], writes=[wout])
        hTs = [p.sb(f"hT{i}", [128, KC, 512], F32) for i in range(2)]
        ot = [p.sb(f"ot{i}", [64, 16, 512], BF16) for i in range(2)]
        pso = [p.ps(f"pso{i}", [128, 512], F32) for i in range(2)]
        io = 0
        for ti, (t0, tw) in enumerate(tiles_of(S)):
            hT, o_ = hTs[ti % 2], ot[ti % 2]
            p.dma("sp", hT[:, :, :tw], c.H[:, :, t0:t0 + tw].rearrange("k p t -> p k t"), reads=[c.H], writes=[hT])
            p.dma("sp", o_[:, :, :tw], c.OT[:, :, t0:t0 + tw].rearrange("h p t -> p h t"), reads=[c.OT], writes=[o_])
            for oc in range(KC):
                O = pso[io % 2]
                io += 1
                for h in range(16):
                    p.op("pe", lambda e: e.matmul(O[:, :tw], lhsT=wout[:, h, oc * 128:(oc + 1) * 128], rhs=o_[:, h, :tw], start=(h == 0), stop=(h == 15)),
                         reads=[wout, o_], writes=[O], same_ok=True)
                if c.dbg_mix is not None:
                    dm = p.sb("dm", [128, 512], F32) if oc == 0 and ti == 0 else dm
                    p.op("dve", lambda e: e.tensor_copy(out=dm[:, :tw], in_=O[:, :tw]), reads=[O], writes=[dm])
                    p.dma("sp", c.dbg_mix[oc, :, t0:t0 + tw], dm[:, :tw], reads=[dm], writes=[c.dbg_mix])
                p.op("dve", lambda e: e.tensor_tensor(out=hT[:, oc, :tw], in0=hT[:, oc, :tw], in1=O[:, :tw], op=ALU.add), reads=[O, hT], writes=[hT])
            p.dma("sp", c.H[:, :, t0:t0 + tw].rearrange("k p t -> p k t"), hT[:, :, :tw], reads=[hT], writes=[c.H])


def build(S, mixers=(0, 1, 2), dn_layers=(0, 3), npool=2560):
    nc = bass.Bass("TRN2", target_bir_lowering=False)
    p = Prog(nc)
    c = Ctx()
    c.S = S
    c.npool = npool
    NT = S + ST
    c.xp = p.dram("xp", [S, D], F32, "ExternalInput")
    c.xs = p.dram("xs", [ST, D], F32, "ExternalInput")
    c.w_gate = p.dram("w_gate", [DEPTH, 128, KC, FH], F32, "ExternalInput")
    c.w_up = p.dram("w_up", [DEPTH, 128, KC, FH], F32, "ExternalInput")
    c.w_down = p.dram("w_down", [DEPTH, 128, HC, D], F32, "ExternalInput")
    c.nffn_d = p.dram("nffn", [128, DEPTH, KC], F32, "ExternalInput")
    c.nmix_d = p.dram("nmix", [128, DEPTH, KC], F32, "ExternalInput")
    c.ident_d = p.dram("ident", [128, 128], F32, "ExternalInput")
    c.yp = p.dram("yp", [S, D], F32, "ExternalOutput")
    c.ys = p.dram("ys", [ST, D], F32, "ExternalOutput")
    c.H = p.dram("H", [KC, 128, NT], F32)
    c.dn_w_in = p.dram("dn_w_in", [2, 128, KC, DN_W], F32, "ExternalInput")
    c.dn_cw = p.dram("dn_cw", [2, 128, DN_CC, 4], F32, "ExternalInput")
    c.dn_ab = p.dram("dn_ab", [2, 64, 2], F32, "ExternalInput")
    c.dn_cs_in = p.dram("dn_cs_in", [2, SB * 3, 4096], F32, "ExternalInput")
    c.dn_S_in = p.dram("dn_S_in", [2, SB, 16, 128, 128], F32, "ExternalInput")
    c.dn_ng = p.dram("dn_ng", [2, 128, 1], F32, "ExternalInput")
    c.dn_w_out = p.dram("dn_w_out", [2, 128, 16, D], F32, "ExternalInput")
    offp, totp = dn_pack_layout(128, 1)
    offs, tots = dn_pack_layout(ST, SB)
    c.dnc_p = p.dram("dnc_p", [128, totp], F32, "ExternalInput")
    c.dnc_s = p.dram("dnc_s", [ST, tots], F32, "ExternalInput")
    c.dncm_p = p.dram("dncm_p", [128, 1, 128], F32, "ExternalInput")
    c.dncm_s = p.dram("dncm_s", [128, SB, ST], F32, "ExternalInput")
    c.sel16_d = p.dram("sel16", [16, 16, 128], F32, "ExternalInput")
    dk = "ExternalOutput" if os.environ.get("DNDBG") else "Internal"
    c.QKVT = p.dram("QKVT", [DN_CC, 128, NT], F32, dk)
    c.ZT = p.dram("ZT", [16, 128, NT], BF16, dk)
    c.GT = p.dram("GT", [64, NT], F32, dk)
    c.ns_w_in = p.dram("ns_w_in", [1, 128, KC, NSA_W], F32, "ExternalInput")
    c.ns_gains = p.dram("ns_gains", [1, 128, 4], F32, "ExternalInput")
    c.ns_rotT = p.dram("ns_rotT", [128, 128], F32, "ExternalInput")
    c.ns_cos = p.dram("ns_cos", [128, NT], F32, "ExternalInput")
    c.ns_sin = p.dram("ns_sin", [128, NT], F32, "ExternalInput")
    c.ns_swa_k = p.dram("ns_swa_k", [1, SB, 512, 256], F32, "ExternalInput")
    c.ns_swa_v = p.dram("ns_swa_v", [1, SB, 512, 256], F32, "ExternalInput")
    c.QT = p.dram("QT", [8, 128, NT], BF16)
    c.QRT = p.dram("QRT", [8, 128, NT], BF16)
    c.KST = p.dram("KST", [2, 128, NT], BF16)
    c.KWT = p.dram("KWT", [2, 128, NT], BF16)
    c.KCT = p.dram("KCT", [2, 128, NT], BF16)
    c.VCT = p.dram("VCT", [2, 128, NT], BF16)
    c.VS = p.dram("VS", [NT, 256], BF16)
    c.VW = p.dram("VW", [NT, 256], BF16)
    c.GTn = p.dram("GTn", [48, NT], F32)
    c.OT = p.dram("OT", [16, 64, NT], BF16)
    TV = PAST + 128
    c.KCTv = p.dram("KCTv", [2, 128, TV], BF16)
    c.VCTv = p.dram("VCTv", [2, 128, TV], BF16)
    c.KSTv = p.dram("KSTv", [2, 128, TV], BF16)
    c.KWTv = p.dram("KWTv", [2, 128, TV], BF16)
    c.VSv = p.dram("VSv", [TV, 256], BF16)
    c.VWv = p.dram("VWv", [TV, 256], BF16)
    c.QTv = p.dram("QTv", [8, 128, 128], BF16)
    c.QRTv = p.dram("QRTv", [8, 128, 128], BF16)
    c.GTv = p.dram("GTv", [48, 128], F32)
    npool = c.npool
    c.ns_cmp_k = p.dram("ns_cmp_k", [npool * 128, 256], F32, "ExternalInput")
    c.ns_cmp_v = p.dram("ns_cmp_v", [npool * 128, 256], F32, "ExternalInput")
    c.ns_slc_k = p.dram("ns_slc_k", [npool * 128, 256], F32, "ExternalInput")
    c.ns_slc_v = p.dram("ns_slc_v", [npool * 128, 256], F32, "ExternalInput")
    c.ns_pt = p.dram("ns_pt", [SB, 64], I32, "ExternalInput")
    c.ns_iota = p.dram("ns_iota", [128, 1], F32, "ExternalInput")
    c.ns_wck = p.dram("ns_wck", [1, 128, 32, 64], F32, "ExternalInput")
    c.ns_wcv = p.dram("ns_wcv", [1, 128, 32, 64], F32, "ExternalInput")
    c.ns_pekT = p.dram("ns_pekT", [1, 64, 32], F32, "ExternalInput")
    c.ns_pevT = p.dram("ns_pevT", [1, 64, 32], F32, "ExternalInput")
    c.ns_w_out = p.dram("ns_w_out", [1, 64, 16, D], F32, "ExternalInput")
    c.ns_wtab = p.dram("ns_wtab", [128, 256], F32, "ExternalInput")
    c.ns_c2s = p.dram("ns_c2s", [128, 4, 128], F32, "ExternalInput")
    c.ns_sel48 = p.dram("ns_sel48", [48, 48, 64], F32, "ExternalInput")
    c.dbg_mix = p.dram("dbg_mix", [KC, 128, NT], F32, "ExternalOutput") if os.environ.get("NSDBG") else None
    c.sg_w_in = p.dram("sg_w_in", [1, 128, KC, 2048], F32, "ExternalInput")
    c.sg_w_out = p.dram("sg_w_out", [1, 128, KC, D], F32, "ExternalInput")
    c.sg_ln_g = p.dram("sg_ln_g", [1, 128, KC], F32, "ExternalInput")
    c.sg_ln_b = p.dram("sg_ln_b", [1, 128, KC], F32, "ExternalInput")
    c.sg_wspT = p.dram("sg_wspT", [1, 128, KC, 128], F32, "ExternalInput")
    c.sg_bsp4 = p.dram("sg_bsp4", [1, KC, 512], F32, "ExternalInput")
    c.o_sgv = p.dram("o_sgv", [1, ST, D], F32, "ExternalOutput")
    c.o_dS_p = p.dram("o_dS_p", [2, 16, 128, 128], F32, "ExternalOutput")
    c.o_dS_s = p.dram("o_dS_s", [2, SB, 16, 128, 128], F32, "ExternalOutput")
    c.o_dc_p = p.dram("o_dc_p", [2, 3, 4096], F32, "ExternalOutput")
    c.o_dc_s = p.dram("o_dc_s", [2, SB, 3, 4096], F32, "ExternalOutput")
    c.o_swa_p = p.dram("o_swa_p", [2, min(512, S), 256], F32, "ExternalOutput")
    c.o_swa_s = p.dram("o_swa_s", [2, SB, 512, 256], F32, "ExternalOutput")
    c.o_new_p = p.dram("o_new_p", [4, S, 256], F32, "ExternalOutput")
    c.o_new_s = p.dram("o_new_s", [4, ST, 256], F32, "ExternalOutput")
    c.ident = p.sb("ident", [128, 128], F32)
    c.ones = p.sb("ones", [128, 128], F32)
    c.nffn = p.sb("nffn", [128, DEPTH, KC], F32)
    c.nmix = p.sb("nmix", [128, DEPTH, KC], F32)
    p.dma("sp", c.ident[:, :], c.ident_d[:, :], writes=[c.ident])
    p.dma("sp", c.nffn[:, :, :], c.nffn_d[:, :, :], writes=[c.nffn])
    p.dma("sp", c.nmix[:, :, :], c.nmix_d[:, :, :], writes=[c.nmix])
    p.op("dve", lambda e: e.memset(c.ones[:, :], 1.0), writes=[c.ones])
    c.eps_norm = p.sb("eps_norm", [128, 1], F32)
    p.op("dve", lambda e: e.memset(c.eps_norm[:, :], NORM_EPS), writes=[c.eps_norm])
    c.onesD = p.sb("onesD", [128, 128], F32)
    p.op("dve", lambda e: e.memset(c.onesD[:, :], 1.0 / D), writes=[c.onesD])
    phase_in(p, c)
    for li in range(DEPTH):
        kind, j = li % 3, li // 3
        if kind == 0 and 0 in mixers and li in dn_layers:
            phase_dn1(p, c, li, j)
            if DNSTOP >= 2:
                phase_dn2(p, c, li, j)
        if kind == 1 and 1 in mixers:
            phase_sg(p, c, li, j)
        if kind == 2 and 2 in mixers:
            phase_nsa1(p, c, li, j)
            if NSSTOP >= 2:
                phase_nsa2(p, c, li, j)
        phase_ffn(p, c, li)
    phase_out(p, c)
    p.close()
    c.ninst = p.ninst
    return nc, c


def chunked(w, nchunk):
    K, N = w.shape
    return np.ascontiguousarray(w.reshape(nchunk, 128, N).transpose(1, 0, 2))


def featmajor(v):
    L, F = v.shape
    return np.ascontiguousarray(v.reshape(L, F // 128, 128).transpose(2, 0, 1))


def sg_inputs(w_in, ln_g, ln_b, wsp, bsp, w_out):
    n = w_in.shape[0]
    return {
        "sg_w_in": np.stack([chunked(np.asarray(w_in[i]), KC) for i in range(n)]),
        "sg_w_out": np.stack([chunked(np.asarray(w_out[i]), KC) for i in range(n)]),
        "sg_ln_g": np.stack([featmajor(np.asarray(ln_g[i:i + 1]))[:, 0, :] for i in range(n)]),
        "sg_ln_b": np.stack([featmajor(np.asarray(ln_b[i:i + 1]))[:, 0, :] for i in range(n)]),
        "sg_wspT": np.ascontiguousarray(np.asarray(wsp).transpose(0, 3, 1, 2)),
        "sg_bsp4": np.ascontiguousarray(np.tile(np.asarray(bsp), (1, 1, 4))),
    }


def dn_inputs(w_in, conv_w, a_log, dt_bias, norm, w_out):
    n = w_in.shape[0]
    wi = []
    for i in range(n):
        w = np.asarray(w_in[i], np.float32)
        wp = np.zeros((D, DN_W), np.float32)
        wp[:, 0:6144] = w[:, 0:6144]
        wp[:, 6144:6160] = w[:, 6144:6160]
        wp[:, 6176:6192] = w[:, 6160:6176]
        wi.append(chunked(wp, KC))
    ab = np.zeros((n, 64, 2), np.float32)
    ab[:, 32:48, 0] = np.asarray(a_log)
    ab[:, 32:48, 1] = np.asarray(dt_bias)
    return {
        "dn_w_in": np.stack(wi),
        "dn_cw": np.ascontiguousarray(np.asarray(conv_w, np.float32).reshape(n, 4, DN_CC, 128).transpose(0, 3, 2, 1)),
        "dn_ab": ab,
        "dn_ng": np.ascontiguousarray(np.asarray(norm, np.float32).reshape(n, 128, 1)),
        "dn_w_out": np.stack([chunked(np.asarray(w_out[i], np.float32), 16) for i in range(n)]),
    }


def dn_state_inputs(state_delta, state_conv, core):
    sd = np.asarray(state_delta, np.float32)[:, core * SB:(core + 1) * SB]
    sc = np.asarray(state_conv, np.float32)[:, core * SB:(core + 1) * SB]
    return {"dn_S_in": np.ascontiguousarray(sd), "dn_cs_in": np.ascontiguousarray(sc.reshape(2, SB * 3, 4096))}


def const_inputs():
    pk_p, cm_p = dn_consts(128, 1)
    pk_s, cm_s = dn_consts(ST, SB)
    sel = np.zeros((16, 16, 128), np.float32)
    for h in range(16):
        sel[h, h, :] = 1.0
    return {"dnc_p": pk_p, "dnc_s": pk_s, "dncm_p": cm_p, "dncm_s": cm_s, "sel16": sel}


def nsa_inputs(w_in, q_norm, k_norm, S):
    f = np.float32
    n = w_in.shape[0]
    gains = np.zeros((n, 128, 4), f)
    for i in range(n):
        gains[i, :, 0] = np.tile(np.asarray(q_norm[i], f), 2)
        for kk in range(3):
            gains[i, :, 1 + kk] = np.tile(np.asarray(k_norm[i, kk], f), 2)
    rot = np.zeros((128, 128), f)
    for blk in range(2):
        for d in range(32):
            rot[blk * 64 + d + 32, blk * 64 + d] = -1.0
            rot[blk * 64 + d, blk * 64 + d + 32] = 1.0
    pos = np.concatenate([np.arange(S), np.tile(PAST + np.arange(SL), SB)]).astype(f)
    inv = (10000.0 ** (-np.arange(32, dtype=f) / 32)).astype(f)
    ang = pos[None, :] * np.tile(inv, 4)[:, None]
    return {
        "ns_w_in": np.stack([chunked(np.asarray(w_in[i], f), KC) for i in range(n)]),
        "ns_gains": gains, "ns_rotT": rot,
        "ns_cos": np.cos(ang).astype(f), "ns_sin": np.sin(ang).astype(f),
    }


def nsa_cache_inputs(swa_k, swa_v, pools, page_table, core):
    f = np.float32
    sl = slice(core * SB, (core + 1) * SB)
    d = {"ns_swa_k": np.ascontiguousarray(np.asarray(swa_k, f)[:, sl].reshape(1, SB, 512, 256)),
         "ns_swa_v": np.ascontiguousarray(np.asarray(swa_v, f)[:, sl].reshape(1, SB, 512, 256)),
         "ns_pt": np.ascontiguousarray(np.asarray(page_table, np.int32)[sl]),
         "ns_iota": np.arange(128, dtype=f).reshape(128, 1)}
    for nm in ("cmp_k", "cmp_v", "slc_k", "slc_v"):
        a = np.asarray(pools["cache_" + nm], f)
        d["ns_" + nm] = a.reshape(a.shape[1] * 128, 256)
    return d


def nsa_attn_inputs(pe_k, pe_v, w_ck, w_cv, w_out, S):
    f = np.float32
    n = np.asarray(w_ck).shape[0]
    def wl(w):
        a = np.asarray(w, f).reshape(n, 32, 64, 64).transpose(0, 2, 1, 3)
        return np.ascontiguousarray(np.concatenate([a, a], 1))
    d = {"ns_wck": wl(w_ck), "ns_wcv": wl(w_cv),
         "ns_pekT": np.ascontiguousarray(np.asarray(pe_k, f).transpose(0, 2, 1)),
         "ns_pevT": np.ascontiguousarray(np.asarray(pe_v, f).transpose(0, 2, 1)),
         "ns_w_out": np.ascontiguousarray(np.asarray(w_out, f).reshape(n, 16, 64, D).transpose(0, 2, 1, 3))}
    d.update(nsa_consts())
    return d


def kernel(**inp):
    x_prompt = np.asarray(inp["x_prompt"], np.float32)
    x_sample = np.asarray(inp["x_sample"], np.float32)
    B, S, _ = x_prompt.shape
    npool = int(np.asarray(inp["cache_cmp_k"]).shape[1])
    nc, c = build(S, mixers=MIXERS, npool=npool)
    shared = {
        "w_gate": np.stack([chunked(np.asarray(inp["ffn_w_gate"][i]), KC) for i in range(DEPTH)]),
        "w_up": np.stack([chunked(np.asarray(inp["ffn_w_up"][i]), KC) for i in range(DEPTH)]),
        "w_down": np.stack([chunked(np.asarray(inp["ffn_w_down"][i]), HC) for i in range(DEPTH)]),
        "nffn": featmajor(np.asarray(inp["norm_ffn"], np.float32)),
        "nmix": featmajor(np.asarray(inp["norm_mix"], np.float32)),
        "ident": np.eye(128, dtype=np.float32),
    }
    shared.update(sg_inputs(inp["sg_w_in"], inp["sg_ln_g"], inp["sg_ln_b"], inp["sg_w_spatial"], inp["sg_b_spatial"],
                            inp["sg_w_out"]))
    shared.update(dn_inputs(inp["dn_w_in"], inp["dn_conv_w"], inp["dn_a_log"], inp["dn_dt_bias"], inp["dn_norm"], inp["dn_w_out"]))
    shared.update(const_inputs())
    shared.update(nsa_inputs(np.asarray(inp["nsa_w_in"]), np.asarray(inp["nsa_q_norm"]), np.asarray(inp["nsa_k_norm"]), S))
    shared.update(nsa_attn_inputs(inp["nsa_cmp_pe_k"], inp["nsa_cmp_pe_v"], inp["nsa_cmp_w_k"], inp["nsa_cmp_w_v"], inp["nsa_w_out"], S))
    pools = {k: inp[k] for k in ("cache_cmp_k", "cache_cmp_v", "cache_slc_k", "cache_slc_v")}
    in_maps = []
    for core in range(NCORES):
        m = dict(shared)
        m["xp"] = np.ascontiguousarray(x_prompt[core % B])
        m["xs"] = np.ascontiguousarray(x_sample[core * SB:(core + 1) * SB].reshape(ST, D))
        m.update(dn_state_inputs(inp["state_delta"], inp["state_conv"], core))
        cc = nsa_cache_inputs(inp["cache_swa_k"], inp["cache_swa_v"], pools, inp["page_table"], core)
        if core > 0:
            for nm in ("ns_cmp_k", "ns_cmp_v", "ns_slc_k", "ns_slc_v"):
                cc[nm] = in_maps[0][nm]
        m.update(cc)
        in_maps.append(m)
    res = run_bass_kernel_spmd(nc, in_maps, core_ids=list(range(NCORES)))
    r = res.results
    cat = np.concatenate
    NB = NCORES * SB
    y_prompt = np.stack([r[b]["yp"] for b in range(B)])
    y_sample = cat([r[k]["ys"].reshape(SB, SL, D) for k in range(NCORES)], 0)
    dS_p = np.stack([r[b]["o_dS_p"] for b in range(B)], 1)
    dS_s = cat([r[k]["o_dS_s"] for k in range(NCORES)], 1)
    dc_p = np.stack([r[b]["o_dc_p"] for b in range(B)], 1)
    dc_s = cat([r[k]["o_dc_s"] for k in range(NCORES)], 1)
    sgv = cat([r[k]["o_sgv"].reshape(1, SB, SL, D) for k in range(NCORES)], 1)
    W = r[0]["o_swa_p"].shape[1]
    swa_p = [np.stack([r[b]["o_swa_p"][i] for b in range(B)]).reshape(1, B, W, 4, 64) for i in range(2)]
    swa_s = [cat([r[k]["o_swa_s"][i] for k in range(NCORES)], 0).reshape(1, NB, 512, 4, 64) for i in range(2)]
    new_p = [np.stack([r[b]["o_new_p"][i] for b in range(B)]).reshape(1, B, S, 4, 64) for i in range(4)]
    new_s = [cat([r[k]["o_new_s"][i].reshape(SB, SL, 4, 64) for k in range(NCORES)], 0).reshape(1, NB, SL, 4, 64) for i in range(4)]
    return (y_prompt, y_sample, dS_p, dS_s, dc_p, dc_s, sgv, swa_p[0], swa_p[1], swa_s[0], swa_s[1],
            new_p[0], new_p[1], new_p[2], new_p[3], new_s[0], new_s[1], new_s[2], new_s[3])
```

```python
import contextlib
import numpy as np
import concourse.bass as bass
import concourse.mybir as mybir
from concourse.bass_utils import run_bass_kernel_spmd

F32 = mybir.dt.float32
BF16 = mybir.dt.bfloat16
I32 = mybir.dt.int32
AF = mybir.ActivationFunctionType
ALU = mybir.AluOpType
AX = mybir.AxisListType

D = 1024
KC = 8
FH = 2816
HC = 22
DEPTH = 4
NCORES = 8
SB = 4
SL = 4
ST = SB * SL
NORM_EPS = 1e-6


class T:
    def __init__(self, h, name):
        self.h = h
        self.name = name
        self.w = None
        self.r = {}

    def __getitem__(self, k):
        return self.h[k]


class Prog:
    def __init__(self, nc):
        self.nc = nc
        self.engs = {}
        self.stack = []
        for nm, h in (("pe", nc.tensor), ("act", nc.scalar), ("dve", nc.vector), ("pool", nc.gpsimd), ("sp", nc.sync)):
            self.engs[nm] = dict(h=h, sem=self._sem("s_" + nm), cnt=0, known={})
        self.dq = {}
        for q in ("sp", "pool"):
            self.dq[q] = dict(sems=[self._sem(f"d_{q}{i}") for i in range(8)], cnt=[0] * 8, nxt=0)
        self.ninst = 0

    def _sem(self, name):
        cm = self.nc.semaphore(name)
        s = cm.__enter__()
        self.stack.append(cm)
        return s

    def sb(self, name, shape, dt):
        self.uid = getattr(self, "uid", 0) + 1
        cm = self.nc.sbuf_tensor(f"sb{self.uid}_{name}", shape, dt)
        h = cm.__enter__()
        self.stack.append(cm)
        return T(h, name)

    def ps(self, name, shape, dt):
        self.uid = getattr(self, "uid", 0) + 1
        cm = self.nc.psum_tensor(f"ps{self.uid}_{name}", shape, dt)
        h = cm.__enter__()
        self.stack.append(cm)
        t = T(h, name)
        t.psum = True
        return t

    def dram(self, name, shape, dt, kind="Internal"):
        h = self.nc.dram_tensor(name, shape, dt, kind=kind)
        return T(h.ap(), name)

    @contextlib.contextmanager
    def scope(self):
        n = len(self.stack)
        self.barrier()
        yield
        self.barrier()
        while len(self.stack) > n:
            self.stack.pop().__exit__(None, None, None)

    def _need(self, e, src, n):
        E = self.engs[e]
        if E["known"].get(src, 0) >= n:
            return
        if src in self.engs:
            sem, val = self.engs[src]["sem"], n
        else:
            q, i = src
            sem, val = self.dq[q]["sems"][i], 16 * n
        E["h"].wait_ge(sem, val)
        E["known"][src] = n
        self.ninst += 1

    def _deps(self, e, reads, writes, same_ok=False):
        for t in reads:
            if t.w is not None and not (same_ok and t.w[0] == e):
                self._need(e, *t.w)
            if getattr(t, "psum", False):
                for src, n in t.r.items():
                    if src != e:
                        self._need(e, src, n)
        for t in writes:
            if t.w is not None and not (same_ok and t.w[0] == e):
                self._need(e, *t.w)
            for src, n in t.r.items():
                if src == e:
                    continue
                self._need(e, src, n)

    def _mark(self, stream, n, reads, writes):
        for t in writes:
            t.w = (stream, n)
            t.r = {}
        for t in reads:
            if t in writes:
                continue
            t.r[stream] = n

    def op(self, e, fn, reads=(), writes=(), same_ok=False):
        E = self.engs[e]
        if e != "pe":
            same_ok = False
        self._deps(e, reads, writes, same_ok)
        ins = fn(E["h"])
        E["cnt"] += 1
        ins.then_inc(E["sem"], 1)
        self._mark(e, E["cnt"], reads, writes)
        self.ninst += 1
        return ins

    def dma(self, q, out, in_, reads=(), writes=(), **kw):
        Q = self.dq[q]
        i = Q["nxt"]
        Q["nxt"] = (i + 1) % len(Q["sems"])
        stream = (q, i)
        if Q["cnt"][i] > 0:
            self._need(q, stream, Q["cnt"][i])
        self._deps(q, reads, writes)
        ins = self.engs[q]["h"].dma_start(out=out, in_=in_, **kw)
        Q["cnt"][i] += 1
        ins.then_inc(Q["sems"][i], 16)
        self._mark(stream, Q["cnt"][i], reads, writes)
        self.ninst += 1
        return ins

    def idma(self, out, in_, idx_ap, reads=(), writes=()):
        Q = self.dq["pool"]
        i = Q["nxt"]
        Q["nxt"] = (i + 1) % len(Q["sems"])
        stream = ("pool", i)
        if Q["cnt"][i] > 0:
            self._need("pool", stream, Q["cnt"][i])
        self._deps("pool", reads, writes)
        ins = self.nc.gpsimd.indirect_dma_start(out=out, out_offset=None, in_=in_,
                                                in_offset=bass.IndirectOffsetOnAxis(ap=idx_ap, axis=0))
        Q["cnt"][i] += 1
        ins.then_inc(Q["sems"][i], 16)
        self._mark(stream, Q["cnt"][i], reads, writes)
        self.ninst += 1
        return ins

    def barrier(self):
        streams = [(e, self.engs[e]["cnt"]) for e in ("pe", "act", "dve", "pool") if self.engs[e]["cnt"]]
        for q in self.dq:
            for i, c in enumerate(self.dq[q]["cnt"]):
                if c:
                    streams.append(((q, i), c))
        for e in ("pe", "act", "dve", "pool", "sp"):
            for s, c in streams:
                if s != e:
                    self._need(e, s, c)

    def close(self):
        self.barrier()
        while self.stack:
            self.stack.pop().__exit__(None, None, None)


class Ctx:
    pass


def tiles_of(S):
    tl = [(t0, 512) for t0 in range(0, S, 512)]
    tl.append((S, ST))
    return tl


def phase_in(p, c):
    with p.scope():
        xin = [p.sb(f"xin{i}", [128, D], F32) for i in range(4)]
        ho = [p.sb(f"ho{i}", [128, KC, 512], F32) for i in range(2)]
        pst = [p.ps(f"pst{i}", [128, 512], F32) for i in range(4)]
        it = 0
        for ti, (t0, tw) in enumerate(tiles_of(c.S)):
            nsub = (tw + 127) // 128
            for j in range(nsub):
                r = min(128, tw - j * 128)
                src = c.xp[t0 + j * 128:t0 + j * 128 + r, :] if t0 < c.S else c.xs[:, :]
                p.dma("sp", xin[j][:r, :], src, writes=[xin[j]])
            hb = ho[ti % 2]
            for kc in range(KC):
                P = pst[it % 4]
                it += 1
                for j in range(nsub):
                    r = min(128, tw - j * 128)
                    p.op("pe", lambda e: e.transpose(P[:, j * 128:j * 128 + r], xin[j][:r, kc * 128:(kc + 1) * 128], c.ident[:r, :r]),
                         reads=[xin[j]], writes=[P], same_ok=True)
                eng = "dve" if kc % 2 == 0 else "act"
                if eng == "dve":
                    p.op("dve", lambda e: e.tensor_copy(out=hb[:, kc, :tw], in_=P[:, :tw]), reads=[P], writes=[hb], same_ok=True)
                else:
                    p.op("act", lambda e: e.copy(out=hb[:, kc, :tw], in_=P[:, :tw]), reads=[P], writes=[hb], same_ok=True)
            p.dma("pool", c.H[:, :, t0:t0 + tw].rearrange("k p t -> p k t"), hb[:, :, :tw], reads=[hb], writes=[c.H])


def phase_out(p, c):
    with p.scope():
        hi = [p.sb(f"hi{i}", [128, KC, 512], F32) for i in range(2)]
        yo = [p.sb(f"yo{i}", [128, D], F32) for i in range(2)]
        pst = [p.ps(f"pso{i}", [128, 512], F32) for i in range(4)]
        it = 0
        for ti, (t0, tw) in enumerate(tiles_of(c.S)):
            hb = hi[ti % 2]
            p.dma("sp", hb[:, :, :tw], c.H[:, :, t0:t0 + tw].rearrange("k p t -> p k t"), reads=[c.H], writes=[hb])
            nsub = (tw + 127) // 128
            for j in range(nsub):
                r = min(128, tw - j * 128)
                yb = yo[it % 2]
                for half in range(2):
                    P = pst[(2 * it + half) % 4]
                    for k4 in range(4):
                        kc = half * 4 + k4
                        p.op("pe", lambda e: e.transpose(P[:r, k4 * 128:(k4 + 1) * 128], hb[:, kc, j * 128:j * 128 + r], c.ident[:, :]),
                             reads=[hb], writes=[P], same_ok=True)
                    if half == 0:
                        p.op("dve", lambda e: e.tensor_copy(out=yb[:r, 0:512], in_=P[:r, :]), reads=[P], writes=[yb], same_ok=True)
                    else:
                        p.op("act", lambda e: e.copy(out=yb[:r, 512:1024], in_=P[:r, :]), reads=[P], writes=[yb], same_ok=True)
                it += 1
                dst = c.yp[t0 + j * 128:t0 + j * 128 + r, :] if t0 < c.S else c.ys[:, :]
                p.dma("pool", dst, yb[:r, :], reads=[yb], writes=[c.yp if t0 < c.S else c.ys])


def rmsnorm_tile(p, c, hT, xn, tw, gcol, ps_stat, tmp, rstd):
    for kc in range(KC):
        sq = tmp[kc % 2]
        p.op("act", lambda e: e.activation(out=sq[:, :tw], in_=hT[:, kc, :tw], func=AF.Square), reads=[hT], writes=[sq])
        p.op("pe", lambda e: e.matmul(ps_stat[:, :tw], lhsT=c.onesD[:, :], rhs=sq[:, :tw], start=(kc == 0), stop=(kc == KC - 1)),
             reads=[sq, c.onesD], writes=[ps_stat], same_ok=True)
    p.op("act", lambda e: e.activation(out=rstd[:, :tw], in_=ps_stat[:, :tw], func=AF.Sqrt, bias=c.eps_norm[:, 0:1], scale=1.0),
         reads=[ps_stat, c.eps_norm], writes=[rstd])
    p.op("dve", lambda e: e.reciprocal(out=rstd[:, :tw], in_=rstd[:, :tw]), reads=[rstd], writes=[rstd])
    for kc in range(KC):
        p.op("dve", lambda e: e.scalar_tensor_tensor(out=xn[:, kc, :tw], in0=hT[:, kc, :tw], scalar=gcol(kc), in1=rstd[:, :tw],
                                                   op0=ALU.mult, op1=ALU.mult), reads=[hT, rstd], writes=[xn], same_ok=True)


def phase_ffn(p, c, li):
    with p.scope():
        wg = p.sb("wg", [128, KC, FH], BF16)
        wu = p.sb("wu", [128, KC, FH], BF16)
        wd = p.sb("wd", [128, HC, D], BF16)
        for kc in range(KC):
            p.dma("pool", wg[:, kc, :], c.w_gate[li, :, kc, :], writes=[wg])
            p.dma("pool", wu[:, kc, :], c.w_up[li, :, kc, :], writes=[wu])
        for hc in range(HC):
            p.dma("pool", wd[:, hc, :], c.w_down[li, :, hc, :], writes=[wd])
        hTs = [p.sb(f"hT{i}", [128, KC, 512], F32) for i in range(2)]
        xn = p.sb("xn", [128, KC, 512], BF16)
        hh = p.sb("hh", [128, HC, 512], BF16)
        tmp = [p.sb(f"sq{i}", [128, 512], F32) for i in range(2)]
        rstd = p.sb("rstd", [128, 512], F32)
        sg = [p.sb(f"sg{i}", [128, 512], F32) for i in range(2)]
        ps_stat = p.ps("ps_stat", [128, 512], F32)
        psg = [p.ps(f"psg{i}", [128, 512], F32) for i in range(2)]
        psu = [p.ps(f"psu{i}", [128, 512], F32) for i in range(2)]
        pso = [p.ps(f"pso{i}", [128, 512], F32) for i in range(2)]
        it = 0
        io = 0
        for ti, (t0, tw) in enumerate(tiles_of(c.S)):
            hT = hTs[ti % 2]
            p.dma("sp", hT[:, :, :tw], c.H[:, :, t0:t0 + tw].rearrange("k p t -> p k t"), reads=[c.H], writes=[hT])
            rmsnorm_tile(p, c, hT, xn, tw, lambda kc: c.nffn[:, li, kc:kc + 1], ps_stat, tmp, rstd)
            for hc in range(HC):
                G, U, S_ = psg[it % 2], psu[it % 2], sg[it % 2]
                it += 1
                for kc in range(KC):
                    p.op("pe", lambda e: e.matmul(G[:, :tw], lhsT=wg[:, kc, hc * 128:(hc + 1) * 128], rhs=xn[:, kc, :tw],
                                                  start=(kc == 0), stop=(kc == KC - 1)), reads=[wg, xn], writes=[G], same_ok=True)
                for kc in range(KC):
                    p.op("pe", lambda e: e.matmul(U[:, :tw], lhsT=wu[:, kc, hc * 128:(hc + 1) * 128], rhs=xn[:, kc, :tw],
                                                  start=(kc == 0), stop=(kc == KC - 1)), reads=[wu, xn], writes=[U], same_ok=True)
                p.op("act", lambda e: e.activation(out=S_[:, :tw], in_=G[:, :tw], func=AF.Silu), reads=[G], writes=[S_])
                p.op("dve", lambda e: e.tensor_tensor(out=hh[:, hc, :tw], in0=S_[:, :tw], in1=U[:, :tw], op=ALU.mult),
                     reads=[S_, U], writes=[hh], same_ok=True)
            for oc in range(KC):
                O = pso[io % 2]
                io += 1
                for hc in range(HC):
                    p.op("pe", lambda e: e.matmul(O[:, :tw], lhsT=wd[:, hc, oc * 128:(oc + 1) * 128], rhs=hh[:, hc, :tw],
                                                  start=(hc == 0), stop=(hc == HC - 1)), reads=[wd, hh], writes=[O], same_ok=True)
                p.op("dve", lambda e: e.tensor_tensor(out=hT[:, oc, :tw], in0=hT[:, oc, :tw], in1=O[:, :tw], op=ALU.add),
                     reads=[O, hT], writes=[hT], same_ok=True)
            p.dma("sp", c.H[:, :, t0:t0 + tw].rearrange("k p t -> p k t"), hT[:, :, :tw], reads=[hT], writes=[c.H])


LN_EPS = 1e-5
import os
SGSTOP = int(os.environ.get('SGSTOP', '9'))
SGSUB = int(os.environ.get('SGSUB', '9'))
SGEV = os.environ.get('SGEV', 'dve')
DNSTOP = int(os.environ.get('DNSTOP', '9'))
NSSTOP = int(os.environ.get('NSSTOP', '9'))
DN2STOP = int(os.environ.get('DN2STOP', '9'))
MIXERS = (0, 1, 2)
GELU_C = 1.5957691216057308


def phase_sg(p, c, li, j):
    S = c.S
    with p.scope():
        win = p.sb("sgwin", [128, KC, 2048], BF16)
        wout = p.sb("sgwout", [128, KC, D], BF16)
        for kc in range(KC):
            p.dma("pool", win[:, kc, :], c.sg_w_in[j, :, kc, :], writes=[win])
            p.dma("pool", wout[:, kc, :], c.sg_w_out[j, :, kc, :], writes=[wout])
        lng = p.sb("lng", [128, KC], F32)
        lnb = p.sb("lnb", [128, KC], F32)
        p.dma("sp", lng[:, :], c.sg_ln_g[j], writes=[lng])
        p.dma("sp", lnb[:, :], c.sg_ln_b[j], writes=[lnb])
        wspf = p.sb("wspf", [128, KC, 128], F32)
        wsp = p.sb("wsp", [128, KC, 128], BF16)
        p.dma("sp", wspf[:, :, :], c.sg_wspT[j], writes=[wspf])
        for g in range(KC):
            p.op("pool", lambda e: e.affine_select(out=wspf[:, g, :], in_=wspf[:, g, :], pattern=[[1, 128]], compare_op=ALU.is_ge,
                                                   fill=0.0, base=0, channel_multiplier=-1), reads=[wspf], writes=[wspf], same_ok=True)
        p.op("dve", lambda e: e.tensor_copy(out=wsp[:, :, :], in_=wspf[:, :, :]), reads=[wspf], writes=[wsp])
        bsp = p.sb("bsp", [128, KC, 512], F32)
        p.dma("sp", bsp[:, :, :], c.sg_bsp4[j].partition_broadcast(128), writes=[bsp])
        wsm_f = p.sb("wsmf", [ST, KC, ST], F32)
        wsm = p.sb("wsm", [ST, KC, ST], BF16)
        p.op("dve", lambda e: e.memset(wsm_f[:, :, :], 0.0), writes=[wsm_f])
        for b in range(SB):
            p.dma("sp", wsm_f[b * SL:(b + 1) * SL, :, b * SL:(b + 1) * SL], wspf[0:SL, :, 0:SL], reads=[wspf], writes=[wsm_f])
        p.op("dve", lambda e: e.tensor_copy(out=wsm[:, :, :], in_=wsm_f[:, :, :]), reads=[wsm_f], writes=[wsm])
        bsm = p.sb("bsm", [128, KC, ST], F32)
        for b in range(SB):
            p.op("pool", lambda e: e.tensor_copy(out=bsm[:, :, b * SL:(b + 1) * SL], in_=bsp[:, :, 0:SL]), reads=[bsp], writes=[bsm], same_ok=True)
        eps_ln = p.sb("eps_ln", [128, 1], F32)
        p.op("dve", lambda e: e.memset(eps_ln[:, :], LN_EPS), writes=[eps_ln])

        hTs = [p.sb(f"hT{i}", [128, KC, 512], F32) for i in range(2)]
        xn = p.sb("xn", [128, KC, 512], BF16)
        uT = p.sb("uT", [128, KC, 512], BF16)
        vT = p.sb("vT", [128, KC, 512], F32)
        um = p.sb("um", [128, KC, 512], BF16)
        vtok = [p.sb(f"vtok{i}", [128, D], BF16) for i in range(4)]
        vtokf = p.sb("vtokf", [ST, D], F32)
        tmp = [p.sb(f"sq{i}", [128, 512], F32) for i in range(2)]
        t2 = [p.sb(f"t2{i}", [128, 512], F32) for i in range(2)]
        rstd = p.sb("rstd", [128, 512], F32)
        mean = p.sb("mean", [128, 512], F32)
        ps_stat = p.ps("ps_stat", [128, 512], F32)
        ps_s2 = p.ps("ps_s2", [128, 512], F32)
        psA = [p.ps(f"psA{i}", [128, 512], F32) for i in range(2)]
        psT = [p.ps(f"psT{i}", [128, 512], F32) for i in range(2)]
        psM = [p.ps(f"psM{i}", [128, 512], F32) for i in range(2)]
        ia = 0
        itr = 0
        im = 0
        for ti, (t0, tw) in enumerate(tiles_of(S)):
            smp = t0 >= S
            hT = hTs[ti % 2]
            p.dma("sp", hT[:, :, :tw], c.H[:, :, t0:t0 + tw].rearrange("k p t -> p k t"), reads=[c.H], writes=[hT])
            rmsnorm_tile(p, c, hT, xn, tw, lambda kc: c.nmix[:, li, kc:kc + 1], ps_stat, tmp, rstd)
            for fc in range(16):
                A = psA[ia % 2]
                x2, tt = tmp[ia % 2], t2[ia % 2]
                ia += 1
                for kc in range(KC):
                    p.op("pe", lambda e: e.matmul(A[:, :tw], lhsT=win[:, kc, fc * 128:(fc + 1) * 128], rhs=xn[:, kc, :tw],
                                                  start=(kc == 0), stop=(kc == KC - 1)), reads=[win, xn], writes=[A], same_ok=True)
                p.op("act", lambda e: e.activation(out=x2[:, :tw], in_=A[:, :tw], func=AF.Square), reads=[A], writes=[x2])
                p.op("dve", lambda e: e.tensor_scalar(out=x2[:, :tw], in0=x2[:, :tw], scalar1=0.044715, scalar2=1.0, op0=ALU.mult, op1=ALU.add),
                     reads=[x2], writes=[x2])
                p.op("dve", lambda e: e.tensor_tensor(out=x2[:, :tw], in0=x2[:, :tw], in1=A[:, :tw], op=ALU.mult), reads=[x2, A], writes=[x2])
                p.op("act", lambda e: e.activation(out=tt[:, :tw], in_=x2[:, :tw], func=AF.Sigmoid, scale=GELU_C), reads=[x2], writes=[tt])
                dst, dk = (uT, fc) if fc < 8 else (vT, fc - 8)
                p.op("dve", lambda e: e.tensor_tensor(out=dst[:, dk, :tw], in0=tt[:, :tw], in1=A[:, :tw], op=ALU.mult),
                     reads=[tt, A], writes=[dst], same_ok=True)
            for kc in range(KC if SGSTOP >= 2 else 0):
                sq = tmp[kc % 2]
                p.op("act", lambda e: e.activation(out=sq[:, :tw], in_=vT[:, kc, :tw], func=AF.Square), reads=[vT], writes=[sq])
                p.op("pe", lambda e: e.matmul(ps_s2[:, :tw], lhsT=c.onesD[:, :], rhs=sq[:, :tw], start=(kc == 0), stop=(kc == KC - 1)),
                     reads=[sq, c.onesD], writes=[ps_s2], same_ok=True)
            if SGSTOP < 2:
                p.dma("sp", c.H[:, :, t0:t0 + tw].rearrange("k p t -> p k t"), hT[:, :, :tw], reads=[hT], writes=[c.H])
                continue
            for kc in range(KC):
                p.op("pe", lambda e: e.matmul(ps_stat[:, :tw], lhsT=c.onesD[:, :], rhs=vT[:, kc, :tw], start=(kc == 0), stop=(kc == KC - 1)),
                     reads=[vT, c.onesD], writes=[ps_stat], same_ok=True)
            p.op("act", lambda e: e.copy(out=mean[:, :tw], in_=ps_stat[:, :tw]), reads=[ps_stat], writes=[mean])
            if SGSUB < 2:
                p.dma("sp", c.H[:, :, t0:t0 + tw].rearrange("k p t -> p k t"), hT[:, :, :tw], reads=[hT], writes=[c.H])
                continue
            m2 = tmp[0]
            p.op("dve", lambda e: e.tensor_tensor(out=m2[:, :tw], in0=mean[:, :tw], in1=mean[:, :tw], op=ALU.mult), reads=[mean], writes=[m2])
            p.op("dve", lambda e: e.tensor_tensor(out=m2[:, :tw], in0=ps_s2[:, :tw], in1=m2[:, :tw], op=ALU.subtract), reads=[ps_s2, m2], writes=[m2])
            if SGSUB < 3:
                p.dma("sp", c.H[:, :, t0:t0 + tw].rearrange("k p t -> p k t"), hT[:, :, :tw], reads=[hT], writes=[c.H])
                continue
            p.op("act", lambda e: e.activation(out=rstd[:, :tw], in_=m2[:, :tw], func=AF.Sqrt, bias=eps_ln[:, 0:1], scale=1.0),
                 reads=[m2, eps_ln], writes=[rstd])
            p.op("dve", lambda e: e.reciprocal(out=rstd[:, :tw], in_=rstd[:, :tw]), reads=[rstd], writes=[rstd])
            if SGSUB < 4:
                p.dma("sp", c.H[:, :, t0:t0 + tw].rearrange("k p t -> p k t"), hT[:, :, :tw], reads=[hT], writes=[c.H])
                continue
            for kc in range(KC):
                p.op("dve", lambda e: e.tensor_tensor(out=vT[:, kc, :tw], in0=vT[:, kc, :tw], in1=mean[:, :tw], op=ALU.subtract),
                     reads=[vT, mean], writes=[vT], same_ok=True)
                p.op("dve", lambda e: e.tensor_tensor(out=vT[:, kc, :tw], in0=vT[:, kc, :tw], in1=rstd[:, :tw], op=ALU.mult),
                     reads=[vT, rstd], writes=[vT], same_ok=True)
                p.op("dve", lambda e: e.tensor_scalar(out=vT[:, kc, :tw], in0=vT[:, kc, :tw], scalar1=lng[:, kc:kc + 1], scalar2=lnb[:, kc:kc + 1],
                                                      op0=ALU.mult, op1=ALU.add), reads=[vT, lng, lnb], writes=[vT], same_ok=True)
            if SGSTOP < 3:
                p.dma("sp", c.H[:, :, t0:t0 + tw].rearrange("k p t -> p k t"), hT[:, :, :tw], reads=[hT], writes=[c.H])
                continue
            nsub = (tw + 127) // 128
            for sj in range(nsub):
                r = min(128, tw - sj * 128)
                for half in range(2):
                    P = psT[itr % 2]
                    itr += 1
                    for g4 in range(4):
                        g = half * 4 + g4
                        p.op("pe", lambda e: e.transpose(P[:r, g4 * 128:(g4 + 1) * 128], vT[:, g, sj * 128:sj * 128 + r], c.ident[:, :]),
                             reads=[vT], writes=[P], same_ok=True)
                    eng = "act" if half else "dve"
                    if SGEV:
                        eng = SGEV
                    if SGSUB == 5:
                        continue
                    if smp:
                        p.op("dve", lambda e: e.tensor_copy(out=vtokf[:r, half * 512:(half + 1) * 512], in_=P[:r, :]),
                             reads=[P], writes=[vtokf], same_ok=True)
                    if SGSUB == 6:
                        continue
                    if eng == "dve":
                        p.op("dve", lambda e: e.tensor_copy(out=vtok[sj][:r, half * 512:(half + 1) * 512], in_=P[:r, :]),
                             reads=[P], writes=[vtok[sj]], same_ok=True)
                    else:
                        p.op("act", lambda e: e.copy(out=vtok[sj][:r, half * 512:(half + 1) * 512], in_=P[:r, :]),
                             reads=[P], writes=[vtok[sj]], same_ok=True)
            if smp and SGSUB > 7:
                p.dma("pool", c.o_sgv[j], vtokf[:, :], reads=[vtokf], writes=[c.o_sgv])
            if SGSTOP < 4:
                p.dma("sp", c.H[:, :, t0:t0 + tw].rearrange("k p t -> p k t"), hT[:, :, :tw], reads=[hT], writes=[c.H])
                continue
            for g in range(KC):
                M = psM[im % 2]
                tb = t2[im % 2]
                im += 1
                for sj in range(nsub):
                    r = min(128, tw - sj * 128)
                    rhs = wsm[:r, g, :r] if smp else wsp[:, g, :]
                    p.op("pe", lambda e: e.matmul(M[:, sj * 128:sj * 128 + r], lhsT=vtok[sj][:r, g * 128:(g + 1) * 128], rhs=rhs, start=True, stop=True),
                         reads=[vtok[sj], wsm if smp else wsp], writes=[M], same_ok=True)
                bias = bsm[:, g, :tw] if smp else bsp[:, g, :tw]
                p.op("dve", lambda e: e.tensor_tensor(out=tb[:, :tw], in0=M[:, :tw], in1=bias, op=ALU.add), reads=[M, bsm, bsp], writes=[tb])
                p.op("pool", lambda e: e.tensor_tensor(out=um[:, g, :tw], in0=tb[:, :tw], in1=uT[:, g, :tw], op=ALU.mult),
                     reads=[tb, uT], writes=[um])
            if SGSTOP < 5:
                p.dma("sp", c.H[:, :, t0:t0 + tw].rearrange("k p t -> p k t"), hT[:, :, :tw], reads=[hT], writes=[c.H])
                continue
            for oc in range(KC):
                O = psA[ia % 2]
                ia += 1
                for g in range(KC):
                    p.op("pe", lambda e: e.matmul(O[:, :tw], lhsT=wout[:, g, oc * 128:(oc + 1) * 128], rhs=um[:, g, :tw],
                                                  start=(g == 0), stop=(g == KC - 1)), reads=[wout, um], writes=[O], same_ok=True)
                p.op("dve", lambda e: e.tensor_tensor(out=hT[:, oc, :tw], in0=hT[:, oc, :tw], in1=O[:, :tw], op=ALU.add),
                     reads=[O, hT], writes=[hT], same_ok=True)
            p.dma("sp", c.H[:, :, t0:t0 + tw].rearrange("k p t -> p k t"), hT[:, :, :tw], reads=[hT], writes=[c.H])


DN_CC = 32
DN_W = 6208


def phase_dn1(p, c, li, j):
    S = c.S
    with p.scope():
        win = p.sb("dnwin", [128, KC, DN_W], BF16)
        for kc in range(KC):
            p.dma("pool", win[:, kc, :], c.dn_w_in[j, :, kc, :], writes=[win])
        cw = p.sb("cw", [128, DN_CC, 4], F32)
        p.dma("sp", cw[:, :, :], c.dn_cw[j], writes=[cw])
        ab = p.sb("ab", [64, 2], F32)
        p.dma("sp", ab[:, :], c.dn_ab[j], writes=[ab])
        nalog = p.sb("nalog", [64, 1], F32)
        p.op("act", lambda e: e.activation(out=nalog[:, :], in_=ab[:, 0:1], func=AF.Exp), reads=[ab], writes=[nalog])
        p.op("dve", lambda e: e.tensor_scalar(out=nalog[:, :], in0=nalog[:, :], scalar1=-1.0, scalar2=None, op0=ALU.mult),
             reads=[nalog], writes=[nalog])
        one_c = p.sb("one_c", [128, 1], F32)
        p.op("dve", lambda e: e.memset(one_c[:, :], 1.0), writes=[one_c])
        eps_c = p.sb("eps_c", [128, 1], F32)
        p.op("dve", lambda e: e.memset(eps_c[:, :], 1e-6), writes=[eps_c])
        carry = p.sb("carry", [128, DN_CC, 3], F32)
        p.op("dve", lambda e: e.memset(carry[:, :, :], 0.0), writes=[carry])
        cs_tok = p.sb("cs_tok", [SB * 3, 4096], F32)
        p.dma("sp", cs_tok[:, :], c.dn_cs_in[j], writes=[cs_tok])
        cs_fm = p.sb("cs_fm", [128, DN_CC, SB * 3], F32)
        newc = p.sb("newc", [128, DN_CC, SB * 3], F32)
        psA = [p.ps(f"psA{i}", [128, 512], F32) for i in range(3)]
        for cc in range(DN_CC):
            P = psA[cc % 2]
            p.op("pe", lambda e: e.transpose(P[:, 0:SB * 3], cs_tok[:, cc * 128:(cc + 1) * 128], c.ident[:SB * 3, :SB * 3]),
                 reads=[cs_tok], writes=[P])
            p.op("dve", lambda e: e.tensor_copy(out=cs_fm[:, cc, :], in_=P[:, 0:SB * 3]), reads=[P], writes=[cs_fm])

        hTs = [p.sb(f"hT{i}", [128, KC, 512], F32) for i in range(2)]
        xn = p.sb("xn", [128, KC, 512], BF16)
        tmp = [p.sb(f"sq{i}", [128, 512], F32) for i in range(2)]
        rstd = p.sb("rstd", [128, 512], F32)
        full = [p.sb(f"full{i}", [128, 520], F32) for i in range(3)]
        fulls = [p.sb(f"fulls{i}", [128, SB, 7], F32) for i in range(3)]
        acc = [p.sb(f"acc{i}", [128, 512], F32) for i in range(3)]
        cs = [p.sb(f"cs{i}", [128, 512], F32) for i in range(3)]
        rn = [p.sb(f"rn{i}", [128, 512], F32) for i in range(3)]
        zo = [p.sb(f"zo{i}", [128, 512], BF16) for i in range(3)]
        gt = p.sb("gt", [64, 512], F32)
        ps_stat = p.ps("ps_stat", [128, 512], F32)
        psN = [p.ps(f"psN{i}", [128, 512], F32) for i in range(3)]
        p.op("dve", lambda e: e.memset(gt[:, :], 0.0), writes=[gt])
        ia = 0
        for ti, (t0, tw) in enumerate(tiles_of(S)):
            smp = t0 >= S
            hT = hTs[ti % 2]
            p.dma("sp", hT[:, :, :tw], c.H[:, :, t0:t0 + tw].rearrange("k p t -> p k t"), reads=[c.H], writes=[hT])
            rmsnorm_tile(p, c, hT, xn, tw, lambda kc: c.nmix[:, li, kc:kc + 1], ps_stat, tmp, rstd)
            for cc in range(DN_CC):
                A = psA[ia % 3]
                fu, ac, co, rr, NP = full[ia % 3], acc[ia % 3], cs[ia % 3], rn[ia % 3], psN[ia % 3]
                fs = fulls[ia % 3]
                ia += 1
                for kc in range(KC):
                    p.op("pe", lambda e: e.matmul(A[:, :tw], lhsT=win[:, kc, cc * 128:(cc + 1) * 128], rhs=xn[:, kc, :tw],
                                                  start=(kc == 0), stop=(kc == KC - 1)), reads=[win, xn], writes=[A], same_ok=True)
                if not smp:
                    p.op("act", lambda e: e.activation(out=fu[:, 3:3 + tw], in_=A[:, :tw], func=AF.Copy), reads=[A], writes=[fu])
                    p.op("dve", lambda e: e.tensor_copy(out=fu[:, 0:3], in_=carry[:, cc, :]), reads=[carry], writes=[fu])
                    p.op("dve", lambda e: e.tensor_copy(out=carry[:, cc, :], in_=fu[:, tw:tw + 3]), reads=[fu], writes=[carry])
                    taps = [fu[:, jj:jj + tw] for jj in range(4)]
                    accv, cov, rrv = ac[:, :tw], co[:, :tw], rr[:, :tw]
                    srcs = [fu]
                else:
                    p.op("dve", lambda e: e.tensor_copy(out=fs[:, :, 3:7], in_=A[:, :tw].rearrange("p (b t) -> p b t", b=SB)),
                         reads=[A], writes=[fs])
                    p.op("dve", lambda e: e.tensor_copy(out=fs[:, :, 0:3], in_=cs_fm[:, cc, :].rearrange("p (b t) -> p b t", b=SB)),
                         reads=[cs_fm], writes=[fs])
                    p.op("dve", lambda e: e.tensor_copy(out=newc[:, cc, :].rearrange("p (b t) -> p b t", b=SB), in_=fs[:, :, 4:7]),
                         reads=[fs], writes=[newc])
                    taps = [fs[:, :, jj:jj + SL] for jj in range(4)]
                    accv = ac[:, :tw].rearrange("p (b t) -> p b t", b=SB)
                    cov, rrv = co[:, :tw], rr[:, :tw]
                    srcs = [fs]
                p.op("dve", lambda e: e.tensor_scalar(out=accv, in0=taps[0], scalar1=cw[:, cc, 0:1], scalar2=None, op0=ALU.mult),
                     reads=srcs + [cw], writes=[ac])
                for jj in range(1, 4):
                    p.op("dve", lambda e: e.scalar_tensor_tensor(out=accv, in0=taps[jj], scalar=cw[:, cc, jj:jj + 1], in1=accv,
                                                                 op0=ALU.mult, op1=ALU.add), reads=srcs + [cw, ac], writes=[ac])
                p.op("act", lambda e: e.activation(out=cov, in_=ac[:, :tw], func=AF.Silu), reads=[ac], writes=[co])
                if cc < 16:
                    p.op("act", lambda e: e.activation(out=rrv, in_=cov, func=AF.Square), reads=[co], writes=[rr])
                    p.op("pe", lambda e: e.matmul(NP[:, :tw], lhsT=c.ones[:, :], rhs=rrv, start=True, stop=True),
                         reads=[rr, c.ones], writes=[NP])
                    p.op("act", lambda e: e.activation(out=rrv, in_=NP[:, :tw], func=AF.Sqrt, bias=eps_c[:, 0:1], scale=1.0),
                         reads=[NP, eps_c], writes=[rr])
                    p.op("dve", lambda e: e.reciprocal(out=rrv, in_=rrv), reads=[rr], writes=[rr])
                    sc = 128 ** -0.5 if cc < 8 else 1.0
                    p.op("dve", lambda e: e.scalar_tensor_tensor(out=cov, in0=cov, scalar=sc, in1=rrv, op0=ALU.mult, op1=ALU.mult),
                         reads=[co, rr], writes=[co])
                p.dma("sp", c.QKVT[cc, :, t0:t0 + tw], cov, reads=[co], writes=[c.QKVT])
            for zc in range(16):
                A = psA[ia % 3]
                Z = zo[ia % 3]
                ia += 1
                for kc in range(KC):
                    p.op("pe", lambda e: e.matmul(A[:, :tw], lhsT=win[:, kc, 4096 + zc * 128:4096 + (zc + 1) * 128], rhs=xn[:, kc, :tw],
                                                  start=(kc == 0), stop=(kc == KC - 1)), reads=[win, xn], writes=[A], same_ok=True)
                p.op("act", lambda e: e.activation(out=Z[:, :tw], in_=A[:, :tw], func=AF.Silu), reads=[A], writes=[Z])
                p.dma("sp", c.ZT[zc, :, t0:t0 + tw], Z[:, :tw], reads=[Z], writes=[c.ZT])
            A = psA[ia % 3]
            ia += 1
            for kc in range(KC):
                p.op("pe", lambda e: e.matmul(A[:64, :tw], lhsT=win[:, kc, 6144:6208], rhs=xn[:, kc, :tw],
                                              start=(kc == 0), stop=(kc == KC - 1)), reads=[win, xn], writes=[A], same_ok=True)
            p.op("act", lambda e: e.activation(out=gt[0:16, :tw], in_=A[0:16, :tw], func=AF.Sigmoid), reads=[A], writes=[gt])
            p.op("act", lambda e: e.activation(out=gt[32:48, :tw], in_=A[32:48, :tw], func=AF.Exp, bias=ab[32:48, 1:2], scale=1.0),
                 reads=[A, ab, gt], writes=[gt])
            p.op("act", lambda e: e.activation(out=gt[32:48, :tw], in_=gt[32:48, :tw], func=AF.Ln, bias=one_c[32:48, 0:1], scale=1.0),
                 reads=[gt, one_c], writes=[gt])
            p.op("dve", lambda e: e.tensor_scalar(out=gt[32:48, :tw], in0=gt[32:48, :tw], scalar1=nalog[32:48, 0:1], scalar2=None, op0=ALU.mult),
                 reads=[gt, nalog], writes=[gt])
            p.dma("sp", c.GT[:, t0:t0 + tw], gt[:, :tw], reads=[gt], writes=[c.GT])
        ctok = cs_tok
        for which in range(2):
            n = 3 if which == 0 else SB * 3
            for cc in range(DN_CC):
                P = psA[cc % 2]
                src = carry[:, cc, :] if which == 0 else newc[:, cc, :]
                p.op("pe", lambda e: e.transpose(P[:n, 0:128], src, c.ident[:, :]), reads=[carry, newc], writes=[P])
                p.op("dve", lambda e: e.tensor_copy(out=ctok[:n, cc * 128:(cc + 1) * 128], in_=P[:n, 0:128]), reads=[P], writes=[ctok])
            if which == 0:
                p.dma("sp", c.o_dc_p[j], ctok[:3, :], reads=[ctok], writes=[c.o_dc_p])
            else:
                p.dma("sp", c.o_dc_s[j].rearrange("b t c -> (b t) c"), ctok[:, :], reads=[ctok], writes=[c.o_dc_s])


DN_BIG = 30000.0
GRP = 4


def dn_consts(C, nseq):
    sl = C // nseq
    seq = np.arange(C) // sl
    same = seq[:, None] == seq[None, :]
    i = np.arange(C)
    f = np.float32
    triU = (same & (i[:, None] <= i[None, :])).astype(f)
    blk = same.astype(f)
    okA = same & (i[None, :] <= i[:, None])
    maskA = np.where(okA, 0.0, DN_BIG).astype(f)
    okB = same & (i[None, :] >= i[:, None])
    maskB = np.where(okB, 0.0, -DN_BIG).astype(f)
    strict = (same & (i[None, :] < i[:, None])).astype(f)
    seqones = np.zeros((C, nseq, 128), f)
    seqmask = np.zeros((C, nseq), f)
    colmask = np.zeros((128, nseq, C), f)
    for b in range(nseq):
        seqones[seq == b, b, :] = 1.0
        seqmask[seq == b, b] = 1.0
        colmask[:, b, seq == b] = 1.0
    pack = np.concatenate([triU, blk, maskA, maskB, strict, seqmask, seqones.reshape(C, nseq * 128)], 1)
    return np.ascontiguousarray(pack), np.ascontiguousarray(colmask)


def dn_pack_layout(C, nseq):
    off = {}
    o = 0
    for nm, w in (("triU", C), ("blk", C), ("maskA", C), ("maskB", C), ("strict", C), ("seqmask", nseq), ("seqones", nseq * 128)):
        off[nm] = (o, o + w)
        o += w
    return off, o


def phase_dn2(p, c, li, j):
    S = c.S
    with p.scope():
        wout = p.sb("dnwout", [128, 16, D], BF16)
        for h in range(16):
            p.dma("pool", wout[:, h, :], c.dn_w_out[j, :, h, :], writes=[wout])
        ng = p.sb("ng", [128, 1], F32)
        p.dma("sp", ng[:, :], c.dn_ng[j], writes=[ng])
        sel = p.sb("sel16", [16, 16, 128], F32)
        p.dma("sp", sel[:, :, :], c.sel16_d[:, :, :], writes=[sel])
        eps_c = p.sb("eps_c", [128, 1], F32)
        p.op("dve", lambda e: e.memset(eps_c[:, :], 1e-6), writes=[eps_c])
        onesDV = p.sb("onesDV", [128, 128], F32)
        p.op("dve", lambda e: e.memset(onesDV[:, :], 1.0 / 128), writes=[onesDV])
        geo = {}
        for key, C, nseq, pk_d, cm_d in (("p", 128, 1, c.dnc_p, c.dncm_p), ("s", ST, SB, c.dnc_s, c.dncm_s)):
            off, tot = dn_pack_layout(C, nseq)
            pk = p.sb(f"pk_{key}", [C, tot], F32)
            p.dma("sp", pk[:, :], pk_d[:, :], writes=[pk])
            cm = p.sb(f"cm_{key}", [128, nseq, C], F32)
            p.dma("sp", cm[:, :, :], cm_d[:, :, :], writes=[cm])
            geo[key] = (C, nseq, pk, off, cm)
        Sp = p.sb("Sp", [128, 16, 128], F32)
        Spb = p.sb("Spb", [128, 16, 128], BF16)
        Ss = p.sb("Ss", [128, SB * 16, 128], F32)
        Ssb = p.sb("Ssb", [128, SB * 16, 128], BF16)
        p.op("dve", lambda e: e.memset(Sp[:, :, :], 0.0), writes=[Sp])
        p.op("pool", lambda e: e.memset(Spb[:, :, :], 0.0), writes=[Spb])
        p.dma("sp", Ss[:, :, :], c.dn_S_in[j].rearrange("b h k v -> k (b h) v"), writes=[Ss])
        p.op("dve", lambda e: e.tensor_copy(out=Ssb[:, :, :], in_=Ss[:, :, :]), reads=[Ss], writes=[Ssb])

        gtc = p.sb("gtc", [64, 128], F32)
        bg = p.sb("bg", [128, 64], F32)
        gc = p.sb("gc", [128, 16], F32)
        ngc = p.sb("ngc", [128, 16], F32)
        gcT = p.sb("gcT", [16, 128], F32)
        glt = p.sb("glt", [128, 16], F32)
        egc = p.sb("egc", [128, 16], F32)
        bek = p.sb("bek", [128, 16], F32)
        kdc = p.sb("kdc", [128, 16], F32)
        glb = p.sb("glb", [128, SB, 16], F32)
        kT = [p.sb(f"kT{i}", [128, 128], F32) for i in range(2)]
        qT = [p.sb(f"qT{i}", [128, 128], F32) for i in range(2)]
        kT2 = [p.sb(f"kT2{i}", [128, 128], F32) for i in range(2)]
        KKs = [p.sb(f"KK{i}", [128, 128], F32) for i in range(2)]
        KQs = [p.sb(f"KQ{i}", [128, 128], F32) for i in range(2)]
        ktok = [p.sb(f"ktok{i}", [128, 128], F32) for i in range(2)]
        vT = [p.sb(f"vT{i}", [128, 128], F32) for i in range(GRP)]
        Dm = [p.sb(f"Dm{i}", [128, 128], F32) for i in range(GRP)]
        DTm = [p.sb(f"DTm{i}", [128, 128], F32) for i in range(GRP)]
        Eb = [p.sb(f"Eb{i}", [128, 128], F32) for i in range(GRP)]
        NM = [[p.sb(f"NM{i}_{k}", [128, 256], F32) for k in range(2)] for i in range(GRP)]
        X = [[p.sb(f"X{i}_{k}", [128, 256], F32) for k in range(2)] for i in range(GRP)]
        qgT = [p.sb(f"qgT{i}", [128, 128], BF16) for i in range(GRP)]
        qkT = [p.sb(f"qkT{i}", [128, 128], BF16) for i in range(GRP)]
        wT = [p.sb(f"wT{i}", [128, 128], BF16) for i in range(GRP)]
        kde = [p.sb(f"kde{i}", [128, 128], BF16) for i in range(GRP)]
        vnew = [p.sb(f"vnew{i}", [128, 128], BF16) for i in range(GRP)]
        wTb = [p.sb(f"wTb{i}", [128, SB, ST], BF16) for i in range(GRP)]
        qgTb = [p.sb(f"qgTb{i}", [128, SB, ST], BF16) for i in range(GRP)]
        kdeb = [p.sb(f"kdeb{i}", [ST, SB, 128], BF16) for i in range(GRP)]
        sqo = [p.sb(f"sqo{i}", [128, 128], F32) for i in range(GRP)]
        rso = [p.sb(f"rso{i}", [128, 128], F32) for i in range(GRP)]
        ont = p.sb("ont", [128, 16, 128], BF16)
        zT = p.sb("zT", [128, 16, 128], BF16)
        hch = p.sb("hch", [128, KC, 128], F32)
        PH = [p.ps(f"PH{i}", [128, 512], F32) for i in range(GRP)]
        PR = [p.ps(f"PR{i}", [128, 512], F32) for i in range(4)]

        chunks = [("p", t0) for t0 in range(0, S, 128)] + [("s", S)]
        for key, t0 in chunks:
            C, nseq, pk, off, cm = geo[key]
            smp = key == "s"
            St, Sb = (Ss, Ssb) if smp else (Sp, Spb)
            nst = 2 if smp else 7
            def pkv(nm, r0=0, r1=None):
                a, b = off[nm]
                return pk[:C, a:b]
            p.dma("sp", gtc[:, :C], c.GT[:, t0:t0 + C], reads=[c.GT], writes=[gtc])
            p.dma("sp", zT[:, :, :C], c.ZT[:, :, t0:t0 + C].rearrange("h p t -> p h t"), reads=[c.ZT], writes=[zT])
            p.dma("sp", hch[:, :, :C], c.H[:, :, t0:t0 + C].rearrange("k p t -> p k t"), reads=[c.H], writes=[hch])
            R0 = PR[0]
            p.op("pe", lambda e: e.transpose(R0[:C, 0:64], gtc[:, :C], c.ident[:64, :64]), reads=[gtc], writes=[R0])
            p.op("dve", lambda e: e.tensor_copy(out=bg[:C, :], in_=R0[:C, 0:64]), reads=[R0], writes=[bg])
            R1 = PR[1]
            p.op("pe", lambda e: e.matmul(R1[:C, 0:16], lhsT=pkv("triU"), rhs=bg[:C, 32:48], start=True, stop=True), reads=[pk, bg], writes=[R1])
            p.op("pe", lambda e: e.matmul(R1[:C, 16:32], lhsT=pkv("blk"), rhs=bg[:C, 32:48], start=True, stop=True), reads=[pk, bg], writes=[R1], same_ok=True)
            p.op("pe", lambda e: e.matmul(R1[:16, 64:64 + C], lhsT=bg[:C, 32:48], rhs=pkv("triU"), start=True, stop=True), reads=[pk, bg], writes=[R1], same_ok=True)
            a0, _ = off["seqones"]
            for b in range(nseq):
                p.op("pe", lambda e: e.matmul(R1[:, 256 + b * 16:256 + (b + 1) * 16], lhsT=pk[:C, a0 + b * 128:a0 + (b + 1) * 128], rhs=bg[:C, 32:48],
                                              start=True, stop=True), reads=[pk, bg], writes=[R1], same_ok=True)
            p.op("dve", lambda e: e.tensor_copy(out=gc[:C, :], in_=R1[:C, 0:16]), reads=[R1], writes=[gc])
            p.op("dve", lambda e: e.tensor_scalar(out=ngc[:C, :], in0=R1[:C, 0:16], scalar1=-1.0, scalar2=None, op0=ALU.mult), reads=[R1], writes=[ngc])
            p.op("dve", lambda e: e.tensor_tensor(out=glt[:C, :], in0=R1[:C, 16:32], in1=gc[:C, :], op=ALU.subtract), reads=[R1, gc], writes=[glt])
            p.op("dve", lambda e: e.tensor_copy(out=gcT[:, :C], in_=R1[:16, 64:64 + C]), reads=[R1], writes=[gcT])
            p.op("act", lambda e: e.activation(out=glb[:, :nseq, :], in_=R1[:, 256:256 + nseq * 16].rearrange("p (b h) -> p b h", b=nseq), func=AF.Exp),
                 reads=[R1], writes=[glb])
            p.op("act", lambda e: e.activation(out=egc[:C, :], in_=gc[:C, :], func=AF.Exp), reads=[gc], writes=[egc])
            p.op("act", lambda e: e.activation(out=kdc[:C, :], in_=glt[:C, :], func=AF.Exp), reads=[glt], writes=[kdc])
            p.op("dve", lambda e: e.tensor_tensor(out=bek[:C, :], in0=egc[:C, :], in1=bg[:C, 0:16], op=ALU.mult), reads=[egc, bg], writes=[bek])

            if DN2STOP < 2:
                continue
            for g0 in range(0, 16, GRP):
                heads = list(range(g0, g0 + GRP))
                for qi in range(2):
                    hk = g0 // 2 + qi
                    p.dma("sp", kT[qi][:, :C], c.QKVT[8 + hk, :, t0:t0 + C], reads=[c.QKVT], writes=[kT[qi]])
                    p.dma("sp", qT[qi][:, :C], c.QKVT[hk, :, t0:t0 + C], reads=[c.QKVT], writes=[qT[qi]])
                    p.dma("sp", kT2[qi][:, :C], c.QKVT[8 + hk, :, t0:t0 + C], reads=[c.QKVT], writes=[kT2[qi]])
                for gi, h in enumerate(heads):
                    p.dma("sp", vT[gi][:, :C], c.QKVT[16 + h, :, t0:t0 + C], reads=[c.QKVT], writes=[vT[gi]])
                for qi in range(2):
                    R = PR[2 + qi]
                    p.op("pe", lambda e: e.matmul(R[:C, 0:C], lhsT=kT[qi][:, :C], rhs=kT2[qi][:, :C], start=True, stop=True), reads=[kT[qi], kT2[qi]], writes=[R])
                    p.op("pe", lambda e: e.matmul(R[:C, 128:128 + C], lhsT=kT[qi][:, :C], rhs=qT[qi][:, :C], start=True, stop=True),
                         reads=[kT[qi], qT[qi]], writes=[R], same_ok=True)
                    p.op("pe", lambda e: e.transpose(R[:C, 256:384], kT[qi][:, :C], c.ident[:, :]), reads=[kT[qi]], writes=[R], same_ok=True)
                    p.op("dve", lambda e: e.tensor_tensor(out=KKs[qi][:C, :C], in0=R[:C, 0:C], in1=pkv("strict"), op=ALU.mult), reads=[R, pk], writes=[KKs[qi]])
                    p.op("dve", lambda e: e.tensor_copy(out=KQs[qi][:C, :C], in_=R[:C, 128:128 + C]), reads=[R], writes=[KQs[qi]])
                    p.op("dve", lambda e: e.tensor_copy(out=ktok[qi][:C, :], in_=R[:C, 256:384]), reads=[R], writes=[ktok[qi]])
                if DN2STOP < 3:
                    continue
                for gi, h in enumerate(heads):
                    qi = gi // 2
                    P = PH[gi]
                    p.op("pe", lambda e: e.transpose(P[:C, 0:128], vT[gi][:, :C], c.ident[:, :]), reads=[vT[gi]], writes=[P])
                    p.op("pe", lambda e: e.matmul(P[:, 128:128 + C], lhsT=sel[:, h, :], rhs=gcT[:, :C], start=True, stop=True),
                         reads=[sel, gcT], writes=[P], same_ok=True)
                    p.op("pe", lambda e: e.matmul(P[:C, 256:256 + C], lhsT=sel[:, h, :C], rhs=gcT[:, :C], start=True, stop=False),
                         reads=[sel, gcT], writes=[P], same_ok=True)
                    p.op("pe", lambda e: e.matmul(P[:C, 256:256 + C], lhsT=c.ident[:C, :C], rhs=pkv("maskA"), start=False, stop=True),
                         reads=[pk], writes=[P], same_ok=True)
                    p.op("pe", lambda e: e.matmul(P[:C, 384:384 + C], lhsT=sel[:, h, :C], rhs=gcT[:, :C], start=True, stop=False),
                         reads=[sel, gcT], writes=[P], same_ok=True)
                    p.op("pe", lambda e: e.matmul(P[:C, 384:384 + C], lhsT=c.ident[:C, :C], rhs=pkv("maskB"), start=False, stop=True),
                         reads=[pk], writes=[P], same_ok=True)
                    X0 = X[gi][0]
                    p.op("dve", lambda e: e.tensor_scalar(out=X0[:C, 0:128], in0=P[:C, 0:128], scalar1=bg[:C, h:h + 1], scalar2=None, op0=ALU.mult),
                         reads=[P, bg], writes=[X0])
                    p.op("act", lambda e: e.activation(out=X0[:C, 128:256], in_=ktok[qi][:C, :], func=AF.Copy, scale=bek[:C, h:h + 1]),
                         reads=[ktok[qi], bek], writes=[X0])
                    p.op("act", lambda e: e.activation(out=Eb[gi][:, :C], in_=P[:, 128:128 + C], func=AF.Exp), reads=[P], writes=[Eb[gi]])
                    p.op("act", lambda e: e.activation(out=Dm[gi][:C, :C], in_=P[:C, 256:256 + C], func=AF.Exp, bias=gc[:C, h:h + 1], scale=-1.0),
                         reads=[P, gc], writes=[Dm[gi]])
                    p.op("act", lambda e: e.activation(out=DTm[gi][:C, :C], in_=P[:C, 384:384 + C], func=AF.Exp, bias=ngc[:C, h:h + 1], scale=1.0),
                         reads=[P, ngc], writes=[DTm[gi]])
                    NM0 = NM[gi][0]
                    p.op("dve", lambda e: e.scalar_tensor_tensor(out=NM0[:C, 0:C], in0=KKs[qi][:C, :C], scalar=bg[:C, h:h + 1], in1=Dm[gi][:C, :C],
                                                                 op0=ALU.mult, op1=ALU.mult), reads=[KKs[qi], bg, Dm[gi]], writes=[NM0])
                    p.op("dve", lambda e: e.tensor_tensor(out=qkT[gi][:C, :C], in0=KQs[qi][:C, :C], in1=DTm[gi][:C, :C], op=ALU.mult),
                         reads=[KQs[qi], DTm[gi]], writes=[qkT[gi]])
                    p.op("dve", lambda e: e.tensor_tensor(out=qgT[gi][:, :C], in0=qT[qi][:, :C], in1=Eb[gi][:, :C], op=ALU.mult),
                         reads=[qT[qi], Eb[gi]], writes=[qgT[gi]])
                    p.op("act", lambda e: e.activation(out=kde[gi][:C, :], in_=ktok[qi][:C, :], func=AF.Copy, scale=kdc[:C, h:h + 1]),
                         reads=[ktok[qi], kdc], writes=[kde[gi]])
                if DN2STOP < 4:
                    continue
                for gi, h in enumerate(heads):
                    P = PH[gi]
                    p.op("pe", lambda e: e.transpose(P[:C, 0:C], NM[gi][0][:C, 0:C], c.ident[:C, :C]), reads=[NM[gi][0]], writes=[P])
                for gi, h in enumerate(heads):
                    P = PH[gi]
                    p.op("act", lambda e: e.activation(out=NM[gi][0][:C, 128:128 + C], in_=P[:C, 0:C], func=AF.Copy), reads=[P], writes=[NM[gi][0]])
                if DN2STOP < 5:
                    continue
                for k in range(nst):
                    cur, nxt = k % 2, (k + 1) % 2
                    last = k == nst - 1
                    for gi, h in enumerate(heads):
                        P = PH[gi]
                        NMk, Xk = NM[gi][cur], X[gi][cur]
                        p.op("pe", lambda e: e.matmul(P[:C, 256:512], lhsT=NMk[:C, 128:128 + C], rhs=Xk[:C, :], start=True, stop=True),
                             reads=[NMk, Xk], writes=[P])
                        if not last:
                            p.op("pe", lambda e: e.matmul(P[:C, 0:C], lhsT=NMk[:C, 128:128 + C], rhs=NMk[:C, 0:C], start=True, stop=True),
                                 reads=[NMk], writes=[P], same_ok=True)
                            p.op("pe", lambda e: e.matmul(P[:C, 128:128 + C], lhsT=NMk[:C, 0:C], rhs=NMk[:C, 128:128 + C], start=True, stop=True),
                                 reads=[NMk], writes=[P], same_ok=True)
                    for gi, h in enumerate(heads):
                        P = PH[gi]
                        Xk, Xn = X[gi][cur], X[gi][nxt]
                        p.op("dve", lambda e: e.tensor_tensor(out=Xn[:C, :], in0=Xk[:C, :], in1=P[:C, 256:512], op=(ALU.subtract if k == 0 else ALU.add)),
                             reads=[Xk, P], writes=[Xn])
                        if not last:
                            NMn = NM[gi][nxt]
                            if C == 128:
                                p.op("act", lambda e: e.activation(out=NMn[:C, :], in_=P[:C, 0:256], func=AF.Copy), reads=[P], writes=[NMn])
                            else:
                                p.op("act", lambda e: e.activation(out=NMn[:C, 0:C], in_=P[:C, 0:C], func=AF.Copy), reads=[P], writes=[NMn])
                                p.op("act", lambda e: e.activation(out=NMn[:C, 128:128 + C], in_=P[:C, 128:128 + C], func=AF.Copy), reads=[P], writes=[NMn])
                if DN2STOP < 6:
                    continue
                fin = nst % 2
                for gi, h in enumerate(heads):
                    P = PH[gi]
                    Xf = X[gi][fin]
                    p.op("pe", lambda e: e.transpose(P[:, 0:C], Xf[:C, 128:256], c.ident[:C, :C]), reads=[Xf], writes=[P])
                    p.op("act", lambda e: e.activation(out=wT[gi][:, :C], in_=P[:, 0:C], func=AF.Copy), reads=[P], writes=[wT[gi]])
                    if smp:
                        for b in range(nseq):
                            p.op("pool", lambda e: e.tensor_tensor(out=wTb[gi][:, b, :], in0=wT[gi][:, :C], in1=cm[:, b, :], op=ALU.mult),
                                 reads=[wT[gi], cm], writes=[wTb[gi]])
                            p.op("pool", lambda e: e.tensor_tensor(out=qgTb[gi][:, b, :], in0=qgT[gi][:, :C], in1=cm[:, b, :], op=ALU.mult),
                                 reads=[qgT[gi], cm], writes=[qgTb[gi]])
                            a1, _ = off["seqmask"]
                            p.op("pool", lambda e: e.tensor_scalar(out=kdeb[gi][:C, b, :], in0=kde[gi][:C, :], scalar1=pk[:C, a1 + b:a1 + b + 1], scalar2=None,
                                                                   op0=ALU.mult), reads=[kde[gi], pk], writes=[kdeb[gi]])
                for gi, h in enumerate(heads):
                    P = PH[gi]
                    for b in range(nseq):
                        lw = wTb[gi][:, b, :] if smp else wT[gi][:, :C]
                        p.op("pe", lambda e: e.matmul(P[:C, 128:256], lhsT=lw, rhs=Sb[:, b * 16 + h, :], start=(b == 0), stop=(b == nseq - 1)),
                             reads=[wTb[gi] if smp else wT[gi], Sb], writes=[P], same_ok=True)
                for gi, h in enumerate(heads):
                    P = PH[gi]
                    Xf = X[gi][fin]
                    p.op("dve", lambda e: e.tensor_tensor(out=vnew[gi][:C, :], in0=Xf[:C, 0:128], in1=P[:C, 128:256], op=ALU.subtract),
                         reads=[Xf, P], writes=[vnew[gi]])
                for gi, h in enumerate(heads):
                    P = PH[gi]
                    for b in range(nseq):
                        rq = qgTb[gi][:, b, :] if smp else qgT[gi][:, :C]
                        p.op("pe", lambda e: e.matmul(P[:, 256:256 + C], lhsT=Sb[:, b * 16 + h, :], rhs=rq, start=(b == 0), stop=False),
                             reads=[qgTb[gi] if smp else qgT[gi], Sb], writes=[P], same_ok=True)
                    p.op("pe", lambda e: e.matmul(P[:, 256:256 + C], lhsT=vnew[gi][:C, :], rhs=qkT[gi][:C, :C], start=False, stop=True),
                         reads=[vnew[gi], qkT[gi]], writes=[P], same_ok=True)
                    R = PR[gi % 2]
                    for b in range(nseq):
                        lk = kdeb[gi][:C, b, :] if smp else kde[gi][:C, :]
                        p.op("pe", lambda e: e.matmul(R[:, b * 128:(b + 1) * 128], lhsT=lk, rhs=vnew[gi][:C, :], start=True, stop=True),
                             reads=[kdeb[gi] if smp else kde[gi], vnew[gi]], writes=[R], same_ok=True)
                    for b in range(nseq):
                        si = b * 16 + h
                        p.op("dve", lambda e: e.scalar_tensor_tensor(out=St[:, si, :], in0=St[:, si, :], scalar=glb[:, b, h:h + 1], in1=R[:, b * 128:(b + 1) * 128],
                                                                     op0=ALU.mult, op1=ALU.add), reads=[St, glb, R], writes=[St])
                        p.op("act", lambda e: e.activation(out=Sb[:, si, :], in_=St[:, si, :], func=AF.Copy), reads=[St], writes=[Sb])
                if DN2STOP < 7:
                    continue
                for gi, h in enumerate(heads):
                    P = PH[gi]
                    p.op("act", lambda e: e.activation(out=sqo[gi][:, :C], in_=P[:, 256:256 + C], func=AF.Square), reads=[P], writes=[sqo[gi]])
                    p.op("pe", lambda e: e.matmul(P[:, 0:C], lhsT=onesDV[:, :], rhs=sqo[gi][:, :C], start=True, stop=True), reads=[onesDV, sqo[gi]], writes=[P], same_ok=True)
                for gi, h in enumerate(heads):
                    P = PH[gi]
                    p.op("act", lambda e: e.activation(out=rso[gi][:, :C], in_=P[:, 0:C], func=AF.Sqrt, bias=eps_c[:, 0:1], scale=1.0),
                         reads=[P, eps_c], writes=[rso[gi]])
                    p.op("dve", lambda e: e.reciprocal(out=rso[gi][:, :C], in_=rso[gi][:, :C]), reads=[rso[gi]], writes=[rso[gi]])
                    p.op("dve", lambda e: e.tensor_tensor(out=rso[gi][:, :C], in0=rso[gi][:, :C], in1=P[:, 256:256 + C], op=ALU.mult),
                         reads=[rso[gi], P], writes=[rso[gi]])
                    p.op("dve", lambda e: e.scalar_tensor_tensor(out=ont[:, h, :C], in0=rso[gi][:, :C], scalar=ng[:, 0:1], in1=zT[:, h, :C],
                                                                 op0=ALU.mult, op1=ALU.mult), reads=[rso[gi], ng, zT], writes=[ont])
            if DN2STOP < 8:
                continue
            for oc in range(KC):
                R = PR[oc % 4]
                for h in range(16):
                    p.op("pe", lambda e: e.matmul(R[:, 0:C], lhsT=wout[:, h, oc * 128:(oc + 1) * 128], rhs=ont[:, h, :C], start=(h == 0), stop=(h == 15)),
                         reads=[wout, ont], writes=[R], same_ok=True)
                p.op("dve", lambda e: e.tensor_tensor(out=hch[:, oc, :C], in0=hch[:, oc, :C], in1=R[:, 0:C], op=ALU.add), reads=[hch, R], writes=[hch])
            p.dma("sp", c.H[:, :, t0:t0 + C].rearrange("k p t -> p k t"), hch[:, :, :C], reads=[hch], writes=[c.H])
        p.dma("sp", c.o_dS_p[j].rearrange("h k v -> k h v"), Sp[:, :, :], reads=[Sp], writes=[c.o_dS_p])
        p.dma("sp", c.o_dS_s[j].rearrange("b h k v -> k (b h) v"), Ss[:, :, :], reads=[Ss], writes=[c.o_dS_s])


NSA_W = 2608
PAST = 8192


def phase_nsa1(p, c, li, j):
    S = c.S
    with p.scope():
        win = p.sb("nswin", [128, KC, NSA_W], BF16)
        for kc in range(KC):
            p.dma("pool", win[:, kc, :], c.ns_w_in[j, :, kc, :], writes=[win])
        gains = p.sb("nsg", [128, 4], F32)
        p.dma("sp", gains[:, :], c.ns_gains[j], writes=[gains])
        rotT = p.sb("rotT", [128, 128], F32)
        p.dma("sp", rotT[:, :], c.ns_rotT[:, :], writes=[rotT])
        blk64 = p.sb("blk64", [128, 128], F32)
        p.op("dve", lambda e: e.memset(blk64[:, :], 0.0), writes=[blk64])
        p.op("dve", lambda e: e.memset(blk64[0:64, 0:64], 1.0 / 64), writes=[blk64])
        p.op("dve", lambda e: e.memset(blk64[64:128, 64:128], 1.0 / 64), writes=[blk64])
        eps_c = p.sb("eps_c", [128, 1], F32)
        p.op("dve", lambda e: e.memset(eps_c[:, :], NORM_EPS), writes=[eps_c])

        hTs = [p.sb(f"hT{i}", [128, KC, 512], F32) for i in range(2)]
        xn = p.sb("xn", [128, KC, 512], BF16)
        tmp = [p.sb(f"sq{i}", [128, 512], F32) for i in range(2)]
        rstd = p.sb("rstd", [128, 512], F32)
        cosT = p.sb("cosT", [128, 512], F32)
        sinT = p.sb("sinT", [128, 512], F32)
        xq = [p.sb(f"xq{i}", [128, 512], F32) for i in range(2)]
        xr = [p.sb(f"xr{i}", [128, 512], F32) for i in range(2)]
        rs = [p.sb(f"rs{i}", [128, 512], F32) for i in range(2)]
        ob = [p.sb(f"ob{i}", [128, 512], BF16) for i in range(2)]
        ob2 = [p.sb(f"ob2{i}", [128, 512], BF16) for i in range(2)]
        gt = p.sb("gt", [48, 512], F32)
        tokf = [p.sb(f"tokf{i}", [128, 1536], F32) for i in range(2)]
        tokb = [p.sb(f"tokb{i}", [128, 512], BF16) for i in range(2)]
        kfm = p.sb("kfm", [128, 4, 512], F32)
        ps_stat = p.ps("ps_stat", [128, 512], F32)
        psA = [p.ps(f"psA{i}", [128, 512], F32) for i in range(2)]
        psB = [p.ps(f"psB{i}", [128, 512], F32) for i in range(2)]
        psT = [p.ps(f"psT{i}", [128, 512], F32) for i in range(3)]
        ia = 0
        it = 0
        W = min(512, S)
        for ti, (t0, tw) in enumerate(tiles_of(S)):
            smp = t0 >= S
            hT = hTs[ti % 2]
            p.dma("sp", hT[:, :, :tw], c.H[:, :, t0:t0 + tw].rearrange("k p t -> p k t"), reads=[c.H], writes=[hT])
            p.dma("sp", cosT[:, :tw], c.ns_cos[:, t0:t0 + tw], writes=[cosT])
            p.dma("sp", sinT[:, :tw], c.ns_sin[:, t0:t0 + tw], writes=[sinT])
            rmsnorm_tile(p, c, hT, xn, tw, lambda kc: c.nmix[:, li, kc:kc + 1], ps_stat, tmp, rstd)
            for ch in list(range(0, 14)) + [16, 17]:
                A, B = psA[ia % 2], psB[ia % 2]
                x_, xr_, r_, o_, o2_ = xq[ia % 2], xr[ia % 2], rs[ia % 2], ob[ia % 2], ob2[ia % 2]
                ia += 1
                for kc in range(KC):
                    p.op("pe", lambda e: e.matmul(A[:, :tw], lhsT=win[:, kc, ch * 128:(ch + 1) * 128], rhs=xn[:, kc, :tw],
                                                  start=(kc == 0), stop=(kc == KC - 1)), reads=[win, xn], writes=[A], same_ok=True)
                if ch in (8, 9, 10, 11):
                    p.op("dve", lambda e: e.tensor_copy(out=o_[:, :tw], in_=A[:, :tw]), reads=[A], writes=[o_])
                    dst = c.KCT if ch < 10 else c.VCT
                    p.dma("sp", dst[ch % 2, :, t0:t0 + tw], o_[:, :tw], reads=[o_], writes=[dst])
                    continue
                gcol = 0 if ch < 8 else (2 if ch < 14 else 3)
                p.op("act", lambda e: e.activation(out=r_[:, :tw], in_=A[:, :tw], func=AF.Square), reads=[A], writes=[r_])
                p.op("pe", lambda e: e.matmul(B[:, :tw], lhsT=blk64[:, :], rhs=r_[:, :tw], start=True, stop=True), reads=[blk64, r_], writes=[B])
                p.op("act", lambda e: e.activation(out=r_[:, :tw], in_=B[:, :tw], func=AF.Sqrt, bias=eps_c[:, 0:1], scale=1.0),
                     reads=[B, eps_c], writes=[r_])
                p.op("dve", lambda e: e.reciprocal(out=r_[:, :tw], in_=r_[:, :tw]), reads=[r_], writes=[r_])
                p.op("dve", lambda e: e.scalar_tensor_tensor(out=x_[:, :tw], in0=A[:, :tw], scalar=gains[:, gcol:gcol + 1], in1=r_[:, :tw],
                                                             op0=ALU.mult, op1=ALU.mult), reads=[A, gains, r_], writes=[x_])
                if ch < 8:
                    p.op("pool", lambda e: e.tensor_copy(out=o_[:, :tw], in_=x_[:, :tw]), reads=[x_], writes=[o_])
                    p.dma("sp", c.QT[ch, :, t0:t0 + tw], o_[:, :tw], reads=[o_], writes=[c.QT])
                p.op("pe", lambda e: e.matmul(B[:, :tw], lhsT=rotT[:, :], rhs=x_[:, :tw], start=True, stop=True), reads=[rotT, x_], writes=[B])
                p.op("dve", lambda e: e.tensor_tensor(out=xr_[:, :tw], in0=B[:, :tw], in1=sinT[:, :tw], op=ALU.mult), reads=[B, sinT], writes=[xr_])
                p.op("pool", lambda e: e.tensor_tensor(out=x_[:, :tw], in0=x_[:, :tw], in1=cosT[:, :tw], op=ALU.mult), reads=[x_, cosT], writes=[x_])
                if ch < 8:
                    p.op("dve", lambda e: e.tensor_tensor(out=o2_[:, :tw], in0=x_[:, :tw], in1=xr_[:, :tw], op=ALU.add), reads=[x_, xr_], writes=[o2_])
                    p.dma("sp", c.QRT[ch, :, t0:t0 + tw], o2_[:, :tw], reads=[o2_], writes=[c.QRT])
                else:
                    ki = (ch - 12) if ch < 14 else (ch - 14)
                    p.op("dve", lambda e: e.tensor_tensor(out=kfm[:, ki, :tw], in0=x_[:, :tw], in1=xr_[:, :tw], op=ALU.add), reads=[x_, xr_], writes=[kfm])
                    p.op("pool", lambda e: e.tensor_copy(out=o2_[:, :tw], in_=kfm[:, ki, :tw]), reads=[kfm], writes=[o2_])
                    dst = c.KST if ch < 14 else c.KWT
                    p.dma("sp", dst[ch % 2, :, t0:t0 + tw], o2_[:, :tw], reads=[o2_], writes=[dst])
            A = psA[ia % 2]
            ia += 1
            for kc in range(KC):
                p.op("pe", lambda e: e.matmul(A[:48, :tw], lhsT=win[:, kc, 2560:2608], rhs=xn[:, kc, :tw],
                                              start=(kc == 0), stop=(kc == KC - 1)), reads=[win, xn], writes=[A], same_ok=True)
            p.op("act", lambda e: e.activation(out=gt[:, :tw], in_=A[:48, :tw], func=AF.Sigmoid), reads=[A], writes=[gt])
            p.dma("sp", c.GTn[:, t0:t0 + tw], gt[:, :tw], reads=[gt], writes=[c.GTn])
            nsub = (tw + 127) // 128
            for sj in range(nsub):
                r = min(128, tw - sj * 128)
                tf = tokf[sj % 2]
                for cb in range(3):
                    Pt = psT[cb]
                    for kc in range(KC):
                        p.op("pe", lambda e: e.matmul(Pt[:r, :], lhsT=xn[:, kc, sj * 128:sj * 128 + r], rhs=win[:, kc, 1024 + cb * 512:1536 + cb * 512],
                                                      start=(kc == 0), stop=(kc == KC - 1)), reads=[win, xn], writes=[Pt], same_ok=True)
                    p.op("dve", lambda e: e.tensor_copy(out=tf[:r, cb * 512:(cb + 1) * 512], in_=Pt[:r, :]), reads=[Pt], writes=[tf])
                for ki in range(4):
                    Pt = psT[ki % 3]
                    p.op("pe", lambda e: e.transpose(Pt[:r, 0:128], kfm[:, ki, sj * 128:sj * 128 + r], c.ident[:, :]), reads=[kfm], writes=[Pt])
                    col = (512 if ki < 2 else 1024) + (ki % 2) * 128
                    p.op("dve", lambda e: e.tensor_copy(out=tf[:r, col:col + 128], in_=Pt[:r, 0:128]), reads=[Pt], writes=[tf])
                tb = tokb[sj % 2]
                p.op("pool", lambda e: e.tensor_copy(out=tb[:r, 0:256], in_=tf[:r, 768:1024]), reads=[tf], writes=[tb])
                p.op("pool", lambda e: e.tensor_copy(out=tb[:r, 256:512], in_=tf[:r, 1280:1536]), reads=[tf], writes=[tb])
                r0 = t0 + sj * 128
                p.dma("sp", c.VS[r0:r0 + r, :], tb[:r, 0:256], reads=[tb], writes=[c.VS])
                p.dma("sp", c.VW[r0:r0 + r, :], tb[:r, 256:512], reads=[tb], writes=[c.VW])
                if not smp:
                    for oi in range(4):
                        p.dma("sp", c.o_new_p[oi, r0:r0 + r, :], tf[:r, oi * 256:(oi + 1) * 256], reads=[tf], writes=[c.o_new_p])
                    if r0 >= S - W:
                        for oi in range(2):
                            p.dma("sp", c.o_swa_p[oi, r0 - (S - W):r0 - (S - W) + r, :], tf[:r, 1024 + oi * 256:1280 + oi * 256],
                                  reads=[tf], writes=[c.o_swa_p])
                else:
                    for oi in range(4):
                        p.dma("sp", c.o_new_s[oi, :, :], tf[:r, oi * 256:(oi + 1) * 256], reads=[tf], writes=[c.o_new_s])
                    for b in range(SB):
                        for oi in range(2):
                            p.dma("sp", c.o_swa_s[oi, b, 512 - SL:512, :], tf[b * SL:(b + 1) * SL, 1024 + oi * 256:1280 + oi * 256],
                                  reads=[tf], writes=[c.o_swa_s])
        for oi, src in enumerate((c.ns_swa_k, c.ns_swa_v)):
            for b in range(SB):
                p.dma("sp", c.o_swa_s[oi, b, 0:512 - SL, :], src[j, b, SL:512, :], writes=[c.o_swa_s])


NEG = -1.0e9
BONUS = 1.0e4


def nsa_consts():
    f = np.float32
    ql = np.arange(128)[:, None]
    x = np.arange(256)[None, :]
    jp = x - 128
    W = np.zeros((128, 256), f)
    W = np.where(jp > 1, NEG, W)
    W = np.where((jp == 1) & (ql >= 64), BONUS, W)
    W = np.where((jp == 1) & (ql < 64), NEG, W)
    W = np.where(jp == 0, BONUS, W)
    W = np.where((jp == -1) & (ql < 64), BONUS, W)
    n = np.arange(512)[:, None]
    jj = np.arange(128)[None, :]
    ov = np.minimum(16 * n + 32, 64 * jj + 64) - np.maximum(16 * n, 64 * jj)
    c2s = (np.clip(ov, 0, None) / 32.0).astype(f)
    c2s[511] = 0.0
    c2s = np.ascontiguousarray(c2s.reshape(4, 128, 128).transpose(1, 0, 2))
    sel48 = np.zeros((48, 48, 64), f)
    for r in range(48):
        sel48[r, r, :] = 1.0
    return {"ns_wtab": W.astype(f), "ns_c2s": c2s, "ns_sel48": sel48}


def nsa_seq(p, c, j, env, sq):
    T = sq["T"]
    NTL = T // 128
    ncmp = sq["ncmp"]
    scale = 0.125
    (wck, wcv, pekT, pevT, g0, ones64, onesb, identb, wtab, c2s, sel48, tiny) = env["consts"]
    (kcs, vcs, ksT, kwT, vs, vw, kcmpT, vcmp, Qgs, QRgs, Ggs, Pc, Pn, E, Pm, sc, sc2, m8, selb, selx, rden, gb, oacc, ob, sqk, rsk, pet, pvr) = env["tiles"]
    (PS_S, PS_T, PS_DEN, PS_O, PS_IMP, PS_X0) = env["psum"]
    PS_X = [PS_X0, PS_IMP]
    for hk in range(4):
        ch, p0 = hk // 2, (hk % 2) * 64
        p.dma("sp", kcs[:, :T], sq["KCT"][ch, p0:p0 + 64, 0:T], reads=[sq["KCT_t"]], writes=[kcs])
        p.dma("sp", vcs[:, :T], sq["VCT"][ch, p0:p0 + 64, 0:T], reads=[sq["VCT_t"]], writes=[vcs])
        p.dma("sp", ksT[:, :T], sq["KST"][ch, p0:p0 + 64, 0:T], reads=[sq["KST_t"]], writes=[ksT])
        kw0 = sq.get("kw0", 0)
        p.dma("sp", kwT[:, kw0:T], sq["KWT"][ch, p0:p0 + 64, kw0:T], reads=[sq["KWT_t"]], writes=[kwT])
        p.dma("sp", vs[:, :NTL, :], sq["VS"][0:T, hk * 64:(hk + 1) * 64].rearrange("(n p) d -> p n d", p=128), reads=[sq["VS_t"]], writes=[vs])
        p.dma("sp", vw[:, kw0 // 128:NTL, :], sq["VW"][kw0:T, hk * 64:(hk + 1) * 64].rearrange("(n p) d -> p n d", p=128), reads=[sq["VW_t"]], writes=[vw])
        X0 = PS_X[0]
        kv = kcs[:, 0:16 * (ncmp + 1)].rearrange("p (n s) -> p n s", s=16)
        for l in range(32):
            rhs = kv[:, 0:ncmp, l] if l < 16 else kv[:, 1:ncmp + 1, l - 16]
            p.op("pe", lambda e: e.matmul(X0[:64, :ncmp], lhsT=wck[0:64, l, :], rhs=rhs, start=(l == 0), stop=(l == 31)),
                 reads=[wck, kcs], writes=[X0], same_ok=True)
        p.op("dve", lambda e: e.tensor_scalar(out=sqk[:, :ncmp], in0=X0[:64, :ncmp], scalar1=pet[:, 0:1], scalar2=None, op0=ALU.add),
             reads=[X0, pet], writes=[sqk])
        p.op("act", lambda e: e.activation(out=rsk[:, :ncmp], in_=sqk[:, :ncmp], func=AF.Square), reads=[sqk], writes=[rsk])
        X1 = PS_X[1]
        p.op("pe", lambda e: e.matmul(X1[:64, :ncmp], lhsT=ones64[:, :], rhs=rsk[:, :ncmp], start=True, stop=True), reads=[ones64, rsk], writes=[X1])
        p.op("act", lambda e: e.activation(out=rsk[:, :ncmp], in_=X1[:64, :ncmp], func=AF.Sqrt, bias=tiny[:64, 1:2], scale=1.0),
             reads=[X1, tiny], writes=[rsk])
        p.op("dve", lambda e: e.reciprocal(out=rsk[:, :ncmp], in_=rsk[:, :ncmp]), reads=[rsk], writes=[rsk])
        p.op("dve", lambda e: e.memset(kcmpT[:, :], 0.0), writes=[kcmpT])
        p.op("dve", lambda e: e.scalar_tensor_tensor(out=kcmpT[:, :ncmp], in0=sqk[:, :ncmp], scalar=g0[:, 0:1], in1=rsk[:, :ncmp], op0=ALU.mult, op1=ALU.mult),
             reads=[sqk, g0, rsk], writes=[kcmpT])
        vv = vcs[:, 0:16 * (ncmp + 1)].rearrange("p (n s) -> p n s", s=16)
        nct = (ncmp + 127) // 128
        p.op("pool", lambda e: e.memset(vcmp[:, :, :], 0.0), writes=[vcmp])
        for nt in range(nct):
            n0 = nt * 128
            nn = min(128, ncmp - n0)
            Xv = PS_X[nt % 2]
            for l in range(32):
                lhsT = vv[:, n0:n0 + nn, l] if l < 16 else vv[:, n0 + 1:n0 + nn + 1, l - 16]
                p.op("pe", lambda e: e.matmul(Xv[:nn, 0:64], lhsT=lhsT, rhs=wcv[0:64, l, :], start=(l == 0), stop=False),
                     reads=[wcv, vcs], writes=[Xv], same_ok=True)
            p.op("pe", lambda e: e.matmul(Xv[:nn, 0:64], lhsT=onesb[0:1, :nn], rhs=pvr[0:1, :], start=False, stop=True),
                 reads=[onesb, pvr], writes=[Xv], same_ok=True)
            p.op("dve", lambda e: e.tensor_copy(out=vcmp[:nn, nt, :], in_=Xv[:nn, 0:64]), reads=[Xv], writes=[vcmp])
        for blk_i, (bi, qc0, o0) in enumerate(sq["blocks"]):
            Qg, QRg, Gg = Qgs[blk_i % 2], QRgs[blk_i % 2], Ggs[blk_i % 2]
            for g in range(4):
                qh = 4 * hk + g
                p.dma("sp", Qg[:, g, :], sq["QT"][qh // 2, (qh % 2) * 64:(qh % 2) * 64 + 64, qc0:qc0 + 128], reads=[sq["QT_t"]], writes=[Qg])
                p.dma("sp", QRg[:, g, :], sq["QRT"][qh // 2, (qh % 2) * 64:(qh % 2) * 64 + 64, qc0:qc0 + 128], reads=[sq["QRT_t"]], writes=[QRg])
            p.dma("sp", Gg[:, :], sq["GT"][:, qc0:qc0 + 128], reads=[sq["GT_t"]], writes=[Gg])
            Qf = Qg[:, :, :].rearrange("p g q -> p (g q)")
            QRf = QRg[:, :, :].rearrange("p g q -> p (g q)")
            ncl = min(nct, (8 * bi + 6) // 128 + 1)
            for nt in range(ncl):
                Sp = PS_S[nt % 2]
                p.op("pe", lambda e: e.matmul(Sp[:, :], lhsT=kcmpT[:, nt * 128:(nt + 1) * 128], rhs=Qf, start=True, stop=True),
                     reads=[kcmpT, Qg], writes=[Sp])
                p.op("act", lambda e: e.activation(out=Pc[:, nt, :], in_=Sp[:, :], func=AF.Exp, scale=scale), reads=[Sp], writes=[Pc])
                p.op("pool", lambda e: e.affine_select(out=Pc[:, nt, :].rearrange("p (g q) -> p g q", g=4), in_=Pc[:, nt, :].rearrange("p (g q) -> p g q", g=4),
                                                       pattern=[[0, 4], [1, 128]], compare_op=ALU.is_ge, fill=0.0,
                                                       base=128 * bi - 31 - 2048 * nt, channel_multiplier=-16), reads=[Pc], writes=[Pc])
                p.op("pe", lambda e: e.matmul(PS_DEN[:, :], lhsT=onesb[:, :], rhs=Pc[:, nt, :], start=(nt == 0), stop=(nt == ncl - 1)),
                     reads=[onesb, Pc], writes=[PS_DEN], same_ok=True)
            p.op("dve", lambda e: e.tensor_scalar(out=rden[:, :], in0=PS_DEN[:, :], scalar1=1e-30, scalar2=None, op0=ALU.max), reads=[PS_DEN], writes=[rden])
            p.op("dve", lambda e: e.reciprocal(out=rden[:, :], in_=rden[:, :]), reads=[rden], writes=[rden])
            for nt in range(ncl):
                p.op("dve", lambda e: e.tensor_tensor(out=Pn[:, nt, :], in0=Pc[:, nt, :], in1=rden[:, :], op=ALU.mult), reads=[Pc, rden], writes=[Pn])
            for nt in range(ncl):
                p.op("pe", lambda e: e.matmul(PS_O[:64, :], lhsT=vcmp[:, nt, :], rhs=Pn[:, nt, :], start=(nt == 0), stop=(nt == ncl - 1)),
                     reads=[vcmp, Pn], writes=[PS_O], same_ok=True)
            k = 0
            for nt in range(ncl):
                for g in range(4):
                    p.op("pe", lambda e: e.matmul(PS_IMP[:, 0:128], lhsT=Pn[:, nt, g * 128:(g + 1) * 128], rhs=c2s[:, nt, :], start=(k == 0), stop=(k == 4 * ncl - 1)),
                         reads=[Pn, c2s], writes=[PS_IMP], same_ok=True)
                    k += 1
            p.op("dve", lambda e: e.memset(sc[:, 128:136], NEG), writes=[sc])
            p.op("dve", lambda e: e.tensor_tensor(out=sc[:, 0:128], in0=PS_IMP[:, 0:128], in1=wtab[:, 128 - 2 * bi:256 - 2 * bi], op=ALU.add),
                 reads=[PS_IMP, wtab], writes=[sc])
            if bi == 64:
                p.op("dve", lambda e: e.tensor_copy(out=sc[:, 128:130], in_=wtab[:, 128:130]), reads=[wtab], writes=[sc])
            for br in range(3):
                Xg = PS_X0
                for g in range(4):
                    r = (4 * hk + g) * 3 + br
                    p.op("pe", lambda e: e.matmul(Xg[:64, g * 128:(g + 1) * 128], lhsT=sel48[:, r, :], rhs=Gg[:, :], start=True, stop=True),
                         reads=[sel48, Gg], writes=[Xg], same_ok=True)
                p.op("dve", lambda e: e.tensor_copy(out=gb[:, br, :], in_=Xg[:64, 0:512]), reads=[Xg], writes=[gb])
            p.op("dve", lambda e: e.tensor_tensor(out=oacc[:, :], in0=PS_O[:64, :], in1=gb[:, 0, :], op=ALU.mult), reads=[PS_O, gb], writes=[oacc])
            if bi >= 1:
                p.op("dve", lambda e: e.tensor_scalar(out=sc[:, 0:1], in0=sc[:, 0:1], scalar1=BONUS, scalar2=None, op0=ALU.add), reads=[sc], writes=[sc])
            p.op("dve", lambda e: e.max(out=m8[:, :], in_=sc[:, :]), reads=[sc], writes=[m8])
            p.op("dve", lambda e: e.match_replace(out=sc2[:, :], in_to_replace=m8[:, :], in_values=sc[:, :], imm_value=-3.0e9), reads=[sc, m8], writes=[sc2])
            p.op("dve", lambda e: e.max(out=m8[:, :], in_=sc2[:, :]), reads=[sc2], writes=[m8])
            p.op("dve", lambda e: e.match_replace(out=sc2[:, :], in_to_replace=m8[:, :], in_values=sc2[:, :], imm_value=-3.0e9), reads=[sc2, m8], writes=[sc2])
            p.op("dve", lambda e: e.tensor_tensor(out=sc[:, :], in0=sc[:, :], in1=sc2[:, :], op=ALU.subtract), reads=[sc, sc2], writes=[sc])
            p.op("dve", lambda e: e.tensor_scalar(out=selb[:, :], in0=sc[:, :], scalar1=1.0, scalar2=None, op0=ALU.min), reads=[sc], writes=[selb])
            nb = 2 * (bi + 1)
            p.op("dve", lambda e: e.tensor_copy(out=selx[:, 0:nb * 64].rearrange("p (j k) -> p j k", k=64),
                                                in_=selb[:, 0:nb].unsqueeze(2).to_broadcast([128, nb, 64])), reads=[selb], writes=[selx])
            def slc_front(kt):
                Sp, Tp = PS_S[kt % 2], PS_T[kt % 2]
                p.op("pe", lambda e: e.matmul(Sp[:, :], lhsT=ksT[:, kt * 128:(kt + 1) * 128], rhs=QRf, start=True, stop=True), reads=[ksT, QRg], writes=[Sp])
                p.op("pe", lambda e: e.transpose(Tp[:, 0:128], selx[:, kt * 128:(kt + 1) * 128], identb[:, :]), reads=[selx, identb], writes=[Tp])

            def slc_back(kt):
                Sp, Tp = PS_S[kt % 2], PS_T[kt % 2]
                Ek, Pk = E[kt % 2], Pm[kt % 2]
                p.op("act", lambda e: e.activation(out=Ek[:, :], in_=Sp[:, :], func=AF.Exp, scale=scale), reads=[Sp], writes=[Ek])
                p.op("dve", lambda e: e.tensor_tensor(out=Pk[:, :].rearrange("p (g q) -> p g q", g=4), in0=Ek[:, :].rearrange("p (g q) -> p g q", g=4),
                                                      in1=Tp[:, 0:128].unsqueeze(1).to_broadcast([128, 4, 128]), op=ALU.mult), reads=[Ek, Tp], writes=[Pk])
                if kt == bi:
                    p.op("pool", lambda e: e.affine_select(out=Pk[:, :].rearrange("p (g q) -> p g q", g=4), in_=Pk[:, :].rearrange("p (g q) -> p g q", g=4),
                                                           pattern=[[0, 4], [1, 128]], compare_op=ALU.is_ge, fill=0.0, base=0, channel_multiplier=-1),
                         reads=[Pk], writes=[Pk])
                p.op("pe", lambda e: e.matmul(PS_DEN[:64, :], lhsT=onesb[:, 0:64], rhs=Pk[:, :], start=(kt == 0), stop=(kt == bi)),
                     reads=[onesb, Pk], writes=[PS_DEN], same_ok=True)
                p.op("pe", lambda e: e.matmul(PS_O[:64, :], lhsT=vs[:, kt, :], rhs=Pk[:, :], start=(kt == 0), stop=(kt == bi)),
                     reads=[vs, Pk], writes=[PS_O], same_ok=True)

            slc_front(0)
            for kt in range(bi + 1):
                if kt + 1 <= bi:
                    slc_front(kt + 1)
                slc_back(kt)
            p.op("dve", lambda e: e.tensor_scalar(out=rden[:64, :], in0=PS_DEN[:64, :], scalar1=1e-30, scalar2=None, op0=ALU.max), reads=[PS_DEN], writes=[rden])
            p.op("dve", lambda e: e.reciprocal(out=rden[:64, :], in_=rden[:64, :]), reads=[rden], writes=[rden])
            p.op("pool", lambda e: e.tensor_tensor(out=rden[:64, :], in0=rden[:64, :], in1=gb[:, 1, :], op=ALU.mult), reads=[rden, gb], writes=[rden])
            p.op("dve", lambda e: e.tensor_tensor(out=rden[:64, :], in0=rden[:64, :], in1=PS_O[:64, :], op=ALU.mult), reads=[rden, PS_O], writes=[rden])
            p.op("pool", lambda e: e.tensor_tensor(out=oacc[:, :], in0=oacc[:, :], in1=rden[:64, :], op=ALU.add), reads=[oacc, rden], writes=[oacc])
            kts = list(range(max(0, bi - 4), bi + 1))

            def swa_front(kt):
                Sp = PS_S[kt % 2]
                p.op("pe", lambda e: e.matmul(Sp[:, :], lhsT=kwT[:, kt * 128:(kt + 1) * 128], rhs=QRf, start=True, stop=True), reads=[kwT, QRg], writes=[Sp])

            def swa_back(ii, kt):
                Sp = PS_S[kt % 2]
                Ek = E[kt % 2]
                p.op("act", lambda e: e.activation(out=Ek[:, :], in_=Sp[:, :], func=AF.Exp, scale=scale), reads=[Sp], writes=[Ek])
                if kt == bi:
                    p.op("pool", lambda e: e.affine_select(out=Ek[:, :].rearrange("p (g q) -> p g q", g=4), in_=Ek[:, :].rearrange("p (g q) -> p g q", g=4),
                                                           pattern=[[0, 4], [1, 128]], compare_op=ALU.is_ge, fill=0.0, base=0, channel_multiplier=-1),
                         reads=[Ek], writes=[Ek])
                if kt == bi - 4:
                    p.op("pool", lambda e: e.affine_select(out=Ek[:, :].rearrange("p (g q) -> p g q", g=4), in_=Ek[:, :].rearrange("p (g q) -> p g q", g=4),
                                                           pattern=[[0, 4], [-1, 128]], compare_op=ALU.is_ge, fill=0.0, base=0, channel_multiplier=1),
                         reads=[Ek], writes=[Ek])
                p.op("pe", lambda e: e.matmul(PS_DEN[:64, :], lhsT=onesb[:, 0:64], rhs=Ek[:, :], start=(ii == 0), stop=(ii == len(kts) - 1)),
                     reads=[onesb, Ek], writes=[PS_DEN], same_ok=True)
                p.op("pe", lambda e: e.matmul(PS_O[:64, :], lhsT=vw[:, kt, :], rhs=Ek[:, :], start=(ii == 0), stop=(ii == len(kts) - 1)),
                     reads=[vw, Ek], writes=[PS_O], same_ok=True)

            swa_front(kts[0])
            for ii, kt in enumerate(kts):
                if ii + 1 < len(kts):
                    swa_front(kts[ii + 1])
                swa_back(ii, kt)
            p.op("dve", lambda e: e.tensor_scalar(out=rden[:64, :], in0=PS_DEN[:64, :], scalar1=1e-30, scalar2=None, op0=ALU.max), reads=[PS_DEN], writes=[rden])
            p.op("dve", lambda e: e.reciprocal(out=rden[:64, :], in_=rden[:64, :]), reads=[rden], writes=[rden])
            p.op("pool", lambda e: e.tensor_tensor(out=rden[:64, :], in0=rden[:64, :], in1=gb[:, 2, :], op=ALU.mult), reads=[rden, gb], writes=[rden])
            p.op("dve", lambda e: e.tensor_tensor(out=rden[:64, :], in0=rden[:64, :], in1=PS_O[:64, :], op=ALU.mult), reads=[rden, PS_O], writes=[rden])
            p.op("dve", lambda e: e.tensor_tensor(out=ob[:, :], in0=oacc[:, :], in1=rden[:64, :], op=ALU.add), reads=[oacc, rden], writes=[ob])
            nq = sq["nq"]
            for g in range(4):
                p.dma("sp", c.OT[4 * hk + g, :, o0:o0 + nq], ob[:, g * 128:g * 128 + nq], reads=[ob], writes=[c.OT])


def phase_nsa2(p, c, li, j):
    S = c.S
    with p.scope():
        f32c = lambda nm, shape, src: (lambda t: (p.dma("sp", t[tuple(slice(None) for _ in shape)], src, writes=[t]), t)[1])(p.sb(nm, shape, F32))
        wck = p.sb("wck", [128, 32, 64], BF16)
        wcv = p.sb("wcv", [128, 32, 64], BF16)
        p.dma("pool", wck[:, :, :], c.ns_wck[j], writes=[wck])
        p.dma("pool", wcv[:, :, :], c.ns_wcv[j], writes=[wcv])
        pekT = p.sb("pekT", [64, 32], BF16)
        pevT = p.sb("pevT", [64, 32], BF16)
        p.dma("pool", pekT[:, :], c.ns_pekT[j], writes=[pekT])
        p.dma("pool", pevT[:, :], c.ns_pevT[j], writes=[pevT])
        g0 = p.sb("g0", [64, 1], F32)
        p.dma("sp", g0[:, :], c.ns_gains[j, 0:64, 1:2], writes=[g0], allow_slow_non_contiguous=True)
        ones64 = p.sb("ones64", [64, 64], F32)
        p.op("dve", lambda e: e.memset(ones64[:, :], 1.0 / 64), writes=[ones64])
        onesb = p.sb("onesb", [128, 128], BF16)
        p.op("dve", lambda e: e.memset(onesb[:, :], 1.0), writes=[onesb])
        identb = p.sb("identb", [128, 128], BF16)
        p.op("dve", lambda e: e.tensor_copy(out=identb[:, :], in_=c.ident[:, :]), reads=[c.ident], writes=[identb])
        wtab = p.sb("wtab", [128, 256], F32)
        p.dma("sp", wtab[:, :], c.ns_wtab[:, :], writes=[wtab])
        c2s = p.sb("c2s", [128, 4, 128], BF16)
        p.dma("pool", c2s[:, :, :], c.ns_c2s[:, :, :], writes=[c2s])
        sel48 = p.sb("sel48", [48, 48, 64], F32)
        p.dma("sp", sel48[:, :, :], c.ns_sel48[:, :, :], writes=[sel48])
        tiny = p.sb("tiny", [128, 2], F32)
        p.op("dve", lambda e: e.memset(tiny[:, 0:1], 1e-30), writes=[tiny])
        p.op("dve", lambda e: e.memset(tiny[:, 1:2], NORM_EPS), writes=[tiny])
        TMAX = max(S, PAST + 128)
        kcs = p.sb("kcs", [64, TMAX], BF16)
        vcs = p.sb("vcs", [64, TMAX], BF16)
        ksT = p.sb("ksT", [64, TMAX], BF16)
        kwT = p.sb("kwT", [64, TMAX], BF16)
        vs = p.sb("vs", [128, TMAX // 128, 64], BF16)
        vw = p.sb("vw", [128, TMAX // 128, 64], BF16)
        kcmpT = p.sb("kcmpT", [64, 512], BF16)
        vcmp = p.sb("vcmp", [128, 4, 64], BF16)
        Qg = [p.sb(f"Qg{i}", [64, 4, 128], BF16) for i in range(2)]
        QRg = [p.sb(f"QRg{i}", [64, 4, 128], BF16) for i in range(2)]
        Gg = [p.sb(f"Gg{i}", [48, 128], F32) for i in range(2)]
        Pc = p.sb("Pc", [128, 4, 512], BF16)
        Pn = p.sb("Pn", [128, 4, 512], BF16)
        E = [p.sb(f"E{i}", [128, 512], BF16) for i in range(2)]
        Pm = [p.sb(f"Pm{i}", [128, 512], BF16) for i in range(2)]
        sc = p.sb("sc", [128, 136], F32)
        sc2 = p.sb("sc2", [128, 136], F32)
        m8 = p.sb("m8", [128, 8], F32)
        selb = p.sb("selb", [128, 136], BF16)
        selx = p.sb("selx", [128, TMAX], BF16)
        rden = p.sb("rden", [128, 512], F32)
        gb = p.sb("gb", [64, 3, 512], F32)
        oacc = p.sb("oacc", [64, 512], F32)
        ob = p.sb("ob", [64, 512], BF16)
        sqk = p.sb("sqk", [64, 512], F32)
        rsk = p.sb("rsk", [64, 512], F32)
        pet = p.sb("pet", [64, 1], F32)
        pvr = p.sb("pvr", [1, 64], BF16)
        PS_S = [p.ps(f"PS_S{i}", [128, 512], F32) for i in range(2)]
        PS_T = [p.ps(f"PS_T{i}", [128, 1024], BF16) for i in range(2)]
        PS_DEN = p.ps("PS_DEN", [128, 512], F32)
        PS_O = p.ps("PS_O", [128, 512], F32)
        PS_IMP = p.ps("PS_IMP", [128, 512], F32)
        PS_X = [p.ps("PS_X0", [128, 512], F32), PS_IMP]
        for l in range(32):
            p.op("pe", lambda e: e.matmul(PS_X[0][:64, 0:1], lhsT=wck[0:64, l, :], rhs=pekT[:, l:l + 1], start=(l == 0), stop=(l == 31)),
                 reads=[wck, pekT], writes=[PS_X[0]], same_ok=True)
        p.op("dve", lambda e: e.tensor_copy(out=pet[:, :], in_=PS_X[0][:64, 0:1]), reads=[PS_X[0]], writes=[pet])
        for l in range(32):
            p.op("pe", lambda e: e.matmul(PS_X[1][0:1, 0:64], lhsT=pevT[:, l:l + 1], rhs=wcv[0:64, l, :], start=(l == 0), stop=(l == 31)),
                 reads=[wcv, pevT], writes=[PS_X[1]], same_ok=True)
        p.op("dve", lambda e: e.tensor_copy(out=pvr[:, :], in_=PS_X[1][0:1, 0:64]), reads=[PS_X[1]], writes=[pvr])
        env = dict(consts=(wck, wcv, pekT, pevT, g0, ones64, onesb, identb, wtab, c2s, sel48, tiny),
                   tiles=(kcs, vcs, ksT, kwT, vs, vw, kcmpT, vcmp, Qg, QRg, Gg, Pc, Pn, E, Pm, sc, sc2, m8, selb, selx, rden, gb, oacc, ob, sqk, rsk, pet, pvr),
                   psum=(PS_S, PS_T, PS_DEN, PS_O, PS_IMP, PS_X[0]))
        sqp = dict(T=S, ncmp=S // 16 - 1, nq=128, KCT=c.KCT, VCT=c.VCT, KST=c.KST, KWT=c.KWT, VS=c.VS, VW=c.VW, QT=c.QT, QRT=c.QRT, GT=c.GTn,
                   KCT_t=c.KCT, VCT_t=c.VCT, KST_t=c.KST, KWT_t=c.KWT, VS_t=c.VS, VW_t=c.VW, QT_t=c.QT, QRT_t=c.QRT, GT_t=c.GTn,
                   blocks=[(bi, bi * 128, bi * 128) for bi in range(S // 128)])
        nsa_seq(p, c, j, env, sqp)
        TV = PAST + 128
        iof = p.sb("iof", [128, 1], F32)
        p.dma("sp", iof[:, :], c.ns_iota[:, :], writes=[iof])
        ptb = p.sb("ptb", [128, 64], I32)
        ptf = p.sb("ptf", [128, 64], F32)
        idx = p.sb("idx", [128, 64], I32)
        pgs = [p.sb(f"pgs{i}", [128, 256], BF16) for i in range(4)]
        stg = [p.sb(f"stg{i}", [128, 1024], BF16) for i in range(2)]
        zz = p.sb("zz", [128, 512], BF16)
        zf = p.sb("zf", [48, 128], F32)
        p.op("dve", lambda e: e.memset(zz[:, :], 0.0), writes=[zz])
        p.op("dve", lambda e: e.memset(zf[:, :], 0.0), writes=[zf])
        for tns in (c.KCTv, c.VCTv, c.KSTv, c.KWTv):
            for chn in range(2):
                p.dma("sp", tns[chn, :, PAST:TV], zz[:, 0:128], reads=[zz], writes=[tns])
        for tns in (c.VSv, c.VWv):
            p.dma("sp", tns[PAST:TV, :], zz[:, 0:256], reads=[zz], writes=[tns])
        for tns in (c.QTv, c.QRTv):
            for chn in range(8):
                p.dma("sp", tns[chn, :, :], zz[:, 0:128], reads=[zz], writes=[tns])
        p.dma("sp", c.GTv[:, :], zf[:, :], reads=[zf], writes=[c.GTv])
        for b in range(SB):
            p.dma("sp", ptb[:, :], c.ns_pt[b].partition_broadcast(128), writes=[ptb])
            p.op("dve", lambda e: e.tensor_copy(out=ptf[:, :], in_=ptb[:, :]), reads=[ptb], writes=[ptf])
            p.op("dve", lambda e: e.tensor_scalar(out=ptf[:, :], in0=ptf[:, :], scalar1=128.0, scalar2=iof[:, 0:1], op0=ALU.mult, op1=ALU.add),
                 reads=[ptf, iof], writes=[ptf])
            p.op("dve", lambda e: e.tensor_copy(out=idx[:, :], in_=ptf[:, :]), reads=[ptf], writes=[idx])
            igrp = 0
            for pool_d, dst in ((c.ns_cmp_k, c.KCTv), (c.ns_cmp_v, c.VCTv), (c.ns_slc_k, c.KSTv)):
                for pg4 in range(16):
                    sg_ = stg[igrp % 2]
                    igrp += 1
                    for q in range(4):
                        k = pg4 * 4 + q
                        p.idma(pgs[q][:, :], pool_d[:, :], idx[:, k:k + 1], reads=[idx, pool_d], writes=[pgs[q]])
                    for q in range(4):
                        for chn in range(2):
                            p.op("pe", lambda e: e.transpose(PS_T[0][:, (chn * 4 + q) * 128:(chn * 4 + q + 1) * 128], pgs[q][:, chn * 128:(chn + 1) * 128], identb[:, :]),
                                 reads=[pgs[q], identb], writes=[PS_T[0]], same_ok=True)
                    p.op("dve", lambda e: e.tensor_copy(out=sg_[:, :], in_=PS_T[0][:, :]), reads=[PS_T[0]], writes=[sg_])
                    for chn in range(2):
                        p.dma("sp", dst[chn, :, pg4 * 512:(pg4 + 1) * 512], sg_[:, chn * 512:(chn + 1) * 512], reads=[sg_], writes=[dst])
            for k in range(64):
                pq = pgs[k % 4]
                p.idma(pq[:, :], c.ns_slc_v[:, :], idx[:, k:k + 1], reads=[idx, c.ns_slc_v], writes=[pq])
                p.dma("sp", c.VSv[k * 128:(k + 1) * 128, :], pq[:, :], reads=[pq], writes=[c.VSv])
            sg_ = stg[igrp % 2]
            igrp += 1
            for q in range(4):
                p.dma("pool", pgs[q][:, :], c.ns_swa_k[j, b, q * 128:(q + 1) * 128, :], writes=[pgs[q]])
            for q in range(4):
                for chn in range(2):
                    p.op("pe", lambda e: e.transpose(PS_T[0][:, (chn * 4 + q) * 128:(chn * 4 + q + 1) * 128], pgs[q][:, chn * 128:(chn + 1) * 128], identb[:, :]),
                         reads=[pgs[q], identb], writes=[PS_T[0]], same_ok=True)
            p.op("dve", lambda e: e.tensor_copy(out=sg_[:, :], in_=PS_T[0][:, :]), reads=[PS_T[0]], writes=[sg_])
            for chn in range(2):
                p.dma("sp", c.KWTv[chn, :, PAST - 512:PAST], sg_[:, chn * 512:(chn + 1) * 512], reads=[sg_], writes=[c.KWTv])
            for q in range(4):
                p.dma("pool", pgs[q][:, :], c.ns_swa_v[j, b, q * 128:(q + 1) * 128, :], writes=[pgs[q]])
                p.dma("sp", c.VWv[PAST - 512 + q * 128:PAST - 512 + (q + 1) * 128, :], pgs[q][:, :], reads=[pgs[q]], writes=[c.VWv])
            s0 = S + b * SL
            for src, dst in ((c.KCT, c.KCTv), (c.VCT, c.VCTv), (c.KST, c.KSTv), (c.KWT, c.KWTv)):
                p.dma("sp", dst[:, :, PAST:PAST + SL], src[:, :, s0:s0 + SL], reads=[src], writes=[dst])
            for src, dst in ((c.VS, c.VSv), (c.VW, c.VWv)):
                p.dma("sp", dst[PAST:PAST + SL, :], src[s0:s0 + SL, :], reads=[src], writes=[dst])
            for src, dst in ((c.QT, c.QTv), (c.QRT, c.QRTv)):
                p.dma("sp", dst[:, :, 0:SL], src[:, :, s0:s0 + SL], reads=[src], writes=[dst])
            p.dma("sp", c.GTv[:, 0:SL], c.GTn[:, s0:s0 + SL], reads=[c.GTn], writes=[c.GTv])
            sqv = dict(T=TV, kw0=PAST - 512, ncmp=PAST // 16 - 1, nq=SL, KCT=c.KCTv, VCT=c.VCTv, KST=c.KSTv, KWT=c.KWTv, VS=c.VSv, VW=c.VWv, QT=c.QTv, QRT=c.QRTv, GT=c.GTv,
                       KCT_t=c.KCTv, VCT_t=c.VCTv, KST_t=c.KSTv, KWT_t=c.KWTv, VS_t=c.VSv, VW_t=c.VWv, QT_t=c.QTv, QRT_t=c.QRTv, GT_t=c.GTv,
                       blocks=[(PAST // 128, 0, s0)])
            nsa_seq(p, c, j, env, sqv)
    with p.scope():
        wout = p.sb("nswout", [64, 16, D], BF16)
        p.dma("pool", wout[:, :, :], c.ns_w_out[j], writes=[wout])
        hTs = [p.sb(f"hT{i}", [128, KC, 512], F32) for i in range(2)]
        ot = [p.sb(f"ot{i}", [64, 16, 512], BF16) for i in range(2)]
        pso = [p.ps(f"pso{i}", [128, 512], F32) for i in range(2)]
        io = 0
        for ti, (t0, tw) in enumerate(tiles_of(S)):
            hT, o_ = hTs[ti % 2], ot[ti % 2]
            p.dma("sp", hT[:, :, :tw], c.H[:, :, t0:t0 + tw].rearrange("k p t -> p k t"), reads=[c.H], writes=[hT])
            p.dma("sp", o_[:, :, :tw], c.OT[:, :, t0:t0 + tw].rearrange("h p t -> p h t"), reads=[c.OT], writes=[o_])
            for oc in range(KC):
                O = pso[io % 2]
                io += 1
                for h in range(16):
                    p.op("pe", lambda e: e.matmul(O[:, :tw], lhsT=wout[:, h, oc * 128:(oc + 1) * 128], rhs=o_[:, h, :tw], start=(h == 0), stop=(h == 15)),
                         reads=[wout, o_], writes=[O], same_ok=True)
                if c.dbg_mix is not None:
                    dm = p.sb("dm", [128, 512], F32) if oc == 0 and ti == 0 else dm
                    p.op("dve", lambda e: e.tensor_copy(out=dm[:, :tw], in_=O[:, :tw]), reads=[O], writes=[dm])
                    p.dma("sp", c.dbg_mix[oc, :, t0:t0 + tw], dm[:, :tw], reads=[dm], writes=[c.dbg_mix])
                p.op("dve", lambda e: e.tensor_tensor(out=hT[:, oc, :tw], in0=hT[:, oc, :tw], in1=O[:, :tw], op=ALU.add), reads=[O, hT], writes=[hT])
            p.dma("sp", c.H[:, :, t0:t0 + tw].rearrange("k p t -> p k t"), hT[:, :, :tw], reads=[hT], writes=[c.H])


def build(S, mixers=(0, 1, 2), dn_layers=(0, 3), npool=2560):
    nc = bass.Bass("TRN2", target_bir_lowering=False)
    p = Prog(nc)
    c = Ctx()
    c.S = S
    c.npool = npool
    NT = S + ST
    c.xp = p.dram("xp", [S, D], F32, "ExternalInput")
    c.xs = p.dram("xs", [ST, D], F32, "ExternalInput")
    c.w_gate = p.dram("w_gate", [DEPTH, 128, KC, FH], F32, "ExternalInput")
    c.w_up = p.dram("w_up", [DEPTH, 128, KC, FH], F32, "ExternalInput")
    c.w_down = p.dram("w_down", [DEPTH, 128, HC, D], F32, "ExternalInput")
    c.nffn_d = p.dram("nffn", [128, DEPTH, KC], F32, "ExternalInput")
    c.nmix_d = p.dram("nmix", [128, DEPTH, KC], F32, "ExternalInput")
    c.ident_d = p.dram("ident", [128, 128], F32, "ExternalInput")
    c.yp = p.dram("yp", [S, D], F32, "ExternalOutput")
    c.ys = p.dram("ys", [ST, D], F32, "ExternalOutput")
    c.H = p.dram("H", [KC, 128, NT], F32)
    c.dn_w_in = p.dram("dn_w_in", [2, 128, KC, DN_W], F32, "ExternalInput")
    c.dn_cw = p.dram("dn_cw", [2, 128, DN_CC, 4], F32, "ExternalInput")
    c.dn_ab = p.dram("dn_ab", [2, 64, 2], F32, "ExternalInput")
    c.dn_cs_in = p.dram("dn_cs_in", [2, SB * 3, 4096], F32, "ExternalInput")
    c.dn_S_in = p.dram("dn_S_in", [2, SB, 16, 128, 128], F32, "ExternalInput")
    c.dn_ng = p.dram("dn_ng", [2, 128, 1], F32, "ExternalInput")
    c.dn_w_out = p.dram("dn_w_out", [2, 128, 16, D], F32, "ExternalInput")
    offp, totp = dn_pack_layout(128, 1)
    offs, tots = dn_pack_layout(ST, SB)
    c.dnc_p = p.dram("dnc_p", [128, totp], F32, "ExternalInput")
    c.dnc_s = p.dram("dnc_s", [ST, tots], F32, "ExternalInput")
    c.dncm_p = p.dram("dncm_p", [128, 1, 128], F32, "ExternalInput")
    c.dncm_s = p.dram("dncm_s", [128, SB, ST], F32, "ExternalInput")
    c.sel16_d = p.dram("sel16", [16, 16, 128], F32, "ExternalInput")
    dk = "ExternalOutput" if os.environ.get("DNDBG") else "Internal"
    c.QKVT = p.dram("QKVT", [DN_CC, 128, NT], F32, dk)
    c.ZT = p.dram("ZT", [16, 128, NT], BF16, dk)
    c.GT = p.dram("GT", [64, NT], F32, dk)
    c.ns_w_in = p.dram("ns_w_in", [1, 128, KC, NSA_W], F32, "ExternalInput")
    c.ns_gains = p.dram("ns_gains", [1, 128, 4], F32, "ExternalInput")
    c.ns_rotT = p.dram("ns_rotT", [128, 128], F32, "ExternalInput")
    c.ns_cos = p.dram("ns_cos", [128, NT], F32, "ExternalInput")
    c.ns_sin = p.dram("ns_sin", [128, NT], F32, "ExternalInput")
    c.ns_swa_k = p.dram("ns_swa_k", [1, SB, 512, 256], F32, "ExternalInput")
    c.ns_swa_v = p.dram("ns_swa_v", [1, SB, 512, 256], F32, "ExternalInput")
    c.QT = p.dram("QT", [8, 128, NT], BF16)
    c.QRT = p.dram("QRT", [8, 128, NT], BF16)
    c.KST = p.dram("KST", [2, 128, NT], BF16)
    c.KWT = p.dram("KWT", [2, 128, NT], BF16)
    c.KCT = p.dram("KCT", [2, 128, NT], BF16)
    c.VCT = p.dram("VCT", [2, 128, NT], BF16)
    c.VS = p.dram("VS", [NT, 256], BF16)
    c.VW = p.dram("VW", [NT, 256], BF16)
    c.GTn = p.dram("GTn", [48, NT], F32)
    c.OT = p.dram("OT", [16, 64, NT], BF16)
    TV = PAST + 128
    c.KCTv = p.dram("KCTv", [2, 128, TV], BF16)
    c.VCTv = p.dram("VCTv", [2, 128, TV], BF16)
    c.KSTv = p.dram("KSTv", [2, 128, TV], BF16)
    c.KWTv = p.dram("KWTv", [2, 128, TV], BF16)
    c.VSv = p.dram("VSv", [TV, 256], BF16)
    c.VWv = p.dram("VWv", [TV, 256], BF16)
    c.QTv = p.dram("QTv", [8, 128, 128], BF16)
    c.QRTv = p.dram("QRTv", [8, 128, 128], BF16)
    c.GTv = p.dram("GTv", [48, 128], F32)
    npool = c.npool
    c.ns_cmp_k = p.dram("ns_cmp_k", [npool * 128, 256], F32, "ExternalInput")
    c.ns_cmp_v = p.dram("ns_cmp_v", [npool * 128, 256], F32, "ExternalInput")
    c.ns_slc_k = p.dram("ns_slc_k", [npool * 128, 256], F32, "ExternalInput")
    c.ns_slc_v = p.dram("ns_slc_v", [npool * 128, 256], F32, "ExternalInput")
    c.ns_pt = p.dram("ns_pt", [SB, 64], I32, "ExternalInput")
    c.ns_iota = p.dram("ns_iota", [128, 1], F32, "ExternalInput")
    c.ns_wck = p.dram("ns_wck", [1, 128, 32, 64], F32, "ExternalInput")
    c.ns_wcv = p.dram("ns_wcv", [1, 128, 32, 64], F32, "ExternalInput")
    c.ns_pekT = p.dram("ns_pekT", [1, 64, 32], F32, "ExternalInput")
    c.ns_pevT = p.dram("ns_pevT", [1, 64, 32], F32, "ExternalInput")
    c.ns_w_out = p.dram("ns_w_out", [1, 64, 16, D], F32, "ExternalInput")
    c.ns_wtab = p.dram("ns_wtab", [128, 256], F32, "ExternalInput")
    c.ns_c2s = p.dram("ns_c2s", [128, 4, 128], F32, "ExternalInput")
    c.ns_sel48 = p.dram("ns_sel48", [48, 48, 64], F32, "ExternalInput")
    c.dbg_mix = p.dram("dbg_mix", [KC, 128, NT], F32, "ExternalOutput") if os.environ.get("NSDBG") else None
    c.sg_w_in = p.dram("sg_w_in", [1, 128, KC, 2048], F32, "ExternalInput")
    c.sg_w_out = p.dram("sg_w_out", [1, 128, KC, D], F32, "ExternalInput")
    c.sg_ln_g = p.dram("sg_ln_g", [1, 128, KC], F32, "ExternalInput")
    c.sg_ln_b = p.dram("sg_ln_b", [1, 128, KC], F32, "ExternalInput")
    c.sg_wspT = p.dram("sg_wspT", [1, 128, KC, 128], F32, "ExternalInput")
    c.sg_bsp4 = p.dram("sg_bsp4", [1, KC, 512], F32, "ExternalInput")
    c.o_sgv = p.dram("o_sgv", [1, ST, D], F32, "ExternalOutput")
    c.o_dS_p = p.dram("o_dS_p", [2, 16, 128, 128], F32, "ExternalOutput")
    c.o_dS_s = p.dram("o_dS_s", [2, SB, 16, 128, 128], F32, "ExternalOutput")
    c.o_dc_p = p.dram("o_dc_p", [2, 3, 4096], F32, "ExternalOutput")
    c.o_dc_s = p.dram("o_dc_s", [2, SB, 3, 4096], F32, "ExternalOutput")
    c.o_swa_p = p.dram("o_swa_p", [2, min(512, S), 256], F32, "ExternalOutput")
    c.o_swa_s = p.dram("o_swa_s", [2, SB, 512, 256], F32, "ExternalOutput")
    c.o_new_p = p.dram("o_new_p", [4, S, 256], F32, "ExternalOutput")
    c.o_new_s = p.dram("o_new_s", [4, ST, 256], F32, "ExternalOutput")
    c.ident = p.sb("ident", [128, 128], F32)
    c.ones = p.sb("ones", [128, 128], F32)
    c.nffn = p.sb("nffn", [128, DEPTH, KC], F32)
    c.nmix = p.sb("nmix", [128, DEPTH, KC], F32)
    p.dma("sp", c.ident[:, :], c.ident_d[:, :], writes=[c.ident])
    p.dma("sp", c.nffn[:, :, :], c.nffn_d[:, :, :], writes=[c.nffn])
    p.dma("sp", c.nmix[:, :, :], c.nmix_d[:, :, :], writes=[c.nmix])
    p.op("dve", lambda e: e.memset(c.ones[:, :], 1.0), writes=[c.ones])
    c.eps_norm = p.sb("eps_norm", [128, 1], F32)
    p.op("dve", lambda e: e.memset(c.eps_norm[:, :], NORM_EPS), writes=[c.eps_norm])
    c.onesD = p.sb("onesD", [128, 128], F32)
    p.op("dve", lambda e: e.memset(c.onesD[:, :], 1.0 / D), writes=[c.onesD])
    phase_in(p, c)
    for li in range(DEPTH):
        kind, j = li % 3, li // 3
        if kind == 0 and 0 in mixers and li in dn_layers:
            phase_dn1(p, c, li, j)
            if DNSTOP >= 2:
                phase_dn2(p, c, li, j)
        if kind == 1 and 1 in mixers:
            phase_sg(p, c, li, j)
        if kind == 2 and 2 in mixers:
            phase_nsa1(p, c, li, j)
            if NSSTOP >= 2:
                phase_nsa2(p, c, li, j)
        phase_ffn(p, c, li)
    phase_out(p, c)
    p.close()
    c.ninst = p.ninst
    return nc, c


def chunked(w, nchunk):
    K, N = w.shape
    return np.ascontiguousarray(w.reshape(nchunk, 128, N).transpose(1, 0, 2))


def featmajor(v):
    L, F = v.shape
    return np.ascontiguousarray(v.reshape(L, F // 128, 128).transpose(2, 0, 1))


def sg_inputs(w_in, ln_g, ln_b, wsp, bsp, w_out):
    n = w_in.shape[0]
    return {
        "sg_w_in": np.stack([chunked(np.asarray(w_in[i]), KC) for i in range(n)]),
        "sg_w_out": np.stack([chunked(np.asarray(w_out[i]), KC) for i in range(n)]),
        "sg_ln_g": np.stack([featmajor(np.asarray(ln_g[i:i + 1]))[:, 0, :] for i in range(n)]),
        "sg_ln_b": np.stack([featmajor(np.asarray(ln_b[i:i + 1]))[:, 0, :] for i in range(n)]),
        "sg_wspT": np.ascontiguousarray(np.asarray(wsp).transpose(0, 3, 1, 2)),
        "sg_bsp4": np.ascontiguousarray(np.tile(np.asarray(bsp), (1, 1, 4))),
    }


def dn_inputs(w_in, conv_w, a_log, dt_bias, norm, w_out):
    n = w_in.shape[0]
    wi = []
    for i in range(n):
        w = np.asarray(w_in[i], np.float32)
        wp = np.zeros((D, DN_W), np.float32)
        wp[:, 0:6144] = w[:, 0:6144]
        wp[:, 6144:6160] = w[:, 6144:6160]
        wp[:, 6176:6192] = w[:, 6160:6176]
        wi.append(chunked(wp, KC))
    ab = np.zeros((n, 64, 2), np.float32)
    ab[:, 32:48, 0] = np.asarray(a_log)
    ab[:, 32:48, 1] = np.asarray(dt_bias)
    return {
        "dn_w_in": np.stack(wi),
        "dn_cw": np.ascontiguousarray(np.asarray(conv_w, np.float32).reshape(n, 4, DN_CC, 128).transpose(0, 3, 2, 1)),
        "dn_ab": ab,
        "dn_ng": np.ascontiguousarray(np.asarray(norm, np.float32).reshape(n, 128, 1)),
        "dn_w_out": np.stack([chunked(np.asarray(w_out[i], np.float32), 16) for i in range(n)]),
    }


def dn_state_inputs(state_delta, state_conv, core):
    sd = np.asarray(state_delta, np.float32)[:, core * SB:(core + 1) * SB]
    sc = np.asarray(state_conv, np.float32)[:, core * SB:(core + 1) * SB]
    return {"dn_S_in": np.ascontiguousarray(sd), "dn_cs_in": np.ascontiguousarray(sc.reshape(2, SB * 3, 4096))}


def const_inputs():
    pk_p, cm_p = dn_consts(128, 1)
    pk_s, cm_s = dn_consts(ST, SB)
    sel = np.zeros((16, 16, 128), np.float32)
    for h in range(16):
        sel[h, h, :] = 1.0
    return {"dnc_p": pk_p, "dnc_s": pk_s, "dncm_p": cm_p, "dncm_s": cm_s, "sel16": sel}


def nsa_inputs(w_in, q_norm, k_norm, S):
    f = np.float32
    n = w_in.shape[0]
    gains = np.zeros((n, 128, 4), f)
    for i in range(n):
        gains[i, :, 0] = np.tile(np.asarray(q_norm[i], f), 2)
        for kk in range(3):
            gains[i, :, 1 + kk] = np.tile(np.asarray(k_norm[i, kk], f), 2)
    rot = np.zeros((128, 128), f)
    for blk in range(2):
        for d in range(32):
            rot[blk * 64 + d + 32, blk * 64 + d] = -1.0
            rot[blk * 64 + d, blk * 64 + d + 32] = 1.0
    pos = np.concatenate([np.arange(S), np.tile(PAST + np.arange(SL), SB)]).astype(f)
    inv = (10000.0 ** (-np.arange(32, dtype=f) / 32)).astype(f)
    ang = pos[None, :] * np.tile(inv, 4)[:, None]
    return {
        "ns_w_in": np.stack([chunked(np.asarray(w_in[i], f), KC) for i in range(n)]),
        "ns_gains": gains, "ns_rotT": rot,
        "ns_cos": np.cos(ang).astype(f), "ns_sin": np.sin(ang).astype(f),
    }


def nsa_cache_inputs(swa_k, swa_v, pools, page_table, core):
    f = np.float32
    sl = slice(core * SB, (core + 1) * SB)
    d = {"ns_swa_k": np.ascontiguousarray(np.asarray(swa_k, f)[:, sl].reshape(1, SB, 512, 256)),
         "ns_swa_v": np.ascontiguousarray(np.asarray(swa_v, f)[:, sl].reshape(1, SB, 512, 256)),
         "ns_pt": np.ascontiguousarray(np.asarray(page_table, np.int32)[sl]),
         "ns_iota": np.arange(128, dtype=f).reshape(128, 1)}
    for nm in ("cmp_k", "cmp_v", "slc_k", "slc_v"):
        a = np.asarray(pools["cache_" + nm], f)
        d["ns_" + nm] = a.reshape(a.shape[1] * 128, 256)
    return d


def nsa_attn_inputs(pe_k, pe_v, w_ck, w_cv, w_out, S):
    f = np.float32
    n = np.asarray(w_ck).shape[0]
    def wl(w):
        a = np.asarray(w, f).reshape(n, 32, 64, 64).transpose(0, 2, 1, 3)
        return np.ascontiguousarray(np.concatenate([a, a], 1))
    d = {"ns_wck": wl(w_ck), "ns_wcv": wl(w_cv),
         "ns_pekT": np.ascontiguousarray(np.asarray(pe_k, f).transpose(0, 2, 1)),
         "ns_pevT": np.ascontiguousarray(np.asarray(pe_v, f).transpose(0, 2, 1)),
         "ns_w_out": np.ascontiguousarray(np.asarray(w_out, f).reshape(n, 16, 64, D).transpose(0, 2, 1, 3))}
    d.update(nsa_consts())
    return d


def kernel(**inp):
    x_prompt = np.asarray(inp["x_prompt"], np.float32)
    x_sample = np.asarray(inp["x_sample"], np.float32)
    B, S, _ = x_prompt.shape
    npool = int(np.asarray(inp["cache_cmp_k"]).shape[1])
    nc, c = build(S, mixers=MIXERS, npool=npool)
    shared = {
        "w_gate": np.stack([chunked(np.asarray(inp["ffn_w_gate"][i]), KC) for i in range(DEPTH)]),
        "w_up": np.stack([chunked(np.asarray(inp["ffn_w_up"][i]), KC) for i in range(DEPTH)]),
        "w_down": np.stack([chunked(np.asarray(inp["ffn_w_down"][i]), HC) for i in range(DEPTH)]),
        "nffn": featmajor(np.asarray(inp["norm_ffn"], np.float32)),
        "nmix": featmajor(np.asarray(inp["norm_mix"], np.float32)),
        "ident": np.eye(128, dtype=np.float32),
    }
    shared.update(sg_inputs(inp["sg_w_in"], inp["sg_ln_g"], inp["sg_ln_b"], inp["sg_w_spatial"], inp["sg_b_spatial"],
                            inp["sg_w_out"]))
    shared.update(dn_inputs(inp["dn_w_in"], inp["dn_conv_w"], inp["dn_a_log"], inp["dn_dt_bias"], inp["dn_norm"], inp["dn_w_out"]))
    shared.update(const_inputs())
    shared.update(nsa_inputs(np.asarray(inp["nsa_w_in"]), np.asarray(inp["nsa_q_norm"]), np.asarray(inp["nsa_k_norm"]), S))
    shared.update(nsa_attn_inputs(inp["nsa_cmp_pe_k"], inp["nsa_cmp_pe_v"], inp["nsa_cmp_w_k"], inp["nsa_cmp_w_v"], inp["nsa_w_out"], S))
    pools = {k: inp[k] for k in ("cache_cmp_k", "cache_cmp_v", "cache_slc_k", "cache_slc_v")}
    in_maps = []
    for core in range(NCORES):
        m = dict(shared)
        m["xp"] = np.ascontiguousarray(x_prompt[core % B])
        m["xs"] = np.ascontiguousarray(x_sample[core * SB:(core + 1) * SB].reshape(ST, D))
        m.update(dn_state_inputs(inp["state_delta"], inp["state_conv"], core))
        cc = nsa_cache_inputs(inp["cache_swa_k"], inp["cache_swa_v"], pools, inp["page_table"], core)
        if core > 0:
            for nm in ("ns_cmp_k", "ns_cmp_v", "ns_slc_k", "ns_slc_v"):
                cc[nm] = in_maps[0][nm]
        m.update(cc)
        in_maps.append(m)
    res = run_bass_kernel_spmd(nc, in_maps, core_ids=list(range(NCORES)))
    r = res.results
    cat = np.concatenate
    NB = NCORES * SB
    y_prompt = np.stack([r[b]["yp"] for b in range(B)])
    y_sample = cat([r[k]["ys"].reshape(SB, SL, D) for k in range(NCORES)], 0)
    dS_p = np.stack([r[b]["o_dS_p"] for b in range(B)], 1)
    dS_s = cat([r[k]["o_dS_s"] for k in range(NCORES)], 1)
    dc_p = np.stack([r[b]["o_dc_p"] for b in range(B)], 1)
    dc_s = cat([r[k]["o_dc_s"] for k in range(NCORES)], 1)
    sgv = cat([r[k]["o_sgv"].reshape(1, SB, SL, D) for k in range(NCORES)], 1)
    W = r[0]["o_swa_p"].shape[1]
    swa_p = [np.stack([r[b]["o_swa_p"][i] for b in range(B)]).reshape(1, B, W, 4, 64) for i in range(2)]
    swa_s = [cat([r[k]["o_swa_s"][i] for k in range(NCORES)], 0).reshape(1, NB, 512, 4, 64) for i in range(2)]
    new_p = [np.stack([r[b]["o_new_p"][i] for b in range(B)]).reshape(1, B, S, 4, 64) for i in range(4)]
    new_s = [cat([r[k]["o_new_s"][i].reshape(SB, SL, 4, 64) for k in range(NCORES)], 0).reshape(1, NB, SL, 4, 64) for i in range(4)]
    return (y_prompt, y_sample, dS_p, dS_s, dc_p, dc_s, sgv, swa_p[0], swa_p[1], swa_s[0], swa_s[1],
            new_p[0], new_p[1], new_p[2], new_p[3], new_s[0], new_s[1], new_s[2], new_s[3])
```

```python
import contextlib
import numpy as np
import concourse.bass as bass
import concourse.mybir as mybir
from concourse.bass_utils import run_bass_kernel_spmd

F32 = mybir.dt.float32
BF16 = mybir.dt.bfloat16
I32 = mybir.dt.int32
AF = mybir.ActivationFunctionType
ALU = mybir.AluOpType
AX = mybir.AxisListType

D = 1024
KC = 8
FH = 2816
HC = 22
DEPTH = 4
NCORES = 8
SB = 4
SL = 4
ST = SB * SL
NORM_EPS = 1e-6


class T:
    def __init__(self, h, name):
        self.h = h
        self.name = name
        self.w = None
        self.r = {}

    def __getitem__(self, k):
        return self.h[k]


class Prog:
    def __init__(self, nc):
        self.nc = nc
        self.engs = {}
        self.stack = []
        for nm, h in (("pe", nc.tensor), ("act", nc.scalar), ("dve", nc.vector), ("pool", nc.gpsimd), ("sp", nc.sync)):
            self.engs[nm] = dict(h=h, sem=self._sem("s_" + nm), cnt=0, known={})
        self.dq = {}
        for q in ("sp", "pool"):
            self.dq[q] = dict(sems=[self._sem(f"d_{q}{i}") for i in range(8)], cnt=[0] * 8, nxt=0)
        self.ninst = 0

    def _sem(self, name):
        cm = self.nc.semaphore(name)
        s = cm.__enter__()
        self.stack.append(cm)
        return s

    def sb(self, name, shape, dt):
        self.uid = getattr(self, "uid", 0) + 1
        cm = self.nc.sbuf_tensor(f"sb{self.uid}_{name}", shape, dt)
        h = cm.__enter__()
        self.stack.append(cm)
        return T(h, name)

    def ps(self, name, shape, dt):
        self.uid = getattr(self, "uid", 0) + 1
        cm = self.nc.psum_tensor(f"ps{self.uid}_{name}", shape, dt)
        h = cm.__enter__()
        self.stack.append(cm)
        t = T(h, name)
        t.psum = True
        return t

    def dram(self, name, shape, dt, kind="Internal"):
        h = self.nc.dram_tensor(name, shape, dt, kind=kind)
        return T(h.ap(), name)

    @contextlib.contextmanager
    def scope(self):
        n = len(self.stack)
        self.barrier()
        yield
        self.barrier()
        while len(self.stack) > n:
            self.stack.pop().__exit__(None, None, None)

    def _need(self, e, src, n):
        E = self.engs[e]
        if E["known"].get(src, 0) >= n:
            return
        if src in self.engs:
            sem, val = self.engs[src]["sem"], n
        else:
            q, i = src
            sem, val = self.dq[q]["sems"][i], 16 * n
        E["h"].wait_ge(sem, val)
        E["known"][src] = n
        self.ninst += 1

    def _deps(self, e, reads, writes, same_ok=False):
        for t in reads:
            if t.w is not None and not (same_ok and t.w[0] == e):
                self._need(e, *t.w)
            if getattr(t, "psum", False):
                for src, n in t.r.items():
                    if src != e:
                        self._need(e, src, n)
        for t in writes:
            if t.w is not None and not (same_ok and t.w[0] == e):
                self._need(e, *t.w)
            for src, n in t.r.items():
                if src == e:
                    continue
                self._need(e, src, n)

    def _mark(self, stream, n, reads, writes):
        for t in writes:
            t.w = (stream, n)
            t.r = {}
        for t in reads:
            if t in writes:
                continue
            t.r[stream] = n

    def op(self, e, fn, reads=(), writes=(), same_ok=False):
        E = self.engs[e]
        if e != "pe":
            same_ok = False
        self._deps(e, reads, writes, same_ok)
        ins = fn(E["h"])
        E["cnt"] += 1
        ins.then_inc(E["sem"], 1)
        self._mark(e, E["cnt"], reads, writes)
        self.ninst += 1
        return ins

    def dma(self, q, out, in_, reads=(), writes=(), **kw):
        Q = self.dq[q]
        i = Q["nxt"]
        Q["nxt"] = (i + 1) % len(Q["sems"])
        stream = (q, i)
        if Q["cnt"][i] > 0:
            self._need(q, stream, Q["cnt"][i])
        self._deps(q, reads, writes)
        ins = self.engs[q]["h"].dma_start(out=out, in_=in_, **kw)
        Q["cnt"][i] += 1
        ins.then_inc(Q["sems"][i], 16)
        self._mark(stream, Q["cnt"][i], reads, writes)
        self.ninst += 1
        return ins

    def idma(self, out, in_, idx_ap, reads=(), writes=()):
        Q = self.dq["pool"]
        i = Q["nxt"]
        Q["nxt"] = (i + 1) % len(Q["sems"])
        stream = ("pool", i)
        if Q["cnt"][i] > 0:
            self._need("pool", stream, Q["cnt"][i])
        self._deps("pool", reads, writes)
        ins = self.nc.gpsimd.indirect_dma_start(out=out, out_offset=None, in_=in_,
                                                in_offset=bass.IndirectOffsetOnAxis(ap=idx_ap, axis=0))
        Q["cnt"][i] += 1
        ins.then_inc(Q["sems"][i], 16)
        self._mark(stream, Q["cnt"][i], reads, writes)
        self.ninst += 1
        return ins

    def barrier(self):
        streams = [(e, self.engs[e]["cnt"]) for e in ("pe", "act", "dve", "pool") if self.engs[e]["cnt"]]
        for q in self.dq:
            for i, c in enumerate(self.dq[q]["cnt"]):
                if c:
                    streams.append(((q, i), c))
        for e in ("pe", "act", "dve", "pool", "sp"):
            for s, c in streams:
                if s != e:
                    self._need(e, s, c)

    def close(self):
        self.barrier()
        while self.stack:
            self.stack.pop().__exit__(None, None, None)


class Ctx:
    pass


def tiles_of(S):
    tl = [(t0, 512) for t0 in range(0, S, 512)]
    tl.append((S, ST))
    return tl


def phase_in(p, c):
    with p.scope():
        xin = [p.sb(f"xin{i}", [128, D], F32) for i in range(4)]
        ho = [p.sb(f"ho{i}", [128, KC, 512], F32) for i in range(2)]
        pst = [p.ps(f"pst{i}", [128, 512], F32) for i in range(4)]
        it = 0
        for ti, (t0, tw) in enumerate(tiles_of(c.S)):
            nsub = (tw + 127) // 128
            for j in range(nsub):
                r = min(128, tw - j * 128)
                src = c.xp[t0 + j * 128:t0 + j * 128 + r, :] if t0 < c.S else c.xs[:, :]
                p.dma("sp", xin[j][:r, :], src, writes=[xin[j]])
            hb = ho[ti % 2]
            for kc in range(KC):
                P = pst[it % 4]
                it += 1
                for j in range(nsub):
                    r = min(128, tw - j * 128)
                    p.op("pe", lambda e: e.transpose(P[:, j * 128:j * 128 + r], xin[j][:r, kc * 128:(kc + 1) * 128], c.ident[:r, :r]),
                         reads=[xin[j]], writes=[P], same_ok=True)
                eng = "dve" if kc % 2 == 0 else "act"
                if eng == "dve":
                    p.op("dve", lambda e: e.tensor_copy(out=hb[:, kc, :tw], in_=P[:, :tw]), reads=[P], writes=[hb], same_ok=True)
                else:
                    p.op("act", lambda e: e.copy(out=hb[:, kc, :tw], in_=P[:, :tw]), reads=[P], writes=[hb], same_ok=True)
            p.dma("pool", c.H[:, :, t0:t0 + tw].rearrange("k p t -> p k t"), hb[:, :, :tw], reads=[hb], writes=[c.H])


def phase_out(p, c):
    with p.scope():
        hi = [p.sb(f"hi{i}", [128, KC, 512], F32) for i in range(2)]
        yo = [p.sb(f"yo{i}", [128, D], F32) for i in range(2)]
        pst = [p.ps(f"pso{i}", [128, 512], F32) for i in range(4)]
        it = 0
        for ti, (t0, tw) in enumerate(tiles_of(c.S)):
            hb = hi[ti % 2]
            p.dma("sp", hb[:, :, :tw], c.H[:, :, t0:t0 + tw].rearrange("k p t -> p k t"), reads=[c.H], writes=[hb])
            nsub = (tw + 127) // 128
            for j in range(nsub):
                r = min(128, tw - j * 128)
                yb = yo[it % 2]
                for half in range(2):
                    P = pst[(2 * it + half) % 4]
                    for k4 in range(4):
                        kc = half * 4 + k4
                        p.op("pe", lambda e: e.transpose(P[:r, k4 * 128:(k4 + 1) * 128], hb[:, kc, j * 128:j * 128 + r], c.ident[:, :]),
                             reads=[hb], writes=[P], same_ok=True)
                    if half == 0:
                        p.op("dve", lambda e: e.tensor_copy(out=yb[:r, 0:512], in_=P[:r, :]), reads=[P], writes=[yb], same_ok=True)
                    else:
                        p.op("act", lambda e: e.copy(out=yb[:r, 512:1024], in_=P[:r, :]), reads=[P], writes=[yb], same_ok=True)
                it += 1
                dst = c.yp[t0 + j * 128:t0 + j * 128 + r, :] if t0 < c.S else c.ys[:, :]
                p.dma("pool", dst, yb[:r, :], reads=[yb], writes=[c.yp if t0 < c.S else c.ys])


def rmsnorm_tile(p, c, hT, xn, tw, gcol, ps_stat, tmp, rstd):
    for kc in range(KC):
        sq = tmp[kc % 2]
        p.op("act", lambda e: e.activation(out=sq[:, :tw], in_=hT[:, kc, :tw], func=AF.Square), reads=[hT], writes=[sq])
        p.op("pe", lambda e: e.matmul(ps_stat[:, :tw], lhsT=c.onesD[:, :], rhs=sq[:, :tw], start=(kc == 0), stop=(kc == KC - 1)),
             reads=[sq, c.onesD], writes=[ps_stat], same_ok=True)
    p.op("act", lambda e: e.activation(out=rstd[:, :tw], in_=ps_stat[:, :tw], func=AF.Sqrt, bias=c.eps_norm[:, 0:1], scale=1.0),
         reads=[ps_stat, c.eps_norm], writes=[rstd])
    p.op("dve", lambda e: e.reciprocal(out=rstd[:, :tw], in_=rstd[:, :tw]), reads=[rstd], writes=[rstd])
    for kc in range(KC):
        p.op("dve", lambda e: e.scalar_tensor_tensor(out=xn[:, kc, :tw], in0=hT[:, kc, :tw], scalar=gcol(kc), in1=rstd[:, :tw],
                                                   op0=ALU.mult, op1=ALU.mult), reads=[hT, rstd], writes=[xn], same_ok=True)


def phase_ffn(p, c, li):
    with p.scope():
        wg = p.sb("wg", [128, KC, FH], BF16)
        wu = p.sb("wu", [128, KC, FH], BF16)
        wd = p.sb("wd", [128, HC, D], BF16)
        for kc in range(KC):
            p.dma("pool", wg[:, kc, :], c.w_gate[li, :, kc, :], writes=[wg])
            p.dma("pool", wu[:, kc, :], c.w_up[li, :, kc, :], writes=[wu])
        for hc in range(HC):
            p.dma("pool", wd[:, hc, :], c.w_down[li, :, hc, :], writes=[wd])
        hTs = [p.sb(f"hT{i}", [128, KC, 512], F32) for i in range(2)]
        xn = p.sb("xn", [128, KC, 512], BF16)
        hh = p.sb("hh", [128, HC, 512], BF16)
        tmp = [p.sb(f"sq{i}", [128, 512], F32) for i in range(2)]
        rstd = p.sb("rstd", [128, 512], F32)
        sg = [p.sb(f"sg{i}", [128, 512], F32) for i in range(2)]
        ps_stat = p.ps("ps_stat", [128, 512], F32)
        psg = [p.ps(f"psg{i}", [128, 512], F32) for i in range(2)]
        psu = [p.ps(f"psu{i}", [128, 512], F32) for i in range(2)]
        pso = [p.ps(f"pso{i}", [128, 512], F32) for i in range(2)]
        it = 0
        io = 0
        for ti, (t0, tw) in enumerate(tiles_of(c.S)):
            hT = hTs[ti % 2]
            p.dma("sp", hT[:, :, :tw], c.H[:, :, t0:t0 + tw].rearrange("k p t -> p k t"), reads=[c.H], writes=[hT])
            rmsnorm_tile(p, c, hT, xn, tw, lambda kc: c.nffn[:, li, kc:kc + 1], ps_stat, tmp, rstd)
            for hc in range(HC):
                G, U, S_ = psg[it % 2], psu[it % 2], sg[it % 2]
                it += 1
                for kc in range(KC):
                    p.op("pe", lambda e: e.matmul(G[:, :tw], lhsT=wg[:, kc, hc * 128:(hc + 1) * 128], rhs=xn[:, kc, :tw],
                                                  start=(kc == 0), stop=(kc == KC - 1)), reads=[wg, xn], writes=[G], same_ok=True)
                for kc in range(KC):
                    p.op("pe", lambda e: e.matmul(U[:, :tw], lhsT=wu[:, kc, hc * 128:(hc + 1) * 128], rhs=xn[:, kc, :tw],
                                                  start=(kc == 0), stop=(kc == KC - 1)), reads=[wu, xn], writes=[U], same_ok=True)
                p.op("act", lambda e: e.activation(out=S_[:, :tw], in_=G[:, :tw], func=AF.Silu), reads=[G], writes=[S_])
                p.op("dve", lambda e: e.tensor_tensor(out=hh[:, hc, :tw], in0=S_[:, :tw], in1=U[:, :tw], op=ALU.mult),
                     reads=[S_, U], writes=[hh], same_ok=True)
            for oc in range(KC):
                O = pso[io % 2]
                io += 1
                for hc in range(HC):
                    p.op("pe", lambda e: e.matmul(O[:, :tw], lhsT=wd[:, hc, oc * 128:(oc + 1) * 128], rhs=hh[:, hc, :tw],
                                                  start=(hc == 0), stop=(hc == HC - 1)), reads=[wd, hh], writes=[O], same_ok=True)
                p.op("dve", lambda e: e.tensor_tensor(out=hT[:, oc, :tw], in0=hT[:, oc, :tw], in1=O[:, :tw], op=ALU.add),
                     reads=[O, hT], writes=[hT], same_ok=True)
            p.dma("pool", c.H[:, :, t0:t0 + tw].rearrange("k p t -> p k t"), hT[:, :, :tw], reads=[hT], writes=[c.H])


LN_EPS = 1e-5
import os
SGSTOP = int(os.environ.get('SGSTOP', '9'))
SGSUB = int(os.environ.get('SGSUB', '9'))
SGEV = os.environ.get('SGEV', 'dve')
DNSTOP = int(os.environ.get('DNSTOP', '9'))
NSSTOP = int(os.environ.get('NSSTOP', '9'))
DN2STOP = int(os.environ.get('DN2STOP', '9'))
MIXERS = (0, 1, 2)
GELU_C = 1.5957691216057308


def phase_sg(p, c, li, j):
    S = c.S
    with p.scope():
        win = p.sb("sgwin", [128, KC, 2048], BF16)
        wout = p.sb("sgwout", [128, KC, D], BF16)
        for kc in range(KC):
            p.dma("pool", win[:, kc, :], c.sg_w_in[j, :, kc, :], writes=[win])
            p.dma("pool", wout[:, kc, :], c.sg_w_out[j, :, kc, :], writes=[wout])
        lng = p.sb("lng", [128, KC], F32)
        lnb = p.sb("lnb", [128, KC], F32)
        p.dma("sp", lng[:, :], c.sg_ln_g[j], writes=[lng])
        p.dma("sp", lnb[:, :], c.sg_ln_b[j], writes=[lnb])
        wspf = p.sb("wspf", [128, KC, 128], F32)
        wsp = p.sb("wsp", [128, KC, 128], BF16)
        p.dma("sp", wspf[:, :, :], c.sg_wspT[j], writes=[wspf])
        for g in range(KC):
            p.op("pool", lambda e: e.affine_select(out=wspf[:, g, :], in_=wspf[:, g, :], pattern=[[1, 128]], compare_op=ALU.is_ge,
                                                   fill=0.0, base=0, channel_multiplier=-1), reads=[wspf], writes=[wspf], same_ok=True)
        p.op("dve", lambda e: e.tensor_copy(out=wsp[:, :, :], in_=wspf[:, :, :]), reads=[wspf], writes=[wsp])
        bsp = p.sb("bsp", [128, KC, 512], F32)
        p.dma("sp", bsp[:, :, :], c.sg_bsp4[j].partition_broadcast(128), writes=[bsp])
        wsm_f = p.sb("wsmf", [ST, KC, ST], F32)
        wsm = p.sb("wsm", [ST, KC, ST], BF16)
        p.op("dve", lambda e: e.memset(wsm_f[:, :, :], 0.0), writes=[wsm_f])
        for b in range(SB):
            p.dma("sp", wsm_f[b * SL:(b + 1) * SL, :, b * SL:(b + 1) * SL], wspf[0:SL, :, 0:SL], reads=[wspf], writes=[wsm_f])
        p.op("dve", lambda e: e.tensor_copy(out=wsm[:, :, :], in_=wsm_f[:, :, :]), reads=[wsm_f], writes=[wsm])
        bsm = p.sb("bsm", [128, KC, ST], F32)
        for b in range(SB):
            p.op("pool", lambda e: e.tensor_copy(out=bsm[:, :, b * SL:(b + 1) * SL], in_=bsp[:, :, 0:SL]), reads=[bsp], writes=[bsm], same_ok=True)
        eps_ln = p.sb("eps_ln", [128, 1], F32)
        p.op("dve", lambda e: e.memset(eps_ln[:, :], LN_EPS), writes=[eps_ln])

        hTs = [p.sb(f"hT{i}", [128, KC, 512], F32) for i in range(2)]
        xn = p.sb("xn", [128, KC, 512], BF16)
        uT = p.sb("uT", [128, KC, 512], BF16)
        vT = p.sb("vT", [128, KC, 512], F32)
        um = p.sb("um", [128, KC, 512], BF16)
        vtok = [p.sb(f"vtok{i}", [128, D], BF16) for i in range(4)]
        vtokf = p.sb("vtokf", [ST, D], F32)
        tmp = [p.sb(f"sq{i}", [128, 512], F32) for i in range(2)]
        t2 = [p.sb(f"t2{i}", [128, 512], F32) for i in range(2)]
        rstd = p.sb("rstd", [128, 512], F32)
        mean = p.sb("mean", [128, 512], F32)
        ps_stat = p.ps("ps_stat", [128, 512], F32)
        ps_s2 = p.ps("ps_s2", [128, 512], F32)
        psA = [p.ps(f"psA{i}", [128, 512], F32) for i in range(2)]
        psT = [p.ps(f"psT{i}", [128, 512], F32) for i in range(2)]
        psM = [p.ps(f"psM{i}", [128, 512], F32) for i in range(2)]
        ia = 0
        itr = 0
        im = 0
        for ti, (t0, tw) in enumerate(tiles_of(S)):
            smp = t0 >= S
            hT = hTs[ti % 2]
            p.dma("sp", hT[:, :, :tw], c.H[:, :, t0:t0 + tw].rearrange("k p t -> p k t"), reads=[c.H], writes=[hT])
            rmsnorm_tile(p, c, hT, xn, tw, lambda kc: c.nmix[:, li, kc:kc + 1], ps_stat, tmp, rstd)
            for fc in range(16):
                A = psA[ia % 2]
                x2, tt = tmp[ia % 2], t2[ia % 2]
                ia += 1
                for kc in range(KC):
                    p.op("pe", lambda e: e.matmul(A[:, :tw], lhsT=win[:, kc, fc * 128:(fc + 1) * 128], rhs=xn[:, kc, :tw],
                                                  start=(kc == 0), stop=(kc == KC - 1)), reads=[win, xn], writes=[A], same_ok=True)
                p.op("act", lambda e: e.activation(out=x2[:, :tw], in_=A[:, :tw], func=AF.Square), reads=[A], writes=[x2])
                p.op("dve", lambda e: e.tensor_scalar(out=x2[:, :tw], in0=x2[:, :tw], scalar1=0.044715, scalar2=1.0, op0=ALU.mult, op1=ALU.add),
                     reads=[x2], writes=[x2])
                p.op("dve", lambda e: e.tensor_tensor(out=x2[:, :tw], in0=x2[:, :tw], in1=A[:, :tw], op=ALU.mult), reads=[x2, A], writes=[x2])
                p.op("act", lambda e: e.activation(out=tt[:, :tw], in_=x2[:, :tw], func=AF.Sigmoid, scale=GELU_C), reads=[x2], writes=[tt])
                dst, dk = (uT, fc) if fc < 8 else (vT, fc - 8)
                p.op("dve", lambda e: e.tensor_tensor(out=dst[:, dk, :tw], in0=tt[:, :tw], in1=A[:, :tw], op=ALU.mult),
                     reads=[tt, A], writes=[dst], same_ok=True)
            for kc in range(KC if SGSTOP >= 2 else 0):
                sq = tmp[kc % 2]
                p.op("act", lambda e: e.activation(out=sq[:, :tw], in_=vT[:, kc, :tw], func=AF.Square), reads=[vT], writes=[sq])
                p.op("pe", lambda e: e.matmul(ps_s2[:, :tw], lhsT=c.onesD[:, :], rhs=sq[:, :tw], start=(kc == 0), stop=(kc == KC - 1)),
                     reads=[sq, c.onesD], writes=[ps_s2], same_ok=True)
            if SGSTOP < 2:
                p.dma("pool", c.H[:, :, t0:t0 + tw].rearrange("k p t -> p k t"), hT[:, :, :tw], reads=[hT], writes=[c.H])
                continue
            for kc in range(KC):
                p.op("pe", lambda e: e.matmul(ps_stat[:, :tw], lhsT=c.onesD[:, :], rhs=vT[:, kc, :tw], start=(kc == 0), stop=(kc == KC - 1)),
                     reads=[vT, c.onesD], writes=[ps_stat], same_ok=True)
            p.op("act", lambda e: e.copy(out=mean[:, :tw], in_=ps_stat[:, :tw]), reads=[ps_stat], writes=[mean])
            if SGSUB < 2:
                p.dma("pool", c.H[:, :, t0:t0 + tw].rearrange("k p t -> p k t"), hT[:, :, :tw], reads=[hT], writes=[c.H])
                continue
            m2 = tmp[0]
            p.op("dve", lambda e: e.tensor_tensor(out=m2[:, :tw], in0=mean[:, :tw], in1=mean[:, :tw], op=ALU.mult), reads=[mean], writes=[m2])
            p.op("dve", lambda e: e.tensor_tensor(out=m2[:, :tw], in0=ps_s2[:, :tw], in1=m2[:, :tw], op=ALU.subtract), reads=[ps_s2, m2], writes=[m2])
            if SGSUB < 3:
                p.dma("pool", c.H[:, :, t0:t0 + tw].rearrange("k p t -> p k t"), hT[:, :, :tw], reads=[hT], writes=[c.H])
                continue
            p.op("act", lambda e: e.activation(out=rstd[:, :tw], in_=m2[:, :tw], func=AF.Sqrt, bias=eps_ln[:, 0:1], scale=1.0),
                 reads=[m2, eps_ln], writes=[rstd])
            p.op("dve", lambda e: e.reciprocal(out=rstd[:, :tw], in_=rstd[:, :tw]), reads=[rstd], writes=[rstd])
            if SGSUB < 4:
                p.dma("pool", c.H[:, :, t0:t0 + tw].rearrange("k p t -> p k t"), hT[:, :, :tw], reads=[hT], writes=[c.H])
                continue
            for kc in range(KC):
                p.op("dve", lambda e: e.tensor_tensor(out=vT[:, kc, :tw], in0=vT[:, kc, :tw], in1=mean[:, :tw], op=ALU.subtract),
                     reads=[vT, mean], writes=[vT], same_ok=True)
                p.op("dve", lambda e: e.tensor_tensor(out=vT[:, kc, :tw], in0=vT[:, kc, :tw], in1=rstd[:, :tw], op=ALU.mult),
                     reads=[vT, rstd], writes=[vT], same_ok=True)
                p.op("dve", lambda e: e.tensor_scalar(out=vT[:, kc, :tw], in0=vT[:, kc, :tw], scalar1=lng[:, kc:kc + 1], scalar2=lnb[:, kc:kc + 1],
                                                      op0=ALU.mult, op1=ALU.add), reads=[vT, lng, lnb], writes=[vT], same_ok=True)
            if SGSTOP < 3:
                p.dma("pool", c.H[:, :, t0:t0 + tw].rearrange("k p t -> p k t"), hT[:, :, :tw], reads=[hT], writes=[c.H])
                continue
            nsub = (tw + 127) // 128
            for sj in range(nsub):
                r = min(128, tw - sj * 128)
                for half in range(2):
                    P = psT[itr % 2]
                    itr += 1
                    for g4 in range(4):
                        g = half * 4 + g4
                        p.op("pe", lambda e: e.transpose(P[:r, g4 * 128:(g4 + 1) * 128], vT[:, g, sj * 128:sj * 128 + r], c.ident[:, :]),
                             reads=[vT], writes=[P], same_ok=True)
                    eng = "act" if half else "dve"
                    if SGEV:
                        eng = SGEV
                    if SGSUB == 5:
                        continue
                    if smp:
                        p.op("dve", lambda e: e.tensor_copy(out=vtokf[:r, half * 512:(half + 1) * 512], in_=P[:r, :]),
                             reads=[P], writes=[vtokf], same_ok=True)
                    if SGSUB == 6:
                        continue
                    if eng == "dve":
                        p.op("dve", lambda e: e.tensor_copy(out=vtok[sj][:r, half * 512:(half + 1) * 512], in_=P[:r, :]),
                             reads=[P], writes=[vtok[sj]], same_ok=True)
                    else:
                        p.op("act", lambda e: e.copy(out=vtok[sj][:r, half * 512:(half + 1) * 512], in_=P[:r, :]),
                             reads=[P], writes=[vtok[sj]], same_ok=True)
            if smp and SGSUB > 7:
                p.dma("pool", c.o_sgv[j], vtokf[:, :], reads=[vtokf], writes=[c.o_sgv])
            if SGSTOP < 4:
                p.dma("pool", c.H[:, :, t0:t0 + tw].rearrange("k p t -> p k t"), hT[:, :, :tw], reads=[hT], writes=[c.H])
                continue
            for g in range(KC):
                M = psM[im % 2]
                tb = t2[im % 2]
                im += 1
                for sj in range(nsub):
                    r = min(128, tw - sj * 128)
                    rhs = wsm[:r, g, :r] if smp else wsp[:, g, :]
                    p.op("pe", lambda e: e.matmul(M[:, sj * 128:sj * 128 + r], lhsT=vtok[sj][:r, g * 128:(g + 1) * 128], rhs=rhs, start=True, stop=True),
                         reads=[vtok[sj], wsm if smp else wsp], writes=[M], same_ok=True)
                bias = bsm[:, g, :tw] if smp else bsp[:, g, :tw]
                p.op("dve", lambda e: e.tensor_tensor(out=tb[:, :tw], in0=M[:, :tw], in1=bias, op=ALU.add), reads=[M, bsm, bsp], writes=[tb])
                p.op("pool", lambda e: e.tensor_tensor(out=um[:, g, :tw], in0=tb[:, :tw], in1=uT[:, g, :tw], op=ALU.mult),
                     reads=[tb, uT], writes=[um])
            if SGSTOP < 5:
                p.dma("pool", c.H[:, :, t0:t0 + tw].rearrange("k p t -> p k t"), hT[:, :, :tw], reads=[hT], writes=[c.H])
                continue
            for oc in range(KC):
                O = psA[ia % 2]
                ia += 1
                for g in range(KC):
                    p.op("pe", lambda e: e.matmul(O[:, :tw], lhsT=wout[:, g, oc * 128:(oc + 1) * 128], rhs=um[:, g, :tw],
                                                  start=(g == 0), stop=(g == KC - 1)), reads=[wout, um], writes=[O], same_ok=True)
                p.op("dve", lambda e: e.tensor_tensor(out=hT[:, oc, :tw], in0=hT[:, oc, :tw], in1=O[:, :tw], op=ALU.add),
                     reads=[O, hT], writes=[hT], same_ok=True)
            p.dma("pool", c.H[:, :, t0:t0 + tw].rearrange("k p t -> p k t"), hT[:, :, :tw], reads=[hT], writes=[c.H])


DN_CC = 32
DN_W = 6208


def phase_dn1(p, c, li, j):
    S = c.S
    with p.scope():
        win = p.sb("dnwin", [128, KC, DN_W], BF16)
        for kc in range(KC):
            p.dma("pool", win[:, kc, :], c.dn_w_in[j, :, kc, :], writes=[win])
        cw = p.sb("cw", [128, DN_CC, 4], F32)
        p.dma("sp", cw[:, :, :], c.dn_cw[j], writes=[cw])
        ab = p.sb("ab", [64, 2], F32)
        p.dma("sp", ab[:, :], c.dn_ab[j], writes=[ab])
        nalog = p.sb("nalog", [64, 1], F32)
        p.op("act", lambda e: e.activation(out=nalog[:, :], in_=ab[:, 0:1], func=AF.Exp), reads=[ab], writes=[nalog])
        p.op("dve", lambda e: e.tensor_scalar(out=nalog[:, :], in0=nalog[:, :], scalar1=-1.0, scalar2=None, op0=ALU.mult),
             reads=[nalog], writes=[nalog])
        one_c = p.sb("one_c", [128, 1], F32)
        p.op("dve", lambda e: e.memset(one_c[:, :], 1.0), writes=[one_c])
        eps_c = p.sb("eps_c", [128, 1], F32)
        p.op("dve", lambda e: e.memset(eps_c[:, :], 1e-6), writes=[eps_c])
        carry = p.sb("carry", [128, DN_CC, 3], F32)
        p.op("dve", lambda e: e.memset(carry[:, :, :], 0.0), writes=[carry])
        cs_tok = p.sb("cs_tok", [SB * 3, 4096], F32)
        p.dma("sp", cs_tok[:, :], c.dn_cs_in[j], writes=[cs_tok])
        cs_fm = p.sb("cs_fm", [128, DN_CC, SB * 3], F32)
        newc = p.sb("newc", [128, DN_CC, SB * 3], F32)
        psA = [p.ps(f"psA{i}", [128, 512], F32) for i in range(3)]
        for cc in range(DN_CC):
            P = psA[cc % 2]
            p.op("pe", lambda e: e.transpose(P[:, 0:SB * 3], cs_tok[:, cc * 128:(cc + 1) * 128], c.ident[:SB * 3, :SB * 3]),
                 reads=[cs_tok], writes=[P])
            p.op("dve", lambda e: e.tensor_copy(out=cs_fm[:, cc, :], in_=P[:, 0:SB * 3]), reads=[P], writes=[cs_fm])

        hTs = [p.sb(f"hT{i}", [128, KC, 512], F32) for i in range(2)]
        xn = p.sb("xn", [128, KC, 512], BF16)
        tmp = [p.sb(f"sq{i}", [128, 512], F32) for i in range(2)]
        rstd = p.sb("rstd", [128, 512], F32)
        full = [p.sb(f"full{i}", [128, 520], F32) for i in range(3)]
        fulls = [p.sb(f"fulls{i}", [128, SB, 7], F32) for i in range(3)]
        acc = [p.sb(f"acc{i}", [128, 512], F32) for i in range(3)]
        cs = [p.sb(f"cs{i}", [128, 512], F32) for i in range(3)]
        rn = [p.sb(f"rn{i}", [128, 512], F32) for i in range(3)]
        zo = [p.sb(f"zo{i}", [128, 512], BF16) for i in range(3)]
        gt = p.sb("gt", [64, 512], F32)
        ps_stat = p.ps("ps_stat", [128, 512], F32)
        psN = [p.ps(f"psN{i}", [128, 512], F32) for i in range(3)]
        p.op("dve", lambda e: e.memset(gt[:, :], 0.0), writes=[gt])
        ia = 0
        for ti, (t0, tw) in enumerate(tiles_of(S)):
            smp = t0 >= S
            hT = hTs[ti % 2]
            if ti == 0:
                p.dma("sp", hT[:, :, :tw], c.H[:, :, t0:t0 + tw].rearrange("k p t -> p k t"), reads=[c.H], writes=[hT])
            rmsnorm_tile(p, c, hT, xn, tw, lambda kc: c.nmix[:, li, kc:kc + 1], ps_stat, tmp, rstd)
            if ti + 1 < len(tiles_of(S)):
                t0n, twn = tiles_of(S)[ti + 1]
                hTn = hTs[(ti + 1) % 2]
                p.dma("sp", hTn[:, :, :twn], c.H[:, :, t0n:t0n + twn].rearrange("k p t -> p k t"), reads=[c.H], writes=[hTn])
            for cc in range(DN_CC):
                A = psA[ia % 3]
                fu, ac, co, rr, NP = full[ia % 3], acc[ia % 3], cs[ia % 3], rn[ia % 3], psN[ia % 3]
                fs = fulls[ia % 3]
                ia += 1
                for kc in range(KC):
                    p.op("pe", lambda e: e.matmul(A[:, :tw], lhsT=win[:, kc, cc * 128:(cc + 1) * 128], rhs=xn[:, kc, :tw],
                                                  start=(kc == 0), stop=(kc == KC - 1)), reads=[win, xn], writes=[A], same_ok=True)
                if not smp:
                    p.op("act", lambda e: e.activation(out=fu[:, 3:3 + tw], in_=A[:, :tw], func=AF.Copy), reads=[A], writes=[fu])
                    p.op("dve", lambda e: e.tensor_copy(out=fu[:, 0:3], in_=carry[:, cc, :]), reads=[carry], writes=[fu])
                    p.op("dve", lambda e: e.tensor_copy(out=carry[:, cc, :], in_=fu[:, tw:tw + 3]), reads=[fu], writes=[carry])
                    taps = [fu[:, jj:jj + tw] for jj in range(4)]
                    accv, cov, rrv = ac[:, :tw], co[:, :tw], rr[:, :tw]
                    srcs = [fu]
                else:
                    p.op("dve", lambda e: e.tensor_copy(out=fs[:, :, 3:7], in_=A[:, :tw].rearrange("p (b t) -> p b t", b=SB)),
                         reads=[A], writes=[fs])
                    p.op("dve", lambda e: e.tensor_copy(out=fs[:, :, 0:3], in_=cs_fm[:, cc, :].rearrange("p (b t) -> p b t", b=SB)),
                         reads=[cs_fm], writes=[fs])
                    p.op("dve", lambda e: e.tensor_copy(out=newc[:, cc, :].rearrange("p (b t) -> p b t", b=SB), in_=fs[:, :, 4:7]),
                         reads=[fs], writes=[newc])
                    taps = [fs[:, :, jj:jj + SL] for jj in range(4)]
                    accv = ac[:, :tw].rearrange("p (b t) -> p b t", b=SB)
                    cov, rrv = co[:, :tw], rr[:, :tw]
                    srcs = [fs]
                p.op("dve", lambda e: e.tensor_scalar(out=accv, in0=taps[0], scalar1=cw[:, cc, 0:1], scalar2=None, op0=ALU.mult),
                     reads=srcs + [cw], writes=[ac])
                for jj in range(1, 4):
                    p.op("dve", lambda e: e.scalar_tensor_tensor(out=accv, in0=taps[jj], scalar=cw[:, cc, jj:jj + 1], in1=accv,
                                                                 op0=ALU.mult, op1=ALU.add), reads=srcs + [cw, ac], writes=[ac])
                p.op("act", lambda e: e.activation(out=cov, in_=ac[:, :tw], func=AF.Silu), reads=[ac], writes=[co])
                if cc < 16:
                    p.op("act", lambda e: e.activation(out=rrv, in_=cov, func=AF.Square), reads=[co], writes=[rr])
                    p.op("pe", lambda e: e.matmul(NP[:, :tw], lhsT=c.ones[:, :], rhs=rrv, start=True, stop=True),
                         reads=[rr, c.ones], writes=[NP])
                    p.op("act", lambda e: e.activation(out=rrv, in_=NP[:, :tw], func=AF.Sqrt, bias=eps_c[:, 0:1], scale=1.0),
                         reads=[NP, eps_c], writes=[rr])
                    p.op("dve", lambda e: e.reciprocal(out=rrv, in_=rrv), reads=[rr], writes=[rr])
                    sc = 128 ** -0.5 if cc < 8 else 1.0
                    p.op("dve", lambda e: e.scalar_tensor_tensor(out=cov, in0=cov, scalar=sc, in1=rrv, op0=ALU.mult, op1=ALU.mult),
                         reads=[co, rr], writes=[co])
                p.dma("sp", c.QKVT[cc, :, t0:t0 + tw], cov, reads=[co], writes=[c.QKVT])
            for zc in range(16):
                A = psA[ia % 3]
                Z = zo[ia % 3]
                ia += 1
                for kc in range(KC):
                    p.op("pe", lambda e: e.matmul(A[:, :tw], lhsT=win[:, kc, 4096 + zc * 128:4096 + (zc + 1) * 128], rhs=xn[:, kc, :tw],
                                                  start=(kc == 0), stop=(kc == KC - 1)), reads=[win, xn], writes=[A], same_ok=True)
                p.op("act", lambda e: e.activation(out=Z[:, :tw], in_=A[:, :tw], func=AF.Silu), reads=[A], writes=[Z])
                p.dma("sp", c.ZT[zc, :, t0:t0 + tw], Z[:, :tw], reads=[Z], writes=[c.ZT])
            A = psA[ia % 3]
            ia += 1
            for kc in range(KC):
                p.op("pe", lambda e: e.matmul(A[:64, :tw], lhsT=win[:, kc, 6144:6208], rhs=xn[:, kc, :tw],
                                              start=(kc == 0), stop=(kc == KC - 1)), reads=[win, xn], writes=[A], same_ok=True)
            p.op("act", lambda e: e.activation(out=gt[0:16, :tw], in_=A[0:16, :tw], func=AF.Sigmoid), reads=[A], writes=[gt])
            p.op("act", lambda e: e.activation(out=gt[32:48, :tw], in_=A[32:48, :tw], func=AF.Exp, bias=ab[32:48, 1:2], scale=1.0),
                 reads=[A, ab, gt], writes=[gt])
            p.op("act", lambda e: e.activation(out=gt[32:48, :tw], in_=gt[32:48, :tw], func=AF.Ln, bias=one_c[32:48, 0:1], scale=1.0),
                 reads=[gt, one_c], writes=[gt])
            p.op("dve", lambda e: e.tensor_scalar(out=gt[32:48, :tw], in0=gt[32:48, :tw], scalar1=nalog[32:48, 0:1], scalar2=None, op0=ALU.mult),
                 reads=[gt, nalog], writes=[gt])
            p.dma("sp", c.GT[:, t0:t0 + tw], gt[:, :tw], reads=[gt], writes=[c.GT])
        ctok = cs_tok
        for which in range(2):
            n = 3 if which == 0 else SB * 3
            for cc in range(DN_CC):
                P = psA[cc % 2]
                src = carry[:, cc, :] if which == 0 else newc[:, cc, :]
                p.op("pe", lambda e: e.transpose(P[:n, 0:128], src, c.ident[:, :]), reads=[carry, newc], writes=[P])
                p.op("dve", lambda e: e.tensor_copy(out=ctok[:n, cc * 128:(cc + 1) * 128], in_=P[:n, 0:128]), reads=[P], writes=[ctok])
            if which == 0:
                p.dma("sp", c.o_dc_p[j], ctok[:3, :], reads=[ctok], writes=[c.o_dc_p])
            else:
                p.dma("sp", c.o_dc_s[j].rearrange("b t c -> (b t) c"), ctok[:, :], reads=[ctok], writes=[c.o_dc_s])


DN_BIG = 30000.0
GRP = 4


def dn_consts(C, nseq):
    sl = C // nseq
    seq = np.arange(C) // sl
    same = seq[:, None] == seq[None, :]
    i = np.arange(C)
    f = np.float32
    triU = (same & (i[:, None] <= i[None, :])).astype(f)
    blk = same.astype(f)
    okA = same & (i[None, :] <= i[:, None])
    maskA = np.where(okA, 0.0, DN_BIG).astype(f)
    okB = same & (i[None, :] >= i[:, None])
    maskB = np.where(okB, 0.0, -DN_BIG).astype(f)
    strict = (same & (i[None, :] < i[:, None])).astype(f)
    seqones = np.zeros((C, nseq, 128), f)
    seqmask = np.zeros((C, nseq), f)
    colmask = np.zeros((128, nseq, C), f)
    for b in range(nseq):
        seqones[seq == b, b, :] = 1.0
        seqmask[seq == b, b] = 1.0
        colmask[:, b, seq == b] = 1.0
    pack = np.concatenate([triU, blk, maskA, maskB, strict, seqmask, seqones.reshape(C, nseq * 128)], 1)
    return np.ascontiguousarray(pack), np.ascontiguousarray(colmask)


def dn_pack_layout(C, nseq):
    off = {}
    o = 0
    for nm, w in (("triU", C), ("blk", C), ("maskA", C), ("maskB", C), ("strict", C), ("seqmask", nseq), ("seqones", nseq * 128)):
        off[nm] = (o, o + w)
        o += w
    return off, o


def phase_dn2(p, c, li, j):
    S = c.S
    with p.scope():
        wout = p.sb("dnwout", [128, 16, D], BF16)
        for h in range(16):
            p.dma("pool", wout[:, h, :], c.dn_w_out[j, :, h, :], writes=[wout])
        ng = p.sb("ng", [128, 1], F32)
        p.dma("sp", ng[:, :], c.dn_ng[j], writes=[ng])
        sel = p.sb("sel16", [16, 16, 128], F32)
        p.dma("sp", sel[:, :, :], c.sel16_d[:, :, :], writes=[sel])
        eps_c = p.sb("eps_c", [128, 1], F32)
        p.op("dve", lambda e: e.memset(eps_c[:, :], 1e-6), writes=[eps_c])
        onesDV = p.sb("onesDV", [128, 128], F32)
        p.op("dve", lambda e: e.memset(onesDV[:, :], 1.0 / 128), writes=[onesDV])
        geo = {}
        for key, C, nseq, pk_d, cm_d in (("p", 128, 1, c.dnc_p, c.dncm_p), ("s", ST, SB, c.dnc_s, c.dncm_s)):
            off, tot = dn_pack_layout(C, nseq)
            pk = p.sb(f"pk_{key}", [C, tot], F32)
            p.dma("sp", pk[:, :], pk_d[:, :], writes=[pk])
            cm = p.sb(f"cm_{key}", [128, nseq, C], F32)
            p.dma("sp", cm[:, :, :], cm_d[:, :, :], writes=[cm])
            geo[key] = (C, nseq, pk, off, cm)
        Sp = p.sb("Sp", [128, 16, 128], F32)
        Spb = p.sb("Spb", [128, 16, 128], BF16)
        Ss = p.sb("Ss", [128, SB * 16, 128], F32)
        Ssb = p.sb("Ssb", [128, SB * 16, 128], BF16)
        p.op("dve", lambda e: e.memset(Sp[:, :, :], 0.0), writes=[Sp])
        p.op("pool", lambda e: e.memset(Spb[:, :, :], 0.0), writes=[Spb])
        p.dma("sp", Ss[:, :, :], c.dn_S_in[j].rearrange("b h k v -> k (b h) v"), writes=[Ss])
        p.op("dve", lambda e: e.tensor_copy(out=Ssb[:, :, :], in_=Ss[:, :, :]), reads=[Ss], writes=[Ssb])

        gtc = p.sb("gtc", [64, 128], F32)
        bg = p.sb("bg", [128, 64], F32)
        gc = p.sb("gc", [128, 16], F32)
        ngc = p.sb("ngc", [128, 16], F32)
        gcT = p.sb("gcT", [16, 128], F32)
        glt = p.sb("glt", [128, 16], F32)
        egc = p.sb("egc", [128, 16], F32)
        bek = p.sb("bek", [128, 16], F32)
        kdc = p.sb("kdc", [128, 16], F32)
        glb = p.sb("glb", [128, SB, 16], F32)
        kT = [p.sb(f"kT{i}", [128, 128], F32) for i in range(2)]
        qT = [p.sb(f"qT{i}", [128, 128], F32) for i in range(2)]
        kT2 = [p.sb(f"kT2{i}", [128, 128], F32) for i in range(2)]
        KKs = [p.sb(f"KK{i}", [128, 128], F32) for i in range(2)]
        KQs = [p.sb(f"KQ{i}", [128, 128], F32) for i in range(2)]
        ktok = [p.sb(f"ktok{i}", [128, 128], F32) for i in range(2)]
        vT = [p.sb(f"vT{i}", [128, 128], F32) for i in range(GRP)]
        Dm = [p.sb(f"Dm{i}", [128, 128], F32) for i in range(GRP)]
        DTm = [p.sb(f"DTm{i}", [128, 128], F32) for i in range(GRP)]
        Eb = [p.sb(f"Eb{i}", [128, 128], F32) for i in range(GRP)]
        NM = [[p.sb(f"NM{i}_{k}", [128, 256], F32) for k in range(2)] for i in range(GRP)]
        X = [[p.sb(f"X{i}_{k}", [128, 256], F32) for k in range(2)] for i in range(GRP)]
        qgT = [p.sb(f"qgT{i}", [128, 128], BF16) for i in range(GRP)]
        qkT = [p.sb(f"qkT{i}", [128, 128], BF16) for i in range(GRP)]
        wT = [p.sb(f"wT{i}", [128, 128], BF16) for i in range(GRP)]
        kde = [p.sb(f"kde{i}", [128, 128], BF16) for i in range(GRP)]
        vnew = [p.sb(f"vnew{i}", [128, 128], BF16) for i in range(GRP)]
        wTb = [p.sb(f"wTb{i}", [128, SB, ST], BF16) for i in range(GRP)]
        qgTb = [p.sb(f"qgTb{i}", [128, SB, ST], BF16) for i in range(GRP)]
        kdeb = [p.sb(f"kdeb{i}", [ST, SB, 128], BF16) for i in range(GRP)]
        sqo = [p.sb(f"sqo{i}", [128, 128], F32) for i in range(GRP)]
        rso = [p.sb(f"rso{i}", [128, 128], F32) for i in range(GRP)]
        ont = p.sb("ont", [128, 16, 128], BF16)
        zT = p.sb("zT", [128, 16, 128], BF16)
        hch = p.sb("hch", [128, KC, 128], F32)
        PH = [p.ps(f"PH{i}", [128, 512], F32) for i in range(GRP)]
        PR = [p.ps(f"PR{i}", [128, 512], F32) for i in range(4)]

        chunks = [("p", t0) for t0 in range(0, S, 128)] + [("s", S)]
        for key, t0 in chunks:
            C, nseq, pk, off, cm = geo[key]
            smp = key == "s"
            St, Sb = (Ss, Ssb) if smp else (Sp, Spb)
            nst = 2 if smp else 7
            def pkv(nm, r0=0, r1=None):
                a, b = off[nm]
                return pk[:C, a:b]
            p.dma("sp", gtc[:, :C], c.GT[:, t0:t0 + C], reads=[c.GT], writes=[gtc])
            p.dma("sp", zT[:, :, :C], c.ZT[:, :, t0:t0 + C].rearrange("h p t -> p h t"), reads=[c.ZT], writes=[zT])
            p.dma("sp", hch[:, :, :C], c.H[:, :, t0:t0 + C].rearrange("k p t -> p k t"), reads=[c.H], writes=[hch])
            R0 = PR[0]
            p.op("pe", lambda e: e.transpose(R0[:C, 0:64], gtc[:, :C], c.ident[:64, :64]), reads=[gtc], writes=[R0])
            p.op("dve", lambda e: e.tensor_copy(out=bg[:C, :], in_=R0[:C, 0:64]), reads=[R0], writes=[bg])
            R1 = PR[1]
            p.op("pe", lambda e: e.matmul(R1[:C, 0:16], lhsT=pkv("triU"), rhs=bg[:C, 32:48], start=True, stop=True), reads=[pk, bg], writes=[R1])
            p.op("pe", lambda e: e.matmul(R1[:C, 16:32], lhsT=pkv("blk"), rhs=bg[:C, 32:48], start=True, stop=True), reads=[pk, bg], writes=[R1], same_ok=True)
            p.op("pe", lambda e: e.matmul(R1[:16, 64:64 + C], lhsT=bg[:C, 32:48], rhs=pkv("triU"), start=True, stop=True), reads=[pk, bg], writes=[R1], same_ok=True)
            a0, _ = off["seqones"]
            for b in range(nseq):
                p.op("pe", lambda e: e.matmul(R1[:, 256 + b * 16:256 + (b + 1) * 16], lhsT=pk[:C, a0 + b * 128:a0 + (b + 1) * 128], rhs=bg[:C, 32:48],
                                              start=True, stop=True), reads=[pk, bg], writes=[R1], same_ok=True)
            p.op("dve", lambda e: e.tensor_copy(out=gc[:C, :], in_=R1[:C, 0:16]), reads=[R1], writes=[gc])
            p.op("dve", lambda e: e.tensor_scalar(out=ngc[:C, :], in0=R1[:C, 0:16], scalar1=-1.0, scalar2=None, op0=ALU.mult), reads=[R1], writes=[ngc])
            p.op("dve", lambda e: e.tensor_tensor(out=glt[:C, :], in0=R1[:C, 16:32], in1=gc[:C, :], op=ALU.subtract), reads=[R1, gc], writes=[glt])
            p.op("dve", lambda e: e.tensor_copy(out=gcT[:, :C], in_=R1[:16, 64:64 + C]), reads=[R1], writes=[gcT])
            p.op("act", lambda e: e.activation(out=glb[:, :nseq, :], in_=R1[:, 256:256 + nseq * 16].rearrange("p (b h) -> p b h", b=nseq), func=AF.Exp),
                 reads=[R1], writes=[glb])
            p.op("act", lambda e: e.activation(out=egc[:C, :], in_=gc[:C, :], func=AF.Exp), reads=[gc], writes=[egc])
            p.op("act", lambda e: e.activation(out=kdc[:C, :], in_=glt[:C, :], func=AF.Exp), reads=[glt], writes=[kdc])
            p.op("dve", lambda e: e.tensor_tensor(out=bek[:C, :], in0=egc[:C, :], in1=bg[:C, 0:16], op=ALU.mult), reads=[egc, bg], writes=[bek])

            if DN2STOP < 2:
                continue
            for g0 in range(0, 16, GRP):
                heads = list(range(g0, g0 + GRP))
                for qi in range(2):
                    hk = g0 // 2 + qi
                    p.dma("sp", kT[qi][:, :C], c.QKVT[8 + hk, :, t0:t0 + C], reads=[c.QKVT], writes=[kT[qi]])
                    p.dma("sp", qT[qi][:, :C], c.QKVT[hk, :, t0:t0 + C], reads=[c.QKVT], writes=[qT[qi]])
                    p.dma("sp", kT2[qi][:, :C], c.QKVT[8 + hk, :, t0:t0 + C], reads=[c.QKVT], writes=[kT2[qi]])
                for gi, h in enumerate(heads):
                    p.dma("sp", vT[gi][:, :C], c.QKVT[16 + h, :, t0:t0 + C], reads=[c.QKVT], writes=[vT[gi]])
                for qi in range(2):
                    R = PR[2 + qi]
                    p.op("pe", lambda e: e.matmul(R[:C, 0:C], lhsT=kT[qi][:, :C], rhs=kT2[qi][:, :C], start=True, stop=True), reads=[kT[qi], kT2[qi]], writes=[R])
                    p.op("pe", lambda e: e.matmul(R[:C, 128:128 + C], lhsT=kT[qi][:, :C], rhs=qT[qi][:, :C], start=True, stop=True),
                         reads=[kT[qi], qT[qi]], writes=[R], same_ok=True)
                    p.op("pe", lambda e: e.transpose(R[:C, 256:384], kT[qi][:, :C], c.ident[:, :]), reads=[kT[qi]], writes=[R], same_ok=True)
                    p.op("dve", lambda e: e.tensor_tensor(out=KKs[qi][:C, :C], in0=R[:C, 0:C], in1=pkv("strict"), op=ALU.mult), reads=[R, pk], writes=[KKs[qi]])
                    p.op("dve", lambda e: e.tensor_copy(out=KQs[qi][:C, :C], in_=R[:C, 128:128 + C]), reads=[R], writes=[KQs[qi]])
                    p.op("dve", lambda e: e.tensor_copy(out=ktok[qi][:C, :], in_=R[:C, 256:384]), reads=[R], writes=[ktok[qi]])
                if DN2STOP < 3:
                    continue
                for gi, h in enumerate(heads):
                    qi = gi // 2
                    P = PH[gi]
                    p.op("pe", lambda e: e.transpose(P[:C, 0:128], vT[gi][:, :C], c.ident[:, :]), reads=[vT[gi]], writes=[P])
                    p.op("pe", lambda e: e.matmul(P[:, 128:128 + C], lhsT=sel[:, h, :], rhs=gcT[:, :C], start=True, stop=True),
                         reads=[sel, gcT], writes=[P], same_ok=True)
                    p.op("pe", lambda e: e.matmul(P[:C, 256:256 + C], lhsT=sel[:, h, :C], rhs=gcT[:, :C], start=True, stop=False),
                         reads=[sel, gcT], writes=[P], same_ok=True)
                    p.op("pe", lambda e: e.matmul(P[:C, 256:256 + C], lhsT=c.ident[:C, :C], rhs=pkv("maskA"), start=False, stop=True),
                         reads=[pk], writes=[P], same_ok=True)
                    p.op("pe", lambda e: e.matmul(P[:C, 384:384 + C], lhsT=sel[:, h, :C], rhs=gcT[:, :C], start=True, stop=False),
                         reads=[sel, gcT], writes=[P], same_ok=True)
                    p.op("pe", lambda e: e.matmul(P[:C, 384:384 + C], lhsT=c.ident[:C, :C], rhs=pkv("maskB"), start=False, stop=True),
                         reads=[pk], writes=[P], same_ok=True)
                    X0 = X[gi][0]
                    p.op("dve", lambda e: e.tensor_scalar(out=X0[:C, 0:128], in0=P[:C, 0:128], scalar1=bg[:C, h:h + 1], scalar2=None, op0=ALU.mult),
                         reads=[P, bg], writes=[X0])
                    p.op("act", lambda e: e.activation(out=X0[:C, 128:256], in_=ktok[qi][:C, :], func=AF.Copy, scale=bek[:C, h:h + 1]),
                         reads=[ktok[qi], bek], writes=[X0])
                    p.op("act", lambda e: e.activation(out=Eb[gi][:, :C], in_=P[:, 128:128 + C], func=AF.Exp), reads=[P], writes=[Eb[gi]])
                    p.op("act", lambda e: e.activation(out=Dm[gi][:C, :C], in_=P[:C, 256:256 + C], func=AF.Exp, bias=gc[:C, h:h + 1], scale=-1.0),
                         reads=[P, gc], writes=[Dm[gi]])
                    p.op("act", lambda e: e.activation(out=DTm[gi][:C, :C], in_=P[:C, 384:384 + C], func=AF.Exp, bias=ngc[:C, h:h + 1], scale=1.0),
                         reads=[P, ngc], writes=[DTm[gi]])
                    NM0 = NM[gi][0]
                    p.op("dve", lambda e: e.scalar_tensor_tensor(out=NM0[:C, 0:C], in0=KKs[qi][:C, :C], scalar=bg[:C, h:h + 1], in1=Dm[gi][:C, :C],
                                                                 op0=ALU.mult, op1=ALU.mult), reads=[KKs[qi], bg, Dm[gi]], writes=[NM0])
                    p.op("dve", lambda e: e.tensor_tensor(out=qkT[gi][:C, :C], in0=KQs[qi][:C, :C], in1=DTm[gi][:C, :C], op=ALU.mult),
                         reads=[KQs[qi], DTm[gi]], writes=[qkT[gi]])
                    p.op("dve", lambda e: e.tensor_tensor(out=qgT[gi][:, :C], in0=qT[qi][:, :C], in1=Eb[gi][:, :C], op=ALU.mult),
                         reads=[qT[qi], Eb[gi]], writes=[qgT[gi]])
                    p.op("act", lambda e: e.activation(out=kde[gi][:C, :], in_=ktok[qi][:C, :], func=AF.Copy, scale=kdc[:C, h:h + 1]),
                         reads=[ktok[qi], kdc], writes=[kde[gi]])
                if DN2STOP < 4:
                    continue
                for gi, h in enumerate(heads):
                    P = PH[gi]
                    p.op("pe", lambda e: e.transpose(P[:C, 0:C], NM[gi][0][:C, 0:C], c.ident[:C, :C]), reads=[NM[gi][0]], writes=[P])
                for gi, h in enumerate(heads):
                    P = PH[gi]
                    p.op("act", lambda e: e.activation(out=NM[gi][0][:C, 128:128 + C], in_=P[:C, 0:C], func=AF.Copy), reads=[P], writes=[NM[gi][0]])
                if DN2STOP < 5:
                    continue
                for k in range(nst):
                    cur, nxt = k % 2, (k + 1) % 2
                    last = k == nst - 1
                    for gi, h in enumerate(heads):
                        P = PH[gi]
                        NMk, Xk = NM[gi][cur], X[gi][cur]
                        p.op("pe", lambda e: e.matmul(P[:C, 256:512], lhsT=NMk[:C, 128:128 + C], rhs=Xk[:C, :], start=True, stop=True),
                             reads=[NMk, Xk], writes=[P])
                        if not last:
                            p.op("pe", lambda e: e.matmul(P[:C, 0:C], lhsT=NMk[:C, 128:128 + C], rhs=NMk[:C, 0:C], start=True, stop=True),
                                 reads=[NMk], writes=[P], same_ok=True)
                            p.op("pe", lambda e: e.matmul(P[:C, 128:128 + C], lhsT=NMk[:C, 0:C], rhs=NMk[:C, 128:128 + C], start=True, stop=True),
                                 reads=[NMk], writes=[P], same_ok=True)
                    for gi, h in enumerate(heads):
                        P = PH[gi]
                        Xk, Xn = X[gi][cur], X[gi][nxt]
                        p.op("dve", lambda e: e.tensor_tensor(out=Xn[:C, :], in0=Xk[:C, :], in1=P[:C, 256:512], op=(ALU.subtract if k == 0 else ALU.add)),
                             reads=[Xk, P], writes=[Xn])
                        if not last:
                            NMn = NM[gi][nxt]
                            if C == 128:
                                p.op("act", lambda e: e.activation(out=NMn[:C, :], in_=P[:C, 0:256], func=AF.Copy), reads=[P], writes=[NMn])
                            else:
                                p.op("act", lambda e: e.activation(out=NMn[:C, 0:C], in_=P[:C, 0:C], func=AF.Copy), reads=[P], writes=[NMn])
                                p.op("act", lambda e: e.activation(out=NMn[:C, 128:128 + C], in_=P[:C, 128:128 + C], func=AF.Copy), reads=[P], writes=[NMn])
                if DN2STOP < 6:
                    continue
                fin = nst % 2
                for gi, h in enumerate(heads):
                    P = PH[gi]
                    Xf = X[gi][fin]
                    p.op("pe", lambda e: e.transpose(P[:, 0:C], Xf[:C, 128:256], c.ident[:C, :C]), reads=[Xf], writes=[P])
                    p.op("act", lambda e: e.activation(out=wT[gi][:, :C], in_=P[:, 0:C], func=AF.Copy), reads=[P], writes=[wT[gi]])
                    if smp:
                        for b in range(nseq):
                            p.op("pool", lambda e: e.tensor_tensor(out=wTb[gi][:, b, :], in0=wT[gi][:, :C], in1=cm[:, b, :], op=ALU.mult),
                                 reads=[wT[gi], cm], writes=[wTb[gi]])
                            p.op("pool", lambda e: e.tensor_tensor(out=qgTb[gi][:, b, :], in0=qgT[gi][:, :C], in1=cm[:, b, :], op=ALU.mult),
                                 reads=[qgT[gi], cm], writes=[qgTb[gi]])
                            a1, _ = off["seqmask"]
                            p.op("pool", lambda e: e.tensor_scalar(out=kdeb[gi][:C, b, :], in0=kde[gi][:C, :], scalar1=pk[:C, a1 + b:a1 + b + 1], scalar2=None,
                                                                   op0=ALU.mult), reads=[kde[gi], pk], writes=[kdeb[gi]])
                for gi, h in enumerate(heads):
                    P = PH[gi]
                    for b in range(nseq):
                        lw = wTb[gi][:, b, :] if smp else wT[gi][:, :C]
                        p.op("pe", lambda e: e.matmul(P[:C, 128:256], lhsT=lw, rhs=Sb[:, b * 16 + h, :], start=(b == 0), stop=(b == nseq - 1)),
                             reads=[wTb[gi] if smp else wT[gi], Sb], writes=[P], same_ok=True)
                for gi, h in enumerate(heads):
                    P = PH[gi]
                    Xf = X[gi][fin]
                    p.op("dve", lambda e: e.tensor_tensor(out=vnew[gi][:C, :], in0=Xf[:C, 0:128], in1=P[:C, 128:256], op=ALU.subtract),
                         reads=[Xf, P], writes=[vnew[gi]])
                for gi, h in enumerate(heads):
                    P = PH[gi]
                    for b in range(nseq):
                        rq = qgTb[gi][:, b, :] if smp else qgT[gi][:, :C]
                        p.op("pe", lambda e: e.matmul(P[:, 256:256 + C], lhsT=Sb[:, b * 16 + h, :], rhs=rq, start=(b == 0), stop=False),
                             reads=[qgTb[gi] if smp else qgT[gi], Sb], writes=[P], same_ok=True)
                    p.op("pe", lambda e: e.matmul(P[:, 256:256 + C], lhsT=vnew[gi][:C, :], rhs=qkT[gi][:C, :C], start=False, stop=True),
                         reads=[vnew[gi], qkT[gi]], writes=[P], same_ok=True)
                    R = PR[gi % 2]
                    for b in range(nseq):
                        lk = kdeb[gi][:C, b, :] if smp else kde[gi][:C, :]
                        p.op("pe", lambda e: e.matmul(R[:, b * 128:(b + 1) * 128], lhsT=lk, rhs=vnew[gi][:C, :], start=True, stop=True),
                             reads=[kdeb[gi] if smp else kde[gi], vnew[gi]], writes=[R], same_ok=True)
                    for b in range(nseq):
                        si = b * 16 + h
                        p.op("dve", lambda e: e.scalar_tensor_tensor(out=St[:, si, :], in0=St[:, si, :], scalar=glb[:, b, h:h + 1], in1=R[:, b * 128:(b + 1) * 128],
                                                                     op0=ALU.mult, op1=ALU.add), reads=[St, glb, R], writes=[St])
                        p.op("act", lambda e: e.activation(out=Sb[:, si, :], in_=St[:, si, :], func=AF.Copy), reads=[St], writes=[Sb])
                if DN2STOP < 7:
                    continue
                for gi, h in enumerate(heads):
                    P = PH[gi]
                    p.op("act", lambda e: e.activation(out=sqo[gi][:, :C], in_=P[:, 256:256 + C], func=AF.Square), reads=[P], writes=[sqo[gi]])
                    p.op("pe", lambda e: e.matmul(P[:, 0:C], lhsT=onesDV[:, :], rhs=sqo[gi][:, :C], start=True, stop=True), reads=[onesDV, sqo[gi]], writes=[P], same_ok=True)
                for gi, h in enumerate(heads):
                    P = PH[gi]
                    p.op("act", lambda e: e.activation(out=rso[gi][:, :C], in_=P[:, 0:C], func=AF.Sqrt, bias=eps_c[:, 0:1], scale=1.0),
                         reads=[P, eps_c], writes=[rso[gi]])
                    p.op("dve", lambda e: e.reciprocal(out=rso[gi][:, :C], in_=rso[gi][:, :C]), reads=[rso[gi]], writes=[rso[gi]])
                    p.op("dve", lambda e: e.tensor_tensor(out=rso[gi][:, :C], in0=rso[gi][:, :C], in1=P[:, 256:256 + C], op=ALU.mult),
                         reads=[rso[gi], P], writes=[rso[gi]])
                    p.op("dve", lambda e: e.scalar_tensor_tensor(out=ont[:, h, :C], in0=rso[gi][:, :C], scalar=ng[:, 0:1], in1=zT[:, h, :C],
                                                                 op0=ALU.mult, op1=ALU.mult), reads=[rso[gi], ng, zT], writes=[ont])
            if DN2STOP < 8:
                continue
            for oc in range(KC):
                R = PR[oc % 4]
                for h in range(16):
                    p.op("pe", lambda e: e.matmul(R[:, 0:C], lhsT=wout[:, h, oc * 128:(oc + 1) * 128], rhs=ont[:, h, :C], start=(h == 0), stop=(h == 15)),
                         reads=[wout, ont], writes=[R], same_ok=True)
                p.op("dve", lambda e: e.tensor_tensor(out=hch[:, oc, :C], in0=hch[:, oc, :C], in1=R[:, 0:C], op=ALU.add), reads=[hch, R], writes=[hch])
            p.dma("pool", c.H[:, :, t0:t0 + C].rearrange("k p t -> p k t"), hch[:, :, :C], reads=[hch], writes=[c.H])
        p.dma("sp", c.o_dS_p[j].rearrange("h k v -> k h v"), Sp[:, :, :], reads=[Sp], writes=[c.o_dS_p])
        p.dma("sp", c.o_dS_s[j].rearrange("b h k v -> k (b h) v"), Ss[:, :, :], reads=[Ss], writes=[c.o_dS_s])


NSA_W = 2608
PAST = 8192


def phase_nsa1(p, c, li, j):
    S = c.S
    with p.scope():
        win = p.sb("nswin", [128, KC, NSA_W], BF16)
        for kc in range(KC):
            p.dma("pool", win[:, kc, :], c.ns_w_in[j, :, kc, :], writes=[win])
        gains = p.sb("nsg", [128, 4], F32)
        p.dma("sp", gains[:, :], c.ns_gains[j], writes=[gains])
        rotT = p.sb("rotT", [128, 128], F32)
        p.dma("sp", rotT[:, :], c.ns_rotT[:, :], writes=[rotT])
        blk64 = p.sb("blk64", [128, 128], F32)
        p.op("dve", lambda e: e.memset(blk64[:, :], 0.0), writes=[blk64])
        p.op("dve", lambda e: e.memset(blk64[0:64, 0:64], 1.0 / 64), writes=[blk64])
        p.op("dve", lambda e: e.memset(blk64[64:128, 64:128], 1.0 / 64), writes=[blk64])
        eps_c = p.sb("eps_c", [128, 1], F32)
        p.op("dve", lambda e: e.memset(eps_c[:, :], NORM_EPS), writes=[eps_c])

        hTs = [p.sb(f"hT{i}", [128, KC, 512], F32) for i in range(2)]
        xn = p.sb("xn", [128, KC, 512], BF16)
        tmp = [p.sb(f"sq{i}", [128, 512], F32) for i in range(2)]
        rstd = p.sb("rstd", [128, 512], F32)
        cosT = p.sb("cosT", [128, 512], F32)
        sinT = p.sb("sinT", [128, 512], F32)
        xq = [p.sb(f"xq{i}", [128, 512], F32) for i in range(2)]
        xr = [p.sb(f"xr{i}", [128, 512], F32) for i in range(2)]
        rs = [p.sb(f"rs{i}", [128, 512], F32) for i in range(2)]
        ob = [p.sb(f"ob{i}", [128, 512], BF16) for i in range(2)]
        ob2 = [p.sb(f"ob2{i}", [128, 512], BF16) for i in range(2)]
        gt = p.sb("gt", [48, 512], F32)
        tokf = [p.sb(f"tokf{i}", [128, 1536], F32) for i in range(2)]
        tokb = [p.sb(f"tokb{i}", [128, 512], BF16) for i in range(2)]
        kfm = p.sb("kfm", [128, 4, 512], F32)
        ps_stat = p.ps("ps_stat", [128, 512], F32)
        psA = [p.ps(f"psA{i}", [128, 512], F32) for i in range(2)]
        psB = [p.ps(f"psB{i}", [128, 512], F32) for i in range(2)]
        psT = [p.ps(f"psT{i}", [128, 512], F32) for i in range(3)]
        ia = 0
        it = 0
        W = min(512, S)
        for ti, (t0, tw) in enumerate(tiles_of(S)):
            smp = t0 >= S
            hT = hTs[ti % 2]
            if ti == 0:
                p.dma("sp", hT[:, :, :tw], c.H[:, :, t0:t0 + tw].rearrange("k p t -> p k t"), reads=[c.H], writes=[hT])
            p.dma("sp", cosT[:, :tw], c.ns_cos[:, t0:t0 + tw], writes=[cosT])
            p.dma("sp", sinT[:, :tw], c.ns_sin[:, t0:t0 + tw], writes=[sinT])
            rmsnorm_tile(p, c, hT, xn, tw, lambda kc: c.nmix[:, li, kc:kc + 1], ps_stat, tmp, rstd)
            if ti + 1 < len(tiles_of(S)):
                t0n, twn = tiles_of(S)[ti + 1]
                hTn = hTs[(ti + 1) % 2]
                p.dma("sp", hTn[:, :, :twn], c.H[:, :, t0n:t0n + twn].rearrange("k p t -> p k t"), reads=[c.H], writes=[hTn])
            for ch in list(range(0, 14)) + [16, 17]:
                A, B = psA[ia % 2], psB[ia % 2]
                x_, xr_, r_, o_, o2_ = xq[ia % 2], xr[ia % 2], rs[ia % 2], ob[ia % 2], ob2[ia % 2]
                ia += 1
                for kc in range(KC):
                    p.op("pe", lambda e: e.matmul(A[:, :tw], lhsT=win[:, kc, ch * 128:(ch + 1) * 128], rhs=xn[:, kc, :tw],
                                                  start=(kc == 0), stop=(kc == KC - 1)), reads=[win, xn], writes=[A], same_ok=True)
                if ch in (8, 9, 10, 11):
                    p.op("dve", lambda e: e.tensor_copy(out=o_[:, :tw], in_=A[:, :tw]), reads=[A], writes=[o_])
                    dst = c.KCT if ch < 10 else c.VCT
                    p.dma("sp", dst[ch % 2, :, t0:t0 + tw], o_[:, :tw], reads=[o_], writes=[dst])
                    continue
                gcol = 0 if ch < 8 else (2 if ch < 14 else 3)
                p.op("act", lambda e: e.activation(out=r_[:, :tw], in_=A[:, :tw], func=AF.Square), reads=[A], writes=[r_])
                p.op("pe", lambda e: e.matmul(B[:, :tw], lhsT=blk64[:, :], rhs=r_[:, :tw], start=True, stop=True), reads=[blk64, r_], writes=[B])
                p.op("act", lambda e: e.activation(out=r_[:, :tw], in_=B[:, :tw], func=AF.Sqrt, bias=eps_c[:, 0:1], scale=1.0),
                     reads=[B, eps_c], writes=[r_])
                p.op("dve", lambda e: e.reciprocal(out=r_[:, :tw], in_=r_[:, :tw]), reads=[r_], writes=[r_])
                p.op("dve", lambda e: e.scalar_tensor_tensor(out=x_[:, :tw], in0=A[:, :tw], scalar=gains[:, gcol:gcol + 1], in1=r_[:, :tw],
                                                             op0=ALU.mult, op1=ALU.mult), reads=[A, gains, r_], writes=[x_])
                if ch < 8:
                    p.op("pool", lambda e: e.tensor_copy(out=o_[:, :tw], in_=x_[:, :tw]), reads=[x_], writes=[o_])
                    p.dma("sp", c.QT[ch, :, t0:t0 + tw], o_[:, :tw], reads=[o_], writes=[c.QT])
                p.op("pe", lambda e: e.matmul(B[:, :tw], lhsT=rotT[:, :], rhs=x_[:, :tw], start=True, stop=True), reads=[rotT, x_], writes=[B])
                p.op("dve", lambda e: e.tensor_tensor(out=xr_[:, :tw], in0=B[:, :tw], in1=sinT[:, :tw], op=ALU.mult), reads=[B, sinT], writes=[xr_])
                p.op("pool", lambda e: e.tensor_tensor(out=x_[:, :tw], in0=x_[:, :tw], in1=cosT[:, :tw], op=ALU.mult), reads=[x_, cosT], writes=[x_])
                if ch < 8:
                    p.op("dve", lambda e: e.tensor_tensor(out=o2_[:, :tw], in0=x_[:, :tw], in1=xr_[:, :tw], op=ALU.add), reads=[x_, xr_], writes=[o2_])
                    p.dma("sp", c.QRT[ch, :, t0:t0 + tw], o2_[:, :tw], reads=[o2_], writes=[c.QRT])
                else:
                    ki = (ch - 12) if ch < 14 else (ch - 14)
                    p.op("dve", lambda e: e.tensor_tensor(out=kfm[:, ki, :tw], in0=x_[:, :tw], in1=xr_[:, :tw], op=ALU.add), reads=[x_, xr_], writes=[kfm])
                    p.op("pool", lambda e: e.tensor_copy(out=o2_[:, :tw], in_=kfm[:, ki, :tw]), reads=[kfm], writes=[o2_])
                    dst = c.KST if ch < 14 else c.KWT
                    p.dma("sp", dst[ch % 2, :, t0:t0 + tw], o2_[:, :tw], reads=[o2_], writes=[dst])
            A = psA[ia % 2]
            ia += 1
            for kc in range(KC):
                p.op("pe", lambda e: e.matmul(A[:48, :tw], lhsT=win[:, kc, 2560:2608], rhs=xn[:, kc, :tw],
                                              start=(kc == 0), stop=(kc == KC - 1)), reads=[win, xn], writes=[A], same_ok=True)
            p.op("act", lambda e: e.activation(out=gt[:, :tw], in_=A[:48, :tw], func=AF.Sigmoid), reads=[A], writes=[gt])
            p.dma("sp", c.GTn[:, t0:t0 + tw], gt[:, :tw], reads=[gt], writes=[c.GTn])
            nsub = (tw + 127) // 128
            for sj in range(nsub):
                r = min(128, tw - sj * 128)
                tf = tokf[sj % 2]
                for cb in range(3):
                    Pt = psT[cb]
                    for kc in range(KC):
                        p.op("pe", lambda e: e.matmul(Pt[:r, :], lhsT=xn[:, kc, sj * 128:sj * 128 + r], rhs=win[:, kc, 1024 + cb * 512:1536 + cb * 512],
                                                      start=(kc == 0), stop=(kc == KC - 1)), reads=[win, xn], writes=[Pt], same_ok=True)
                    p.op("dve", lambda e: e.tensor_copy(out=tf[:r, cb * 512:(cb + 1) * 512], in_=Pt[:r, :]), reads=[Pt], writes=[tf])
                for ki in range(4):
                    Pt = psT[ki % 3]
                    p.op("pe", lambda e: e.transpose(Pt[:r, 0:128], kfm[:, ki, sj * 128:sj * 128 + r], c.ident[:, :]), reads=[kfm], writes=[Pt])
                    col = (512 if ki < 2 else 1024) + (ki % 2) * 128
                    p.op("dve", lambda e: e.tensor_copy(out=tf[:r, col:col + 128], in_=Pt[:r, 0:128]), reads=[Pt], writes=[tf])
                tb = tokb[sj % 2]
                p.op("pool", lambda e: e.tensor_copy(out=tb[:r, 0:256], in_=tf[:r, 768:1024]), reads=[tf], writes=[tb])
                p.op("pool", lambda e: e.tensor_copy(out=tb[:r, 256:512], in_=tf[:r, 1280:1536]), reads=[tf], writes=[tb])
                r0 = t0 + sj * 128
                p.dma("sp", c.VS[r0:r0 + r, :], tb[:r, 0:256], reads=[tb], writes=[c.VS])
                p.dma("sp", c.VW[r0:r0 + r, :], tb[:r, 256:512], reads=[tb], writes=[c.VW])
                if not smp:
                    for oi in range(4):
                        p.dma("sp", c.o_new_p[oi, r0:r0 + r, :], tf[:r, oi * 256:(oi + 1) * 256], reads=[tf], writes=[c.o_new_p])
                    if r0 >= S - W:
                        for oi in range(2):
                            p.dma("sp", c.o_swa_p[oi, r0 - (S - W):r0 - (S - W) + r, :], tf[:r, 1024 + oi * 256:1280 + oi * 256],
                                  reads=[tf], writes=[c.o_swa_p])
                else:
                    for oi in range(4):
                        p.dma("sp", c.o_new_s[oi, :, :], tf[:r, oi * 256:(oi + 1) * 256], reads=[tf], writes=[c.o_new_s])
                    for b in range(SB):
                        for oi in range(2):
                            p.dma("sp", c.o_swa_s[oi, b, 512 - SL:512, :], tf[b * SL:(b + 1) * SL, 1024 + oi * 256:1280 + oi * 256],
                                  reads=[tf], writes=[c.o_swa_s])
        for oi, src in enumerate((c.ns_swa_k, c.ns_swa_v)):
            for b in range(SB):
                p.dma("sp", c.o_swa_s[oi, b, 0:512 - SL, :], src[j, b, SL:512, :], writes=[c.o_swa_s])


NEG = -1.0e9
BONUS = 1.0e4


def nsa_consts():
    f = np.float32
    ql = np.arange(128)[:, None]
    x = np.arange(256)[None, :]
    jp = x - 128
    W = np.zeros((128, 256), f)
    W = np.where(jp > 1, NEG, W)
    W = np.where((jp == 1) & (ql >= 64), BONUS, W)
    W = np.where((jp == 1) & (ql < 64), NEG, W)
    W = np.where(jp == 0, BONUS, W)
    W = np.where((jp == -1) & (ql < 64), BONUS, W)
    n = np.arange(512)[:, None]
    jj = np.arange(128)[None, :]
    ov = np.minimum(16 * n + 32, 64 * jj + 64) - np.maximum(16 * n, 64 * jj)
    c2s = (np.clip(ov, 0, None) / 32.0).astype(f)
    c2s[511] = 0.0
    c2s = np.ascontiguousarray(c2s.reshape(4, 128, 128).transpose(1, 0, 2))
    sel48 = np.zeros((48, 48, 64), f)
    for r in range(48):
        sel48[r, r, :] = 1.0
    return {"ns_wtab": W.astype(f), "ns_c2s": c2s, "ns_sel48": sel48}


def nsa_seq(p, c, j, env, sq):
    T = sq["T"]
    NTL = T // 128
    ncmp = sq["ncmp"]
    scale = 0.125
    (wck, wcv, pekT, pevT, g0, ones64, onesb, identb, wtab, c2s, sel48, tiny) = env["consts"]
    (kcs, vcs, ksT, kwT, vs, vw, kcmpT, vcmp, Qgs, QRgs, Ggs, Pc, Pn, E, Pm, sc, sc2, m8, selb, selx, rden, gb, oacc, ob, sqk, rsk, pet, pvr) = env["tiles"]
    (PS_S, PS_T, PS_DEN, PS_O, PS_IMP, PS_X0) = env["psum"]
    PS_X = [PS_X0, PS_IMP]
    for hk in range(4):
        ch, p0 = hk // 2, (hk % 2) * 64
        p.dma("sp", kcs[:, :T], sq["KCT"][ch, p0:p0 + 64, 0:T], reads=[sq["KCT_t"]], writes=[kcs])
        p.dma("sp", vcs[:, :T], sq["VCT"][ch, p0:p0 + 64, 0:T], reads=[sq["VCT_t"]], writes=[vcs])
        p.dma("sp", ksT[:, :T], sq["KST"][ch, p0:p0 + 64, 0:T], reads=[sq["KST_t"]], writes=[ksT])
        kw0 = sq.get("kw0", 0)
        p.dma("sp", kwT[:, kw0:T], sq["KWT"][ch, p0:p0 + 64, kw0:T], reads=[sq["KWT_t"]], writes=[kwT])
        p.dma("sp", vs[:, :NTL, :], sq["VS"][0:T, hk * 64:(hk + 1) * 64].rearrange("(n p) d -> p n d", p=128), reads=[sq["VS_t"]], writes=[vs])
        p.dma("sp", vw[:, kw0 // 128:NTL, :], sq["VW"][kw0:T, hk * 64:(hk + 1) * 64].rearrange("(n p) d -> p n d", p=128), reads=[sq["VW_t"]], writes=[vw])
        X0 = PS_X[0]
        kv = kcs[:, 0:16 * (ncmp + 1)].rearrange("p (n s) -> p n s", s=16)
        for l in range(32):
            rhs = kv[:, 0:ncmp, l] if l < 16 else kv[:, 1:ncmp + 1, l - 16]
            p.op("pe", lambda e: e.matmul(X0[:64, :ncmp], lhsT=wck[0:64, l, :], rhs=rhs, start=(l == 0), stop=(l == 31)),
                 reads=[wck, kcs], writes=[X0], same_ok=True)
        p.op("dve", lambda e: e.tensor_scalar(out=sqk[:, :ncmp], in0=X0[:64, :ncmp], scalar1=pet[:, 0:1], scalar2=None, op0=ALU.add),
             reads=[X0, pet], writes=[sqk])
        p.op("act", lambda e: e.activation(out=rsk[:, :ncmp], in_=sqk[:, :ncmp], func=AF.Square), reads=[sqk], writes=[rsk])
        X1 = PS_X[1]
        p.op("pe", lambda e: e.matmul(X1[:64, :ncmp], lhsT=ones64[:, :], rhs=rsk[:, :ncmp], start=True, stop=True), reads=[ones64, rsk], writes=[X1])
        p.op("act", lambda e: e.activation(out=rsk[:, :ncmp], in_=X1[:64, :ncmp], func=AF.Sqrt, bias=tiny[:64, 1:2], scale=1.0),
             reads=[X1, tiny], writes=[rsk])
        p.op("dve", lambda e: e.reciprocal(out=rsk[:, :ncmp], in_=rsk[:, :ncmp]), reads=[rsk], writes=[rsk])
        p.op("dve", lambda e: e.memset(kcmpT[:, :], 0.0), writes=[kcmpT])
        p.op("dve", lambda e: e.scalar_tensor_tensor(out=kcmpT[:, :ncmp], in0=sqk[:, :ncmp], scalar=g0[:, 0:1], in1=rsk[:, :ncmp], op0=ALU.mult, op1=ALU.mult),
             reads=[sqk, g0, rsk], writes=[kcmpT])
        vv = vcs[:, 0:16 * (ncmp + 1)].rearrange("p (n s) -> p n s", s=16)
        nct = (ncmp + 127) // 128
        p.op("pool", lambda e: e.memset(vcmp[:, :, :], 0.0), writes=[vcmp])
        for nt in range(nct):
            n0 = nt * 128
            nn = min(128, ncmp - n0)
            Xv = PS_X[nt % 2]
            for l in range(32):
                lhsT = vv[:, n0:n0 + nn, l] if l < 16 else vv[:, n0 + 1:n0 + nn + 1, l - 16]
                p.op("pe", lambda e: e.matmul(Xv[:nn, 0:64], lhsT=lhsT, rhs=wcv[0:64, l, :], start=(l == 0), stop=False),
                     reads=[wcv, vcs], writes=[Xv], same_ok=True)
            p.op("pe", lambda e: e.matmul(Xv[:nn, 0:64], lhsT=onesb[0:1, :nn], rhs=pvr[0:1, :], start=False, stop=True),
                 reads=[onesb, pvr], writes=[Xv], same_ok=True)
            p.op("dve", lambda e: e.tensor_copy(out=vcmp[:nn, nt, :], in_=Xv[:nn, 0:64]), reads=[Xv], writes=[vcmp])
        for blk_i, (bi, qc0, o0) in enumerate(sq["blocks"]):
            Qg, QRg, Gg = Qgs[blk_i % 2], QRgs[blk_i % 2], Ggs[blk_i % 2]
            for g in range(4):
                qh = 4 * hk + g
                p.dma("sp", Qg[:, g, :], sq["QT"][qh // 2, (qh % 2) * 64:(qh % 2) * 64 + 64, qc0:qc0 + 128], reads=[sq["QT_t"]], writes=[Qg])
                p.dma("sp", QRg[:, g, :], sq["QRT"][qh // 2, (qh % 2) * 64:(qh % 2) * 64 + 64, qc0:qc0 + 128], reads=[sq["QRT_t"]], writes=[QRg])
            p.dma("sp", Gg[:, :], sq["GT"][:, qc0:qc0 + 128], reads=[sq["GT_t"]], writes=[Gg])
            Qf = Qg[:, :, :].rearrange("p g q -> p (g q)")
            QRf = QRg[:, :, :].rearrange("p g q -> p (g q)")
            ncl = min(nct, (8 * bi + 6) // 128 + 1)
            for nt in range(ncl):
                Sp = PS_S[nt % 2]
                p.op("pe", lambda e: e.matmul(Sp[:, :], lhsT=kcmpT[:, nt * 128:(nt + 1) * 128], rhs=Qf, start=True, stop=True),
                     reads=[kcmpT, Qg], writes=[Sp])
                p.op("act", lambda e: e.activation(out=Pc[:, nt, :], in_=Sp[:, :], func=AF.Exp, scale=scale), reads=[Sp], writes=[Pc])
                p.op("pool", lambda e: e.affine_select(out=Pc[:, nt, :].rearrange("p (g q) -> p g q", g=4), in_=Pc[:, nt, :].rearrange("p (g q) -> p g q", g=4),
                                                       pattern=[[0, 4], [1, 128]], compare_op=ALU.is_ge, fill=0.0,
                                                       base=128 * bi - 31 - 2048 * nt, channel_multiplier=-16), reads=[Pc], writes=[Pc])
                p.op("pe", lambda e: e.matmul(PS_DEN[:, :], lhsT=onesb[:, :], rhs=Pc[:, nt, :], start=(nt == 0), stop=(nt == ncl - 1)),
                     reads=[onesb, Pc], writes=[PS_DEN], same_ok=True)
            p.op("dve", lambda e: e.tensor_scalar(out=rden[:, :], in0=PS_DEN[:, :], scalar1=1e-30, scalar2=None, op0=ALU.max), reads=[PS_DEN], writes=[rden])
            p.op("dve", lambda e: e.reciprocal(out=rden[:, :], in_=rden[:, :]), reads=[rden], writes=[rden])
            for nt in range(ncl):
                p.op("dve", lambda e: e.tensor_tensor(out=Pn[:, nt, :], in0=Pc[:, nt, :], in1=rden[:, :], op=ALU.mult), reads=[Pc, rden], writes=[Pn])
            for nt in range(ncl):
                p.op("pe", lambda e: e.matmul(PS_O[:64, :], lhsT=vcmp[:, nt, :], rhs=Pn[:, nt, :], start=(nt == 0), stop=(nt == ncl - 1)),
                     reads=[vcmp, Pn], writes=[PS_O], same_ok=True)
            k = 0
            for nt in range(ncl):
                for g in range(4):
                    p.op("pe", lambda e: e.matmul(PS_IMP[:, 0:128], lhsT=Pn[:, nt, g * 128:(g + 1) * 128], rhs=c2s[:, nt, :], start=(k == 0), stop=(k == 4 * ncl - 1)),
                         reads=[Pn, c2s], writes=[PS_IMP], same_ok=True)
                    k += 1
            p.op("dve", lambda e: e.memset(sc[:, 128:136], NEG), writes=[sc])
            p.op("dve", lambda e: e.tensor_tensor(out=sc[:, 0:128], in0=PS_IMP[:, 0:128], in1=wtab[:, 128 - 2 * bi:256 - 2 * bi], op=ALU.add),
                 reads=[PS_IMP, wtab], writes=[sc])
            if bi == 64:
                p.op("dve", lambda e: e.tensor_copy(out=sc[:, 128:130], in_=wtab[:, 128:130]), reads=[wtab], writes=[sc])
            for br in range(3):
                Xg = PS_X0
                for g in range(4):
                    r = (4 * hk + g) * 3 + br
                    p.op("pe", lambda e: e.matmul(Xg[:64, g * 128:(g + 1) * 128], lhsT=sel48[:, r, :], rhs=Gg[:, :], start=True, stop=True),
                         reads=[sel48, Gg], writes=[Xg], same_ok=True)
                p.op("dve", lambda e: e.tensor_copy(out=gb[:, br, :], in_=Xg[:64, 0:512]), reads=[Xg], writes=[gb])
            p.op("dve", lambda e: e.tensor_tensor(out=oacc[:, :], in0=PS_O[:64, :], in1=gb[:, 0, :], op=ALU.mult), reads=[PS_O, gb], writes=[oacc])
            if bi >= 1:
                p.op("dve", lambda e: e.tensor_scalar(out=sc[:, 0:1], in0=sc[:, 0:1], scalar1=BONUS, scalar2=None, op0=ALU.add), reads=[sc], writes=[sc])
            p.op("dve", lambda e: e.max(out=m8[:, :], in_=sc[:, :]), reads=[sc], writes=[m8])
            p.op("dve", lambda e: e.match_replace(out=sc2[:, :], in_to_replace=m8[:, :], in_values=sc[:, :], imm_value=-3.0e9), reads=[sc, m8], writes=[sc2])
            p.op("dve", lambda e: e.max(out=m8[:, :], in_=sc2[:, :]), reads=[sc2], writes=[m8])
            p.op("dve", lambda e: e.match_replace(out=sc2[:, :], in_to_replace=m8[:, :], in_values=sc2[:, :], imm_value=-3.0e9), reads=[sc2, m8], writes=[sc2])
            p.op("dve", lambda e: e.tensor_tensor(out=sc[:, :], in0=sc[:, :], in1=sc2[:, :], op=ALU.subtract), reads=[sc, sc2], writes=[sc])
            p.op("dve", lambda e: e.tensor_scalar(out=selb[:, :], in0=sc[:, :], scalar1=1.0, scalar2=None, op0=ALU.min), reads=[sc], writes=[selb])
            nb = 2 * (bi + 1)
            p.op("dve", lambda e: e.tensor_copy(out=selx[:, 0:nb * 64].rearrange("p (j k) -> p j k", k=64),
                                                in_=selb[:, 0:nb].unsqueeze(2).to_broadcast([128, nb, 64])), reads=[selb], writes=[selx])
            def slc_front(kt):
                Sp, Tp = PS_S[kt % 2], PS_T[kt % 2]
                p.op("pe", lambda e: e.matmul(Sp[:, :], lhsT=ksT[:, kt * 128:(kt + 1) * 128], rhs=QRf, start=True, stop=True), reads=[ksT, QRg], writes=[Sp])
                p.op("pe", lambda e: e.transpose(Tp[:, 0:128], selx[:, kt * 128:(kt + 1) * 128], identb[:, :]), reads=[selx, identb], writes=[Tp])

            def slc_back(kt):
                Sp, Tp = PS_S[kt % 2], PS_T[kt % 2]
                Ek, Pk = E[kt % 2], Pm[kt % 2]
                p.op("act", lambda e: e.activation(out=Ek[:, :], in_=Sp[:, :], func=AF.Exp, scale=scale), reads=[Sp], writes=[Ek])
                p.op("dve", lambda e: e.tensor_tensor(out=Pk[:, :].rearrange("p (g q) -> p g q", g=4), in0=Ek[:, :].rearrange("p (g q) -> p g q", g=4),
                                                      in1=Tp[:, 0:128].unsqueeze(1).to_broadcast([128, 4, 128]), op=ALU.mult), reads=[Ek, Tp], writes=[Pk])
                if kt == bi:
                    p.op("pool", lambda e: e.affine_select(out=Pk[:, :].rearrange("p (g q) -> p g q", g=4), in_=Pk[:, :].rearrange("p (g q) -> p g q", g=4),
                                                           pattern=[[0, 4], [1, 128]], compare_op=ALU.is_ge, fill=0.0, base=0, channel_multiplier=-1),
                         reads=[Pk], writes=[Pk])
                p.op("pe", lambda e: e.matmul(PS_DEN[:64, :], lhsT=onesb[:, 0:64], rhs=Pk[:, :], start=(kt == 0), stop=(kt == bi)),
                     reads=[onesb, Pk], writes=[PS_DEN], same_ok=True)
                p.op("pe", lambda e: e.matmul(PS_O[:64, :], lhsT=vs[:, kt, :], rhs=Pk[:, :], start=(kt == 0), stop=(kt == bi)),
                     reads=[vs, Pk], writes=[PS_O], same_ok=True)

            slc_front(0)
            for kt in range(bi + 1):
                if kt + 1 <= bi:
                    slc_front(kt + 1)
                slc_back(kt)
            p.op("dve", lambda e: e.tensor_scalar(out=rden[:64, :], in0=PS_DEN[:64, :], scalar1=1e-30, scalar2=None, op0=ALU.max), reads=[PS_DEN], writes=[rden])
            p.op("dve", lambda e: e.reciprocal(out=rden[:64, :], in_=rden[:64, :]), reads=[rden], writes=[rden])
            p.op("pool", lambda e: e.tensor_tensor(out=rden[:64, :], in0=rden[:64, :], in1=gb[:, 1, :], op=ALU.mult), reads=[rden, gb], writes=[rden])
            p.op("dve", lambda e: e.tensor_tensor(out=rden[:64, :], in0=rden[:64, :], in1=PS_O[:64, :], op=ALU.mult), reads=[rden, PS_O], writes=[rden])
            p.op("pool", lambda e: e.tensor_tensor(out=oacc[:, :], in0=oacc[:, :], in1=rden[:64, :], op=ALU.add), reads=[oacc, rden], writes=[oacc])
            kts = list(range(max(0, bi - 4), bi + 1))

            def swa_front(kt):
                Sp = PS_S[kt % 2]
                p.op("pe", lambda e: e.matmul(Sp[:, :], lhsT=kwT[:, kt * 128:(kt + 1) * 128], rhs=QRf, start=True, stop=True), reads=[kwT, QRg], writes=[Sp])

            def swa_back(ii, kt):
                Sp = PS_S[kt % 2]
                Ek = E[kt % 2]
                p.op("act", lambda e: e.activation(out=Ek[:, :], in_=Sp[:, :], func=AF.Exp, scale=scale), reads=[Sp], writes=[Ek])
                if kt == bi:
                    p.op("pool", lambda e: e.affine_select(out=Ek[:, :].rearrange("p (g q) -> p g q", g=4), in_=Ek[:, :].rearrange("p (g q) -> p g q", g=4),
                                                           pattern=[[0, 4], [1, 128]], compare_op=ALU.is_ge, fill=0.0, base=0, channel_multiplier=-1),
                         reads=[Ek], writes=[Ek])
                if kt == bi - 4:
                    p.op("pool", lambda e: e.affine_select(out=Ek[:, :].rearrange("p (g q) -> p g q", g=4), in_=Ek[:, :].rearrange("p (g q) -> p g q", g=4),
                                                           pattern=[[0, 4], [-1, 128]], compare_op=ALU.is_ge, fill=0.0, base=0, channel_multiplier=1),
                         reads=[Ek], writes=[Ek])
                p.op("pe", lambda e: e.matmul(PS_DEN[:64, :], lhsT=onesb[:, 0:64], rhs=Ek[:, :], start=(ii == 0), stop=(ii == len(kts) - 1)),
                     reads=[onesb, Ek], writes=[PS_DEN], same_ok=True)
                p.op("pe", lambda e: e.matmul(PS_O[:64, :], lhsT=vw[:, kt, :], rhs=Ek[:, :], start=(ii == 0), stop=(ii == len(kts) - 1)),
                     reads=[vw, Ek], writes=[PS_O], same_ok=True)

            swa_front(kts[0])
            for ii, kt in enumerate(kts):
                if ii + 1 < len(kts):
                    swa_front(kts[ii + 1])
                swa_back(ii, kt)
            p.op("dve", lambda e: e.tensor_scalar(out=rden[:64, :], in0=PS_DEN[:64, :], scalar1=1e-30, scalar2=None, op0=ALU.max), reads=[PS_DEN], writes=[rden])
            p.op("dve", lambda e: e.reciprocal(out=rden[:64, :], in_=rden[:64, :]), reads=[rden], writes=[rden])
            p.op("pool", lambda e: e.tensor_tensor(out=rden[:64, :], in0=rden[:64, :], in1=gb[:, 2, :], op=ALU.mult), reads=[rden, gb], writes=[rden])
            p.op("dve", lambda e: e.tensor_tensor(out=rden[:64, :], in0=rden[:64, :], in1=PS_O[:64, :], op=ALU.mult), reads=[rden, PS_O], writes=[rden])
            p.op("dve", lambda e: e.tensor_tensor(out=ob[:, :], in0=oacc[:, :], in1=rden[:64, :], op=ALU.add), reads=[oacc, rden], writes=[ob])
            nq = sq["nq"]
            p.dma("pool", c.OT[4 * hk:4 * hk + 4, :, o0:o0 + nq].rearrange("h p t -> p h t"),
                  ob[:, :].rearrange("p (g q) -> p g q", g=4)[:, :, 0:nq], reads=[ob], writes=[c.OT])


def phase_nsa2(p, c, li, j):
    S = c.S
    with p.scope():
        f32c = lambda nm, shape, src: (lambda t: (p.dma("sp", t[tuple(slice(None) for _ in shape)], src, writes=[t]), t)[1])(p.sb(nm, shape, F32))
        wck = p.sb("wck", [128, 32, 64], BF16)
        wcv = p.sb("wcv", [128, 32, 64], BF16)
        p.dma("pool", wck[:, :, :], c.ns_wck[j], writes=[wck])
        p.dma("pool", wcv[:, :, :], c.ns_wcv[j], writes=[wcv])
        pekT = p.sb("pekT", [64, 32], BF16)
        pevT = p.sb("pevT", [64, 32], BF16)
        p.dma("pool", pekT[:, :], c.ns_pekT[j], writes=[pekT])
        p.dma("pool", pevT[:, :], c.ns_pevT[j], writes=[pevT])
        g0 = p.sb("g0", [64, 1], F32)
        p.dma("sp", g0[:, :], c.ns_gains[j, 0:64, 1:2], writes=[g0], allow_slow_non_contiguous=True)
        ones64 = p.sb("ones64", [64, 64], F32)
        p.op("dve", lambda e: e.memset(ones64[:, :], 1.0 / 64), writes=[ones64])
        onesb = p.sb("onesb", [128, 128], BF16)
        p.op("dve", lambda e: e.memset(onesb[:, :], 1.0), writes=[onesb])
        identb = p.sb("identb", [128, 128], BF16)
        p.op("dve", lambda e: e.tensor_copy(out=identb[:, :], in_=c.ident[:, :]), reads=[c.ident], writes=[identb])
        wtab = p.sb("wtab", [128, 256], F32)
        p.dma("sp", wtab[:, :], c.ns_wtab[:, :], writes=[wtab])
        c2s = p.sb("c2s", [128, 4, 128], BF16)
        p.dma("pool", c2s[:, :, :], c.ns_c2s[:, :, :], writes=[c2s])
        sel48 = p.sb("sel48", [48, 48, 64], F32)
        p.dma("sp", sel48[:, :, :], c.ns_sel48[:, :, :], writes=[sel48])
        tiny = p.sb("tiny", [128, 2], F32)
        p.op("dve", lambda e: e.memset(tiny[:, 0:1], 1e-30), writes=[tiny])
        p.op("dve", lambda e: e.memset(tiny[:, 1:2], NORM_EPS), writes=[tiny])
        TMAX = max(S, PAST + 128)
        kcs = p.sb("kcs", [64, TMAX], BF16)
        vcs = p.sb("vcs", [64, TMAX], BF16)
        ksT = p.sb("ksT", [64, TMAX], BF16)
        kwT = p.sb("kwT", [64, TMAX], BF16)
        vs = p.sb("vs", [128, TMAX // 128, 64], BF16)
        vw = p.sb("vw", [128, TMAX // 128, 64], BF16)
        kcmpT = p.sb("kcmpT", [64, 512], BF16)
        vcmp = p.sb("vcmp", [128, 4, 64], BF16)
        Qg = [p.sb(f"Qg{i}", [64, 4, 128], BF16) for i in range(2)]
        QRg = [p.sb(f"QRg{i}", [64, 4, 128], BF16) for i in range(2)]
        Gg = [p.sb(f"Gg{i}", [48, 128], F32) for i in range(2)]
        Pc = p.sb("Pc", [128, 4, 512], BF16)
        Pn = p.sb("Pn", [128, 4, 512], BF16)
        E = [p.sb(f"E{i}", [128, 512], BF16) for i in range(2)]
        Pm = [p.sb(f"Pm{i}", [128, 512], BF16) for i in range(2)]
        sc = p.sb("sc", [128, 136], F32)
        sc2 = p.sb("sc2", [128, 136], F32)
        m8 = p.sb("m8", [128, 8], F32)
        selb = p.sb("selb", [128, 136], BF16)
        selx = p.sb("selx", [128, TMAX], BF16)
        rden = p.sb("rden", [128, 512], F32)
        gb = p.sb("gb", [64, 3, 512], F32)
        oacc = p.sb("oacc", [64, 512], F32)
        ob = p.sb("ob", [64, 512], BF16)
        sqk = p.sb("sqk", [64, 512], F32)
        rsk = p.sb("rsk", [64, 512], F32)
        pet = p.sb("pet", [64, 1], F32)
        pvr = p.sb("pvr", [1, 64], BF16)
        PS_S = [p.ps(f"PS_S{i}", [128, 512], F32) for i in range(2)]
        PS_T = [p.ps(f"PS_T{i}", [128, 1024], BF16) for i in range(2)]
        PS_DEN = p.ps("PS_DEN", [128, 512], F32)
        PS_O = p.ps("PS_O", [128, 512], F32)
        PS_IMP = p.ps("PS_IMP", [128, 512], F32)
        PS_X = [p.ps("PS_X0", [128, 512], F32), PS_IMP]
        for l in range(32):
            p.op("pe", lambda e: e.matmul(PS_X[0][:64, 0:1], lhsT=wck[0:64, l, :], rhs=pekT[:, l:l + 1], start=(l == 0), stop=(l == 31)),
                 reads=[wck, pekT], writes=[PS_X[0]], same_ok=True)
        p.op("dve", lambda e: e.tensor_copy(out=pet[:, :], in_=PS_X[0][:64, 0:1]), reads=[PS_X[0]], writes=[pet])
        for l in range(32):
            p.op("pe", lambda e: e.matmul(PS_X[1][0:1, 0:64], lhsT=pevT[:, l:l + 1], rhs=wcv[0:64, l, :], start=(l == 0), stop=(l == 31)),
                 reads=[wcv, pevT], writes=[PS_X[1]], same_ok=True)
        p.op("dve", lambda e: e.tensor_copy(out=pvr[:, :], in_=PS_X[1][0:1, 0:64]), reads=[PS_X[1]], writes=[pvr])
        env = dict(consts=(wck, wcv, pekT, pevT, g0, ones64, onesb, identb, wtab, c2s, sel48, tiny),
                   tiles=(kcs, vcs, ksT, kwT, vs, vw, kcmpT, vcmp, Qg, QRg, Gg, Pc, Pn, E, Pm, sc, sc2, m8, selb, selx, rden, gb, oacc, ob, sqk, rsk, pet, pvr),
                   psum=(PS_S, PS_T, PS_DEN, PS_O, PS_IMP, PS_X[0]))
        sqp = dict(T=S, ncmp=S // 16 - 1, nq=128, KCT=c.KCT, VCT=c.VCT, KST=c.KST, KWT=c.KWT, VS=c.VS, VW=c.VW, QT=c.QT, QRT=c.QRT, GT=c.GTn,
                   KCT_t=c.KCT, VCT_t=c.VCT, KST_t=c.KST, KWT_t=c.KWT, VS_t=c.VS, VW_t=c.VW, QT_t=c.QT, QRT_t=c.QRT, GT_t=c.GTn,
                   blocks=[(bi, bi * 128, bi * 128) for bi in range(S // 128)])
        nsa_seq(p, c, j, env, sqp)
        TV = PAST + 128
        iof = p.sb("iof", [128, 1], F32)
        p.dma("sp", iof[:, :], c.ns_iota[:, :], writes=[iof])
        ptb = p.sb("ptb", [128, 64], I32)
        ptf = p.sb("ptf", [128, 64], F32)
        idx = p.sb("idx", [128, 64], I32)
        pgs = [p.sb(f"pgs{i}", [128, 256], BF16) for i in range(4)]
        stg = [p.sb(f"stg{i}", [128, 1024], BF16) for i in range(2)]
        zz = p.sb("zz", [128, 512], BF16)
        zf = p.sb("zf", [48, 128], F32)
        p.op("dve", lambda e: e.memset(zz[:, :], 0.0), writes=[zz])
        p.op("dve", lambda e: e.memset(zf[:, :], 0.0), writes=[zf])
        for tns in (c.KCTv, c.VCTv, c.KSTv, c.KWTv):
            for chn in range(2):
                p.dma("sp", tns[chn, :, PAST:TV], zz[:, 0:128], reads=[zz], writes=[tns])
        for tns in (c.VSv, c.VWv):
            p.dma("sp", tns[PAST:TV, :], zz[:, 0:256], reads=[zz], writes=[tns])
        for tns in (c.QTv, c.QRTv):
            for chn in range(8):
                p.dma("sp", tns[chn, :, :], zz[:, 0:128], reads=[zz], writes=[tns])
        p.dma("sp", c.GTv[:, :], zf[:, :], reads=[zf], writes=[c.GTv])
        for b in range(SB):
            p.dma("sp", ptb[:, :], c.ns_pt[b].partition_broadcast(128), writes=[ptb])
            p.op("dve", lambda e: e.tensor_copy(out=ptf[:, :], in_=ptb[:, :]), reads=[ptb], writes=[ptf])
            p.op("dve", lambda e: e.tensor_scalar(out=ptf[:, :], in0=ptf[:, :], scalar1=128.0, scalar2=iof[:, 0:1], op0=ALU.mult, op1=ALU.add),
                 reads=[ptf, iof], writes=[ptf])
            p.op("dve", lambda e: e.tensor_copy(out=idx[:, :], in_=ptf[:, :]), reads=[ptf], writes=[idx])
            igrp = 0
            for pool_d, dst in ((c.ns_cmp_k, c.KCTv), (c.ns_cmp_v, c.VCTv), (c.ns_slc_k, c.KSTv)):
                for pg4 in range(16):
                    sg_ = stg[igrp % 2]
                    igrp += 1
                    for q in range(4):
                        k = pg4 * 4 + q
                        p.idma(pgs[q][:, :], pool_d[:, :], idx[:, k:k + 1], reads=[idx, pool_d], writes=[pgs[q]])
                    for q in range(4):
                        for chn in range(2):
                            p.op("pe", lambda e: e.transpose(PS_T[0][:, (chn * 4 + q) * 128:(chn * 4 + q + 1) * 128], pgs[q][:, chn * 128:(chn + 1) * 128], identb[:, :]),
                                 reads=[pgs[q], identb], writes=[PS_T[0]], same_ok=True)
                    p.op("dve", lambda e: e.tensor_copy(out=sg_[:, :], in_=PS_T[0][:, :]), reads=[PS_T[0]], writes=[sg_])
                    for chn in range(2):
                        p.dma("sp", dst[chn, :, pg4 * 512:(pg4 + 1) * 512], sg_[:, chn * 512:(chn + 1) * 512], reads=[sg_], writes=[dst])
            for k in range(64):
                pq = pgs[k % 4]
                p.idma(pq[:, :], c.ns_slc_v[:, :], idx[:, k:k + 1], reads=[idx, c.ns_slc_v], writes=[pq])
                p.dma("sp", c.VSv[k * 128:(k + 1) * 128, :], pq[:, :], reads=[pq], writes=[c.VSv])
            sg_ = stg[igrp % 2]
            igrp += 1
            for q in range(4):
                p.dma("pool", pgs[q][:, :], c.ns_swa_k[j, b, q * 128:(q + 1) * 128, :], writes=[pgs[q]])
            for q in range(4):
                for chn in range(2):
                    p.op("pe", lambda e: e.transpose(PS_T[0][:, (chn * 4 + q) * 128:(chn * 4 + q + 1) * 128], pgs[q][:, chn * 128:(chn + 1) * 128], identb[:, :]),
                         reads=[pgs[q], identb], writes=[PS_T[0]], same_ok=True)
            p.op("dve", lambda e: e.tensor_copy(out=sg_[:, :], in_=PS_T[0][:, :]), reads=[PS_T[0]], writes=[sg_])
            for chn in range(2):
                p.dma("sp", c.KWTv[chn, :, PAST - 512:PAST], sg_[:, chn * 512:(chn + 1) * 512], reads=[sg_], writes=[c.KWTv])
            for q in range(4):
                p.dma("pool", pgs[q][:, :], c.ns_swa_v[j, b, q * 128:(q + 1) * 128, :], writes=[pgs[q]])
                p.dma("sp", c.VWv[PAST - 512 + q * 128:PAST - 512 + (q + 1) * 128, :], pgs[q][:, :], reads=[pgs[q]], writes=[c.VWv])
            s0 = S + b * SL
            for src, dst in ((c.KCT, c.KCTv), (c.VCT, c.VCTv), (c.KST, c.KSTv), (c.KWT, c.KWTv)):
                p.dma("sp", dst[:, :, PAST:PAST + SL], src[:, :, s0:s0 + SL], reads=[src], writes=[dst])
            for src, dst in ((c.VS, c.VSv), (c.VW, c.VWv)):
                p.dma("sp", dst[PAST:PAST + SL, :], src[s0:s0 + SL, :], reads=[src], writes=[dst])
            for src, dst in ((c.QT, c.QTv), (c.QRT, c.QRTv)):
                p.dma("sp", dst[:, :, 0:SL], src[:, :, s0:s0 + SL], reads=[src], writes=[dst])
            p.dma("sp", c.GTv[:, 0:SL], c.GTn[:, s0:s0 + SL], reads=[c.GTn], writes=[c.GTv])
            sqv = dict(T=TV, kw0=PAST - 512, ncmp=PAST // 16 - 1, nq=SL, KCT=c.KCTv, VCT=c.VCTv, KST=c.KSTv, KWT=c.KWTv, VS=c.VSv, VW=c.VWv, QT=c.QTv, QRT=c.QRTv, GT=c.GTv,
                       KCT_t=c.KCTv, VCT_t=c.VCTv, KST_t=c.KSTv, KWT_t=c.KWTv, VS_t=c.VSv, VW_t=c.VWv, QT_t=c.QTv, QRT_t=c.QRTv, GT_t=c.GTv,
                       blocks=[(PAST // 128, 0, s0)])
            nsa_seq(p, c, j, env, sqv)
    with p.scope():
        wout = p.sb("nswout", [64, 16, D], BF16)
        p.dma("pool", wout[:, :, :], c.ns_w_out[j], writes=[wout])
        hTs = [p.sb(f"hT{i}", [128, KC, 512], F32) for i in range(2)]
        ot = [p.sb(f"ot{i}", [64, 16, 512], BF16) for i in range(2)]
        pso = [p.ps(f"pso{i}", [128, 512], F32) for i in range(2)]
        io = 0
        for ti, (t0, tw) in enumerate(tiles_of(S)):
            hT, o_ = hTs[ti % 2], ot[ti % 2]
            p.dma("sp", hT[:, :, :tw], c.H[:, :, t0:t0 + tw].rearrange("k p t -> p k t"), reads=[c.H], writes=[hT])
            p.dma("sp", o_[:, :, :tw], c.OT[:, :, t0:t0 + tw].rearrange("h p t -> p h t"), reads=[c.OT], writes=[o_])
            for oc in range(KC):
                O = pso[io % 2]
                io += 1
                for h in range(16):
                    p.op("pe", lambda e: e.matmul(O[:, :tw], lhsT=wout[:, h, oc * 128:(oc + 1) * 128], rhs=o_[:, h, :tw], start=(h == 0), stop=(h == 15)),
                         reads=[wout, o_], writes=[O], same_ok=True)
                if c.dbg_mix is not None:
                    dm = p.sb("dm", [128, 512], F32) if oc == 0 and ti == 0 else dm
                    p.op("dve", lambda e: e.tensor_copy(out=dm[:, :tw], in_=O[:, :tw]), reads=[O], writes=[dm])
                    p.dma("sp", c.dbg_mix[oc, :, t0:t0 + tw], dm[:, :tw], reads=[dm], writes=[c.dbg_mix])
                p.op("dve", lambda e: e.tensor_tensor(out=hT[:, oc, :tw], in0=hT[:, oc, :tw], in1=O[:, :tw], op=ALU.add), reads=[O, hT], writes=[hT])
            p.dma("pool", c.H[:, :, t0:t0 + tw].rearrange("k p t -> p k t"), hT[:, :, :tw], reads=[hT], writes=[c.H])


def build(S, mixers=(0, 1, 2), dn_layers=(0, 3), npool=2560):
    nc = bass.Bass("TRN2", target_bir_lowering=False)
    p = Prog(nc)
    c = Ctx()
    c.S = S
    c.npool = npool
    NT = S + ST
    c.xp = p.dram("xp", [S, D], F32, "ExternalInput")
    c.xs = p.dram("xs", [ST, D], F32, "ExternalInput")
    c.w_gate = p.dram("w_gate", [DEPTH, 128, KC, FH], F32, "ExternalInput")
    c.w_up = p.dram("w_up", [DEPTH, 128, KC, FH], F32, "ExternalInput")
    c.w_down = p.dram("w_down", [DEPTH, 128, HC, D], F32, "ExternalInput")
    c.nffn_d = p.dram("nffn", [128, DEPTH, KC], F32, "ExternalInput")
    c.nmix_d = p.dram("nmix", [128, DEPTH, KC], F32, "ExternalInput")
    c.ident_d = p.dram("ident", [128, 128], F32, "ExternalInput")
    c.yp = p.dram("yp", [S, D], F32, "ExternalOutput")
    c.ys = p.dram("ys", [ST, D], F32, "ExternalOutput")
    c.H = p.dram("H", [KC, 128, NT], F32)
    c.dn_w_in = p.dram("dn_w_in", [2, 128, KC, DN_W], F32, "ExternalInput")
    c.dn_cw = p.dram("dn_cw", [2, 128, DN_CC, 4], F32, "ExternalInput")
    c.dn_ab = p.dram("dn_ab", [2, 64, 2], F32, "ExternalInput")
    c.dn_cs_in = p.dram("dn_cs_in", [2, SB * 3, 4096], F32, "ExternalInput")
    c.dn_S_in = p.dram("dn_S_in", [2, SB, 16, 128, 128], F32, "ExternalInput")
    c.dn_ng = p.dram("dn_ng", [2, 128, 1], F32, "ExternalInput")
    c.dn_w_out = p.dram("dn_w_out", [2, 128, 16, D], F32, "ExternalInput")
    offp, totp = dn_pack_layout(128, 1)
    offs, tots = dn_pack_layout(ST, SB)
    c.dnc_p = p.dram("dnc_p", [128, totp], F32, "ExternalInput")
    c.dnc_s = p.dram("dnc_s", [ST, tots], F32, "ExternalInput")
    c.dncm_p = p.dram("dncm_p", [128, 1, 128], F32, "ExternalInput")
    c.dncm_s = p.dram("dncm_s", [128, SB, ST], F32, "ExternalInput")
    c.sel16_d = p.dram("sel16", [16, 16, 128], F32, "ExternalInput")
    dk = "ExternalOutput" if os.environ.get("DNDBG") else "Internal"
    c.QKVT = p.dram("QKVT", [DN_CC, 128, NT], F32, dk)
    c.ZT = p.dram("ZT", [16, 128, NT], BF16, dk)
    c.GT = p.dram("GT", [64, NT], F32, dk)
    c.ns_w_in = p.dram("ns_w_in", [1, 128, KC, NSA_W], F32, "ExternalInput")
    c.ns_gains = p.dram("ns_gains", [1, 128, 4], F32, "ExternalInput")
    c.ns_rotT = p.dram("ns_rotT", [128, 128], F32, "ExternalInput")
    c.ns_cos = p.dram("ns_cos", [128, NT], F32, "ExternalInput")
    c.ns_sin = p.dram("ns_sin", [128, NT], F32, "ExternalInput")
    c.ns_swa_k = p.dram("ns_swa_k", [1, SB, 512, 256], F32, "ExternalInput")
    c.ns_swa_v = p.dram("ns_swa_v", [1, SB, 512, 256], F32, "ExternalInput")
    c.QT = p.dram("QT", [8, 128, NT], BF16)
    c.QRT = p.dram("QRT", [8, 128, NT], BF16)
    c.KST = p.dram("KST", [2, 128, NT], BF16)
    c.KWT = p.dram("KWT", [2, 128, NT], BF16)
    c.KCT = p.dram("KCT", [2, 128, NT], BF16)
    c.VCT = p.dram("VCT", [2, 128, NT], BF16)
    c.VS = p.dram("VS", [NT, 256], BF16)
    c.VW = p.dram("VW", [NT, 256], BF16)
    c.GTn = p.dram("GTn", [48, NT], F32)
    c.OT = p.dram("OT", [16, 64, NT], BF16)
    TV = PAST + 128
    c.KCTv = p.dram("KCTv", [2, 128, TV], BF16)
    c.VCTv = p.dram("VCTv", [2, 128, TV], BF16)
    c.KSTv = p.dram("KSTv", [2, 128, TV], BF16)
    c.KWTv = p.dram("KWTv", [2, 128, TV], BF16)
    c.VSv = p.dram("VSv", [TV, 256], BF16)
    c.VWv = p.dram("VWv", [TV, 256], BF16)
    c.QTv = p.dram("QTv", [8, 128, 128], BF16)
    c.QRTv = p.dram("QRTv", [8, 128, 128], BF16)
    c.GTv = p.dram("GTv", [48, 128], F32)
    npool = c.npool
    c.ns_cmp_k = p.dram("ns_cmp_k", [npool * 128, 256], F32, "ExternalInput")
    c.ns_cmp_v = p.dram("ns_cmp_v", [npool * 128, 256], F32, "ExternalInput")
    c.ns_slc_k = p.dram("ns_slc_k", [npool * 128, 256], F32, "ExternalInput")
    c.ns_slc_v = p.dram("ns_slc_v", [npool * 128, 256], F32, "ExternalInput")
    c.ns_pt = p.dram("ns_pt", [SB, 64], I32, "ExternalInput")
    c.ns_iota = p.dram("ns_iota", [128, 1], F32, "ExternalInput")
    c.ns_wck = p.dram("ns_wck", [1, 128, 32, 64], F32, "ExternalInput")
    c.ns_wcv = p.dram("ns_wcv", [1, 128, 32, 64], F32, "ExternalInput")
    c.ns_pekT = p.dram("ns_pekT", [1, 64, 32], F32, "ExternalInput")
    c.ns_pevT = p.dram("ns_pevT", [1, 64, 32], F32, "ExternalInput")
    c.ns_w_out = p.dram("ns_w_out", [1, 64, 16, D], F32, "ExternalInput")
    c.ns_wtab = p.dram("ns_wtab", [128, 256], F32, "ExternalInput")
    c.ns_c2s = p.dram("ns_c2s", [128, 4, 128], F32, "ExternalInput")
    c.ns_sel48 = p.dram("ns_sel48", [48, 48, 64], F32, "ExternalInput")
    c.dbg_mix = p.dram("dbg_mix", [KC, 128, NT], F32, "ExternalOutput") if os.environ.get("NSDBG") else None
    c.sg_w_in = p.dram("sg_w_in", [1, 128, KC, 2048], F32, "ExternalInput")
    c.sg_w_out = p.dram("sg_w_out", [1, 128, KC, D], F32, "ExternalInput")
    c.sg_ln_g = p.dram("sg_ln_g", [1, 128, KC], F32, "ExternalInput")
    c.sg_ln_b = p.dram("sg_ln_b", [1, 128, KC], F32, "ExternalInput")
    c.sg_wspT = p.dram("sg_wspT", [1, 128, KC, 128], F32, "ExternalInput")
    c.sg_bsp4 = p.dram("sg_bsp4", [1, KC, 512], F32, "ExternalInput")
    c.o_sgv = p.dram("o_sgv", [1, ST, D], F32, "ExternalOutput")
    c.o_dS_p = p.dram("o_dS_p", [2, 16, 128, 128], F32, "ExternalOutput")
    c.o_dS_s = p.dram("o_dS_s", [2, SB, 16, 128, 128], F32, "ExternalOutput")
    c.o_dc_p = p.dram("o_dc_p", [2, 3, 4096], F32, "ExternalOutput")
    c.o_dc_s = p.dram("o_dc_s", [2, SB, 3, 4096], F32, "ExternalOutput")
    c.o_swa_p = p.dram("o_swa_p", [2, min(512, S), 256], F32, "ExternalOutput")
    c.o_swa_s = p.dram("o_swa_s", [2, SB, 512, 256], F32, "ExternalOutput")
    c.o_new_p = p.dram("o_new_p", [4, S, 256], F32, "ExternalOutput")
    c.o_new_s = p.dram("o_new_s", [4, ST, 256], F32, "ExternalOutput")
    c.ident = p.sb("ident", [128, 128], F32)
    c.ones = p.sb("ones", [128, 128], F32)
    c.nffn = p.sb("nffn", [128, DEPTH, KC], F32)
    c.nmix = p.sb("nmix", [128, DEPTH, KC], F32)
    p.dma("sp", c.ident[:, :], c.ident_d[:, :], writes=[c.ident])
    p.dma("sp", c.nffn[:, :, :], c.nffn_d[:, :, :], writes=[c.nffn])
    p.dma("sp", c.nmix[:, :, :], c.nmix_d[:, :, :], writes=[c.nmix])
    p.op("dve", lambda e: e.memset(c.ones[:, :], 1.0), writes=[c.ones])
    c.eps_norm = p.sb("eps_norm", [128, 1], F32)
    p.op("dve", lambda e: e.memset(c.eps_norm[:, :], NORM_EPS), writes=[c.eps_norm])
    c.onesD = p.sb("onesD", [128, 128], F32)
    p.op("dve", lambda e: e.memset(c.onesD[:, :], 1.0 / D), writes=[c.onesD])
    phase_in(p, c)
    for li in range(DEPTH):
        kind, j = li % 3, li // 3
        if kind == 0 and 0 in mixers and li in dn_layers:
            phase_dn1(p, c, li, j)
            if DNSTOP >= 2:
                phase_dn2(p, c, li, j)
        if kind == 1 and 1 in mixers:
            phase_sg(p, c, li, j)
        if kind == 2 and 2 in mixers:
            phase_nsa1(p, c, li, j)
            if NSSTOP >= 2:
                phase_nsa2(p, c, li, j)
        phase_ffn(p, c, li)
    phase_out(p, c)
    p.close()
    c.ninst = p.ninst
    return nc, c


def chunked(w, nchunk):
    K, N = w.shape
    return np.ascontiguousarray(w.reshape(nchunk, 128, N).transpose(1, 0, 2))


def featmajor(v):
    L, F = v.shape
    return np.ascontiguousarray(v.reshape(L, F // 128, 128).transpose(2, 0, 1))


def sg_inputs(w_in, ln_g, ln_b, wsp, bsp, w_out):
    n = w_in.shape[0]
    return {
        "sg_w_in": np.stack([chunked(np.asarray(w_in[i]), KC) for i in range(n)]),
        "sg_w_out": np.stack([chunked(np.asarray(w_out[i]), KC) for i in range(n)]),
        "sg_ln_g": np.stack([featmajor(np.asarray(ln_g[i:i + 1]))[:, 0, :] for i in range(n)]),
        "sg_ln_b": np.stack([featmajor(np.asarray(ln_b[i:i + 1]))[:, 0, :] for i in range(n)]),
        "sg_wspT": np.ascontiguousarray(np.asarray(wsp).transpose(0, 3, 1, 2)),
        "sg_bsp4": np.ascontiguousarray(np.tile(np.asarray(bsp), (1, 1, 4))),
    }


def dn_inputs(w_in, conv_w, a_log, dt_bias, norm, w_out):
    n = w_in.shape[0]
    wi = []
    for i in range(n):
        w = np.asarray(w_in[i], np.float32)
        wp = np.zeros((D, DN_W), np.float32)
        wp[:, 0:6144] = w[:, 0:6144]
        wp[:, 6144:6160] = w[:, 6144:6160]
        wp[:, 6176:6192] = w[:, 6160:6176]
        wi.append(chunked(wp, KC))
    ab = np.zeros((n, 64, 2), np.float32)
    ab[:, 32:48, 0] = np.asarray(a_log)
    ab[:, 32:48, 1] = np.asarray(dt_bias)
    return {
        "dn_w_in": np.stack(wi),
        "dn_cw": np.ascontiguousarray(np.asarray(conv_w, np.float32).reshape(n, 4, DN_CC, 128).transpose(0, 3, 2, 1)),
        "dn_ab": ab,
        "dn_ng": np.ascontiguousarray(np.asarray(norm, np.float32).reshape(n, 128, 1)),
        "dn_w_out": np.stack([chunked(np.asarray(w_out[i], np.float32), 16) for i in range(n)]),
    }


def dn_state_inputs(state_delta, state_conv, core):
    sd = np.asarray(state_delta, np.float32)[:, core * SB:(core + 1) * SB]
    sc = np.asarray(state_conv, np.float32)[:, core * SB:(core + 1) * SB]
    return {"dn_S_in": np.ascontiguousarray(sd), "dn_cs_in": np.ascontiguousarray(sc.reshape(2, SB * 3, 4096))}


def const_inputs():
    pk_p, cm_p = dn_consts(128, 1)
    pk_s, cm_s = dn_consts(ST, SB)
    sel = np.zeros((16, 16, 128), np.float32)
    for h in range(16):
        sel[h, h, :] = 1.0
    return {"dnc_p": pk_p, "dnc_s": pk_s, "dncm_p": cm_p, "dncm_s": cm_s, "sel16": sel}


def nsa_inputs(w_in, q_norm, k_norm, S):
    f = np.float32
    n = w_in.shape[0]
    gains = np.zeros((n, 128, 4), f)
    for i in range(n):
        gains[i, :, 0] = np.tile(np.asarray(q_norm[i], f), 2)
        for kk in range(3):
            gains[i, :, 1 + kk] = np.tile(np.asarray(k_norm[i, kk], f), 2)
    rot = np.zeros((128, 128), f)
    for blk in range(2):
        for d in range(32):
            rot[blk * 64 + d + 32, blk * 64 + d] = -1.0
            rot[blk * 64 + d, blk * 64 + d + 32] = 1.0
    pos = np.concatenate([np.arange(S), np.tile(PAST + np.arange(SL), SB)]).astype(f)
    inv = (10000.0 ** (-np.arange(32, dtype=f) / 32)).astype(f)
    ang = pos[None, :] * np.tile(inv, 4)[:, None]
    return {
        "ns_w_in": np.stack([chunked(np.asarray(w_in[i], f), KC) for i in range(n)]),
        "ns_gains": gains, "ns_rotT": rot,
        "ns_cos": np.cos(ang).astype(f), "ns_sin": np.sin(ang).astype(f),
    }


def nsa_cache_inputs(swa_k, swa_v, pools, page_table, core):
    f = np.float32
    sl = slice(core * SB, (core + 1) * SB)
    d = {"ns_swa_k": np.ascontiguousarray(np.asarray(swa_k, f)[:, sl].reshape(1, SB, 512, 256)),
         "ns_swa_v": np.ascontiguousarray(np.asarray(swa_v, f)[:, sl].reshape(1, SB, 512, 256)),
         "ns_pt": np.ascontiguousarray(np.asarray(page_table, np.int32)[sl]),
         "ns_iota": np.arange(128, dtype=f).reshape(128, 1)}
    for nm in ("cmp_k", "cmp_v", "slc_k", "slc_v"):
        a = np.asarray(pools["cache_" + nm], f)
        d["ns_" + nm] = a.reshape(a.shape[1] * 128, 256)
    return d


def nsa_attn_inputs(pe_k, pe_v, w_ck, w_cv, w_out, S):
    f = np.float32
    n = np.asarray(w_ck).shape[0]
    def wl(w):
        a = np.asarray(w, f).reshape(n, 32, 64, 64).transpose(0, 2, 1, 3)
        return np.ascontiguousarray(np.concatenate([a, a], 1))
    d = {"ns_wck": wl(w_ck), "ns_wcv": wl(w_cv),
         "ns_pekT": np.ascontiguousarray(np.asarray(pe_k, f).transpose(0, 2, 1)),
         "ns_pevT": np.ascontiguousarray(np.asarray(pe_v, f).transpose(0, 2, 1)),
         "ns_w_out": np.ascontiguousarray(np.asarray(w_out, f).reshape(n, 16, 64, D).transpose(0, 2, 1, 3))}
    d.update(nsa_consts())
    return d


def kernel(**inp):
    x_prompt = np.asarray(inp["x_prompt"], np.float32)
    x_sample = np.asarray(inp["x_sample"], np.float32)
    B, S, _ = x_prompt.shape
    npool = int(np.asarray(inp["cache_cmp_k"]).shape[1])
    nc, c = build(S, mixers=MIXERS, npool=npool)
    shared = {
        "w_gate": np.stack([chunked(np.asarray(inp["ffn_w_gate"][i]), KC) for i in range(DEPTH)]),
        "w_up": np.stack([chunked(np.asarray(inp["ffn_w_up"][i]), KC) for i in range(DEPTH)]),
        "w_down": np.stack([chunked(np.asarray(inp["ffn_w_down"][i]), HC) for i in range(DEPTH)]),
        "nffn": featmajor(np.asarray(inp["norm_ffn"], np.float32)),
        "nmix": featmajor(np.asarray(inp["norm_mix"], np.float32)),
        "ident": np.eye(128, dtype=np.float32),
    }
    shared.update(sg_inputs(inp["sg_w_in"], inp["sg_ln_g"], inp["sg_ln_b"], inp["sg_w_spatial"], inp["sg_b_spatial"],
                            inp["sg_w_out"]))
    shared.update(dn_inputs(inp["dn_w_in"], inp["dn_conv_w"], inp["dn_a_log"], inp["dn_dt_bias"], inp["dn_norm"], inp["dn_w_out"]))
    shared.update(const_inputs())
    shared.update(nsa_inputs(np.asarray(inp["nsa_w_in"]), np.asarray(inp["nsa_q_norm"]), np.asarray(inp["nsa_k_norm"]), S))
    shared.update(nsa_attn_inputs(inp["nsa_cmp_pe_k"], inp["nsa_cmp_pe_v"], inp["nsa_cmp_w_k"], inp["nsa_cmp_w_v"], inp["nsa_w_out"], S))
    pools = {k: inp[k] for k in ("cache_cmp_k", "cache_cmp_v", "cache_slc_k", "cache_slc_v")}
    in_maps = []
    for core in range(NCORES):
        m = dict(shared)
        m["xp"] = np.ascontiguousarray(x_prompt[core % B])
        m["xs"] = np.ascontiguousarray(x_sample[core * SB:(core + 1) * SB].reshape(ST, D))
        m.update(dn_state_inputs(inp["state_delta"], inp["state_conv"], core))
        cc = nsa_cache_inputs(inp["cache_swa_k"], inp["cache_swa_v"], pools, inp["page_table"], core)
        if core > 0:
            for nm in ("ns_cmp_k", "ns_cmp_v", "ns_slc_k", "ns_slc_v"):
                cc[nm] = in_maps[0][nm]
        m.update(cc)
        in_maps.append(m)
    res = run_bass_kernel_spmd(nc, in_maps, core_ids=list(range(NCORES)))
    r = res.results
    cat = np.concatenate
    NB = NCORES * SB
    y_prompt = np.stack([r[b]["yp"] for b in range(B)])
    y_sample = cat([r[k]["ys"].reshape(SB, SL, D) for k in range(NCORES)], 0)
    dS_p = np.stack([r[b]["o_dS_p"] for b in range(B)], 1)
    dS_s = cat([r[k]["o_dS_s"] for k in range(NCORES)], 1)
    dc_p = np.stack([r[b]["o_dc_p"] for b in range(B)], 1)
    dc_s = cat([r[k]["o_dc_s"] for k in range(NCORES)], 1)
    sgv = cat([r[k]["o_sgv"].reshape(1, SB, SL, D) for k in range(NCORES)], 1)
    W = r[0]["o_swa_p"].shape[1]
    swa_p = [np.stack([r[b]["o_swa_p"][i] for b in range(B)]).reshape(1, B, W, 4, 64) for i in range(2)]
    swa_s = [cat([r[k]["o_swa_s"][i] for k in range(NCORES)], 0).reshape(1, NB, 512, 4, 64) for i in range(2)]
    new_p = [np.stack([r[b]["o_new_p"][i] for b in range(B)]).reshape(1, B, S, 4, 64) for i in range(4)]
    new_s = [cat([r[k]["o_new_s"][i].reshape(SB, SL, 4, 64) for k in range(NCORES)], 0).reshape(1, NB, SL, 4, 64) for i in range(4)]
    return (y_prompt, y_sample, dS_p, dS_s, dc_p, dc_s, sgv, swa_p[0], swa_p[1], swa_s[0], swa_s[1],
            new_p[0], new_p[1], new_p[2], new_p[3], new_s[0], new_s[1], new_s[2], new_s[3])
```
